# Optimizing a Trainium2 kernel written in Bass

```python
import math
import jax, jax.numpy as jnp
from jax import lax
import numpy as np

D_MODEL = 1024
BATCH = 2
SEQ = 16384
DEPTH = 4

CHUNK = 64
Q_BLOCK = 128
LRU_WIDTH = 256
LRU_HEADS = 4
LRU_HEAD_DIM = LRU_WIDTH // LRU_HEADS
CONV_WIDTH = 4
LRU_C = 8.0
SG_WIDTH = 256
SG_GROUPS = 4
SG_GROUP_DIM = SG_WIDTH // SG_GROUPS
SG_LEN = 128
ATTN_HEADS = 4
ATTN_QK_DIM = 64
ATTN_V_DIM = 2 * ATTN_QK_DIM
ATTN_QK_TOTAL = ATTN_HEADS * 2 * ATTN_QK_DIM
ATTN_WIDTH = ATTN_HEADS * ATTN_V_DIM
N_BUCKETS = 32
MAX_DISTANCE = 2048
N_BRANCH = 3
FFN_HIDDEN = -(-8 * D_MODEL // (3 * 256)) * 256
EPS = 1e-6

_IN_SIZES = (LRU_WIDTH, LRU_WIDTH, SG_WIDTH, SG_WIDTH, ATTN_QK_TOTAL, ATTN_QK_TOTAL, ATTN_WIDTH, N_BRANCH * D_MODEL)
IN_COLS = sum(_IN_SIZES)
IN_SPLITS = tuple(int(s) for s in np.cumsum(_IN_SIZES)[:-1])

kernel_name = "hybrid_rglru_sgu_diffattn_trunk"


def rms_norm(x, g):
    x32 = x.astype(jnp.float32)
    y = x32 * lax.rsqrt(jnp.mean(x32 * x32, axis=-1, keepdims=True) + EPS)
    return (y * g.astype(jnp.float32)).astype(x.dtype)


def layer_norm(x, g, b):
    x32 = x.astype(jnp.float32)
    mu = jnp.mean(x32, axis=-1, keepdims=True)
    var = jnp.mean(jnp.square(x32 - mu), axis=-1, keepdims=True)
    y = (x32 - mu) * lax.rsqrt(var + EPS)
    return (y * g.astype(jnp.float32) + b.astype(jnp.float32)).astype(x.dtype)


def rglru_branch(xa, ga, conv_w, conv_b, wa, ba, wi, bi, lam):
    B, S, _ = xa.shape
    xp = jnp.pad(xa, ((0, 0), (CONV_WIDTH - 1, 0), (0, 0)))
    xc = conv_b
    for tap in range(CONV_WIDTH):
        xc = xc + xp[:, tap:tap + S] * conv_w[tap]
    xh = xc.reshape(B, S, LRU_HEADS, LRU_HEAD_DIM)
    r = jax.nn.sigmoid(jnp.einsum('bshi,hij->bshj', xh, wa).reshape(B, S, LRU_WIDTH) + ba)
    i = jax.nn.sigmoid(jnp.einsum('bshi,hij->bshj', xh, wi).reshape(B, S, LRU_WIDTH) + bi)
    log_a = (-LRU_C * jax.nn.softplus(-lam.astype(jnp.float32))) * r.astype(jnp.float32)
    a = jnp.exp(log_a)
    mult = jnp.sqrt(-jnp.expm1(2.0 * log_a))
    bterm = mult * (i * xc).astype(jnp.float32)

    def combine(left, right):
        a1, b1 = left
        a2, b2 = right
        return a1 * a2, a2 * b1 + b2

    _, h = lax.associative_scan(combine, (a, bterm), axis=1)
    return h.astype(xa.dtype) * jax.nn.gelu(ga)


def spatial_gating_branch(u, v, ln_g, ln_b, w_s, b_s):
    B, S, _ = v.shape
    vn = layer_norm(v, ln_g, ln_b)
    vb = vn.reshape(B, S // SG_LEN, SG_LEN, SG_GROUPS, SG_GROUP_DIM)
    causal = jnp.tril(jnp.ones((SG_LEN, SG_LEN), dtype=bool))
    ws = jnp.where(causal, w_s, jnp.zeros_like(w_s))
    mixed = jnp.einsum('gts,bnsgc->bntgc', ws, vb) + b_s.T[None, None, :, :, None]
    return u * mixed.reshape(B, S, SG_WIDTH)


def t5_bucket(rel):
    half = N_BUCKETS // 2
    max_exact = half // 2
    ret = jnp.where(rel > 0, half, 0)
    n = jnp.abs(rel)
    nf = jnp.maximum(n, 1).astype(jnp.float32)
    large = max_exact + (jnp.log(nf / max_exact) / math.log(MAX_DISTANCE / max_exact)
                         * (half - max_exact)).astype(jnp.int32)
    large = jnp.minimum(large, half - 1)
    return ret + jnp.where(n < max_exact, n, large)


def diff_attention_branch(q, k, v, q_g, k_g, lq1, lk1, lq2, lk2, sub_g, rel_bias, lam_init):
    B, S, _ = q.shape
    H, dk = ATTN_HEADS, ATTN_QK_DIM
    q = rms_norm(q.reshape(B, S, H, 2, dk), q_g) * (dk ** -0.5)
    k = rms_norm(k.reshape(B, S, H, 2, dk), k_g)
    v = v.reshape(B, S, H, ATTN_V_DIM)
    f32 = jnp.float32
    lam = (jnp.exp(jnp.sum(lq1.astype(f32) * lk1.astype(f32)))
           - jnp.exp(jnp.sum(lq2.astype(f32) * lk2.astype(f32))) + lam_init)
    nb = S // Q_BLOCK
    qb = q.reshape(B, nb, Q_BLOCK, H, 2, dk).transpose(1, 0, 2, 3, 4, 5)
    k_pos = jnp.arange(S, dtype=jnp.int32)
    k_chunk = k_pos // CHUNK

    def block(args):
        qi, blk = args
        q_pos = blk * Q_BLOCK + jnp.arange(Q_BLOCK, dtype=jnp.int32)
        bias = rel_bias[t5_bucket(k_pos[None, :] - q_pos[:, None])]
        allowed = k_chunk[None, :] <= (q_pos // CHUNK)[:, None]
        s = (jnp.einsum('bqhcd,bkhcd->bhcqk', qi, k).astype(f32)
             + bias.transpose(2, 0, 1)[None, :, None].astype(f32))
        s = jnp.where(allowed, s, -jnp.inf)
        p = jax.nn.softmax(s, axis=-1)
        attn = p[:, :, 0] - lam * p[:, :, 1]
        return jnp.einsum('bhqk,bkhe->bqhe', attn.astype(v.dtype), v)

    o = lax.map(block, (qb, jnp.arange(nb, dtype=jnp.int32)))
    o = o.transpose(1, 0, 2, 3, 4).reshape(B, S, H, ATTN_V_DIM)
    o = rms_norm(o, sub_g) * (1.0 - lam_init)
    return o.reshape(B, S, ATTN_WIDTH)


def setup_inputs(seed: int = 0) -> dict:
    key = jax.random.key(seed)
    ks = iter(jax.random.split(key, 40))
    f32 = jnp.float32

    def nrm(shape, scale):
        return jax.random.normal(next(ks), shape, f32) * scale

    def gain(shape):
        return 1.0 + nrm(shape, 0.02)

    L, D = DEPTH, D_MODEL
    u = jax.random.uniform(next(ks), (L, LRU_WIDTH), f32, 0.9, 0.999)
    s = u ** (1.0 / LRU_C)
    lru_lambda = jnp.log(s) - jnp.log1p(-s)
    return {
        "x": nrm((BATCH, SEQ, D), 1.0),
        "ln1_g": gain((L, D)),
        "w_in": nrm((L, D, IN_COLS), D ** -0.5),
        "b_gate": nrm((L, N_BRANCH * D), 0.01),
        "conv_w": nrm((L, CONV_WIDTH, LRU_WIDTH), CONV_WIDTH ** -0.5),
        "conv_b": nrm((L, LRU_WIDTH), 0.01),
        "lru_wa": nrm((L, LRU_HEADS, LRU_HEAD_DIM, LRU_HEAD_DIM), LRU_HEAD_DIM ** -0.5),
        "lru_ba": nrm((L, LRU_WIDTH), 0.01),
        "lru_wi": nrm((L, LRU_HEADS, LRU_HEAD_DIM, LRU_HEAD_DIM), LRU_HEAD_DIM ** -0.5),
        "lru_bi": nrm((L, LRU_WIDTH), 0.01),
        "lru_lambda": lru_lambda,
        "sg_ln_g": gain((L, SG_WIDTH)),
        "sg_ln_b": nrm((L, SG_WIDTH), 0.01),
        "sg_w": nrm((L, SG_GROUPS, SG_LEN, SG_LEN), SG_LEN ** -0.5),
        "sg_b": 1.0 + nrm((L, SG_GROUPS, SG_LEN), 0.01),
        "q_norm_g": gain((L, ATTN_QK_DIM)),
        "k_norm_g": gain((L, ATTN_QK_DIM)),
        "lambda_q1": nrm((L, ATTN_QK_DIM), 0.1),
        "lambda_k1": nrm((L, ATTN_QK_DIM), 0.1),
        "lambda_q2": nrm((L, ATTN_QK_DIM), 0.1),
        "lambda_k2": nrm((L, ATTN_QK_DIM), 0.1),
        "subln_g": gain((L, ATTN_V_DIM)),
        "rel_bias": nrm((N_BUCKETS, ATTN_HEADS), 0.5),
        "w_pa": nrm((L, LRU_WIDTH, D), LRU_WIDTH ** -0.5),
        "w_pb": nrm((L, SG_WIDTH, D), SG_WIDTH ** -0.5),
        "w_pc": nrm((L, ATTN_WIDTH, D), ATTN_WIDTH ** -0.5),
        "w_o": nrm((L, D, D), D ** -0.5),
        "ln2_g": gain((L, D)),
        "w_ff_gate": nrm((L, D, FFN_HIDDEN), D ** -0.5),
        "w_ff_up": nrm((L, D, FFN_HIDDEN), D ** -0.5),
        "w_ff_down": nrm((L, FFN_HIDDEN, D), FFN_HIDDEN ** -0.5),
    }


def reference(x, ln1_g, w_in, b_gate, conv_w, conv_b, lru_wa, lru_ba, lru_wi, lru_bi, lru_lambda,
              sg_ln_g, sg_ln_b, sg_w, sg_b, q_norm_g, k_norm_g, lambda_q1, lambda_k1, lambda_q2,
              lambda_k2, subln_g, rel_bias, w_pa, w_pb, w_pc, w_o, ln2_g, w_ff_gate, w_ff_up,
              w_ff_down):
    B, S, D = x.shape
    for l in range(DEPTH):
        lam_init = 0.8 - 0.6 * math.exp(-0.3 * l)
        h = rms_norm(x, ln1_g[l])
        proj = h @ w_in[l]
        xa, ga, u, v, q, k, vv, gates = jnp.split(proj, IN_SPLITS, axis=-1)
        ya = rglru_branch(xa, ga, conv_w[l], conv_b[l], lru_wa[l], lru_ba[l], lru_wi[l], lru_bi[l],
                          lru_lambda[l])
        yb = spatial_gating_branch(u, v, sg_ln_g[l], sg_ln_b[l], sg_w[l], sg_b[l])
        yc = diff_attention_branch(q, k, vv, q_norm_g[l], k_norm_g[l], lambda_q1[l], lambda_k1[l],
                                   lambda_q2[l], lambda_k2[l], subln_g[l], rel_bias, lam_init)
        g = jax.nn.sigmoid(gates + b_gate[l]).reshape(B, S, N_BRANCH, D)
        merged = (g[:, :, 0] * (ya @ w_pa[l]) + g[:, :, 1] * (yb @ w_pb[l])
                  + g[:, :, 2] * (yc @ w_pc[l]))
        x = x + merged @ w_o[l]
        h2 = rms_norm(x, ln2_g[l])
        x = x + (jax.nn.silu(h2 @ w_ff_gate[l]) * (h2 @ w_ff_up[l])) @ w_ff_down[l]
    return x
```

```python
import contextlib
import math
import numpy as np
import concourse.bass as bass
import concourse.mybir as mybir
from concourse.bass_utils import run_bass_kernel_spmd

F32 = mybir.dt.float32
BF16 = mybir.dt.bfloat16
AF = mybir.ActivationFunctionType
ALU = mybir.AluOpType
AX = mybir.AxisListType

D_MODEL = 1024
DEPTH = 4
N_CORES = 8
EPS = 1e-6
LRU_C = 8.0
FFN_HIDDEN = 2816
IN_COLS = 5632
TT = 512
NEG = -30000.0

STREAMS = ("pe", "act", "dve", "pool", "sp")


class Prog:
    def __init__(self, nc):
        self.nc = nc
        self.streams = {e: [] for e in STREAMS}
        self.count = {}
        self.last_write = {}
        self.readers = {}
        self.waited = {e: {} for e in STREAMS}
        self.pending = {e: [] for e in STREAMS}
        self.nops = 0
        self.use_rank = False
        self.dyn = {}

    def barrier(self):
        for e in STREAMS:
            for k, v in self.count.items():
                if self.waited[e].get(k, 0) < v:
                    self.waited[e][k] = v
                    self.pending[e].append((k, v))
        self.last_write.clear()
        self.readers.clear()

    def op(self, eng, fn, reads=(), writes=(), dma=None, inc=None):
        semkey = dma if dma is not None else eng
        if inc is None:
            inc = 16 if dma is not None else 1
        deps = {}

        def add(k, v, same_ok):
            if k == semkey and eng == "pe" and dma is None and same_ok:
                return
            if deps.get(k, 0) < v:
                deps[k] = v

        for b in reads:
            lw = self.last_write.get(b)
            if lw is not None:
                add(lw[0], lw[1], eng == "pe")
        for b in writes:
            lw = self.last_write.get(b)
            if lw is not None:
                add(lw[0], lw[1], True)
            for k, v in self.readers.get(b, {}).items():
                add(k, v, True)
        waits = self.pending[eng]
        self.pending[eng] = []
        wd = self.waited[eng]
        for k, v in deps.items():
            if wd.get(k, 0) < v:
                wd[k] = v
                waits.append((k, v))
        val = self.count.get(semkey, 0) + inc
        self.count[semkey] = val
        for b in reads:
            self.readers.setdefault(b, {})[semkey] = val
        for b in writes:
            self.last_write[b] = (semkey, val)
            self.readers[b] = {}
        self.streams[eng].append((waits, fn, semkey, inc))
        self.nops += 1

    def emit(self):
        nc = self.nc
        with contextlib.ExitStack() as es:
            sems = {k: es.enter_context(nc.semaphore("s_" + k)) for k in self.count}
            block = es.enter_context(nc.Block())
            final = list(self.count.items())

            def run(name, e):
                for waits, fn, semkey, inc in self.streams[name]:
                    for k, v in waits:
                        e.wait_ge(sems[k], v)
                    fn(e).then_inc(sems[semkey], inc)

            @block.tensor
            def _(e):
                run("pe", e)

            @block.scalar
            def _(e):
                run("act", e)

            @block.vector
            def _(e):
                run("dve", e)

            @block.gpsimd
            def _(e):
                run("pool", e)

            @block.sync
            def _(e):
                if self.use_rank:
                    r = e.snap(e.partition_id() % 4, min_val=0, max_val=3)
                    self.dyn["r"] = r
                    for name, mul in self.dyn_spec.items():
                        self.dyn[name] = e.snap(r * mul, min_val=0, max_val=3 * mul)
                run("sp", e)
                for k, v in final:
                    e.wait_ge(sems[k], v)


class Ctx:
    def __init__(self, nc, es):
        self.nc = nc
        self.es = es
        self.P = Prog(nc)
        self._n = 0

    def din(self, name, shape, dt=F32):
        return self.nc.dram_tensor(name, list(shape), dt, kind="ExternalInput").ap()

    def dout(self, name, shape, dt=F32):
        return self.nc.dram_tensor(name, list(shape), dt, kind="ExternalOutput").ap()

    def use_arena(self, nbytes):
        self.arena = self.es.enter_context(self.nc.sbuf_tensor("arena", [128, nbytes // 2], BF16))
        self.arena_n = nbytes // 2
        self.off = 0
        self.psum_t = self.es.enter_context(self.nc.psum_tensor("psum", [128, 8, 512], F32))

    def arena_reset(self):
        self.off = 0

    def sb(self, name, shape, dt=F32):
        if getattr(self, "arena", None) is None:
            return self.es.enter_context(self.nc.sbuf_tensor(name, list(shape), dt))[:]
        shape = list(shape)
        free = 1
        for d in shape[1:]:
            free *= d
        n16 = free * (2 if dt == F32 else 1)
        n16 = (n16 + 31) // 32 * 32
        assert self.off + n16 <= self.arena_n, "arena overflow at %s: %d + %d > %d" % (name, self.off, n16, self.arena_n)
        ap = self.arena[0:shape[0], self.off:self.off + free * (2 if dt == F32 else 1)]
        self.off += n16
        if dt == F32:
            ap = ap.bitcast(F32)
        if len(shape) > 2:
            names = " ".join("d%d" % i for i in range(len(shape) - 1))
            kw = {"d%d" % i: shape[1 + i] for i in range(len(shape) - 1)}
            ap = ap.rearrange("p (%s) -> p %s" % (names, names), **kw)
        return ap

    def ps(self, name, shape, dt=F32):
        if getattr(self, "arena", None) is None:
            return self.es.enter_context(self.nc.psum_tensor(name, list(shape), dt))[:]
        shape = list(shape)
        ap = self.psum_t[:]
        if shape == [128, 8, 512]:
            return ap
        assert shape == [128, 8, 2, 256], shape
        return ap.rearrange("p b (h n) -> p b h n", h=2)


class Rot:
    def __init__(self, name, n):
        self.name, self.n, self.i = name, n, 0

    def next(self):
        i = self.i % self.n
        self.i += 1
        return i, (self.name, i)


def load_weight_bf16(C, dst, src, K, N, key, stage, stage_rot, scale_ap=None, engines=("act", "pool"),
                     colblk=1536, dma_eng="sp", scale_key=None, also_writes=()):
    P = C.P
    extra = [scale_key] if scale_key is not None else []
    kc_n = K // 128
    ei = 0
    for kc in range(kc_n):
        for c0 in range(0, N, colblk):
            n = min(colblk, N - c0)
            si, skey = stage_rot.next()
            st = stage[:, si, 0:n]
            srcap = src[kc * 128:(kc + 1) * 128, c0:c0 + n]
            P.op(dma_eng, lambda e, st=st, srcap=srcap: e.dma_start(out=st, in_=srcap),
                 writes=[skey], dma="wld%d" % si)
            eng = engines[ei % len(engines)]
            ei += 1
            d = dst[:, kc, c0:c0 + n]
            if eng == "act":
                if scale_ap is not None:
                    sc = scale_ap[:, kc:kc + 1]
                    fn = lambda e, d=d, st=st, sc=sc: e.activation(out=d, in_=st, func=AF.Identity, scale=sc)
                else:
                    fn = lambda e, d=d, st=st: e.copy(out=d, in_=st)
            else:
                if scale_ap is not None:
                    sc = scale_ap[:, kc:kc + 1]
                    fn = lambda e, d=d, st=st, sc=sc: e.tensor_scalar(out=d, in0=st, scalar1=sc, scalar2=None,
                                                                      op0=ALU.mult)
                else:
                    fn = lambda e, d=d, st=st: e.tensor_copy(out=d, in_=st)
            P.op(eng, fn, reads=[skey] + extra, writes=[(key, eng)] + list(also_writes))
    return [(key, e) for e in engines]


def rstd_from_ps(P, psb, pskey, rstd, rkey, n, eps_n):
    P.op("act", lambda e: e.activation(out=rstd[:, 0:n], in_=psb[:, 0:n], func=AF.Ln, bias=eps_n, scale=1.0),
         reads=[pskey], writes=[rkey])
    P.op("act", lambda e: e.activation(out=rstd[:, 0:n], in_=rstd[:, 0:n], func=AF.Exp, scale=-0.5),
         reads=[rkey], writes=[rkey])


def rms_tile(C, xt, xkey, hT, hkey, sq, sqkey, ones, psb, pskey, rstd, rkey, n, eps_n):
    P = C.P
    P.op("pool", lambda e: e.tensor_tensor(out=sq[:, :, 0:n], in0=xt[:, :, 0:n], in1=xt[:, :, 0:n], op=ALU.mult),
         reads=[xkey], writes=[sqkey])

    def mm(e):
        for c in range(8):
            i = e.matmul(psb[:, 0:n], lhsT=ones[:], rhs=sq[:, c, 0:n], start=(c == 0), stop=(c == 7))
        return i
    P.op("pe", mm, reads=[sqkey, "ones"], writes=[pskey])
    rstd_from_ps(P, psb, pskey, rstd, rkey, n, eps_n)
    rb = rstd[:, 0:n].unsqueeze(1).broadcast_to([128, 8, n])
    P.op("dve", lambda e: e.tensor_tensor(out=hT[:, :, 0:n], in0=xt[:, :, 0:n], in1=rb, op=ALU.mult),
         reads=[xkey, rkey], writes=[hkey])


def build_L1(NT):
    nc = bass.Bass("TRN2", target_bir_lowering=False)
    with contextlib.ExitStack() as es:
        C = Ctx(nc, es)
        io = {
            "xT": C.din("xT", [1024, NT]), "g1": C.din("g1", [128, 8]), "w_in": C.din("w_in", [1024, IN_COLS]),
            "sgg": C.din("sgg", [128, 256]), "sgb": C.din("sgb", [128, 256]), "wsT": C.din("wsT", [128, 4, 128]),
            "tril": C.din("tril", [128, 128]), "bsb": C.din("bsb", [128, 2, 128]), "gqk": C.din("gqk", [128, 2]),
            "v": C.dout("v", [4, NT, 128], BF16), "ybT": C.dout("ybT", [2, 128, NT], BF16),
        }
        qk_t = C.dout("qk", [4, 2, 128, NT], BF16)
        xg_t = C.dout("xg", [4, 128, NT], F32)
        io["qk"] = lambda h, which: qk_t[h, which]
        io["xg"] = lambda ch: xg_t[ch]
        emit_L1(C, io, NT)
        C.P.emit()
    return nc


def emit_L1(C, io, NT):
    ntile = NT // TT
    if True:
        P = C.P
        xT, g1, w_in, sgg, sgb, wsT, tril, bsb, gqk = (io[k] for k in ("xT", "g1", "w_in", "sgg", "sgb", "wsT", "tril", "bsb", "gqk"))
        qk_o, v_o, xg_o, yb_o = io["qk"], io["v"], io["xg"], io["ybT"]

        NW = 2560
        wb = C.sb("wb", [128, 8, NW], BF16)
        stage = C.sb("stage", [128, 3, 1536], F32)
        xt = C.sb("xt", [128, 2, 8, TT], F32)
        sq = C.sb("sq", [128, 8, TT], BF16)
        hT = C.sb("hT", [128, 2, 8, TT], BF16)
        rstd = C.sb("rstd", [128, TT], F32)
        ones = C.sb("ones", [128, 128], BF16)
        bones = C.sb("bones", [128, 128], BF16)
        g1s = C.sb("g1s", [128, 8], F32)
        sgg_s = C.sb("sgg_s", [128, 256], F32)
        sgb_s = C.sb("sgb_s", [128, 256], F32)
        wsT_f = C.sb("wsT_f", [128, 4, 128], F32)
        tril_s = C.sb("tril_s", [128, 128], F32)
        wsT_b = C.sb("wsT_b", [128, 4, 128], BF16)
        bsb_s = C.sb("bsb_s", [128, 2, 128], F32)
        gqk_s = C.sb("gqk_s", [128, 2], F32)
        xg_s = C.sb("xg_s", [128, 2, TT], F32)
        u_s = C.sb("u_s", [128, 2, TT], F32)
        qsq = C.sb("qsq", [128, 2, TT], BF16)
        qr = C.sb("qr", [128, 2, TT], F32)
        qn = C.sb("qn", [128, 3, TT], BF16)
        stats = C.sb("stats", [128, 2, 6], F32)
        mv = C.sb("mv", [128, 2, 2], F32)
        vr = C.sb("vr", [128, 2, 1], F32)
        vtmp = C.sb("vtmp", [128, 2, 256], F32)
        vn = C.sb("vn", [128, 4, 256], BF16)
        vv_s = C.sb("vv_s", [128, 2, 4, 512], BF16)
        mtmp = C.sb("mtmp", [128, 2, TT], F32)
        yb_s = C.sb("yb_s", [128, 2, TT], BF16)
        psum = C.ps("psum", [128, 8, TT], F32)
        psr = Rot("ps", 8)

        P.op("dve", lambda e: e.memset(ones[:], 1.0), writes=["ones"])
        P.op("pool", lambda e: e.memset(bones[:], 0.0), writes=["bones"])
        P.op("pool", lambda e: e.memset(bones[0:64, 0:64], 1.0), writes=["bones"])
        P.op("pool", lambda e: e.memset(bones[64:128, 64:128], 1.0), writes=["bones"])
        for dst, src, key in ((g1s, g1, "g1s"), (sgg_s, sgg, "sgg"), (sgb_s, sgb, "sgb"), (wsT_f, wsT, "wsT_f"),
                              (tril_s, tril, "tril"), (bsb_s, bsb, "bsb"), (gqk_s, gqk, "gqk")):
            P.op("sp", lambda e, dst=dst, src=src: e.dma_start(out=dst[:], in_=src), writes=[key], dma="c_" + key)
        P.op("dve", lambda e: e.tensor_scalar(out=g1s[:], in0=g1s[:], scalar1=32.0, scalar2=None, op0=ALU.mult),
             reads=["g1s"], writes=["g1s"])
        P.op("dve", lambda e: e.tensor_scalar(out=gqk_s[:, 1:2], in0=gqk_s[:, 1:2], scalar1=8.0, scalar2=None, op0=ALU.mult),
             reads=["gqk"], writes=["gqk"])
        trb = tril_s[:].unsqueeze(1).broadcast_to([128, 4, 128])
        P.op("dve", lambda e: e.tensor_tensor(out=wsT_b[:], in0=wsT_f[:], in1=trb, op=ALU.mult),
             reads=["wsT_f", "tril"], writes=["wsT_b"])

        def load_x(t):
            b = t % 2
            src = xT[:, t * TT:(t + 1) * TT].rearrange("(c p) n -> p c n", p=128)
            for hh in range(2):
                P.op("sp", lambda e, b=b, src=src, hh=hh: e.dma_start(out=xt[:, b, 4 * hh:4 * hh + 4, :],
                                                                    in_=src[:, 4 * hh:4 * hh + 4, :]),
                     writes=[("xt", b)], dma="xld%d" % b)

        load_x(0)
        wbk = load_weight_bf16(C, wb, w_in[:, 0:NW], 1024, NW, "wb", stage, Rot("stage", 3), scale_ap=g1s,
                               engines=("act", "pool"), colblk=1280, scale_key="g1s")

        def mm_fm(bank, col0, b):
            def fn(e):
                for k in range(8):
                    i = e.matmul(psum[:, bank, :], lhsT=wb[:, k, col0:col0 + 128], rhs=hT[:, b, k, :],
                                 start=(k == 0), stop=(k == 7))
                return i
            return fn

        def mm_tm(bank, col0, ncol, b, blk):
            def fn(e):
                for k in range(8):
                    i = e.matmul(psum[:, bank, 0:ncol], lhsT=hT[:, b, k, blk * 128:(blk + 1) * 128],
                                 rhs=wb[:, k, col0:col0 + ncol], start=(k == 0), stop=(k == 7))
                return i
            return fn

        def norm(t):
            b = t % 2
            bank, pk = psr.next()
            rms_tile(C, xt[:, b], ("xt", b), hT[:, b], ("hT", b), sq, "sq", ones, psum[:, bank, :], pk,
                     rstd, "rstd", TT, 1024 * EPS)

        norm(0)
        for t in range(ntile):
            b = t % 2
            t0 = t * TT
            hk = ("hT", b)
            if t + 1 < ntile:
                load_x(t + 1)
            for ch in range(4):
                bank, pk = psr.next()
                P.op("pe", mm_fm(bank, ch * 128, b), reads=[hk] + wbk, writes=[pk])
                s = ch % 2
                P.op("act", lambda e, bank=bank, s=s: e.copy(out=xg_s[:, s, :], in_=psum[:, bank, :]),
                     reads=[pk], writes=[("xg_s", s)])
                P.op("sp", lambda e, ch=ch, s=s, t0=t0: e.dma_start(out=xg_o(ch)[:, t0:t0 + TT], in_=xg_s[:, s, :]),
                     reads=[("xg_s", s)], dma="st_xg%d" % s)
            for n in range(2):
                bank, pk = psr.next()
                P.op("pe", mm_fm(bank, 512 + n * 128, b), reads=[hk] + wbk, writes=[pk])
                P.op("act", lambda e, bank=bank, n=n: e.copy(out=u_s[:, n, :], in_=psum[:, bank, :]),
                     reads=[pk], writes=[("u_s", n)])
            for blk in range(4):
                bank, pk = psr.next()
                s = blk % 2
                P.op("pe", mm_tm(bank, 768, 256, b, blk), reads=[hk] + wbk, writes=[pk])
                P.op("dve", lambda e, bank=bank, s=s: e.bn_stats(out=stats[:, s, :], in_=psum[:, bank, 0:256]),
                     reads=[pk], writes=[("stats", s)])
                P.op("dve", lambda e, s=s: e.bn_aggr(out=mv[:, s, :], in_=stats[:, s, :]),
                     reads=[("stats", s)], writes=[("mv", s)])
                P.op("act", lambda e, s=s: e.activation(out=vr[:, s, :], in_=mv[:, s, 1:2], func=AF.Ln, bias=EPS, scale=1.0),
                     reads=[("mv", s)], writes=[("vr", s)])
                P.op("act", lambda e, s=s: e.activation(out=vr[:, s, :], in_=vr[:, s, :], func=AF.Exp, scale=-0.5),
                     reads=[("vr", s)], writes=[("vr", s)])
                P.op("dve", lambda e, bank=bank, s=s: e.tensor_scalar(
                    out=vtmp[:, s, :], in0=psum[:, bank, 0:256], scalar1=mv[:, s, 0:1], scalar2=vr[:, s, :],
                    op0=ALU.subtract, op1=ALU.mult), reads=[pk, ("mv", s), ("vr", s)], writes=[("vtmp", s)])
                P.op("pool", lambda e, s=s: e.tensor_tensor(out=vtmp[:, s, :], in0=vtmp[:, s, :], in1=sgg_s[:], op=ALU.mult),
                     reads=[("vtmp", s), "sgg"], writes=[("vtmp", s)])
                P.op("pool", lambda e, s=s, blk=blk: e.tensor_tensor(out=vn[:, blk, :], in0=vtmp[:, s, :], in1=sgb_s[:], op=ALU.add),
                     reads=[("vtmp", s), "sgb"], writes=[("vn", blk)])
            for n in range(2):
                bank, pk = psr.next()

                def mix(e, bank=bank, n=n):
                    for blk in range(4):
                        for gg in range(2):
                            g = 2 * n + gg
                            i = e.matmul(psum[gg * 64:(gg + 1) * 64, bank, blk * 128:(blk + 1) * 128],
                                         lhsT=vn[:, blk, g * 64:(g + 1) * 64], rhs=wsT_b[:, g, :],
                                         start=True, stop=True)
                    return i
                P.op("pe", mix, reads=[("vn", 0), ("vn", 1), ("vn", 2), ("vn", 3), "wsT_b"], writes=[pk])
                bsv = bsb_s[:, n, :].unsqueeze(1).broadcast_to([128, 4, 128])
                P.op("dve", lambda e, bank=bank, n=n, bsv=bsv: e.tensor_tensor(
                    out=mtmp[:, n, :].rearrange("p (b t) -> p b t", b=4),
                    in0=psum[:, bank, :].rearrange("p (b t) -> p b t", b=4), in1=bsv, op=ALU.add),
                    reads=[pk, "bsb"], writes=[("mtmp", n)])
                P.op("pool", lambda e, n=n: e.tensor_tensor(out=yb_s[:, n, :], in0=mtmp[:, n, :], in1=u_s[:, n, :], op=ALU.mult),
                     reads=[("mtmp", n), ("u_s", n)], writes=[("yb_s", n)])
                P.op("sp", lambda e, n=n, t0=t0: e.dma_start(out=yb_o[n, :, t0:t0 + TT], in_=yb_s[:, n, :]),
                     reads=[("yb_s", n)], dma="st_yb%d" % n)
            qrot = 0
            for which in range(2):
                for h in range(4):
                    col0 = 1024 + which * 512 + h * 128
                    bank, pk = psr.next()
                    bank2, pk2 = psr.next()
                    s = qrot % 2
                    s3 = qrot % 3
                    qrot += 1
                    P.op("pe", mm_fm(bank, col0, b), reads=[hk] + wbk, writes=[pk])
                    P.op("act", lambda e, bank=bank, s=s: e.activation(out=qsq[:, s, :], in_=psum[:, bank, :], func=AF.Square),
                         reads=[pk], writes=[("qsq", s)])
                    P.op("pe", lambda e, bank2=bank2, s=s: e.matmul(psum[:, bank2, :], lhsT=bones[:], rhs=qsq[:, s, :],
                                                                    start=True, stop=True),
                         reads=[("qsq", s), "bones"], writes=[pk2])
                    rstd_from_ps(P, psum[:, bank2, :], pk2, qr[:, s, :], ("qr", s), TT, 64 * EPS)
                    P.op("dve", lambda e, bank=bank, s=s, s3=s3, which=which: e.scalar_tensor_tensor(
                        out=qn[:, s3, :], in0=psum[:, bank, :], scalar=gqk_s[:, which:which + 1], in1=qr[:, s, :],
                        op0=ALU.mult, op1=ALU.mult), reads=[pk, ("qr", s), "gqk"], writes=[("qn", s3)])
                    P.op("sp", lambda e, h=h, which=which, s3=s3, t0=t0: e.dma_start(
                        out=qk_o(h, which)[:, t0:t0 + TT], in_=qn[:, s3, :]),
                        reads=[("qn", s3)], dma="st_qn%d" % s3)
            vb = t % 2
            for blk in range(4):
                bank, pk = psr.next()
                P.op("pe", mm_tm(bank, 2048, 512, b, blk), reads=[hk] + wbk, writes=[pk])
                eng = "act" if blk % 2 == 0 else "dve"
                if eng == "act":
                    fn = lambda e, bank=bank, blk=blk, vb=vb: e.copy(out=vv_s[:, vb, blk, :], in_=psum[:, bank, :])
                else:
                    fn = lambda e, bank=bank, blk=blk, vb=vb: e.tensor_copy(out=vv_s[:, vb, blk, :], in_=psum[:, bank, :])
                P.op(eng, fn, reads=[pk], writes=[("vv_s", vb, blk)])
            if t + 1 < ntile:
                norm(t + 1)
            for h in range(4):
                P.op("sp", lambda e, vb=vb, t0=t0, h=h: e.dma_start(
                    out=v_o[h, t0:t0 + TT, :].rearrange("(b p) n -> p b n", p=128), in_=vv_s[:, vb, :, h * 128:(h + 1) * 128]),
                    reads=[("vv_s", vb, k) for k in range(4)], dma="st_vv%d" % vb)


def build_L2(S):
    nc = bass.Bass("TRN2", target_bir_lowering=False)
    with contextlib.ExitStack() as es:
        C = Ctx(nc, es)
        qT = C.din("qT", [128, S], BF16)
        kT = C.din("kT", [128, S], BF16)
        v = C.din("v", [S, 128], BF16)
        xaT = C.din("xaT", [64, S])
        gaT = C.din("gaT", [64, S])
        io = {
            "nsrc": 1,
            "q_src": lambda e, j: qT, "k_src": lambda e, j: kT, "v_src": lambda e, j: v,
            "xa_src": lambda e, j: xaT, "ga_src": lambda e, j: gaT,
            "cw": C.din("cw", [64, 4]), "lvec": C.din("lvec", [64, 4]), "wa": C.din("wa", [64, 64]), "wi": C.din("wi", [64, 64]),
            "lq": C.din("lq", [128, 4, 64]), "subg": C.din("subg", [128, 128]), "rb": C.din("rb", [128, 32]),
            "idx": C.din("idx", [128, 12, 128]), "maskT": C.din("maskT", [128, 128]), "ident": C.din("ident", [128, 128]),
            "lcon": C.din("lcon", [128, 2]),
        }
        yc_t = C.dout("ycT", [128, S], BF16)
        ya_t = C.dout("yaT", [64, S], BF16)
        io["ycT"] = lambda c0, n: yc_t[:, c0:c0 + n]
        io["yaT"] = lambda c0, n: ya_t[:, c0:c0 + n]
        emit_L2(C, io, S)
        C.P.emit()
    return nc


def emit_L2(C, io, S):
    NG = S // TT
    NKT = S // 128
    TL = 512
    NCH = S // TL
    nsrc = io["nsrc"]
    NS = S // nsrc
    if True:
        P = C.P
        cw, lvec, wa, wi, lq, subg, rb, idx, maskT, ident, lcon = (io[k] for k in (
            "cw", "lvec", "wa", "wi", "lq", "subg", "rb", "idx", "maskT", "ident", "lcon"))
        yc_o, ya_o = io["ycT"], io["yaT"]

        q0T = C.sb("q0T", [128, S], BF16)
        q1T = C.sb("q1T", [128, S], BF16)
        kTs = C.sb("kTs", [128, S], BF16)
        v1 = C.sb("v1", [128, NKT, 129], BF16)
        biasT = C.sb("biasT", [128, 12, 128], BF16)
        bias_f = C.sb("bias_f", [128, 12, 128], F32)
        idx_s = C.sb("idx_s", [128, 12, 128], F32)
        btmp = C.sb("btmp", [128, 2, 12, 128], F32)
        mask_s = C.sb("mask_s", [128, 128], F32)
        id_f = C.sb("id_f", [128, 128], F32)
        id_b = C.sb("id_b", [128, 128], BF16)
        rb_s = C.sb("rb_s", [128, 32], F32)
        lq_s = C.sb("lq_s", [128, 4, 64], F32)
        lq_t = C.sb("lq_t", [128, 2, 64], F32)
        lsc = C.sb("lsc", [128, 8], F32)
        lcon_s = C.sb("lcon_s", [128, 2], F32)
        subg_s = C.sb("subg_s", [128, 128], F32)
        pt = C.sb("pt", [128, 3, 2, TT], BF16)
        accs = C.sb("accs", [128, 3, TT], F32)
        rl = C.sb("rl", [128, 8], F32)
        o_s = C.sb("o_s", [128, 4, 128], F32)
        osq = C.sb("osq", [128, 128], F32)
        ss = C.sb("ss", [128, 4], F32)
        y_s = C.sb("y_s", [128, 4, 128], BF16)
        yc_s = C.sb("yc_s", [128, 2, TT], BF16)
        cw_s = C.sb("cw_s", [64, 4], F32)
        lv_s = C.sb("lv_s", [64, 4], F32)
        csc = C.sb("csc", [64, 4], F32)
        w_f = C.sb("w_f", [64, 2, 64], F32)
        w_b = C.sb("w_b", [64, 2, 64], BF16)
        xa_s = C.sb("xa_s", [64, 2, TL + 3], F32)
        ga_s = C.sb("ga_s", [64, 2, TL], F32)
        xc = C.sb("xc", [64, TL], F32)
        xcb = C.sb("xcb", [64, TL], BF16)
        r_s = C.sb("r_s", [64, TL], F32)
        i_s = C.sb("i_s", [64, TL], F32)
        a_s = C.sb("a_s", [64, TL], F32)
        m_s = C.sb("m_s", [64, TL], F32)
        h_s = C.sb("h_s", [64, 2, TL], F32)
        g_s = C.sb("g_s", [64, TL], F32)
        ya_s = C.sb("ya_s", [64, 2, TL], BF16)
        psum = C.ps("psum", [128, 8, TT], F32)
        trp = psum[:, 7, :].bitcast(BF16)

        for dst, src, key in ((cw_s[:], cw, "cw"), (lv_s[:], lvec, "lv"), (w_f[:, 0, :], wa, "wa"), (w_f[:, 1, :], wi, "wi"),
                              (lq_s[:], lq, "lq"), (subg_s[:], subg, "subg"), (rb_s[:], rb, "rb"), (idx_s[:], idx, "idx"),
                              (mask_s[:], maskT, "mask"), (id_f[:], ident, "id_f"), (lcon_s[:], lcon, "lcon")):
            P.op("sp", lambda e, dst=dst, src=src: e.dma_start(out=dst, in_=src), writes=[key], dma="c_" + key)
        P.op("dve", lambda e: e.tensor_copy(out=id_b[:], in_=id_f[:]), reads=["id_f"], writes=["id_b"])
        P.op("dve", lambda e: e.tensor_copy(out=w_b[:], in_=w_f[:]), reads=["wa", "wi"], writes=["w_b"])
        for j in range(2):
            P.op("dve", lambda e, j=j: e.scalar_tensor_tensor(out=lq_t[:, j, :], in0=lq_s[:, 2 * j, :], scalar=1.0,
                                                              in1=lq_s[:, 2 * j + 1, :], op0=ALU.mult, op1=ALU.mult,
                                                              accum_out=lsc[:, j:j + 1]),
                 reads=["lq"], writes=[("lsc", j)])
        P.op("act", lambda e: e.activation(out=lsc[:, 2:4], in_=lsc[:, 0:2], func=AF.Exp),
             reads=[("lsc", 0), ("lsc", 1)], writes=["lsce"])
        P.op("dve", lambda e: e.tensor_tensor(out=lsc[:, 4:5], in0=lsc[:, 2:3], in1=lsc[:, 3:4], op=ALU.subtract),
             reads=["lsce"], writes=["lam"])
        P.op("dve", lambda e: e.tensor_tensor(out=lsc[:, 4:5], in0=lsc[:, 4:5], in1=lcon_s[:, 0:1], op=ALU.add),
             reads=["lam", "lcon"], writes=["lam"])
        P.op("dve", lambda e: e.tensor_scalar(out=lsc[:, 5:6], in0=lsc[:, 4:5], scalar1=-1.0, scalar2=None, op0=ALU.mult),
             reads=["lam"], writes=["nlam"])
        P.op("dve", lambda e: e.tensor_scalar(out=subg_s[:], in0=subg_s[:], scalar1=lcon_s[:, 1:2], scalar2=None, op0=ALU.mult),
             reads=["subg", "lcon"], writes=["subg"])
        P.op("act", lambda e: e.activation(out=csc[:, 0:1], in_=lv_s[:, 3:4], func=AF.Exp, scale=-1.0),
             reads=["lv"], writes=["csc0"])
        P.op("act", lambda e: e.activation(out=csc[:, 0:1], in_=csc[:, 0:1], func=AF.Ln, bias=1.0, scale=1.0),
             reads=["csc0"], writes=["csc0"])
        P.op("dve", lambda e: e.tensor_scalar(out=csc[:, 1:2], in0=csc[:, 0:1], scalar1=-LRU_C, scalar2=None, op0=ALU.mult),
             reads=["csc0"], writes=["csc"])
        P.op("dve", lambda e: e.tensor_scalar(out=csc[:, 2:3], in0=csc[:, 0:1], scalar1=-2.0 * LRU_C, scalar2=None, op0=ALU.mult),
             reads=["csc0"], writes=["csc"])

        P.op("dve", lambda e: e.memset(bias_f[:], 0.0), writes=["bias_f"])
        for bkt in range(32):
            nd = 12 if bkt < 16 else 1
            tb = bkt % 2
            P.op("pool", lambda e, bkt=bkt, nd=nd, tb=tb: e.tensor_single_scalar(
                out=btmp[:, tb, 0:nd, :], in_=idx_s[:, 0:nd, :], scalar=float(bkt), op=ALU.is_equal),
                reads=["idx"], writes=[("btmp", tb)])
            P.op("dve", lambda e, bkt=bkt, nd=nd, tb=tb: e.scalar_tensor_tensor(
                out=bias_f[:, 0:nd, :], in0=btmp[:, tb, 0:nd, :], scalar=rb_s[:, bkt:bkt + 1], in1=bias_f[:, 0:nd, :],
                op0=ALU.mult, op1=ALU.add), reads=[("btmp", tb), "rb", "bias_f"], writes=["bias_f"])
        P.op("dve", lambda e: e.tensor_tensor(out=bias_f[:, 0, :], in0=bias_f[:, 0, :], in1=mask_s[:], op=ALU.add),
             reads=["bias_f", "mask"], writes=["bias_f"])
        P.op("dve", lambda e: e.tensor_copy(out=biasT[:], in_=bias_f[:]), reads=["bias_f"], writes=["biasT"])

        for ch in range(NCH):
            b = ch % 2
            t0 = ch * TL
            js = t0 // NS
            l0 = t0 - js * NS
            if ch == 0:
                P.op("pool", lambda e: e.memset(xa_s[:, 0, 0:3], 0.0), writes=[("xa", 0)])
                P.op("sp", lambda e: e.dma_start(out=xa_s[:, 0, 3:3 + TL], in_=io["xa_src"](e, 0)[:, 0:TL]),
                     reads=io.get("xa_dep", []), writes=[("xa", 0)], dma="ld_xa0")
            else:
                P.op("pool", lambda e, b=b: e.tensor_copy(out=xa_s[:, b, 0:3], in_=xa_s[:, 1 - b, TL:TL + 3]),
                     reads=[("xa", 1 - b)], writes=[("xa", b)])
                P.op("sp", lambda e, b=b, js=js, l0=l0: e.dma_start(out=xa_s[:, b, 3:3 + TL], in_=io["xa_src"](e, js)[:, l0:l0 + TL]),
                     reads=io.get("xa_dep", []), writes=[("xa", b)], dma="ld_xa%d" % b)
            P.op("sp", lambda e, b=b, js=js, l0=l0: e.dma_start(out=ga_s[:, b, :], in_=io["ga_src"](e, js)[:, l0:l0 + TL]),
                 reads=io.get("ga_dep", []), writes=[("ga", b)], dma="ld_ga%d" % b)
            P.op("dve", lambda e, b=b: e.tensor_scalar(out=xc[:], in0=xa_s[:, b, 3:3 + TL], scalar1=cw_s[:, 3:4],
                                                       scalar2=lv_s[:, 0:1], op0=ALU.mult, op1=ALU.add),
                 reads=[("xa", b), "cw", "lv"], writes=["xc"])
            for tap in range(3):
                P.op("dve", lambda e, b=b, tap=tap: e.scalar_tensor_tensor(
                    out=xc[:], in0=xa_s[:, b, tap:tap + TL], scalar=cw_s[:, tap:tap + 1], in1=xc[:],
                    op0=ALU.mult, op1=ALU.add), reads=[("xa", b), "cw", "xc"], writes=["xc"])
            P.op("act", lambda e: e.copy(out=xcb[:], in_=xc[:]), reads=["xc"], writes=["xcb"])
            nh = TL // TT
            for gate in range(2):
                for hh in range(nh):
                    bank = gate * nh + hh
                    P.op("pe", lambda e, gate=gate, hh=hh, bank=bank: e.matmul(
                        psum[0:64, bank, :], lhsT=w_b[:, gate, :], rhs=xcb[:, hh * TT:(hh + 1) * TT], start=True, stop=True),
                        reads=["xcb", "w_b"], writes=[("st", bank // 2)])
            for hh in range(nh):
                P.op("act", lambda e, hh=hh: e.activation(out=r_s[:, hh * TT:(hh + 1) * TT], in_=psum[0:64, hh, :],
                                                          func=AF.Sigmoid, bias=lv_s[:, 1:2], scale=1.0),
                     reads=[("st", hh // 2), "lv"], writes=["r_s"])
            for hh in range(nh):
                P.op("act", lambda e, hh=hh: e.activation(out=i_s[:, hh * TT:(hh + 1) * TT], in_=psum[0:64, nh + hh, :],
                                                          func=AF.Sigmoid, bias=lv_s[:, 2:3], scale=1.0),
                     reads=[("st", (nh + hh) // 2), "lv"], writes=["i_s"])
            P.op("pool", lambda e, b=b: e.tensor_tensor(out=g_s[:], in0=ga_s[:, b, :], in1=ga_s[:, b, :], op=ALU.mult),
                 reads=[("ga", b)], writes=["g_s"])
            P.op("pool", lambda e: e.tensor_scalar(out=g_s[:], in0=g_s[:], scalar1=0.044715, scalar2=1.0, op0=ALU.mult, op1=ALU.add),
                 reads=["g_s"], writes=["g_s"])
            P.op("pool", lambda e, b=b: e.tensor_tensor(out=g_s[:], in0=g_s[:], in1=ga_s[:, b, :], op=ALU.mult),
                 reads=["g_s", ("ga", b)], writes=["g_s"])
            P.op("act", lambda e: e.activation(out=g_s[:], in_=g_s[:], func=AF.Sigmoid, scale=1.5957691216057308),
                 reads=["g_s"], writes=["g_s"])
            P.op("pool", lambda e, b=b: e.tensor_tensor(out=g_s[:], in0=g_s[:], in1=ga_s[:, b, :], op=ALU.mult),
                 reads=["g_s", ("ga", b)], writes=["g_s"])
            P.op("act", lambda e: e.activation(out=a_s[:], in_=r_s[:], func=AF.Exp, scale=csc[:, 1:2]),
                 reads=["r_s", "csc"], writes=["a_s"])
            P.op("act", lambda e: e.activation(out=m_s[:], in_=r_s[:], func=AF.Exp, scale=csc[:, 2:3]),
                 reads=["r_s", "csc"], writes=["m_s"])
            P.op("act", lambda e: e.activation(out=m_s[:], in_=m_s[:], func=AF.Sqrt, bias=1.0, scale=-1.0),
                 reads=["m_s"], writes=["m_s"])
            P.op("pool", lambda e: e.tensor_tensor(out=i_s[:], in0=i_s[:], in1=xc[:], op=ALU.mult),
                 reads=["i_s", "xc"], writes=["i_s"])
            P.op("pool", lambda e: e.tensor_tensor(out=m_s[:], in0=m_s[:], in1=i_s[:], op=ALU.mult),
                 reads=["m_s", "i_s"], writes=["m_s"])
            init = 0.0 if ch == 0 else h_s[:, 1 - b, TL - 1:TL]
            P.op("dve", lambda e, b=b, init=init: e.tensor_tensor_scan(out=h_s[:, b, :], data0=a_s[:], data1=m_s[:],
                                                                       initial=init, op0=ALU.mult, op1=ALU.add),
                 reads=["a_s", "m_s", ("h_s", 1 - b)], writes=[("h_s", b)])
            P.op("dve", lambda e, b=b: e.tensor_tensor(out=ya_s[:, b, :], in0=h_s[:, b, :], in1=g_s[:], op=ALU.mult),
                 reads=[("h_s", b), "g_s"], writes=[("ya_s", b)])
            P.op("sp", lambda e, b=b, t0=t0: e.dma_start(out=ya_o(t0, TL), in_=ya_s[:, b, :]),
                 reads=[("ya_s", b)], writes=[("ya_dram", ch)], dma="st_ya%d" % b)
        if "after_ya" in io:
            io["after_ya"](P, NCH)

        if "pre_attn" in io:
            io["pre_attn"](P)
        P.op("pool", lambda e: e.memset(q0T[64:128, :], 0.0), writes=["q0T"])
        P.op("pool", lambda e: e.memset(q1T[0:64, :], 0.0), writes=["q1T"])
        P.op("pool", lambda e: e.memset(v1[:, :, 128:129], 1.0), writes=["v1ones"])
        if "load_qk" in io:
            io["load_qk"](P, q0T, q1T, kTs)
        else:
            P.op("sp", lambda e: e.dma_start(out=q0T[0:64, :], in_=io["q_src"](e, 0)[0:64, :]), writes=["q0T"], dma="ld_q0")
            P.op("sp", lambda e: e.dma_start(out=q1T[64:128, :], in_=io["q_src"](e, 0)[64:128, :]), writes=["q1T"], dma="ld_q1")
            P.op("sp", lambda e: e.dma_start(out=kTs[:], in_=io["k_src"](e, 0)), writes=["kTs"], dma="ld_k")
        nvd = max(nsrc, NKT // 16)
        for i in range(nvd):
            a, b_ = i * NKT // nvd, (i + 1) * NKT // nvd
            j = (a * 128) // NS
            ra = a * 128 - j * NS
            rb_ = b_ * 128 - j * NS
            P.op("sp", lambda e, a=a, b_=b_, j=j, ra=ra, rb_=rb_: e.dma_start(
                out=v1[:, a:b_, 0:128], in_=io["v_src"](e, j)[ra:rb_, :].rearrange("(kt p) e -> p kt e", p=128)),
                reads=io.get("v_dep", []), writes=[("v1", i)], dma="ld_v%d" % i)

        def acc_ap(a, lo=0, hi=129):
            return psum[:, 4 + a // 3, (a % 3) * 160 + lo:(a % 3) * 160 + hi]

        def accs_ap(a, lo=0, hi=129):
            return accs[:, a // 3, (a % 3) * 160 + lo:(a % 3) * 160 + hi]

        pairs = [(G, kt) for G in range(NG) for kt in range(4 * G + 4)]

        def geom(G, kt):
            jj = max(kt - 4 * G, 0)
            nblk = 4 - jj
            d0 = 4 * G + jj - kt
            return jj, nblk, d0, (d0 <= 8)

        def emit_qk(n):
            G, kt = pairs[n]
            jj, nblk, d0, near = geom(G, kt)
            sb_ = n % 2
            c0, c1 = jj * 128, TT
            q0 = G * TT + c0

            def fn(e):
                for c, qsrc in ((0, q0T), (1, q1T)):
                    i = e.matmul(psum[:, 2 * sb_ + c, c0:c1], lhsT=kTs[:, kt * 128:(kt + 1) * 128],
                                 rhs=qsrc[:, q0:q0 + nblk * 128], start=True, stop=not near)
                    if near:
                        i = e.matmul(psum[:, 2 * sb_ + c, c0:c1], lhsT=id_b[:],
                                     rhs=biasT[:, d0:d0 + nblk, :], start=False, stop=True)
                return i
            P.op("pe", fn, reads=["q0T", "q1T", "kTs", "id_b", "biasT"], writes=[("st", sb_)])

        def emit_exp(n):
            G, kt = pairs[n]
            jj, nblk, d0, near = geom(G, kt)
            sb_, pb = n % 2, n % 3
            c0 = jj * 128
            src = psum[:, 2 * sb_:2 * sb_ + 2, c0:TT]
            dst = pt[:, pb, :, c0:TT]
            if near:
                fn = lambda e: e.activation(out=dst, in_=src, func=AF.Exp)
            else:
                fn = lambda e: e.activation(out=dst, in_=src, func=AF.Exp, bias=rb_s[:, 15:16], scale=1.0)
            P.op("act", fn, reads=[("st", sb_), "rb"], writes=[("pt", pb)])

        def emit_pv(n):
            G, kt = pairs[n]
            jj, nblk, d0, near = geom(G, kt)
            pb = n % 3

            def fn(e):
                for i_ in range(jj, 4):
                    for c in range(2):
                        i = e.matmul(acc_ap(c * 4 + i_), lhsT=pt[:, pb, c, i_ * 128:(i_ + 1) * 128], rhs=v1[:, kt, :],
                                     start=False, stop=False, skip_group_check=True)
                return i
            P.op("pe", fn, reads=[("pt", pb), ("v1", kt * nvd // NKT), "v1ones"], writes=["acc"])

        def emit_evac(G):
            yb_ = G % 2
            P.op("dve", lambda e: e.tensor_copy(out=accs[:], in_=psum[:, 4:7, :]), reads=["acc"], writes=["accs"])
            P.op("dve", lambda e: e.reciprocal(out=rl[:, 0:6].rearrange("p (a b) -> p a b", a=2),
                                               in_=accs[:, 0:2, 128:449:160]), reads=["accs"], writes=["rl"])
            P.op("dve", lambda e: e.reciprocal(out=rl[:, 6:8], in_=accs[:, 2, 128:289:160]), reads=["accs"], writes=["rl"])
            P.op("dve", lambda e: e.tensor_scalar(out=rl[:, 4:8], in0=rl[:, 4:8], scalar1=lsc[:, 5:6], scalar2=None, op0=ALU.mult),
                 reads=["rl", "nlam"], writes=["rl"])
            for i_ in range(4):
                P.op("dve", lambda e, i_=i_: e.tensor_scalar(out=o_s[:, i_, :], in0=accs_ap(i_, 0, 128), scalar1=rl[:, i_:i_ + 1],
                                                             scalar2=None, op0=ALU.mult),
                     reads=["accs", "rl"], writes=[("o_s", i_)])
                P.op("dve", lambda e, i_=i_: e.scalar_tensor_tensor(out=o_s[:, i_, :], in0=accs_ap(4 + i_, 0, 128),
                                                                    scalar=rl[:, 4 + i_:5 + i_], in1=o_s[:, i_, :],
                                                                    op0=ALU.mult, op1=ALU.add),
                     reads=["accs", "rl", ("o_s", i_)], writes=[("o_s", i_)])
                P.op("dve", lambda e, i_=i_: e.scalar_tensor_tensor(out=osq[:], in0=o_s[:, i_, :], scalar=1.0, in1=o_s[:, i_, :],
                                                                    op0=ALU.mult, op1=ALU.mult, accum_out=ss[:, i_:i_ + 1]),
                     reads=[("o_s", i_)], writes=["osq", ("ss", i_)])
            P.op("act", lambda e: e.activation(out=ss[:], in_=ss[:], func=AF.Ln, bias=128 * EPS, scale=1.0),
                 reads=[("ss", k) for k in range(4)], writes=["ssr"])
            P.op("act", lambda e: e.activation(out=ss[:], in_=ss[:], func=AF.Exp, scale=-0.5), reads=["ssr"], writes=["ssr"])
            for i_ in range(4):
                P.op("dve", lambda e, i_=i_: e.scalar_tensor_tensor(out=y_s[:, i_, :], in0=o_s[:, i_, :], scalar=ss[:, i_:i_ + 1],
                                                                    in1=subg_s[:], op0=ALU.mult, op1=ALU.mult),
                     reads=[("o_s", i_), "ssr", "subg"], writes=[("y_s", i_)])

            def tr(e):
                for i_ in range(4):
                    i = e.transpose(trp[:, i_ * 128:(i_ + 1) * 128], y_s[:, i_, :], id_b[:])
                return i
            P.op("pe", tr, reads=[("y_s", k) for k in range(4)] + ["id_b"], writes=["trp"])
            P.op("dve", lambda e, yb_=yb_: e.tensor_copy(out=yc_s[:, yb_, :], in_=trp[:, 0:TT]), reads=["trp"], writes=[("yc_s", yb_)])
            P.op("sp", lambda e, yb_=yb_, G=G: e.dma_start(out=yc_o(G * TT, TT), in_=yc_s[:, yb_, :]),
                 reads=[("yc_s", yb_)], writes=[("yc_dram", G)], dma="st_yc%d" % yb_)
            if "after_yc" in io:
                io["after_yc"](P, G, NG)

        emit_qk(0)
        for n, (G, kt) in enumerate(pairs):
            if n + 1 < len(pairs):
                emit_qk(n + 1)
            if kt == 0:
                P.op("dve", lambda e: e.memset(psum[:, 4:7, :], 0.0), writes=["acc"])
            emit_exp(n)
            emit_pv(n)
            if kt == 4 * G + 3:
                emit_evac(G)


T3 = 256


def build_L3(NT):
    nc = bass.Bass("TRN2", target_bir_lowering=False)
    H = FFN_HIDDEN
    with contextlib.ExitStack() as es:
        C = Ctx(nc, es)
        yaT = C.din("yaT", [256, NT], BF16)
        ybT = C.din("ybT", [256, NT], BF16)
        ycT = C.din("ycT", [512, NT], BF16)

        def y_src(e, kind, i, c0, n):
            if kind == "ya":
                return yaT[64 * i:64 * i + 64, c0:c0 + n]
            if kind == "yb":
                return ybT[128 * i:128 * i + 128, c0:c0 + n]
            return ycT[128 * i:128 * i + 128, c0:c0 + n]
        xo = C.dout("xo", [1024, NT])
        io = {
            "x_in": C.din("xT", [1024, NT]), "x_mid": xo, "x_out": xo,
            "g1": C.din("g1", [128, 8]), "g2": C.din("g2", [128, 8]), "w_in": C.din("w_in", [1024, IN_COLS]),
            "bg": C.din("bg", [128, 24]), "y_src": y_src,
            "w_pa": C.din("w_pa", [256, 1024]), "w_pb": C.din("w_pb", [256, 1024]), "w_pc": C.din("w_pc", [512, 1024]),
            "w_o": C.din("w_o", [1024, 1024]), "w_g": C.din("w_g", [1024, H]), "w_u": C.din("w_u", [1024, H]),
            "w_d": C.din("w_d", [H, 1024]),
        }
        emit_L3(C, io, NT)
        C.P.emit()
    return nc


def emit_L3(C, io, NT):
    ntile = NT // T3
    H = FFN_HIDDEN
    HC = H // 128
    if True:
        P = C.P
        xT, xmid, xo = io["x_in"], io["x_mid"], io["x_out"]
        g1, g2, w_in, bg, w_pa, w_pb, w_pc, w_o, w_g, w_u, w_d = (io[k] for k in (
            "g1", "g2", "w_in", "bg", "w_pa", "w_pb", "w_pc", "w_o", "w_g", "w_u", "w_d"))

        wbuf = C.sb("wbuf", [128, 3 * 8 * H], BF16)
        stage = C.sb("stage", [128, 2, 1408], F32)
        xt = C.sb("xt", [128, 2, 8, T3], F32)
        sq = C.sb("sq", [128, 8, T3], BF16)
        hT = C.sb("hT", [128, 8, T3], BF16)
        rstd = C.sb("rstd", [128, T3], F32)
        ones = C.sb("ones", [128, 128], BF16)
        g1s = C.sb("g1s", [128, 8], F32)
        g2s = C.sb("g2s", [128, 8], F32)
        bg_s = C.sb("bg_s", [128, 24], F32)
        y_s = C.sb("y_s", [128, 2, 8, T3], BF16)
        gs = C.sb("gs", [128, 2, 3, T3], F32)
        mt = C.sb("mt", [128, 2, 3, T3], F32)
        mT = C.sb("mT", [128, 8, T3], BF16)
        sg = C.sb("sg", [128, 2, T3], F32)
        actT = C.sb("actT", [128, HC, T3], BF16)
        psum = C.ps("psum", [128, 8, 2, T3], F32)
        psr = Rot("ps", 8)

        def wview(off, kc, n):
            return wbuf[:, off:off + kc * n].rearrange("p (c n) -> p c n", c=kc)
        wgt_ = wview(0, 8, 3072)
        wpa_ = wview(24576, 2, 1024)
        wpb_ = wview(24576 + 2048, 2, 1024)
        wpc_ = wview(24576 + 4096, 4, 1024)
        wo_ = wview(24576 + 8192, 8, 1024)
        fg_ = wview(0, 8, H)
        fu_ = wview(8 * H, 8, H)
        fd_ = wview(16 * H, HC, 1024)

        P.op("dve", lambda e: e.memset(ones[:], 1.0), writes=["ones"])
        for dst, src, key in ((g1s, g1, "g1s"), (g2s, g2, "g2s"), (bg_s, bg, "bg")):
            P.op("sp", lambda e, dst=dst, src=src: e.dma_start(out=dst[:], in_=src), writes=[key], dma="c_" + key)
        for t_, k_ in ((g1s, "g1s"), (g2s, "g2s")):
            P.op("dve", lambda e, t_=t_: e.tensor_scalar(out=t_[:], in0=t_[:], scalar1=32.0, scalar2=None, op0=ALU.mult),
                 reads=[k_], writes=[k_])

        def load_x(t, src_ap, srckeys):
            b = t % 2
            src = src_ap[:, t * T3:(t + 1) * T3].rearrange("(c p) n -> p c n", p=128)
            P.op("sp", lambda e, b=b, src=src: e.dma_start(out=xt[:, b, :, :], in_=src),
                 reads=srckeys, writes=[("xt", b)], dma="xld%d" % b)

        def load_y(t):
            b = t % 2
            c0 = t * T3
            for h in range(4):
                P.op("sp", lambda e, b=b, h=h, c0=c0: e.dma_start(
                    out=y_s[(h % 2) * 64:(h % 2) * 64 + 64, b, h // 2, :], in_=io["y_src"](e, "ya", h, c0, T3)),
                    writes=[("y_s", b, 0)], dma="yld%d_0" % b)
            for i in range(2):
                P.op("sp", lambda e, b=b, i=i, c0=c0: e.dma_start(out=y_s[:, b, 2 + i, :], in_=io["y_src"](e, "yb", i, c0, T3)),
                     writes=[("y_s", b, 1)], dma="yld%d_1" % b)
            for h in range(4):
                P.op("sp", lambda e, b=b, h=h, c0=c0: e.dma_start(out=y_s[:, b, 4 + h, :], in_=io["y_src"](e, "yc", h, c0, T3)),
                     writes=[("y_s", b, 2)], dma="yld%d_2" % b)

        srot = Rot("stage", 2)
        load_x(0, xT, [])
        load_y(0)
        kg = load_weight_bf16(C, wgt_, w_in[:, 2560:5632], 1024, 3072, "wgt", stage, srot, scale_ap=g1s,
                              colblk=1024, scale_key="g1s")
        kpa = load_weight_bf16(C, wpa_, w_pa, 256, 1024, "wpa", stage, srot, colblk=1024)
        kpb = load_weight_bf16(C, wpb_, w_pb, 256, 1024, "wpb", stage, srot, colblk=1024)
        kpc = load_weight_bf16(C, wpc_, w_pc, 512, 1024, "wpc", stage, srot, colblk=1024)
        ko = load_weight_bf16(C, wo_, w_o, 1024, 1024, "wo", stage, srot, colblk=1024)
        c1keys = kg + kpa + kpb + kpc + ko

        def mm_fm(e, bank, half, wv, kc, col0, rhs_fn):
            for k in range(kc):
                i = e.matmul(psum[:, bank, half, :], lhsT=wv[:, k, col0:col0 + 128], rhs=rhs_fn(k),
                             start=(k == 0), stop=(k == kc - 1))
            return i

        def norm(t):
            b = t % 2
            bank, pk = psr.next()
            rms_tile(C, xt[:, b], ("xt", b), hT, "hT", sq, "sq", ones, psum[:, bank, 0, :], pk, rstd, "rstd", T3, 1024 * EPS)

        def resid(t, wv, kc, rhs_fn, rkeys, outkey, xdst):
            b = t % 2
            for m in range(4):
                bank, pk = psr.next()

                def fn(e, bank=bank, m=m):
                    for hf in range(2):
                        i = mm_fm(e, bank, hf, wv, kc, (2 * m + hf) * 128, rhs_fn)
                    return i
                P.op("pe", fn, reads=rkeys, writes=[pk])
                P.op("dve", lambda e, bank=bank, m=m, b=b: e.tensor_tensor(
                    out=xt[:, b, 2 * m:2 * m + 2, :], in0=xt[:, b, 2 * m:2 * m + 2, :], in1=psum[:, bank, :, :], op=ALU.add),
                    reads=[pk, ("xt", b)], writes=[("xt", b)])
            dst = xdst[:, t * T3:(t + 1) * T3].rearrange("(c p) n -> p c n", p=128)
            P.op("sp", lambda e, b=b, dst=dst: e.dma_start(out=dst, in_=xt[:, b, :, :]),
                 reads=[("xt", b)], writes=[(outkey, t)], dma="st_x%d" % b)

        projs = ((wpa_, 2, 0, kpa), (wpb_, 2, 2, kpb), (wpc_, 4, 4, kpc))
        for t in range(ntile):
            b = t % 2
            if t + 1 < ntile:
                load_x(t + 1, xT, [])
                load_y(t + 1)
            norm(t)
            for n in range(8):
                gb = n % 2
                slots = []
                for br in range(3):
                    bank, pk = psr.next()
                    slots.append((bank, pk))
                    wv, kc, off, kk = projs[br]

                    def fn(e, bank=bank, br=br, n=n, wv=wv, kc=kc, off=off, b=b):
                        mm_fm(e, bank, 0, wgt_, 8, br * 1024 + n * 128, lambda k: hT[:, k, :])
                        return mm_fm(e, bank, 1, wv, kc, n * 128, lambda k: y_s[:, b, off + k, :])
                    P.op("pe", fn, reads=["hT", ("y_s", b, br)] + kg + kk, writes=[pk])
                for br in range(3):
                    bank, pk = slots[br]
                    ch = br * 8 + n
                    P.op("act", lambda e, bank=bank, br=br, ch=ch, gb=gb: e.activation(
                        out=gs[:, gb, br, :], in_=psum[:, bank, 0, :], func=AF.Sigmoid, bias=bg_s[:, ch:ch + 1], scale=1.0),
                        reads=[pk, "bg"], writes=[("gs", gb, br)])
                    P.op("dve", lambda e, bank=bank, br=br, gb=gb: e.tensor_tensor(
                        out=mt[:, gb, br, :], in0=psum[:, bank, 1, :], in1=gs[:, gb, br, :], op=ALU.mult),
                        reads=[pk, ("gs", gb, br)], writes=[("mt", gb, br)])
                P.op("pool", lambda e, gb=gb: e.tensor_tensor(out=mt[:, gb, 0, :], in0=mt[:, gb, 0, :], in1=mt[:, gb, 1, :], op=ALU.add),
                     reads=[("mt", gb, 0), ("mt", gb, 1)], writes=[("mt", gb, 0)])
                P.op("pool", lambda e, gb=gb, n=n: e.tensor_tensor(out=mT[:, n, :], in0=mt[:, gb, 0, :], in1=mt[:, gb, 2, :], op=ALU.add),
                     reads=[("mt", gb, 0), ("mt", gb, 2)], writes=[("mT", n)])
            resid(t, wo_, 8, lambda k: mT[:, k, :], [("mT", k) for k in range(8)] + ko, "xo", xmid)

        kfg = load_weight_bf16(C, fg_, w_g, 1024, H, "fg", stage, srot, scale_ap=g2s, colblk=1408, scale_key="g2s",
                               also_writes=c1keys)
        kfu = load_weight_bf16(C, fu_, w_u, 1024, H, "fu", stage, srot, scale_ap=g2s, colblk=1408, scale_key="g2s",
                               also_writes=c1keys)
        kfd = load_weight_bf16(C, fd_, w_d, H, 1024, "fd", stage, srot, colblk=1024, also_writes=c1keys)
        load_x(0, xmid, [("xo", 0)])
        for t in range(ntile):
            b = t % 2
            if t + 1 < ntile:
                load_x(t + 1, xmid, [("xo", t + 1)])
            norm(t)
            for j in range(HC):
                sb_ = j % 2
                bank, pk = psr.next()

                def fn(e, bank=bank, j=j):
                    mm_fm(e, bank, 0, fg_, 8, j * 128, lambda k: hT[:, k, :])
                    return mm_fm(e, bank, 1, fu_, 8, j * 128, lambda k: hT[:, k, :])
                P.op("pe", fn, reads=["hT"] + kfg + kfu, writes=[pk])
                P.op("act", lambda e, bank=bank, sb_=sb_: e.activation(out=sg[:, sb_, :], in_=psum[:, bank, 0, :], func=AF.Silu),
                     reads=[pk], writes=[("sg", sb_)])
                P.op("dve", lambda e, bank=bank, sb_=sb_, j=j: e.tensor_tensor(out=actT[:, j, :], in0=psum[:, bank, 1, :], in1=sg[:, sb_, :], op=ALU.mult),
                     reads=[pk, ("sg", sb_)], writes=[("actT", j)])
            resid(t, fd_, HC, lambda k: actT[:, k, :], [("actT", k) for k in range(HC)] + kfd, "xo2", xo)


def _c(a):
    return np.ascontiguousarray(a, dtype=np.float32)


def l1_inputs(xT, inp, l):
    sgw = inp["sg_w"][l]
    sgb = inp["sg_b"][l]
    p = np.arange(128)
    bsb = np.stack([sgb[2 * n + p // 64, :] for n in range(2)], axis=1)
    gqk = np.stack([inp["q_norm_g"][l][p % 64], inp["k_norm_g"][l][p % 64]], axis=1)
    return {
        "xT": _c(xT),
        "g1": _c(inp["ln1_g"][l].reshape(8, 128).T),
        "w_in": _c(inp["w_in"][l]),
        "sgg": _c(np.broadcast_to(inp["sg_ln_g"][l], (128, 256))),
        "sgb": _c(np.broadcast_to(inp["sg_ln_b"][l], (128, 256))),
        "wsT": _c(sgw.transpose(2, 0, 1)),
        "tril": _c(np.triu(np.ones((128, 128)))),
        "bsb": _c(bsb),
        "gqk": _c(gqk),
    }


def t5_bucket_np(rel):
    import jax
    import jax.numpy as jnp
    with jax.default_device(jax.devices("cpu")[0]):
        rel = jnp.asarray(rel, jnp.int32)
        half, max_exact = 16, 8
        ret = jnp.where(rel > 0, half, 0)
        n = jnp.abs(rel)
        nf = jnp.maximum(n, 1).astype(jnp.float32)
        large = max_exact + (jnp.log(nf / max_exact) / math.log(2048 / max_exact) * (half - max_exact)).astype(jnp.int32)
        large = jnp.minimum(large, half - 1)
        return np.asarray(ret + jnp.where(n < max_exact, n, large))


_L2_CONST = {}


def l2_consts():
    if not _L2_CONST:
        k = np.arange(128)[:, None, None]
        d = np.arange(12)[None, :, None]
        q = np.arange(128)[None, None, :]
        rel = k - q - 128 * d
        _L2_CONST["idx"] = _c(t5_bucket_np(rel))
        kk = np.arange(128)[:, None]
        qq = np.arange(128)[None, :]
        _L2_CONST["maskT"] = _c(np.where((kk // 64) > (qq // 64), NEG, 0.0))
        _L2_CONST["ident"] = _c(np.eye(128))
    return _L2_CONST


def l2_inputs(qT, kT, v, xaT, gaT, inp, l, h):
    lam_init = 0.8 - 0.6 * math.exp(-0.3 * l)
    cs = l2_consts()
    ch = slice(64 * h, 64 * h + 64)
    lvec = np.stack([inp["conv_b"][l][ch], inp["lru_ba"][l][ch], inp["lru_bi"][l][ch], inp["lru_lambda"][l][ch]], axis=1)
    lq = np.stack([inp["lambda_q1"][l], inp["lambda_k1"][l], inp["lambda_q2"][l], inp["lambda_k2"][l]], axis=0)
    return {
        "qT": qT, "kT": kT, "v": v, "xaT": _c(xaT), "gaT": _c(gaT),
        "cw": _c(inp["conv_w"][l][:, ch].T),
        "lvec": _c(lvec),
        "wa": _c(inp["lru_wa"][l][h]), "wi": _c(inp["lru_wi"][l][h]),
        "lq": _c(np.broadcast_to(lq, (128, 4, 64))),
        "subg": _c(np.broadcast_to(inp["subln_g"][l], (128, 128))),
        "rb": _c(np.broadcast_to(inp["rel_bias"][:, h], (128, 32))),
        "idx": cs["idx"], "maskT": cs["maskT"], "ident": cs["ident"],
        "lcon": _c(np.broadcast_to(np.array([lam_init, (1.0 - lam_init) * math.sqrt(128.0)]), (128, 2))),
    }


def l3_inputs(xT, yaT, ybT, ycT, inp, l):
    return {
        "xT": _c(xT),
        "g1": _c(inp["ln1_g"][l].reshape(8, 128).T),
        "g2": _c(inp["ln2_g"][l].reshape(8, 128).T),
        "w_in": _c(inp["w_in"][l]),
        "bg": _c(inp["b_gate"][l].reshape(24, 128).T),
        "yaT": yaT, "ybT": ybT, "ycT": ycT,
        "w_pa": _c(inp["w_pa"][l]), "w_pb": _c(inp["w_pb"][l]), "w_pc": _c(inp["w_pc"][l]), "w_o": _c(inp["w_o"][l]),
        "w_g": _c(inp["w_ff_gate"][l]), "w_u": _c(inp["w_ff_up"][l]), "w_d": _c(inp["w_ff_down"][l]),
    }


ARENA_BYTES = 212736
GROUPS = [[0, 1, 2, 3], [4, 5, 6, 7]]


def build_fused(S, depth=DEPTH):
    NT = S // 4
    H = FFN_HIDDEN
    L = depth
    nc = bass.Bass("TRN2", target_bir_lowering=False)
    with contextlib.ExitStack() as es:
        C = Ctx(nc, es)
        C.use_arena(ARENA_BYTES)
        P = C.P
        P.use_rank = True
        xT = C.din("xT", [1024, NT])
        xo = C.dout("xo", [1024, NT])
        pin = {}
        for name, shape in (("g1", [L, 128, 8]), ("g2", [L, 128, 8]), ("w_in", [L, 1024, IN_COLS]),
                            ("sgg", [L, 128, 256]), ("sgb", [L, 128, 256]), ("wsT", [L, 128, 4, 128]),
                            ("tril", [128, 128]), ("bsb", [L, 128, 2, 128]), ("gqk", [L, 128, 2]),
                            ("cw", [L, 64, 4]), ("lvec", [L, 64, 4]), ("wa", [L, 64, 64]), ("wi", [L, 64, 64]),
                            ("lq", [L, 128, 4, 64]), ("subg", [L, 128, 128]), ("rb", [128, 32]),
                            ("idx", [128, 12, 128]), ("maskT", [128, 128]), ("ident", [128, 128]), ("lcon", [L, 128, 2]),
                            ("bg", [L, 128, 24]), ("w_pa", [L, 256, 1024]), ("w_pb", [L, 256, 1024]),
                            ("w_pc", [L, 512, 1024]), ("w_o", [L, 1024, 1024]), ("w_g", [L, 1024, H]),
                            ("w_u", [L, 1024, H]), ("w_d", [L, H, 1024])):
            pin[name] = C.din(name, shape)

        def dint(name, shape, dt):
            return nc.dram_tensor(name, list(shape), dt).ap()
        q_in = dint("q_in", [4, 128, NT], BF16)
        q_out = dint("q_out", [4, 512, NT], BF16)
        k_in = dint("k_in", [4, 128, NT], BF16)
        k_out = dint("k_out", [4, 512, NT], BF16)
        v_in = dint("v_in", [4, NT, 128], BF16)
        v_out = dint("v_out", [4, 4 * NT, 128], BF16)
        xa_in = dint("xa_in", [4, 64, NT], F32)
        xa_out = dint("xa_out", [4, 256, NT], F32)
        ga_in = dint("ga_in", [4, 64, NT], F32)
        ga_out = dint("ga_out", [4, 256, NT], F32)
        yc_in = dint("yc_in", [4, 128, NT], BF16)
        yc_out = dint("yc_out", [4, 512, NT], BF16)
        ya_in = dint("ya_in", [4, 64, NT], BF16)
        ya_out = dint("ya_out", [4, 256, NT], BF16)
        v_loc = dint("v_loc", [S, 128], BF16)
        xg_loc = dint("xg_loc", [2, 4, 64, NT], F32)
        yc_loc = dint("yc_loc", [4, 128, NT], BF16)
        ya_loc = dint("ya_loc", [4, 64, NT], BF16)
        yb_x = dint("yb_x", [2, 128, NT], BF16)
        xs1 = dint("xs1", [1024, NT], F32)
        xs2 = dint("xs2", [1024, NT], F32)

        P.dyn_spec = {}

        def rk(name="r"):
            return P.dyn[name]

        def allgather(name, src, dst, reads=()):
            P.op("pool", lambda e: e.collective_compute("AllGather", ALU.bypass, replica_groups=GROUPS, ins=[src], outs=[dst]),
                 reads=list(reads), writes=["cc_" + name], dma="cc_" + name, inc=1)

        for l in range(L):
            par = 0
            x_in = xT if l == 0 else xs2
            x_out = xo if l == L - 1 else xs2
            C.arena_reset()
            io1 = {"xT": x_in, "g1": pin["g1"][l], "w_in": pin["w_in"][l], "sgg": pin["sgg"][l], "sgb": pin["sgb"][l],
                   "wsT": pin["wsT"][l], "tril": pin["tril"], "bsb": pin["bsb"][l], "gqk": pin["gqk"][l],
                   "qk": lambda h, which: (q_in, k_in)[which][h],
                   "v": v_in,
                   "xg": lambda ch: (xa_in, ga_in)[ch // 2][2 * (ch % 2):2 * (ch % 2) + 2].rearrange("h p n -> (h p) n"),
                   "ybT": yb_x}
            emit_L1(C, io1, NT)
            P.barrier()
            for h in range(4):
                allgather("xa", xa_in[h], xa_out[h])
                allgather("ga", ga_in[h], ga_out[h])
            for h in range(4):
                allgather("q", q_in[h], q_out[h])
                allgather("k", k_in[h], k_out[h])
                allgather("v", v_in[h], v_out[h])
            C.arena_reset()
            for a_, srcg in ((0, xa_out), (1, ga_out)):
                P.op("sp", lambda e, a_=a_, srcg=srcg: e.dma_start(
                    out=xg_loc[a_].rearrange("(o j) p n -> o j p n", o=1),
                    in_=srcg.rearrange("h (j p) n -> h j p n", j=4)[bass.ds(rk(), 1), :, :, :]),
                    reads=["cc_xa", "cc_ga"], writes=[("xg_loc", a_)], dma="loc_xg%d" % a_)

            def load_qk(P_, q0T, q1T, kTs):
                def v3(t, rows):
                    return t[rows, :].rearrange("p (j n) -> p j n", j=4)

                def src(g, rows):
                    return g.rearrange("h (j p) n -> h j p n", j=4)[bass.ds(rk(), 1), :, rows, :].rearrange("o j p n -> p (o j) n")
                P_.op("sp", lambda e: e.dma_start(out=v3(q0T, slice(0, 64)), in_=src(q_out, slice(0, 64))),
                      reads=["cc_q"], writes=["q0T"], dma="ld_q0")
                P_.op("sp", lambda e: e.dma_start(out=v3(q1T, slice(64, 128)), in_=src(q_out, slice(64, 128))),
                      reads=["cc_q"], writes=["q1T"], dma="ld_q1")
                P_.op("sp", lambda e: e.dma_start(out=v3(kTs, slice(0, 128)), in_=src(k_out, slice(0, 128))),
                      reads=["cc_k"], writes=["kTs"], dma="ld_k")

            def pre_attn(P_):
                P_.op("sp", lambda e: e.dma_start(out=v_loc.rearrange("(o t) e -> o t e", o=1), in_=v_out[bass.ds(rk(), 1), :, :]),
                      reads=["cc_v"], writes=["v_loc"], dma="loc_v")

            def after_ya(P_, nch):
                per = nch // 4
                for j in range(4):
                    allgather("ya", ya_in[j], ya_out[j], reads=[("ya_dram", c_) for c_ in range(j * per, (j + 1) * per)])

            def after_yc(P_, G, ng):
                per = ng // 4
                if (G + 1) % per == 0:
                    j = G // per
                    allgather("yc", yc_in[j], yc_out[j], reads=[("yc_dram", g_) for g_ in range(j * per, (j + 1) * per)])
            io2 = {"nsrc": 4, "load_qk": load_qk, "after_ya": after_ya, "after_yc": after_yc, "pre_attn": pre_attn,
                   "v_dep": ["v_loc"], "xa_dep": [("xg_loc", 0)], "ga_dep": [("xg_loc", 1)],
                   "v_src": lambda e, j: v_loc[j * NT:(j + 1) * NT, :],
                   "xa_src": lambda e, j: xg_loc[0, j], "ga_src": lambda e, j: xg_loc[1, j],
                   "cw": pin["cw"][l], "lvec": pin["lvec"][l], "wa": pin["wa"][l], "wi": pin["wi"][l], "lq": pin["lq"][l],
                   "subg": pin["subg"][l], "rb": pin["rb"], "idx": pin["idx"], "maskT": pin["maskT"], "ident": pin["ident"],
                   "lcon": pin["lcon"][l],
                   "ycT": lambda c0, n: yc_in[c0 // NT, :, c0 % NT:c0 % NT + n],
                   "yaT": lambda c0, n: ya_in[c0 // NT, :, c0 % NT:c0 % NT + n]}
            emit_L2(C, io2, S)
            P.barrier()
            C.arena_reset()
            P.op("sp", lambda e: e.dma_start(out=yc_loc.rearrange("(o h) p n -> o h p n", o=1),
                                             in_=yc_out.rearrange("j (h p) n -> j h p n", h=4)[bass.ds(rk(), 1), :, :, :]),
                 writes=["yc_loc"], dma="loc_yc")
            P.op("sp", lambda e: e.dma_start(out=ya_loc.rearrange("(o h) p n -> o h p n", o=1),
                                             in_=ya_out.rearrange("j (h p) n -> j h p n", h=4)[bass.ds(rk(), 1), :, :, :]),
                 writes=["ya_loc"], dma="loc_ya")
            P.barrier()

            def y_src(e, kind, i, c0, n):
                if kind == "yb":
                    return yb_x[i, :, c0:c0 + n]
                if kind == "ya":
                    return ya_loc[i, :, c0:c0 + n]
                return yc_loc[i, :, c0:c0 + n]
            io3 = {"x_in": x_in, "x_mid": xs1, "x_out": x_out, "g1": pin["g1"][l], "g2": pin["g2"][l],
                   "w_in": pin["w_in"][l], "bg": pin["bg"][l], "y_src": y_src, "w_pa": pin["w_pa"][l],
                   "w_pb": pin["w_pb"][l], "w_pc": pin["w_pc"][l], "w_o": pin["w_o"][l], "w_g": pin["w_g"][l],
                   "w_u": pin["w_u"][l], "w_d": pin["w_d"][l]}
            emit_L3(C, io3, NT)
            P.barrier()
        P.emit()
    return nc


def fused_inputs(inp, c, S, depth=DEPTH):
    NT = S // 4
    b, r = c // 4, c % 4
    x = inp["x"]
    xT = np.ascontiguousarray(x[b, r * NT:(r + 1) * NT].T)
    dummy = np.zeros((2, 2), np.float32)
    l1 = [l1_inputs(dummy, inp, l) for l in range(depth)]
    l2 = [l2_inputs(None, None, None, dummy, dummy, inp, l, r) for l in range(depth)]
    l3 = [l3_inputs(dummy, None, None, None, inp, l) for l in range(depth)]

    def st(lst, k):
        return np.ascontiguousarray(np.stack([d[k] for d in lst], axis=0))
    m = {"xT": xT}
    for k in ("g1", "w_in", "sgg", "sgb", "wsT", "bsb", "gqk"):
        m[k] = st(l1, k)
    m["tril"] = l1[0]["tril"]
    for k in ("cw", "lvec", "wa", "wi", "lq", "subg", "lcon"):
        m[k] = st(l2, k)
    for k in ("rb", "idx", "maskT", "ident"):
        m[k] = l2[0][k]
    for k in ("g2", "bg", "w_pa", "w_pb", "w_pc", "w_o", "w_g", "w_u", "w_d"):
        m[k] = st(l3, k)
    return m


_PROGS = {}


def kernel(**inputs):
    inp = {k: np.asarray(v) for k, v in inputs.items()}
    x = inp["x"]
    B, S, D = x.shape
    NT = S // 4
    key = ("fused", S)
    if key not in _PROGS:
        _PROGS[key] = build_fused(S)
    nc = _PROGS[key]
    in_maps = [fused_inputs(inp, c, S) for c in range(N_CORES)]
    res = run_bass_kernel_spmd(nc, in_maps, core_ids=list(range(N_CORES))).results
    out = np.empty((B, S, D), dtype=np.float32)
    for c in range(N_CORES):
        out[c // 4, (c % 4) * NT:(c % 4 + 1) * NT] = np.asarray(res[c]["xo"]).T
    return out
```

```python
import contextlib
import math
import numpy as np
import concourse.bass as bass
import concourse.mybir as mybir
from concourse.bass_utils import run_bass_kernel_spmd

F32 = mybir.dt.float32
BF16 = mybir.dt.bfloat16
AF = mybir.ActivationFunctionType
ALU = mybir.AluOpType
AX = mybir.AxisListType

D_MODEL = 1024
DEPTH = 4
N_CORES = 8
EPS = 1e-6
LRU_C = 8.0
FFN_HIDDEN = 2816
IN_COLS = 5632
TT = 512
NEG = -30000.0

STREAMS = ("pe", "act", "dve", "pool", "sp")


class Prog:
    def __init__(self, nc):
        self.nc = nc
        self.streams = {e: [] for e in STREAMS}
        self.count = {}
        self.last_write = {}
        self.readers = {}
        self.waited = {e: {} for e in STREAMS}
        self.pending = {e: [] for e in STREAMS}
        self.nops = 0
        self.use_rank = False
        self.dyn = {}

    def barrier(self):
        for e in STREAMS:
            for k, v in self.count.items():
                if self.waited[e].get(k, 0) < v:
                    self.waited[e][k] = v
                    self.pending[e].append((k, v))
        self.last_write.clear()
        self.readers.clear()

    def op(self, eng, fn, reads=(), writes=(), dma=None, inc=None):
        semkey = dma if dma is not None else eng
        if inc is None:
            inc = 16 if dma is not None else 1
        deps = {}

        def add(k, v, same_ok):
            if k == semkey and eng == "pe" and dma is None and same_ok:
                return
            if deps.get(k, 0) < v:
                deps[k] = v

        for b in reads:
            lw = self.last_write.get(b)
            if lw is not None:
                add(lw[0], lw[1], eng == "pe")
        for b in writes:
            lw = self.last_write.get(b)
            if lw is not None:
                add(lw[0], lw[1], True)
            for k, v in self.readers.get(b, {}).items():
                add(k, v, True)
        waits = self.pending[eng]
        self.pending[eng] = []
        wd = self.waited[eng]
        for k, v in deps.items():
            if wd.get(k, 0) < v:
                wd[k] = v
                waits.append((k, v))
        val = self.count.get(semkey, 0) + inc
        self.count[semkey] = val
        for b in reads:
            self.readers.setdefault(b, {})[semkey] = val
        for b in writes:
            self.last_write[b] = (semkey, val)
            self.readers[b] = {}
        self.streams[eng].append((waits, fn, semkey, inc))
        self.nops += 1

    def emit(self):
        nc = self.nc
        with contextlib.ExitStack() as es:
            sems = {k: es.enter_context(nc.semaphore("s_" + k)) for k in self.count}
            block = es.enter_context(nc.Block())
            final = list(self.count.items())

            def run(name, e):
                for waits, fn, semkey, inc in self.streams[name]:
                    for k, v in waits:
                        e.wait_ge(sems[k], v)
                    fn(e).then_inc(sems[semkey], inc)

            @block.tensor
            def _(e):
                run("pe", e)

            @block.scalar
            def _(e):
                run("act", e)

            @block.vector
            def _(e):
                run("dve", e)

            @block.gpsimd
            def _(e):
                run("pool", e)

            @block.sync
            def _(e):
                if self.use_rank:
                    r = e.snap(e.partition_id() % 4, min_val=0, max_val=3)
                    self.dyn["r"] = r
                    for name, mul in self.dyn_spec.items():
                        self.dyn[name] = e.snap(r * mul, min_val=0, max_val=3 * mul)
                run("sp", e)
                for k, v in final:
                    e.wait_ge(sems[k], v)


class Ctx:
    def __init__(self, nc, es):
        self.nc = nc
        self.es = es
        self.P = Prog(nc)
        self._n = 0

    def din(self, name, shape, dt=F32):
        return self.nc.dram_tensor(name, list(shape), dt, kind="ExternalInput").ap()

    def dout(self, name, shape, dt=F32):
        return self.nc.dram_tensor(name, list(shape), dt, kind="ExternalOutput").ap()

    def use_arena(self, nbytes):
        self.arena = self.es.enter_context(self.nc.sbuf_tensor("arena", [128, nbytes // 2], BF16))
        self.arena_n = nbytes // 2
        self.off = 0
        self.psum_t = self.es.enter_context(self.nc.psum_tensor("psum", [128, 8, 512], F32))

    def arena_reset(self):
        self.off = 0

    def sb(self, name, shape, dt=F32):
        if getattr(self, "arena", None) is None:
            return self.es.enter_context(self.nc.sbuf_tensor(name, list(shape), dt))[:]
        shape = list(shape)
        free = 1
        for d in shape[1:]:
            free *= d
        n16 = free * (2 if dt == F32 else 1)
        n16 = (n16 + 31) // 32 * 32
        assert self.off + n16 <= self.arena_n, "arena overflow at %s: %d + %d > %d" % (name, self.off, n16, self.arena_n)
        ap = self.arena[0:shape[0], self.off:self.off + free * (2 if dt == F32 else 1)]
        self.off += n16
        if dt == F32:
            ap = ap.bitcast(F32)
        if len(shape) > 2:
            names = " ".join("d%d" % i for i in range(len(shape) - 1))
            kw = {"d%d" % i: shape[1 + i] for i in range(len(shape) - 1)}
            ap = ap.rearrange("p (%s) -> p %s" % (names, names), **kw)
        return ap

    def ps(self, name, shape, dt=F32):
        if getattr(self, "arena", None) is None:
            return self.es.enter_context(self.nc.psum_tensor(name, list(shape), dt))[:]
        shape = list(shape)
        ap = self.psum_t[:]
        if shape == [128, 8, 512]:
            return ap
        assert shape == [128, 8, 2, 256], shape
        return ap.rearrange("p b (h n) -> p b h n", h=2)


class Rot:
    def __init__(self, name, n):
        self.name, self.n, self.i = name, n, 0

    def next(self):
        i = self.i % self.n
        self.i += 1
        return i, (self.name, i)


def load_weight_bf16(C, dst, src, K, N, key, stage, stage_rot, scale_ap=None, engines=("act", "dve"),
                     colblk=1536, dma_eng="sp", scale_key=None, also_writes=()):
    P = C.P
    extra = [scale_key] if scale_key is not None else []
    kc_n = K // 128
    ei = 0
    for kc in range(kc_n):
        for c0 in range(0, N, colblk):
            n = min(colblk, N - c0)
            si, skey = stage_rot.next()
            st = stage[:, si, 0:n]
            srcap = src[kc * 128:(kc + 1) * 128, c0:c0 + n]
            P.op(dma_eng, lambda e, st=st, srcap=srcap: e.dma_start(out=st, in_=srcap),
                 writes=[skey], dma="wld%d" % si)
            eng = engines[ei % len(engines)]
            ei += 1
            d = dst[:, kc, c0:c0 + n]
            if eng == "act":
                if scale_ap is not None:
                    sc = scale_ap[:, kc:kc + 1]
                    fn = lambda e, d=d, st=st, sc=sc: e.activation(out=d, in_=st, func=AF.Identity, scale=sc)
                else:
                    fn = lambda e, d=d, st=st: e.copy(out=d, in_=st)
            else:
                if scale_ap is not None:
                    sc = scale_ap[:, kc:kc + 1]
                    fn = lambda e, d=d, st=st, sc=sc: e.tensor_scalar(out=d, in0=st, scalar1=sc, scalar2=None,
                                                                      op0=ALU.mult)
                else:
                    fn = lambda e, d=d, st=st: e.tensor_copy(out=d, in_=st)
            P.op(eng, fn, reads=[skey] + extra, writes=[(key, eng)] + list(also_writes))
    return [(key, e) for e in engines]


def rstd_from_ps(P, psb, pskey, rstd, rkey, n, eps_n):
    P.op("act", lambda e: e.activation(out=rstd[:, 0:n], in_=psb[:, 0:n], func=AF.Ln, bias=eps_n, scale=1.0),
         reads=[pskey], writes=[rkey])
    P.op("act", lambda e: e.activation(out=rstd[:, 0:n], in_=rstd[:, 0:n], func=AF.Exp, scale=-0.5),
         reads=[rkey], writes=[rkey])


def rms_tile(C, xt, xkey, hT, hkey, sq, sqkey, ones, psb, pskey, rstd, rkey, n, eps_n):
    P = C.P
    P.op("act", lambda e: e.activation(out=sq[:, :, 0:n], in_=xt[:, :, 0:n], func=AF.Square),
         reads=[xkey], writes=[sqkey])

    def mm(e):
        for c in range(8):
            i = e.matmul(psb[:, 0:n], lhsT=ones[:], rhs=sq[:, c, 0:n], start=(c == 0), stop=(c == 7))
        return i
    P.op("pe", mm, reads=[sqkey, "ones"], writes=[pskey])
    rstd_from_ps(P, psb, pskey, rstd, rkey, n, eps_n)
    rb = rstd[:, 0:n].unsqueeze(1).broadcast_to([128, 8, n])
    P.op("dve", lambda e: e.tensor_tensor(out=hT[:, :, 0:n], in0=xt[:, :, 0:n], in1=rb, op=ALU.mult),
         reads=[xkey, rkey], writes=[hkey])


def build_L1(NT):
    nc = bass.Bass("TRN2", target_bir_lowering=False)
    with contextlib.ExitStack() as es:
        C = Ctx(nc, es)
        io = {
            "xT": C.din("xT", [1024, NT]), "g1": C.din("g1", [128, 8]), "w_in": C.din("w_in", [1024, IN_COLS]),
            "sgg": C.din("sgg", [128, 256]), "sgb": C.din("sgb", [128, 256]), "wsT": C.din("wsT", [128, 4, 128]),
            "tril": C.din("tril", [128, 128]), "bsb": C.din("bsb", [128, 2, 128]), "gqk": C.din("gqk", [128, 2]),
            "v": C.dout("v", [4, NT, 128], BF16), "ybT": C.dout("ybT", [2, 128, NT], BF16),
        }
        qk_t = C.dout("qk", [4, 2, 128, NT], BF16)
        xg_t = C.dout("xg", [4, 128, NT], F32)
        io["qk"] = lambda h, which: qk_t[h, which]
        io["xg"] = lambda ch: xg_t[ch]
        emit_L1(C, io, NT)
        C.P.emit()
    return nc


def emit_L1(C, io, NT):
    ntile = NT // TT
    if True:
        P = C.P
        xT, g1, w_in, sgg, sgb, wsT, tril, bsb, gqk = (io[k] for k in ("xT", "g1", "w_in", "sgg", "sgb", "wsT", "tril", "bsb", "gqk"))
        qk_o, v_o, xg_o, yb_o = io["qk"], io["v"], io["xg"], io["ybT"]

        NW = 2560
        wb = C.sb("wb", [128, 8, NW], BF16)
        stage = C.sb("stage", [128, 3, 1536], F32)
        xt = C.sb("xt", [128, 2, 8, TT], F32)
        sq = C.sb("sq", [128, 8, TT], BF16)
        hT = C.sb("hT", [128, 2, 8, TT], BF16)
        rstd = C.sb("rstd", [128, TT], F32)
        ones = C.sb("ones", [128, 128], BF16)
        bones = C.sb("bones", [128, 128], BF16)
        g1s = C.sb("g1s", [128, 8], F32)
        sgg_s = C.sb("sgg_s", [128, 256], F32)
        sgb_s = C.sb("sgb_s", [128, 256], F32)
        wsT_f = C.sb("wsT_f", [128, 4, 128], F32)
        tril_s = C.sb("tril_s", [128, 128], F32)
        wsT_b = C.sb("wsT_b", [128, 4, 128], BF16)
        bsb_s = C.sb("bsb_s", [128, 2, 128], F32)
        gqk_s = C.sb("gqk_s", [128, 2], F32)
        xg_s = C.sb("xg_s", [128, 2, TT], F32)
        u_s = C.sb("u_s", [128, 2, TT], F32)
        qsq = C.sb("qsq", [128, 2, TT], BF16)
        qr = C.sb("qr", [128, 2, TT], F32)
        qn = C.sb("qn", [128, 3, TT], BF16)
        stats = C.sb("stats", [128, 2, 6], F32)
        mv = C.sb("mv", [128, 2, 2], F32)
        vr = C.sb("vr", [128, 2, 1], F32)
        vtmp = C.sb("vtmp", [128, 2, 256], F32)
        vn = C.sb("vn", [128, 4, 256], BF16)
        vv_s = C.sb("vv_s", [128, 2, 4, 512], BF16)
        mtmp = C.sb("mtmp", [128, 2, TT], F32)
        yb_s = C.sb("yb_s", [128, 2, TT], BF16)
        psum = C.ps("psum", [128, 8, TT], F32)
        psr = Rot("ps", 8)

        P.op("dve", lambda e: e.memset(ones[:], 1.0), writes=["ones"])
        P.op("pool", lambda e: e.memset(bones[:], 0.0), writes=["bones"])
        P.op("pool", lambda e: e.memset(bones[0:64, 0:64], 1.0), writes=["bones"])
        P.op("pool", lambda e: e.memset(bones[64:128, 64:128], 1.0), writes=["bones"])
        for dst, src, key in ((g1s, g1, "g1s"), (sgg_s, sgg, "sgg"), (sgb_s, sgb, "sgb"), (wsT_f, wsT, "wsT_f"),
                              (tril_s, tril, "tril"), (bsb_s, bsb, "bsb"), (gqk_s, gqk, "gqk")):
            P.op("sp", lambda e, dst=dst, src=src: e.dma_start(out=dst[:], in_=src), writes=[key], dma="c_" + key)
        P.op("dve", lambda e: e.tensor_scalar(out=g1s[:], in0=g1s[:], scalar1=32.0, scalar2=None, op0=ALU.mult),
             reads=["g1s"], writes=["g1s"])
        P.op("dve", lambda e: e.tensor_scalar(out=gqk_s[:, 1:2], in0=gqk_s[:, 1:2], scalar1=8.0, scalar2=None, op0=ALU.mult),
             reads=["gqk"], writes=["gqk"])
        trb = tril_s[:].unsqueeze(1).broadcast_to([128, 4, 128])
        P.op("dve", lambda e: e.tensor_tensor(out=wsT_b[:], in0=wsT_f[:], in1=trb, op=ALU.mult),
             reads=["wsT_f", "tril"], writes=["wsT_b"])

        def load_x(t):
            b = t % 2
            src = xT[:, t * TT:(t + 1) * TT].rearrange("(c p) n -> p c n", p=128)
            for hh in range(2):
                P.op("sp", lambda e, b=b, src=src, hh=hh: e.dma_start(out=xt[:, b, 4 * hh:4 * hh + 4, :],
                                                                    in_=src[:, 4 * hh:4 * hh + 4, :]),
                     writes=[("xt", b)], dma="xld%d" % b)

        load_x(0)
        wbk = load_weight_bf16(C, wb, w_in[:, 0:NW], 1024, NW, "wb", stage, Rot("stage", 3), scale_ap=g1s,
                               engines=("act", "dve"), colblk=1280, scale_key="g1s")

        def mm_fm(bank, col0, b):
            def fn(e):
                for k in range(8):
                    i = e.matmul(psum[:, bank, :], lhsT=wb[:, k, col0:col0 + 128], rhs=hT[:, b, k, :],
                                 start=(k == 0), stop=(k == 7))
                return i
            return fn

        def mm_tm(bank, col0, ncol, b, blk):
            def fn(e):
                for k in range(8):
                    i = e.matmul(psum[:, bank, 0:ncol], lhsT=hT[:, b, k, blk * 128:(blk + 1) * 128],
                                 rhs=wb[:, k, col0:col0 + ncol], start=(k == 0), stop=(k == 7))
                return i
            return fn

        def norm(t):
            b = t % 2
            bank, pk = psr.next()
            rms_tile(C, xt[:, b], ("xt", b), hT[:, b], ("hT", b), sq, "sq", ones, psum[:, bank, :], pk,
                     rstd, "rstd", TT, 1024 * EPS)

        norm(0)
        for t in range(ntile):
            b = t % 2
            t0 = t * TT
            hk = ("hT", b)
            if t + 1 < ntile:
                load_x(t + 1)
            for ch in range(4):
                bank, pk = psr.next()
                P.op("pe", mm_fm(bank, ch * 128, b), reads=[hk] + wbk, writes=[pk])
                s = ch % 2
                P.op("act", lambda e, bank=bank, s=s: e.copy(out=xg_s[:, s, :], in_=psum[:, bank, :]),
                     reads=[pk], writes=[("xg_s", s)])
                P.op("sp", lambda e, ch=ch, s=s, t0=t0: e.dma_start(out=xg_o(ch)[:, t0:t0 + TT], in_=xg_s[:, s, :]),
                     reads=[("xg_s", s)], dma="st_xg%d" % s)
            for n in range(2):
                bank, pk = psr.next()
                P.op("pe", mm_fm(bank, 512 + n * 128, b), reads=[hk] + wbk, writes=[pk])
                P.op("act", lambda e, bank=bank, n=n: e.copy(out=u_s[:, n, :], in_=psum[:, bank, :]),
                     reads=[pk], writes=[("u_s", n)])
            for blk in range(4):
                bank, pk = psr.next()
                s = blk % 2
                P.op("pe", mm_tm(bank, 768, 256, b, blk), reads=[hk] + wbk, writes=[pk])
                P.op("dve", lambda e, bank=bank, s=s: e.bn_stats(out=stats[:, s, :], in_=psum[:, bank, 0:256]),
                     reads=[pk], writes=[("stats", s)])
                P.op("dve", lambda e, s=s: e.bn_aggr(out=mv[:, s, :], in_=stats[:, s, :]),
                     reads=[("stats", s)], writes=[("mv", s)])
                P.op("act", lambda e, s=s: e.activation(out=vr[:, s, :], in_=mv[:, s, 1:2], func=AF.Ln, bias=EPS, scale=1.0),
                     reads=[("mv", s)], writes=[("vr", s)])
                P.op("act", lambda e, s=s: e.activation(out=vr[:, s, :], in_=vr[:, s, :], func=AF.Exp, scale=-0.5),
                     reads=[("vr", s)], writes=[("vr", s)])
                P.op("dve", lambda e, bank=bank, s=s: e.tensor_scalar(
                    out=vtmp[:, s, :], in0=psum[:, bank, 0:256], scalar1=mv[:, s, 0:1], scalar2=vr[:, s, :],
                    op0=ALU.subtract, op1=ALU.mult), reads=[pk, ("mv", s), ("vr", s)], writes=[("vtmp", s)])
                P.op("pool", lambda e, s=s: e.tensor_tensor(out=vtmp[:, s, :], in0=vtmp[:, s, :], in1=sgg_s[:], op=ALU.mult),
                     reads=[("vtmp", s), "sgg"], writes=[("vtmp", s)])
                P.op("pool", lambda e, s=s, blk=blk: e.tensor_tensor(out=vn[:, blk, :], in0=vtmp[:, s, :], in1=sgb_s[:], op=ALU.add),
                     reads=[("vtmp", s), "sgb"], writes=[("vn", blk)])
            for n in range(2):
                bank, pk = psr.next()

                def mix(e, bank=bank, n=n):
                    for blk in range(4):
                        for gg in range(2):
                            g = 2 * n + gg
                            i = e.matmul(psum[gg * 64:(gg + 1) * 64, bank, blk * 128:(blk + 1) * 128],
                                         lhsT=vn[:, blk, g * 64:(g + 1) * 64], rhs=wsT_b[:, g, :],
                                         start=True, stop=True)
                    return i
                P.op("pe", mix, reads=[("vn", 0), ("vn", 1), ("vn", 2), ("vn", 3), "wsT_b"], writes=[pk])
                bsv = bsb_s[:, n, :].unsqueeze(1).broadcast_to([128, 4, 128])
                P.op("dve", lambda e, bank=bank, n=n, bsv=bsv: e.tensor_tensor(
                    out=mtmp[:, n, :].rearrange("p (b t) -> p b t", b=4),
                    in0=psum[:, bank, :].rearrange("p (b t) -> p b t", b=4), in1=bsv, op=ALU.add),
                    reads=[pk, "bsb"], writes=[("mtmp", n)])
                P.op("pool", lambda e, n=n: e.tensor_tensor(out=yb_s[:, n, :], in0=mtmp[:, n, :], in1=u_s[:, n, :], op=ALU.mult),
                     reads=[("mtmp", n), ("u_s", n)], writes=[("yb_s", n)])
                P.op("sp", lambda e, n=n, t0=t0: e.dma_start(out=yb_o[n, :, t0:t0 + TT], in_=yb_s[:, n, :]),
                     reads=[("yb_s", n)], dma="st_yb%d" % n)
            qrot = 0
            for which in range(2):
                for h in range(4):
                    col0 = 1024 + which * 512 + h * 128
                    bank, pk = psr.next()
                    bank2, pk2 = psr.next()
                    s = qrot % 2
                    s3 = qrot % 3
                    qrot += 1
                    P.op("pe", mm_fm(bank, col0, b), reads=[hk] + wbk, writes=[pk])
                    P.op("act", lambda e, bank=bank, s=s: e.activation(out=qsq[:, s, :], in_=psum[:, bank, :], func=AF.Square),
                         reads=[pk], writes=[("qsq", s)])
                    P.op("pe", lambda e, bank2=bank2, s=s: e.matmul(psum[:, bank2, :], lhsT=bones[:], rhs=qsq[:, s, :],
                                                                    start=True, stop=True),
                         reads=[("qsq", s), "bones"], writes=[pk2])
                    rstd_from_ps(P, psum[:, bank2, :], pk2, qr[:, s, :], ("qr", s), TT, 64 * EPS)
                    P.op("dve", lambda e, bank=bank, s=s, s3=s3, which=which: e.scalar_tensor_tensor(
                        out=qn[:, s3, :], in0=psum[:, bank, :], scalar=gqk_s[:, which:which + 1], in1=qr[:, s, :],
                        op0=ALU.mult, op1=ALU.mult), reads=[pk, ("qr", s), "gqk"], writes=[("qn", s3)])
                    P.op("sp", lambda e, h=h, which=which, s3=s3, t0=t0: e.dma_start(
                        out=qk_o(h, which)[:, t0:t0 + TT], in_=qn[:, s3, :]),
                        reads=[("qn", s3)], dma="st_qn%d" % s3)
            vb = t % 2
            for blk in range(4):
                bank, pk = psr.next()
                P.op("pe", mm_tm(bank, 2048, 512, b, blk), reads=[hk] + wbk, writes=[pk])
                eng = "act" if blk % 2 == 0 else "dve"
                if eng == "act":
                    fn = lambda e, bank=bank, blk=blk, vb=vb: e.copy(out=vv_s[:, vb, blk, :], in_=psum[:, bank, :])
                else:
                    fn = lambda e, bank=bank, blk=blk, vb=vb: e.tensor_copy(out=vv_s[:, vb, blk, :], in_=psum[:, bank, :])
                P.op(eng, fn, reads=[pk], writes=[("vv_s", vb, blk)])
            if t + 1 < ntile:
                norm(t + 1)
            for h in range(4):
                P.op("sp", lambda e, vb=vb, t0=t0, h=h: e.dma_start(
                    out=v_o[h, t0:t0 + TT, :].rearrange("(b p) n -> p b n", p=128), in_=vv_s[:, vb, :, h * 128:(h + 1) * 128]),
                    reads=[("vv_s", vb, k) for k in range(4)], dma="st_vv%d" % vb)


def build_L2(S):
    nc = bass.Bass("TRN2", target_bir_lowering=False)
    with contextlib.ExitStack() as es:
        C = Ctx(nc, es)
        qT = C.din("qT", [128, S], BF16)
        kT = C.din("kT", [128, S], BF16)
        v = C.din("v", [S, 128], BF16)
        xaT = C.din("xaT", [64, S])
        gaT = C.din("gaT", [64, S])
        io = {
            "nsrc": 1,
            "q_src": lambda e, j: qT, "k_src": lambda e, j: kT, "v_src": lambda e, j: v,
            "xa_src": lambda e, j: xaT, "ga_src": lambda e, j: gaT,
            "cw": C.din("cw", [64, 4]), "lvec": C.din("lvec", [64, 4]), "wa": C.din("wa", [64, 64]), "wi": C.din("wi", [64, 64]),
            "lq": C.din("lq", [128, 4, 64]), "subg": C.din("subg", [128, 128]), "rb": C.din("rb", [128, 32]),
            "idx": C.din("idx", [128, 12, 128]), "maskT": C.din("maskT", [128, 128]), "ident": C.din("ident", [128, 128]),
            "lcon": C.din("lcon", [128, 2]),
        }
        yc_t = C.dout("ycT", [128, S], BF16)
        ya_t = C.dout("yaT", [64, S], BF16)
        io["ycT"] = lambda c0, n: yc_t[:, c0:c0 + n]
        io["yaT"] = lambda c0, n: ya_t[:, c0:c0 + n]
        emit_L2(C, io, S)
        C.P.emit()
    return nc


def emit_L2(C, io, S):
    NG = S // TT
    NKT = S // 128
    TL = 512
    NCH = S // TL
    nsrc = io["nsrc"]
    NS = S // nsrc
    if True:
        P = C.P
        cw, lvec, wa, wi, lq, subg, rb, idx, maskT, ident, lcon = (io[k] for k in (
            "cw", "lvec", "wa", "wi", "lq", "subg", "rb", "idx", "maskT", "ident", "lcon"))
        yc_o, ya_o = io["ycT"], io["yaT"]

        q0T = C.sb("q0T", [128, S], BF16)
        q1T = C.sb("q1T", [128, S], BF16)
        kTs = C.sb("kTs", [128, S], BF16)
        v1 = C.sb("v1", [128, NKT, 129], BF16)
        biasT = C.sb("biasT", [128, 12, 128], BF16)
        bias_f = C.sb("bias_f", [128, 12, 128], F32)
        idx_s = C.sb("idx_s", [128, 12, 128], F32)
        btmp = C.sb("btmp", [128, 2, 12, 128], F32)
        mask_s = C.sb("mask_s", [128, 128], F32)
        id_f = C.sb("id_f", [128, 128], F32)
        id_b = C.sb("id_b", [128, 128], BF16)
        rb_s = C.sb("rb_s", [128, 32], F32)
        lq_s = C.sb("lq_s", [128, 4, 64], F32)
        lq_t = C.sb("lq_t", [128, 2, 64], F32)
        lsc = C.sb("lsc", [128, 8], F32)
        lcon_s = C.sb("lcon_s", [128, 2], F32)
        subg_s = C.sb("subg_s", [128, 128], F32)
        pt = C.sb("pt", [128, 3, 2, TT], BF16)
        accs = C.sb("accs", [128, 3, TT], F32)
        rl = C.sb("rl", [128, 8], F32)
        o_s = C.sb("o_s", [128, 4, 128], F32)
        osq = C.sb("osq", [128, 128], F32)
        ss = C.sb("ss", [128, 4], F32)
        y_s = C.sb("y_s", [128, 4, 128], BF16)
        yc_s = C.sb("yc_s", [128, 2, TT], BF16)
        cw_s = C.sb("cw_s", [64, 4], F32)
        lv_s = C.sb("lv_s", [64, 4], F32)
        csc = C.sb("csc", [64, 4], F32)
        w_f = C.sb("w_f", [64, 2, 64], F32)
        w_b = C.sb("w_b", [64, 2, 64], BF16)
        xa_s = C.sb("xa_s", [64, 2, TL + 3], F32)
        ga_s = C.sb("ga_s", [64, 2, TL], F32)
        xc = C.sb("xc", [64, TL], F32)
        xcb = C.sb("xcb", [64, TL], BF16)
        r_s = C.sb("r_s", [64, TL], F32)
        i_s = C.sb("i_s", [64, TL], F32)
        a_s = C.sb("a_s", [64, TL], F32)
        m_s = C.sb("m_s", [64, TL], F32)
        h_s = C.sb("h_s", [64, 2, TL], F32)
        g_s = C.sb("g_s", [64, TL], F32)
        ya_s = C.sb("ya_s", [64, 2, TL], BF16)
        psum = C.ps("psum", [128, 8, TT], F32)
        trp = psum[:, 7, :].bitcast(BF16)

        for dst, src, key in ((cw_s[:], cw, "cw"), (lv_s[:], lvec, "lv"), (w_f[:, 0, :], wa, "wa"), (w_f[:, 1, :], wi, "wi"),
                              (lq_s[:], lq, "lq"), (subg_s[:], subg, "subg"), (rb_s[:], rb, "rb"), (idx_s[:], idx, "idx"),
                              (mask_s[:], maskT, "mask"), (id_f[:], ident, "id_f"), (lcon_s[:], lcon, "lcon")):
            P.op("sp", lambda e, dst=dst, src=src: e.dma_start(out=dst, in_=src), writes=[key], dma="c_" + key)
        P.op("dve", lambda e: e.tensor_copy(out=id_b[:], in_=id_f[:]), reads=["id_f"], writes=["id_b"])
        P.op("dve", lambda e: e.tensor_copy(out=w_b[:], in_=w_f[:]), reads=["wa", "wi"], writes=["w_b"])
        for j in range(2):
            P.op("dve", lambda e, j=j: e.scalar_tensor_tensor(out=lq_t[:, j, :], in0=lq_s[:, 2 * j, :], scalar=1.0,
                                                              in1=lq_s[:, 2 * j + 1, :], op0=ALU.mult, op1=ALU.mult,
                                                              accum_out=lsc[:, j:j + 1]),
                 reads=["lq"], writes=[("lsc", j)])
        P.op("act", lambda e: e.activation(out=lsc[:, 2:4], in_=lsc[:, 0:2], func=AF.Exp),
             reads=[("lsc", 0), ("lsc", 1)], writes=["lsce"])
        P.op("dve", lambda e: e.tensor_tensor(out=lsc[:, 4:5], in0=lsc[:, 2:3], in1=lsc[:, 3:4], op=ALU.subtract),
             reads=["lsce"], writes=["lam"])
        P.op("dve", lambda e: e.tensor_tensor(out=lsc[:, 4:5], in0=lsc[:, 4:5], in1=lcon_s[:, 0:1], op=ALU.add),
             reads=["lam", "lcon"], writes=["lam"])
        P.op("dve", lambda e: e.tensor_scalar(out=lsc[:, 5:6], in0=lsc[:, 4:5], scalar1=-1.0, scalar2=None, op0=ALU.mult),
             reads=["lam"], writes=["nlam"])
        P.op("dve", lambda e: e.tensor_scalar(out=subg_s[:], in0=subg_s[:], scalar1=lcon_s[:, 1:2], scalar2=None, op0=ALU.mult),
             reads=["subg", "lcon"], writes=["subg"])
        P.op("act", lambda e: e.activation(out=csc[:, 0:1], in_=lv_s[:, 3:4], func=AF.Exp, scale=-1.0),
             reads=["lv"], writes=["csc0"])
        P.op("act", lambda e: e.activation(out=csc[:, 0:1], in_=csc[:, 0:1], func=AF.Ln, bias=1.0, scale=1.0),
             reads=["csc0"], writes=["csc0"])
        P.op("dve", lambda e: e.tensor_scalar(out=csc[:, 1:2], in0=csc[:, 0:1], scalar1=-LRU_C, scalar2=None, op0=ALU.mult),
             reads=["csc0"], writes=["csc"])
        P.op("dve", lambda e: e.tensor_scalar(out=csc[:, 2:3], in0=csc[:, 0:1], scalar1=-2.0 * LRU_C, scalar2=None, op0=ALU.mult),
             reads=["csc0"], writes=["csc"])

        P.op("dve", lambda e: e.memset(bias_f[:], 0.0), writes=["bias_f"])
        idx_np = l2_consts()["idx"]
        for d_ in range(12):
            for bkt in sorted(set(int(v_) for v_ in np.unique(idx_np[:, d_, :]))):
                P.op("dve", lambda e, bkt=bkt, d_=d_: e.tensor_single_scalar(
                    out=btmp[:, 0, d_, :], in_=idx_s[:, d_, :], scalar=float(bkt), op=ALU.is_equal),
                    reads=["idx"], writes=[("btmp", 0)])
                P.op("dve", lambda e, bkt=bkt, d_=d_: e.scalar_tensor_tensor(
                    out=bias_f[:, d_, :], in0=btmp[:, 0, d_, :], scalar=rb_s[:, bkt:bkt + 1], in1=bias_f[:, d_, :],
                    op0=ALU.mult, op1=ALU.add), reads=[("btmp", 0), "rb", "bias_f"], writes=["bias_f"])
        P.op("dve", lambda e: e.tensor_tensor(out=bias_f[:, 0, :], in0=bias_f[:, 0, :], in1=mask_s[:], op=ALU.add),
             reads=["bias_f", "mask"], writes=["bias_f"])
        P.op("dve", lambda e: e.tensor_copy(out=biasT[:], in_=bias_f[:]), reads=["bias_f"], writes=["biasT"])

        for ch in range(NCH):
            b = ch % 2
            t0 = ch * TL
            js = t0 // NS
            l0 = t0 - js * NS
            if ch == 0:
                P.op("dve", lambda e: e.memset(xa_s[:, 0, 0:3], 0.0), writes=[("xa", 0)])
                P.op("sp", lambda e: e.dma_start(out=xa_s[:, 0, 3:3 + TL], in_=io["xa_src"](e, 0)[:, 0:TL]),
                     reads=io.get("xa_dep", []), writes=[("xa", 0)], dma="ld_xa0")
            else:
                P.op("dve", lambda e, b=b: e.tensor_copy(out=xa_s[:, b, 0:3], in_=xa_s[:, 1 - b, TL:TL + 3]),
                     reads=[("xa", 1 - b)], writes=[("xa", b)])
                P.op("sp", lambda e, b=b, js=js, l0=l0: e.dma_start(out=xa_s[:, b, 3:3 + TL], in_=io["xa_src"](e, js)[:, l0:l0 + TL]),
                     reads=io.get("xa_dep", []), writes=[("xa", b)], dma="ld_xa%d" % b)
            P.op("sp", lambda e, b=b, js=js, l0=l0: e.dma_start(out=ga_s[:, b, :], in_=io["ga_src"](e, js)[:, l0:l0 + TL]),
                 reads=io.get("ga_dep", []), writes=[("ga", b)], dma="ld_ga%d" % b)
            P.op("dve", lambda e, b=b: e.tensor_scalar(out=xc[:], in0=xa_s[:, b, 3:3 + TL], scalar1=cw_s[:, 3:4],
                                                       scalar2=lv_s[:, 0:1], op0=ALU.mult, op1=ALU.add),
                 reads=[("xa", b), "cw", "lv"], writes=["xc"])
            for tap in range(3):
                P.op("dve", lambda e, b=b, tap=tap: e.scalar_tensor_tensor(
                    out=xc[:], in0=xa_s[:, b, tap:tap + TL], scalar=cw_s[:, tap:tap + 1], in1=xc[:],
                    op0=ALU.mult, op1=ALU.add), reads=[("xa", b), "cw", "xc"], writes=["xc"])
            P.op("act", lambda e: e.copy(out=xcb[:], in_=xc[:]), reads=["xc"], writes=["xcb"])
            nh = TL // TT
            for gate in range(2):
                for hh in range(nh):
                    bank = gate * nh + hh
                    P.op("pe", lambda e, gate=gate, hh=hh, bank=bank: e.matmul(
                        psum[0:64, bank, :], lhsT=w_b[:, gate, :], rhs=xcb[:, hh * TT:(hh + 1) * TT], start=True, stop=True),
                        reads=["xcb", "w_b"], writes=[("st", bank // 2)])
            for hh in range(nh):
                P.op("act", lambda e, hh=hh: e.activation(out=r_s[:, hh * TT:(hh + 1) * TT], in_=psum[0:64, hh, :],
                                                          func=AF.Sigmoid, bias=lv_s[:, 1:2], scale=1.0),
                     reads=[("st", hh // 2), "lv"], writes=["r_s"])
            for hh in range(nh):
                P.op("act", lambda e, hh=hh: e.activation(out=i_s[:, hh * TT:(hh + 1) * TT], in_=psum[0:64, nh + hh, :],
                                                          func=AF.Sigmoid, bias=lv_s[:, 2:3], scale=1.0),
                     reads=[("st", (nh + hh) // 2), "lv"], writes=["i_s"])
            P.op("act", lambda e, b=b: e.activation(out=g_s[:], in_=ga_s[:, b, :], func=AF.Square),
                 reads=[("ga", b)], writes=["g_s"])
            P.op("act", lambda e: e.activation(out=g_s[:], in_=g_s[:], func=AF.Identity, bias=1.0, scale=0.044715),
                 reads=["g_s"], writes=["g_s"])
            P.op("dve", lambda e, b=b: e.tensor_tensor(out=g_s[:], in0=g_s[:], in1=ga_s[:, b, :], op=ALU.mult),
                 reads=["g_s", ("ga", b)], writes=["g_s"])
            P.op("act", lambda e: e.activation(out=g_s[:], in_=g_s[:], func=AF.Sigmoid, scale=1.5957691216057308),
                 reads=["g_s"], writes=["g_s"])
            P.op("dve", lambda e, b=b: e.tensor_tensor(out=g_s[:], in0=g_s[:], in1=ga_s[:, b, :], op=ALU.mult),
                 reads=["g_s", ("ga", b)], writes=["g_s"])
            P.op("act", lambda e: e.activation(out=a_s[:], in_=r_s[:], func=AF.Exp, scale=csc[:, 1:2]),
                 reads=["r_s", "csc"], writes=["a_s"])
            P.op("act", lambda e: e.activation(out=m_s[:], in_=r_s[:], func=AF.Exp, scale=csc[:, 2:3]),
                 reads=["r_s", "csc"], writes=["m_s"])
            P.op("act", lambda e: e.activation(out=m_s[:], in_=m_s[:], func=AF.Sqrt, bias=1.0, scale=-1.0),
                 reads=["m_s"], writes=["m_s"])
            P.op("dve", lambda e: e.tensor_tensor(out=i_s[:], in0=i_s[:], in1=xc[:], op=ALU.mult),
                 reads=["i_s", "xc"], writes=["i_s"])
            P.op("dve", lambda e: e.tensor_tensor(out=m_s[:], in0=m_s[:], in1=i_s[:], op=ALU.mult),
                 reads=["m_s", "i_s"], writes=["m_s"])
            init = 0.0 if ch == 0 else h_s[:, 1 - b, TL - 1:TL]
            P.op("dve", lambda e, b=b, init=init: e.tensor_tensor_scan(out=h_s[:, b, :], data0=a_s[:], data1=m_s[:],
                                                                       initial=init, op0=ALU.mult, op1=ALU.add),
                 reads=["a_s", "m_s", ("h_s", 1 - b)], writes=[("h_s", b)])
            P.op("dve", lambda e, b=b: e.tensor_tensor(out=ya_s[:, b, :], in0=h_s[:, b, :], in1=g_s[:], op=ALU.mult),
                 reads=[("h_s", b), "g_s"], writes=[("ya_s", b)])
            P.op("sp", lambda e, b=b, t0=t0: e.dma_start(out=ya_o(t0, TL), in_=ya_s[:, b, :]),
                 reads=[("ya_s", b)], writes=[("ya_dram", ch)], dma="st_ya%d" % b)
        if "after_ya" in io:
            io["after_ya"](P, NCH)

        if "pre_attn" in io:
            io["pre_attn"](P)
        P.op("pool", lambda e: e.memset(q0T[64:128, :], 0.0), writes=["q0T"])
        P.op("pool", lambda e: e.memset(q1T[0:64, :], 0.0), writes=["q1T"])
        P.op("pool", lambda e: e.memset(v1[:, :, 128:129], 1.0), writes=["v1ones"])
        if "load_qk" in io:
            io["load_qk"](P, q0T, q1T, kTs)
        else:
            P.op("sp", lambda e: e.dma_start(out=q0T[0:64, :], in_=io["q_src"](e, 0)[0:64, :]), writes=["q0T"], dma="ld_q0")
            P.op("sp", lambda e: e.dma_start(out=q1T[64:128, :], in_=io["q_src"](e, 0)[64:128, :]), writes=["q1T"], dma="ld_q1")
            P.op("sp", lambda e: e.dma_start(out=kTs[:], in_=io["k_src"](e, 0)), writes=["kTs"], dma="ld_k")
        nvd = max(nsrc, NKT // 16)
        for i in range(nvd):
            a, b_ = i * NKT // nvd, (i + 1) * NKT // nvd
            j = (a * 128) // NS
            ra = a * 128 - j * NS
            rb_ = b_ * 128 - j * NS
            P.op("sp", lambda e, a=a, b_=b_, j=j, ra=ra, rb_=rb_: e.dma_start(
                out=v1[:, a:b_, 0:128], in_=io["v_src"](e, j)[ra:rb_, :].rearrange("(kt p) e -> p kt e", p=128)),
                reads=io.get("v_dep", []), writes=[("v1", i)], dma="ld_v%d" % i)

        def acc_ap(a, lo=0, hi=129):
            return psum[:, 4 + a // 3, (a % 3) * 160 + lo:(a % 3) * 160 + hi]

        def accs_ap(a, lo=0, hi=129):
            return accs[:, a // 3, (a % 3) * 160 + lo:(a % 3) * 160 + hi]

        pairs = [(G, kt) for G in range(NG) for kt in range(4 * G + 4)]

        def geom(G, kt):
            jj = max(kt - 4 * G, 0)
            nblk = 4 - jj
            d0 = 4 * G + jj - kt
            return jj, nblk, d0, (d0 <= 8)

        def emit_qk(n):
            G, kt = pairs[n]
            jj, nblk, d0, near = geom(G, kt)
            sb_ = n % 2
            c0, c1 = jj * 128, TT
            q0 = G * TT + c0

            def fn(e):
                for c, qsrc in ((0, q0T), (1, q1T)):
                    i = e.matmul(psum[:, 2 * sb_ + c, c0:c1], lhsT=kTs[:, kt * 128:(kt + 1) * 128],
                                 rhs=qsrc[:, q0:q0 + nblk * 128], start=True, stop=not near)
                    if near:
                        i = e.matmul(psum[:, 2 * sb_ + c, c0:c1], lhsT=id_b[:],
                                     rhs=biasT[:, d0:d0 + nblk, :], start=False, stop=True)
                return i
            P.op("pe", fn, reads=["q0T", "q1T", "kTs", "id_b", "biasT"], writes=[("st", sb_)])

        def emit_exp(n):
            G, kt = pairs[n]
            jj, nblk, d0, near = geom(G, kt)
            sb_, pb = n % 2, n % 3
            c0 = jj * 128
            src = psum[:, 2 * sb_:2 * sb_ + 2, c0:TT]
            dst = pt[:, pb, :, c0:TT]
            if near:
                fn = lambda e: e.activation(out=dst, in_=src, func=AF.Exp)
            else:
                fn = lambda e: e.activation(out=dst, in_=src, func=AF.Exp, bias=rb_s[:, 15:16], scale=1.0)
            P.op("act", fn, reads=[("st", sb_), "rb"], writes=[("pt", pb)])

        def emit_pv(n):
            G, kt = pairs[n]
            jj, nblk, d0, near = geom(G, kt)
            pb = n % 3

            def fn(e):
                for i_ in range(jj, 4):
                    for c in range(2):
                        i = e.matmul(acc_ap(c * 4 + i_), lhsT=pt[:, pb, c, i_ * 128:(i_ + 1) * 128], rhs=v1[:, kt, :],
                                     start=False, stop=False, skip_group_check=True)
                return i
            P.op("pe", fn, reads=[("pt", pb), ("v1", kt * nvd // NKT), "v1ones"], writes=["acc"])

        def emit_evac(G):
            yb_ = G % 2
            P.op("dve", lambda e: e.tensor_copy(out=accs[:], in_=psum[:, 4:7, :]), reads=["acc"], writes=["accs"])
            P.op("dve", lambda e: e.reciprocal(out=rl[:, 0:6].rearrange("p (a b) -> p a b", a=2),
                                               in_=accs[:, 0:2, 128:449:160]), reads=["accs"], writes=["rl"])
            P.op("dve", lambda e: e.reciprocal(out=rl[:, 6:8], in_=accs[:, 2, 128:289:160]), reads=["accs"], writes=["rl"])
            P.op("dve", lambda e: e.tensor_scalar(out=rl[:, 4:8], in0=rl[:, 4:8], scalar1=lsc[:, 5:6], scalar2=None, op0=ALU.mult),
                 reads=["rl", "nlam"], writes=["rl"])
            for i_ in range(4):
                P.op("dve", lambda e, i_=i_: e.tensor_scalar(out=o_s[:, i_, :], in0=accs_ap(i_, 0, 128), scalar1=rl[:, i_:i_ + 1],
                                                             scalar2=None, op0=ALU.mult),
                     reads=["accs", "rl"], writes=[("o_s", i_)])
                P.op("dve", lambda e, i_=i_: e.scalar_tensor_tensor(out=o_s[:, i_, :], in0=accs_ap(4 + i_, 0, 128),
                                                                    scalar=rl[:, 4 + i_:5 + i_], in1=o_s[:, i_, :],
                                                                    op0=ALU.mult, op1=ALU.add),
                     reads=["accs", "rl", ("o_s", i_)], writes=[("o_s", i_)])
                P.op("dve", lambda e, i_=i_: e.scalar_tensor_tensor(out=osq[:], in0=o_s[:, i_, :], scalar=1.0, in1=o_s[:, i_, :],
                                                                    op0=ALU.mult, op1=ALU.mult, accum_out=ss[:, i_:i_ + 1]),
                     reads=[("o_s", i_)], writes=["osq", ("ss", i_)])
            P.op("act", lambda e: e.activation(out=ss[:], in_=ss[:], func=AF.Ln, bias=128 * EPS, scale=1.0),
                 reads=[("ss", k) for k in range(4)], writes=["ssr"])
            P.op("act", lambda e: e.activation(out=ss[:], in_=ss[:], func=AF.Exp, scale=-0.5), reads=["ssr"], writes=["ssr"])
            for i_ in range(4):
                P.op("dve", lambda e, i_=i_: e.scalar_tensor_tensor(out=y_s[:, i_, :], in0=o_s[:, i_, :], scalar=ss[:, i_:i_ + 1],
                                                                    in1=subg_s[:], op0=ALU.mult, op1=ALU.mult),
                     reads=[("o_s", i_), "ssr", "subg"], writes=[("y_s", i_)])

            def tr(e):
                for i_ in range(4):
                    i = e.transpose(trp[:, i_ * 128:(i_ + 1) * 128], y_s[:, i_, :], id_b[:])
                return i
            P.op("pe", tr, reads=[("y_s", k) for k in range(4)] + ["id_b"], writes=["trp"])
            P.op("dve", lambda e, yb_=yb_: e.tensor_copy(out=yc_s[:, yb_, :], in_=trp[:, 0:TT]), reads=["trp"], writes=[("yc_s", yb_)])
            P.op("sp", lambda e, yb_=yb_, G=G: e.dma_start(out=yc_o(G * TT, TT), in_=yc_s[:, yb_, :]),
                 reads=[("yc_s", yb_)], writes=[("yc_dram", G)], dma="st_yc%d" % yb_)
            if "after_yc" in io:
                io["after_yc"](P, G, NG)

        emit_qk(0)
        for n, (G, kt) in enumerate(pairs):
            if n + 1 < len(pairs):
                emit_qk(n + 1)
            if kt == 0:
                P.op("dve", lambda e: e.memset(psum[:, 4:7, :], 0.0), writes=["acc"])
            emit_exp(n)
            emit_pv(n)
            if kt == 4 * G + 3:
                emit_evac(G)


T3 = 256


def build_L3(NT):
    nc = bass.Bass("TRN2", target_bir_lowering=False)
    H = FFN_HIDDEN
    with contextlib.ExitStack() as es:
        C = Ctx(nc, es)
        yaT = C.din("yaT", [256, NT], BF16)
        ybT = C.din("ybT", [256, NT], BF16)
        ycT = C.din("ycT", [512, NT], BF16)

        def y_src(e, kind, i, c0, n):
            if kind == "ya":
                return yaT[64 * i:64 * i + 64, c0:c0 + n]
            if kind == "yb":
                return ybT[128 * i:128 * i + 128, c0:c0 + n]
            return ycT[128 * i:128 * i + 128, c0:c0 + n]
        xo = C.dout("xo", [1024, NT])
        io = {
            "x_in": C.din("xT", [1024, NT]), "x_mid": xo, "x_out": xo,
            "g1": C.din("g1", [128, 8]), "g2": C.din("g2", [128, 8]), "w_in": C.din("w_in", [1024, IN_COLS]),
            "bg": C.din("bg", [128, 24]), "y_src": y_src,
            "w_pa": C.din("w_pa", [256, 1024]), "w_pb": C.din("w_pb", [256, 1024]), "w_pc": C.din("w_pc", [512, 1024]),
            "w_o": C.din("w_o", [1024, 1024]), "w_g": C.din("w_g", [1024, H]), "w_u": C.din("w_u", [1024, H]),
            "w_d": C.din("w_d", [H, 1024]),
        }
        emit_L3(C, io, NT)
        C.P.emit()
    return nc


def emit_L3(C, io, NT):
    ntile = NT // T3
    H = FFN_HIDDEN
    HC = H // 128
    if True:
        P = C.P
        xT, xmid, xo = io["x_in"], io["x_mid"], io["x_out"]
        g1, g2, w_in, bg, w_pa, w_pb, w_pc, w_o, w_g, w_u, w_d = (io[k] for k in (
            "g1", "g2", "w_in", "bg", "w_pa", "w_pb", "w_pc", "w_o", "w_g", "w_u", "w_d"))

        wbuf = C.sb("wbuf", [128, 3 * 8 * H], BF16)
        stage = C.sb("stage", [128, 2, 1408], F32)
        xt = C.sb("xt", [128, 2, 8, T3], F32)
        sq = C.sb("sq", [128, 8, T3], BF16)
        hT = C.sb("hT", [128, 8, T3], BF16)
        rstd = C.sb("rstd", [128, T3], F32)
        ones = C.sb("ones", [128, 128], BF16)
        g1s = C.sb("g1s", [128, 8], F32)
        g2s = C.sb("g2s", [128, 8], F32)
        bg_s = C.sb("bg_s", [128, 24], F32)
        y_s = C.sb("y_s", [128, 2, 8, T3], BF16)
        gs = C.sb("gs", [128, 2, 3, T3], F32)
        mt = C.sb("mt", [128, 2, 3, T3], F32)
        mT = C.sb("mT", [128, 8, T3], BF16)
        sg = C.sb("sg", [128, 2, T3], F32)
        actT = C.sb("actT", [128, HC, T3], BF16)
        psum = C.ps("psum", [128, 8, 2, T3], F32)
        psr = Rot("ps", 8)

        def wview(off, kc, n):
            return wbuf[:, off:off + kc * n].rearrange("p (c n) -> p c n", c=kc)
        wgt_ = wview(0, 8, 3072)
        wpa_ = wview(24576, 2, 1024)
        wpb_ = wview(24576 + 2048, 2, 1024)
        wpc_ = wview(24576 + 4096, 4, 1024)
        wo_ = wview(24576 + 8192, 8, 1024)
        fg_ = wview(0, 8, H)
        fu_ = wview(8 * H, 8, H)
        fd_ = wview(16 * H, HC, 1024)

        P.op("dve", lambda e: e.memset(ones[:], 1.0), writes=["ones"])
        for dst, src, key in ((g1s, g1, "g1s"), (g2s, g2, "g2s"), (bg_s, bg, "bg")):
            P.op("sp", lambda e, dst=dst, src=src: e.dma_start(out=dst[:], in_=src), writes=[key], dma="c_" + key)
        for t_, k_ in ((g1s, "g1s"), (g2s, "g2s")):
            P.op("dve", lambda e, t_=t_: e.tensor_scalar(out=t_[:], in0=t_[:], scalar1=32.0, scalar2=None, op0=ALU.mult),
                 reads=[k_], writes=[k_])

        def load_x(t, src_ap, srckeys):
            b = t % 2
            src = src_ap[:, t * T3:(t + 1) * T3].rearrange("(c p) n -> p c n", p=128)
            P.op("sp", lambda e, b=b, src=src: e.dma_start(out=xt[:, b, :, :], in_=src),
                 reads=srckeys, writes=[("xt", b)], dma="xld%d" % b)

        def load_y(t):
            b = t % 2
            c0 = t * T3
            for h in range(4):
                P.op("sp", lambda e, b=b, h=h, c0=c0: e.dma_start(
                    out=y_s[(h % 2) * 64:(h % 2) * 64 + 64, b, h // 2, :], in_=io["y_src"](e, "ya", h, c0, T3)),
                    writes=[("y_s", b, 0)], dma="yld%d_0" % b)
            for i in range(2):
                P.op("sp", lambda e, b=b, i=i, c0=c0: e.dma_start(out=y_s[:, b, 2 + i, :], in_=io["y_src"](e, "yb", i, c0, T3)),
                     writes=[("y_s", b, 1)], dma="yld%d_1" % b)
            for h in range(4):
                P.op("sp", lambda e, b=b, h=h, c0=c0: e.dma_start(out=y_s[:, b, 4 + h, :], in_=io["y_src"](e, "yc", h, c0, T3)),
                     writes=[("y_s", b, 2)], dma="yld%d_2" % b)

        srot = Rot("stage", 2)
        load_x(0, xT, [])
        load_y(0)
        kg = load_weight_bf16(C, wgt_, w_in[:, 2560:5632], 1024, 3072, "wgt", stage, srot, scale_ap=g1s,
                              colblk=1024, scale_key="g1s")
        kpa = load_weight_bf16(C, wpa_, w_pa, 256, 1024, "wpa", stage, srot, colblk=1024)
        kpb = load_weight_bf16(C, wpb_, w_pb, 256, 1024, "wpb", stage, srot, colblk=1024)
        kpc = load_weight_bf16(C, wpc_, w_pc, 512, 1024, "wpc", stage, srot, colblk=1024)
        ko = load_weight_bf16(C, wo_, w_o, 1024, 1024, "wo", stage, srot, colblk=1024)
        c1keys = kg + kpa + kpb + kpc + ko

        def mm_fm(e, bank, half, wv, kc, col0, rhs_fn):
            for k in range(kc):
                i = e.matmul(psum[:, bank, half, :], lhsT=wv[:, k, col0:col0 + 128], rhs=rhs_fn(k),
                             start=(k == 0), stop=(k == kc - 1))
            return i

        def norm(t):
            b = t % 2
            bank, pk = psr.next()
            rms_tile(C, xt[:, b], ("xt", b), hT, "hT", sq, "sq", ones, psum[:, bank, 0, :], pk, rstd, "rstd", T3, 1024 * EPS)

        def resid(t, wv, kc, rhs_fn, rkeys, outkey, xdst):
            b = t % 2
            for m in range(4):
                bank, pk = psr.next()

                def fn(e, bank=bank, m=m):
                    for hf in range(2):
                        i = mm_fm(e, bank, hf, wv, kc, (2 * m + hf) * 128, rhs_fn)
                    return i
                P.op("pe", fn, reads=rkeys, writes=[pk])
                P.op("dve", lambda e, bank=bank, m=m, b=b: e.tensor_tensor(
                    out=xt[:, b, 2 * m:2 * m + 2, :], in0=xt[:, b, 2 * m:2 * m + 2, :], in1=psum[:, bank, :, :], op=ALU.add),
                    reads=[pk, ("xt", b)], writes=[("xt", b)])
            dst = xdst[:, t * T3:(t + 1) * T3].rearrange("(c p) n -> p c n", p=128)
            P.op("sp", lambda e, b=b, dst=dst: e.dma_start(out=dst, in_=xt[:, b, :, :]),
                 reads=[("xt", b)], writes=[(outkey, t)], dma="st_x%d" % b)

        projs = ((wpa_, 2, 0, kpa), (wpb_, 2, 2, kpb), (wpc_, 4, 4, kpc))
        for t in range(ntile):
            b = t % 2
            if t + 1 < ntile:
                load_x(t + 1, xT, [])
                load_y(t + 1)
            norm(t)
            for n in range(8):
                gb = n % 2
                slots = []
                for br in range(3):
                    bank, pk = psr.next()
                    slots.append((bank, pk))
                    wv, kc, off, kk = projs[br]

                    def fn(e, bank=bank, br=br, n=n, wv=wv, kc=kc, off=off, b=b):
                        mm_fm(e, bank, 0, wgt_, 8, br * 1024 + n * 128, lambda k: hT[:, k, :])
                        return mm_fm(e, bank, 1, wv, kc, n * 128, lambda k: y_s[:, b, off + k, :])
                    P.op("pe", fn, reads=["hT", ("y_s", b, br)] + kg + kk, writes=[pk])
                for br in range(3):
                    bank, pk = slots[br]
                    ch = br * 8 + n
                    P.op("act", lambda e, bank=bank, br=br, ch=ch, gb=gb: e.activation(
                        out=gs[:, gb, br, :], in_=psum[:, bank, 0, :], func=AF.Sigmoid, bias=bg_s[:, ch:ch + 1], scale=1.0),
                        reads=[pk, "bg"], writes=[("gs", gb, br)])
                    P.op("dve", lambda e, bank=bank, br=br, gb=gb: e.tensor_tensor(
                        out=mt[:, gb, br, :], in0=psum[:, bank, 1, :], in1=gs[:, gb, br, :], op=ALU.mult),
                        reads=[pk, ("gs", gb, br)], writes=[("mt", gb, br)])
                P.op("pool", lambda e, gb=gb: e.tensor_tensor(out=mt[:, gb, 0, :], in0=mt[:, gb, 0, :], in1=mt[:, gb, 1, :], op=ALU.add),
                     reads=[("mt", gb, 0), ("mt", gb, 1)], writes=[("mt", gb, 0)])
                P.op("pool", lambda e, gb=gb, n=n: e.tensor_tensor(out=mT[:, n, :], in0=mt[:, gb, 0, :], in1=mt[:, gb, 2, :], op=ALU.add),
                     reads=[("mt", gb, 0), ("mt", gb, 2)], writes=[("mT", n)])
            resid(t, wo_, 8, lambda k: mT[:, k, :], [("mT", k) for k in range(8)] + ko, "xo", xmid)

        kfg = load_weight_bf16(C, fg_, w_g, 1024, H, "fg", stage, srot, scale_ap=g2s, colblk=1408, scale_key="g2s",
                               also_writes=c1keys)
        kfu = load_weight_bf16(C, fu_, w_u, 1024, H, "fu", stage, srot, scale_ap=g2s, colblk=1408, scale_key="g2s",
                               also_writes=c1keys)
        kfd = load_weight_bf16(C, fd_, w_d, H, 1024, "fd", stage, srot, colblk=1024, also_writes=c1keys)
        load_x(0, xmid, [("xo", 0)])
        for t in range(ntile):
            b = t % 2
            if t + 1 < ntile:
                load_x(t + 1, xmid, [("xo", t + 1)])
            norm(t)
            for j in range(HC):
                sb_ = j % 2
                bank, pk = psr.next()

                def fn(e, bank=bank, j=j):
                    mm_fm(e, bank, 0, fg_, 8, j * 128, lambda k: hT[:, k, :])
                    return mm_fm(e, bank, 1, fu_, 8, j * 128, lambda k: hT[:, k, :])
                P.op("pe", fn, reads=["hT"] + kfg + kfu, writes=[pk])
                P.op("act", lambda e, bank=bank, sb_=sb_: e.activation(out=sg[:, sb_, :], in_=psum[:, bank, 0, :], func=AF.Silu),
                     reads=[pk], writes=[("sg", sb_)])
                P.op("dve", lambda e, bank=bank, sb_=sb_, j=j: e.tensor_tensor(out=actT[:, j, :], in0=psum[:, bank, 1, :], in1=sg[:, sb_, :], op=ALU.mult),
                     reads=[pk, ("sg", sb_)], writes=[("actT", j)])
            resid(t, fd_, HC, lambda k: actT[:, k, :], [("actT", k) for k in range(HC)] + kfd, "xo2", xo)


def _c(a):
    return np.ascontiguousarray(a, dtype=np.float32)


def l1_inputs(xT, inp, l):
    sgw = inp["sg_w"][l]
    sgb = inp["sg_b"][l]
    p = np.arange(128)
    bsb = np.stack([sgb[2 * n + p // 64, :] for n in range(2)], axis=1)
    gqk = np.stack([inp["q_norm_g"][l][p % 64], inp["k_norm_g"][l][p % 64]], axis=1)
    return {
        "xT": _c(xT),
        "g1": _c(inp["ln1_g"][l].reshape(8, 128).T),
        "w_in": _c(inp["w_in"][l]),
        "sgg": _c(np.broadcast_to(inp["sg_ln_g"][l], (128, 256))),
        "sgb": _c(np.broadcast_to(inp["sg_ln_b"][l], (128, 256))),
        "wsT": _c(sgw.transpose(2, 0, 1)),
        "tril": _c(np.triu(np.ones((128, 128)))),
        "bsb": _c(bsb),
        "gqk": _c(gqk),
    }


def t5_bucket_np(rel):
    import jax
    import jax.numpy as jnp
    with jax.default_device(jax.devices("cpu")[0]):
        rel = jnp.asarray(rel, jnp.int32)
        half, max_exact = 16, 8
        ret = jnp.where(rel > 0, half, 0)
        n = jnp.abs(rel)
        nf = jnp.maximum(n, 1).astype(jnp.float32)
        large = max_exact + (jnp.log(nf / max_exact) / math.log(2048 / max_exact) * (half - max_exact)).astype(jnp.int32)
        large = jnp.minimum(large, half - 1)
        return np.asarray(ret + jnp.where(n < max_exact, n, large))


_L2_CONST = {}


def l2_consts():
    if not _L2_CONST:
        k = np.arange(128)[:, None, None]
        d = np.arange(12)[None, :, None]
        q = np.arange(128)[None, None, :]
        rel = k - q - 128 * d
        _L2_CONST["idx"] = _c(t5_bucket_np(rel))
        kk = np.arange(128)[:, None]
        qq = np.arange(128)[None, :]
        _L2_CONST["maskT"] = _c(np.where((kk // 64) > (qq // 64), NEG, 0.0))
        _L2_CONST["ident"] = _c(np.eye(128))
    return _L2_CONST


def l2_inputs(qT, kT, v, xaT, gaT, inp, l, h):
    lam_init = 0.8 - 0.6 * math.exp(-0.3 * l)
    cs = l2_consts()
    ch = slice(64 * h, 64 * h + 64)
    lvec = np.stack([inp["conv_b"][l][ch], inp["lru_ba"][l][ch], inp["lru_bi"][l][ch], inp["lru_lambda"][l][ch]], axis=1)
    lq = np.stack([inp["lambda_q1"][l], inp["lambda_k1"][l], inp["lambda_q2"][l], inp["lambda_k2"][l]], axis=0)
    return {
        "qT": qT, "kT": kT, "v": v, "xaT": _c(xaT), "gaT": _c(gaT),
        "cw": _c(inp["conv_w"][l][:, ch].T),
        "lvec": _c(lvec),
        "wa": _c(inp["lru_wa"][l][h]), "wi": _c(inp["lru_wi"][l][h]),
        "lq": _c(np.broadcast_to(lq, (128, 4, 64))),
        "subg": _c(np.broadcast_to(inp["subln_g"][l], (128, 128))),
        "rb": _c(np.broadcast_to(inp["rel_bias"][:, h], (128, 32))),
        "idx": cs["idx"], "maskT": cs["maskT"], "ident": cs["ident"],
        "lcon": _c(np.broadcast_to(np.array([lam_init, (1.0 - lam_init) * math.sqrt(128.0)]), (128, 2))),
    }


def l3_inputs(xT, yaT, ybT, ycT, inp, l):
    return {
        "xT": _c(xT),
        "g1": _c(inp["ln1_g"][l].reshape(8, 128).T),
        "g2": _c(inp["ln2_g"][l].reshape(8, 128).T),
        "w_in": _c(inp["w_in"][l]),
        "bg": _c(inp["b_gate"][l].reshape(24, 128).T),
        "yaT": yaT, "ybT": ybT, "ycT": ycT,
        "w_pa": _c(inp["w_pa"][l]), "w_pb": _c(inp["w_pb"][l]), "w_pc": _c(inp["w_pc"][l]), "w_o": _c(inp["w_o"][l]),
        "w_g": _c(inp["w_ff_gate"][l]), "w_u": _c(inp["w_ff_up"][l]), "w_d": _c(inp["w_ff_down"][l]),
    }


ARENA_BYTES = 212736
GROUPS = [[0, 1, 2, 3], [4, 5, 6, 7]]


def build_fused(S, depth=DEPTH):
    NT = S // 4
    H = FFN_HIDDEN
    L = depth
    nc = bass.Bass("TRN2", target_bir_lowering=False)
    with contextlib.ExitStack() as es:
        C = Ctx(nc, es)
        C.use_arena(ARENA_BYTES)
        P = C.P
        P.use_rank = True
        xT = C.din("xT", [1024, NT])
        xo = C.dout("xo", [1024, NT])
        pin = {}
        for name, shape in (("g1", [L, 128, 8]), ("g2", [L, 128, 8]), ("w_in", [L, 1024, IN_COLS]),
                            ("sgg", [L, 128, 256]), ("sgb", [L, 128, 256]), ("wsT", [L, 128, 4, 128]),
                            ("tril", [128, 128]), ("bsb", [L, 128, 2, 128]), ("gqk", [L, 128, 2]),
                            ("cw", [L, 64, 4]), ("lvec", [L, 64, 4]), ("wa", [L, 64, 64]), ("wi", [L, 64, 64]),
                            ("lq", [L, 128, 4, 64]), ("subg", [L, 128, 128]), ("rb", [128, 32]),
                            ("idx", [128, 12, 128]), ("maskT", [128, 128]), ("ident", [128, 128]), ("lcon", [L, 128, 2]),
                            ("bg", [L, 128, 24]), ("w_pa", [L, 256, 1024]), ("w_pb", [L, 256, 1024]),
                            ("w_pc", [L, 512, 1024]), ("w_o", [L, 1024, 1024]), ("w_g", [L, 1024, H]),
                            ("w_u", [L, 1024, H]), ("w_d", [L, H, 1024])):
            pin[name] = C.din(name, shape)

        def dint(name, shape, dt):
            return nc.dram_tensor(name, list(shape), dt).ap()
        q_in = dint("q_in", [4, 128, NT], BF16)
        q_out = dint("q_out", [4, 512, NT], BF16)
        k_in = dint("k_in", [4, 128, NT], BF16)
        k_out = dint("k_out", [4, 512, NT], BF16)
        v_in = dint("v_in", [4, NT, 128], BF16)
        v_out = dint("v_out", [4, 4 * NT, 128], BF16)
        xa_in = dint("xa_in", [4, 64, NT], F32)
        xa_out = dint("xa_out", [4, 256, NT], F32)
        ga_in = dint("ga_in", [4, 64, NT], F32)
        ga_out = dint("ga_out", [4, 256, NT], F32)
        yc_in = dint("yc_in", [4, 128, NT], BF16)
        yc_out = dint("yc_out", [4, 512, NT], BF16)
        ya_in = dint("ya_in", [4, 64, NT], BF16)
        ya_out = dint("ya_out", [4, 256, NT], BF16)
        v_loc = dint("v_loc", [S, 128], BF16)
        xg_loc = dint("xg_loc", [2, 4, 64, NT], F32)
        yc_loc = dint("yc_loc", [4, 128, NT], BF16)
        ya_loc = dint("ya_loc", [4, 64, NT], BF16)
        yb_x = dint("yb_x", [2, 128, NT], BF16)
        xs1 = dint("xs1", [1024, NT], F32)
        xs2 = dint("xs2", [1024, NT], F32)

        P.dyn_spec = {}

        def rk(name="r"):
            return P.dyn[name]

        def allgather(name, src, dst, reads=()):
            P.op("pool", lambda e: e.collective_compute("AllGather", ALU.bypass, replica_groups=GROUPS, ins=[src], outs=[dst]),
                 reads=list(reads), writes=["cc_" + name], dma="cc_" + name, inc=1)

        for l in range(L):
            par = 0
            x_in = xT if l == 0 else xs2
            x_out = xo if l == L - 1 else xs2
            C.arena_reset()
            io1 = {"xT": x_in, "g1": pin["g1"][l], "w_in": pin["w_in"][l], "sgg": pin["sgg"][l], "sgb": pin["sgb"][l],
                   "wsT": pin["wsT"][l], "tril": pin["tril"], "bsb": pin["bsb"][l], "gqk": pin["gqk"][l],
                   "qk": lambda h, which: (q_in, k_in)[which][h],
                   "v": v_in,
                   "xg": lambda ch: (xa_in, ga_in)[ch // 2][2 * (ch % 2):2 * (ch % 2) + 2].rearrange("h p n -> (h p) n"),
                   "ybT": yb_x}
            emit_L1(C, io1, NT)
            P.barrier()
            for h in range(4):
                allgather("xa", xa_in[h], xa_out[h])
                allgather("ga", ga_in[h], ga_out[h])
            for h in range(4):
                allgather("q", q_in[h], q_out[h])
                allgather("k", k_in[h], k_out[h])
                allgather("v", v_in[h], v_out[h])
            C.arena_reset()
            for a_, srcg in ((0, xa_out), (1, ga_out)):
                P.op("sp", lambda e, a_=a_, srcg=srcg: e.dma_start(
                    out=xg_loc[a_].rearrange("(o j) p n -> o j p n", o=1),
                    in_=srcg.rearrange("h (j p) n -> h j p n", j=4)[bass.ds(rk(), 1), :, :, :]),
                    reads=["cc_xa", "cc_ga"], writes=[("xg_loc", a_)], dma="loc_xg%d" % a_)

            def load_qk(P_, q0T, q1T, kTs):
                def v3(t, rows):
                    return t[rows, :].rearrange("p (j n) -> p j n", j=4)

                def src(g, rows):
                    return g.rearrange("h (j p) n -> h j p n", j=4)[bass.ds(rk(), 1), :, rows, :].rearrange("o j p n -> p (o j) n")
                P_.op("sp", lambda e: e.dma_start(out=v3(q0T, slice(0, 64)), in_=src(q_out, slice(0, 64))),
                      reads=["cc_q"], writes=["q0T"], dma="ld_q0")
                P_.op("sp", lambda e: e.dma_start(out=v3(q1T, slice(64, 128)), in_=src(q_out, slice(64, 128))),
                      reads=["cc_q"], writes=["q1T"], dma="ld_q1")
                P_.op("sp", lambda e: e.dma_start(out=v3(kTs, slice(0, 128)), in_=src(k_out, slice(0, 128))),
                      reads=["cc_k"], writes=["kTs"], dma="ld_k")

            def pre_attn(P_):
                P_.op("sp", lambda e: e.dma_start(out=v_loc.rearrange("(o t) e -> o t e", o=1), in_=v_out[bass.ds(rk(), 1), :, :]),
                      reads=["cc_v"], writes=["v_loc"], dma="loc_v")

            def after_ya(P_, nch):
                per = nch // 4
                for j in range(4):
                    allgather("ya", ya_in[j], ya_out[j], reads=[("ya_dram", c_) for c_ in range(j * per, (j + 1) * per)])

            def after_yc(P_, G, ng):
                per = ng // 4
                if (G + 1) % per == 0:
                    j = G // per
                    allgather("yc", yc_in[j], yc_out[j], reads=[("yc_dram", g_) for g_ in range(j * per, (j + 1) * per)])
            io2 = {"nsrc": 4, "load_qk": load_qk, "after_ya": after_ya, "after_yc": after_yc, "pre_attn": pre_attn,
                   "v_dep": ["v_loc"], "xa_dep": [("xg_loc", 0)], "ga_dep": [("xg_loc", 1)],
                   "v_src": lambda e, j: v_loc[j * NT:(j + 1) * NT, :],
                   "xa_src": lambda e, j: xg_loc[0, j], "ga_src": lambda e, j: xg_loc[1, j],
                   "cw": pin["cw"][l], "lvec": pin["lvec"][l], "wa": pin["wa"][l], "wi": pin["wi"][l], "lq": pin["lq"][l],
                   "subg": pin["subg"][l], "rb": pin["rb"], "idx": pin["idx"], "maskT": pin["maskT"], "ident": pin["ident"],
                   "lcon": pin["lcon"][l],
                   "ycT": lambda c0, n: yc_in[c0 // NT, :, c0 % NT:c0 % NT + n],
                   "yaT": lambda c0, n: ya_in[c0 // NT, :, c0 % NT:c0 % NT + n]}
            emit_L2(C, io2, S)
            P.barrier()
            C.arena_reset()
            P.op("sp", lambda e: e.dma_start(out=yc_loc.rearrange("(o h) p n -> o h p n", o=1),
                                             in_=yc_out.rearrange("j (h p) n -> j h p n", h=4)[bass.ds(rk(), 1), :, :, :]),
                 writes=["yc_loc"], dma="loc_yc")
            P.op("sp", lambda e: e.dma_start(out=ya_loc.rearrange("(o h) p n -> o h p n", o=1),
                                             in_=ya_out.rearrange("j (h p) n -> j h p n", h=4)[bass.ds(rk(), 1), :, :, :]),
                 writes=["ya_loc"], dma="loc_ya")
            P.barrier()

            def y_src(e, kind, i, c0, n):
                if kind == "yb":
                    return yb_x[i, :, c0:c0 + n]
                if kind == "ya":
                    return ya_loc[i, :, c0:c0 + n]
                return yc_loc[i, :, c0:c0 + n]
            io3 = {"x_in": x_in, "x_mid": xs1, "x_out": x_out, "g1": pin["g1"][l], "g2": pin["g2"][l],
                   "w_in": pin["w_in"][l], "bg": pin["bg"][l], "y_src": y_src, "w_pa": pin["w_pa"][l],
                   "w_pb": pin["w_pb"][l], "w_pc": pin["w_pc"][l], "w_o": pin["w_o"][l], "w_g": pin["w_g"][l],
                   "w_u": pin["w_u"][l], "w_d": pin["w_d"][l]}
            emit_L3(C, io3, NT)
            P.barrier()
        P.emit()
    return nc


def fused_inputs(inp, c, S, depth=DEPTH):
    NT = S // 4
    b, r = c // 4, c % 4
    x = inp["x"]
    xT = np.ascontiguousarray(x[b, r * NT:(r + 1) * NT].T)
    dummy = np.zeros((2, 2), np.float32)
    l1 = [l1_inputs(dummy, inp, l) for l in range(depth)]
    l2 = [l2_inputs(None, None, None, dummy, dummy, inp, l, r) for l in range(depth)]
    l3 = [l3_inputs(dummy, None, None, None, inp, l) for l in range(depth)]

    def st(lst, k):
        return np.ascontiguousarray(np.stack([d[k] for d in lst], axis=0))
    m = {"xT": xT}
    for k in ("g1", "w_in", "sgg", "sgb", "wsT", "bsb", "gqk"):
        m[k] = st(l1, k)
    m["tril"] = l1[0]["tril"]
    for k in ("cw", "lvec", "wa", "wi", "lq", "subg", "lcon"):
        m[k] = st(l2, k)
    for k in ("rb", "idx", "maskT", "ident"):
        m[k] = l2[0][k]
    for k in ("g2", "bg", "w_pa", "w_pb", "w_pc", "w_o", "w_g", "w_u", "w_d"):
        m[k] = st(l3, k)
    return m


_PROGS = {}


def kernel(**inputs):
    inp = {k: np.asarray(v) for k, v in inputs.items()}
    x = inp["x"]
    B, S, D = x.shape
    NT = S // 4
    key = ("fused", S)
    if key not in _PROGS:
        _PROGS[key] = build_fused(S)
    nc = _PROGS[key]
    in_maps = [fused_inputs(inp, c, S) for c in range(N_CORES)]
    res = run_bass_kernel_spmd(nc, in_maps, core_ids=list(range(N_CORES))).results
    out = np.empty((B, S, D), dtype=np.float32)
    for c in range(N_CORES):
        out[c // 4, (c % 4) * NT:(c % 4 + 1) * NT] = np.asarray(res[c]["xo"]).T
    return out
```

```python
import contextlib
import math
import numpy as np
import concourse.bass as bass
import concourse.mybir as mybir
from concourse.bass_utils import run_bass_kernel_spmd

F32 = mybir.dt.float32
BF16 = mybir.dt.bfloat16
AF = mybir.ActivationFunctionType
ALU = mybir.AluOpType
AX = mybir.AxisListType

D_MODEL = 1024
DEPTH = 4
N_CORES = 8
EPS = 1e-6
LRU_C = 8.0
FFN_HIDDEN = 2816
IN_COLS = 5632
TT = 512
NEG = -30000.0

STREAMS = ("pe", "act", "dve", "pool", "sp")


class Prog:
    def __init__(self, nc):
        self.nc = nc
        self.streams = {e: [] for e in STREAMS}
        self.count = {}
        self.last_write = {}
        self.readers = {}
        self.waited = {e: {} for e in STREAMS}
        self.pending = {e: [] for e in STREAMS}
        self.nops = 0
        self.use_rank = False
        self.dyn = {}

    def barrier(self, exclude=(), keep=()):
        for e in STREAMS:
            for k, v in self.count.items():
                if k in exclude:
                    continue
                if self.waited[e].get(k, 0) < v:
                    self.waited[e][k] = v
                    self.pending[e].append((k, v))
        kept = {k: self.last_write[k] for k in keep if k in self.last_write}
        self.last_write.clear()
        self.readers.clear()
        self.last_write.update(kept)

    def op(self, eng, fn, reads=(), writes=(), dma=None, inc=None):
        semkey = dma if dma is not None else eng
        if inc is None:
            inc = 16 if dma is not None else 1
        deps = {}

        def add(k, v, same_ok):
            if k == semkey and eng == "pe" and dma is None and same_ok:
                return
            if deps.get(k, 0) < v:
                deps[k] = v

        for b in reads:
            lw = self.last_write.get(b)
            if lw is not None:
                add(lw[0], lw[1], eng == "pe")
        for b in writes:
            lw = self.last_write.get(b)
            if lw is not None:
                add(lw[0], lw[1], True)
            for k, v in self.readers.get(b, {}).items():
                add(k, v, True)
        waits = self.pending[eng]
        self.pending[eng] = []
        wd = self.waited[eng]
        for k, v in deps.items():
            if wd.get(k, 0) < v:
                wd[k] = v
                waits.append((k, v))
        val = self.count.get(semkey, 0) + inc
        self.count[semkey] = val
        for b in reads:
            self.readers.setdefault(b, {})[semkey] = val
        for b in writes:
            self.last_write[b] = (semkey, val)
            self.readers[b] = {}
        self.streams[eng].append((waits, fn, semkey, inc))
        self.nops += 1

    def emit(self):
        nc = self.nc
        with contextlib.ExitStack() as es:
            sems = {k: es.enter_context(nc.semaphore("s_" + k)) for k in self.count}
            block = es.enter_context(nc.Block())
            final = list(self.count.items())

            def run(name, e):
                for waits, fn, semkey, inc in self.streams[name]:
                    for k, v in waits:
                        e.wait_ge(sems[k], v)
                    fn(e).then_inc(sems[semkey], inc)

            @block.tensor
            def _(e):
                run("pe", e)

            @block.scalar
            def _(e):
                run("act", e)

            @block.vector
            def _(e):
                run("dve", e)

            @block.gpsimd
            def _(e):
                run("pool", e)

            @block.sync
            def _(e):
                if self.use_rank:
                    r = e.snap(e.partition_id() % 4, min_val=0, max_val=3)
                    self.dyn["r"] = r
                    for name, mul in self.dyn_spec.items():
                        self.dyn[name] = e.snap(r * mul, min_val=0, max_val=3 * mul)
                run("sp", e)
                for k, v in final:
                    e.wait_ge(sems[k], v)


class Ctx:
    def __init__(self, nc, es):
        self.nc = nc
        self.es = es
        self.P = Prog(nc)
        self._n = 0

    def din(self, name, shape, dt=F32):
        return self.nc.dram_tensor(name, list(shape), dt, kind="ExternalInput").ap()

    def dout(self, name, shape, dt=F32):
        return self.nc.dram_tensor(name, list(shape), dt, kind="ExternalOutput").ap()

    def use_arena(self, nbytes):
        self.arena = self.es.enter_context(self.nc.sbuf_tensor("arena", [128, nbytes // 2], BF16))
        self.arena_n = nbytes // 2
        self.off = 0
        self.psum_t = self.es.enter_context(self.nc.psum_tensor("psum", [128, 8, 512], F32))

    def arena_reset(self):
        self.off = 0

    def sb(self, name, shape, dt=F32):
        if getattr(self, "arena", None) is None:
            return self.es.enter_context(self.nc.sbuf_tensor(name, list(shape), dt))[:]
        shape = list(shape)
        free = 1
        for d in shape[1:]:
            free *= d
        n16 = free * (2 if dt == F32 else 1)
        n16 = (n16 + 31) // 32 * 32
        assert self.off + n16 <= self.arena_n, "arena overflow at %s: %d + %d > %d" % (name, self.off, n16, self.arena_n)
        ap = self.arena[0:shape[0], self.off:self.off + free * (2 if dt == F32 else 1)]
        self.off += n16
        if dt == F32:
            ap = ap.bitcast(F32)
        if len(shape) > 2:
            names = " ".join("d%d" % i for i in range(len(shape) - 1))
            kw = {"d%d" % i: shape[1 + i] for i in range(len(shape) - 1)}
            ap = ap.rearrange("p (%s) -> p %s" % (names, names), **kw)
        return ap

    def ps(self, name, shape, dt=F32):
        if getattr(self, "arena", None) is None:
            return self.es.enter_context(self.nc.psum_tensor(name, list(shape), dt))[:]
        shape = list(shape)
        ap = self.psum_t[:]
        if shape == [128, 8, 512]:
            return ap
        assert shape == [128, 8, 2, 256], shape
        return ap.rearrange("p b (h n) -> p b h n", h=2)


class Rot:
    def __init__(self, name, n):
        self.name, self.n, self.i = name, n, 0

    def next(self):
        i = self.i % self.n
        self.i += 1
        return i, (self.name, i)


def load_weight_bf16(C, dst, src, K, N, key, stage, stage_rot, scale_ap=None, engines=("act", "dve"),
                     colblk=1536, dma_eng="sp", scale_key=None, also_writes=()):
    P = C.P
    extra = [scale_key] if scale_key is not None else []
    kc_n = K // 128
    ei = 0
    for kc in range(kc_n):
        for c0 in range(0, N, colblk):
            n = min(colblk, N - c0)
            si, skey = stage_rot.next()
            st = stage[:, si, 0:n]
            srcap = src[kc * 128:(kc + 1) * 128, c0:c0 + n]
            P.op(dma_eng, lambda e, st=st, srcap=srcap: e.dma_start(out=st, in_=srcap),
                 writes=[skey], dma="wld%d" % si)
            eng = engines[ei % len(engines)]
            ei += 1
            d = dst[:, kc, c0:c0 + n]
            if eng == "act":
                if scale_ap is not None:
                    sc = scale_ap[:, kc:kc + 1]
                    fn = lambda e, d=d, st=st, sc=sc: e.activation(out=d, in_=st, func=AF.Identity, scale=sc)
                else:
                    fn = lambda e, d=d, st=st: e.copy(out=d, in_=st)
            else:
                if scale_ap is not None:
                    sc = scale_ap[:, kc:kc + 1]
                    fn = lambda e, d=d, st=st, sc=sc: e.tensor_scalar(out=d, in0=st, scalar1=sc, scalar2=None,
                                                                      op0=ALU.mult)
                else:
                    fn = lambda e, d=d, st=st: e.tensor_copy(out=d, in_=st)
            P.op(eng, fn, reads=[skey] + extra, writes=[(key, eng)] + list(also_writes))
    return [(key, e) for e in engines]


def rstd_from_ps(P, psb, pskey, rstd, rkey, n, eps_n):
    P.op("act", lambda e: e.activation(out=rstd[:, 0:n], in_=psb[:, 0:n], func=AF.Ln, bias=eps_n, scale=1.0),
         reads=[pskey], writes=[rkey])
    P.op("act", lambda e: e.activation(out=rstd[:, 0:n], in_=rstd[:, 0:n], func=AF.Exp, scale=-0.5),
         reads=[rkey], writes=[rkey])


def rms_tile(C, xt, xkey, hT, hkey, sq, sqkey, ones, psb, pskey, rstd, rkey, n, eps_n):
    P = C.P
    P.op("act", lambda e: e.activation(out=sq[:, :, 0:n], in_=xt[:, :, 0:n], func=AF.Square),
         reads=[xkey], writes=[sqkey])

    def mm(e):
        for c in range(8):
            i = e.matmul(psb[:, 0:n], lhsT=ones[:], rhs=sq[:, c, 0:n], start=(c == 0), stop=(c == 7))
        return i
    P.op("pe", mm, reads=[sqkey, "ones"], writes=[pskey])
    rstd_from_ps(P, psb, pskey, rstd, rkey, n, eps_n)
    rb = rstd[:, 0:n].unsqueeze(1).broadcast_to([128, 8, n])
    P.op("dve", lambda e: e.tensor_tensor(out=hT[:, :, 0:n], in0=xt[:, :, 0:n], in1=rb, op=ALU.mult),
         reads=[xkey, rkey], writes=[hkey])


def build_L1(NT):
    nc = bass.Bass("TRN2", target_bir_lowering=False)
    with contextlib.ExitStack() as es:
        C = Ctx(nc, es)
        io = {
            "xT": C.din("xT", [1024, NT]), "g1": C.din("g1", [128, 8]), "w_in": C.din("w_in", [1024, IN_COLS]),
            "sgg": C.din("sgg", [128, 256]), "sgb": C.din("sgb", [128, 256]), "wsT": C.din("wsT", [128, 4, 128]),
            "tril": C.din("tril", [128, 128]), "bsb": C.din("bsb", [128, 2, 128]), "gqk": C.din("gqk", [128, 2]),
            "v": C.dout("v", [4, NT, 128], BF16), "ybT": C.dout("ybT", [2, 128, NT], BF16),
        }
        qk_t = C.dout("qk", [4, 2, 128, NT], BF16)
        xg_t = C.dout("xg", [4, 128, NT], F32)
        io["qk"] = lambda h, which: qk_t[h, which]
        io["xg"] = lambda ch: xg_t[ch]
        emit_L1(C, io, NT)
        C.P.emit()
    return nc


def emit_L1(C, io, NT):
    ntile = NT // TT
    if True:
        P = C.P
        xT, g1, w_in, sgg, sgb, wsT, tril, bsb, gqk = (io[k] for k in ("xT", "g1", "w_in", "sgg", "sgb", "wsT", "tril", "bsb", "gqk"))
        qk_o, v_o, xg_o, yb_o = io["qk"], io["v"], io["xg"], io["ybT"]

        NW = 2560
        wb = C.sb("wb", [128, 8, NW], BF16)
        stage = C.sb("stage", [128, 3, 1536], F32)
        xt = C.sb("xt", [128, 2, 8, TT], F32)
        sq = C.sb("sq", [128, 8, TT], BF16)
        hT = C.sb("hT", [128, ntile, 8, TT], BF16)
        rstd = C.sb("rstd", [128, TT], F32)
        ones = C.sb("ones", [128, 128], BF16)
        bones = C.sb("bones", [128, 128], BF16)
        g1s = C.sb("g1s", [128, 8], F32)
        sgg_s = C.sb("sgg_s", [128, 256], F32)
        sgb_s = C.sb("sgb_s", [128, 256], F32)
        wsT_f = C.sb("wsT_f", [128, 4, 128], F32)
        tril_s = C.sb("tril_s", [128, 128], F32)
        wsT_b = C.sb("wsT_b", [128, 4, 128], BF16)
        bsb_s = C.sb("bsb_s", [128, 2, 128], F32)
        gqk_s = C.sb("gqk_s", [128, 2], F32)
        xg_s = C.sb("xg_s", [128, 2, TT], F32)
        u_s = C.sb("u_s", [128, 2, TT], F32)
        qsq = C.sb("qsq", [128, 2, TT], BF16)
        qr = C.sb("qr", [128, 2, TT], F32)
        qn = C.sb("qn", [128, 3, TT], BF16)
        stats = C.sb("stats", [128, 2, 6], F32)
        mv = C.sb("mv", [128, 2, 2], F32)
        vr = C.sb("vr", [128, 2, 1], F32)
        vtmp = C.sb("vtmp", [128, 2, 256], F32)
        vn = C.sb("vn", [128, 4, 256], BF16)
        vv_s = C.sb("vv_s", [128, 2, 4, 512], BF16)
        mtmp = C.sb("mtmp", [128, 2, TT], F32)
        yb_s = C.sb("yb_s", [128, 2, TT], BF16)
        psum = C.ps("psum", [128, 8, TT], F32)
        psr = Rot("ps", 8)

        P.op("dve", lambda e: e.memset(ones[:], 1.0), writes=["ones"])
        P.op("pool", lambda e: e.memset(bones[:], 0.0), writes=["bones"])
        P.op("pool", lambda e: e.memset(bones[0:64, 0:64], 1.0), writes=["bones"])
        P.op("pool", lambda e: e.memset(bones[64:128, 64:128], 1.0), writes=["bones"])
        for dst, src, key in ((g1s, g1, "g1s"), (sgg_s, sgg, "sgg"), (sgb_s, sgb, "sgb"), (wsT_f, wsT, "wsT_f"),
                              (tril_s, tril, "tril"), (bsb_s, bsb, "bsb"), (gqk_s, gqk, "gqk")):
            P.op("sp", lambda e, dst=dst, src=src: e.dma_start(out=dst[:], in_=src), writes=[key], dma="c_" + key)
        P.op("dve", lambda e: e.tensor_scalar(out=g1s[:], in0=g1s[:], scalar1=32.0, scalar2=None, op0=ALU.mult),
             reads=["g1s"], writes=["g1s"])
        P.op("dve", lambda e: e.tensor_scalar(out=gqk_s[:, 1:2], in0=gqk_s[:, 1:2], scalar1=8.0, scalar2=None, op0=ALU.mult),
             reads=["gqk"], writes=["gqk"])
        trb = tril_s[:].unsqueeze(1).broadcast_to([128, 4, 128])
        P.op("dve", lambda e: e.tensor_tensor(out=wsT_b[:], in0=wsT_f[:], in1=trb, op=ALU.mult),
             reads=["wsT_f", "tril"], writes=["wsT_b"])

        def load_x(t):
            b = t % 2
            src = xT[:, t * TT:(t + 1) * TT].rearrange("(c p) n -> p c n", p=128)
            for hh in range(2):
                P.op("sp", lambda e, b=b, src=src, hh=hh: e.dma_start(out=xt[:, b, 4 * hh:4 * hh + 4, :],
                                                                    in_=src[:, 4 * hh:4 * hh + 4, :]),
                     writes=[("xt", b)], dma="xld%d" % b)

        load_x(0)
        wbk = load_weight_bf16(C, wb, w_in[:, 0:NW], 1024, NW, "wb", stage, Rot("stage", 3), scale_ap=g1s,
                               engines=("act", "dve"), colblk=1280, scale_key="g1s")

        def mm_fm(bank, col0, b):
            def fn(e):
                for k in range(8):
                    i = e.matmul(psum[:, bank, :], lhsT=wb[:, k, col0:col0 + 128], rhs=hT[:, b, k, :],
                                 start=(k == 0), stop=(k == 7))
                return i
            return fn

        def mm_tm(bank, col0, ncol, b, blk):
            def fn(e):
                for k in range(8):
                    i = e.matmul(psum[:, bank, 0:ncol], lhsT=hT[:, b, k, blk * 128:(blk + 1) * 128],
                                 rhs=wb[:, k, col0:col0 + ncol], start=(k == 0), stop=(k == 7))
                return i
            return fn

        def norm(t):
            b = t % 2
            bank, pk = psr.next()
            rms_tile(C, xt[:, b], ("xt", b), hT[:, t], ("hT", t), sq, "sq", ones, psum[:, bank, :], pk,
                     rstd, "rstd", TT, 1024 * EPS)

        xg_keys = []
        for t in range(ntile):
            b = t
            t0 = t * TT
            hk = ("hT", t)
            if t + 1 < ntile:
                load_x(t + 1)
            norm(t)
            for ch in range(4):
                bank, pk = psr.next()
                P.op("pe", mm_fm(bank, ch * 128, b), reads=[hk] + wbk, writes=[pk])
                s = ch % 2
                P.op("act", lambda e, bank=bank, s=s: e.copy(out=xg_s[:, s, :], in_=psum[:, bank, :]),
                     reads=[pk], writes=[("xg_s", s)])
                P.op("sp", lambda e, ch=ch, s=s, t0=t0: e.dma_start(out=xg_o(ch)[:, t0:t0 + TT], in_=xg_s[:, s, :]),
                     reads=[("xg_s", s)], writes=[("xg_dram", t, ch)], dma="st_xg%d" % s)
                xg_keys.append(("xg_dram", t, ch))
        if "after_xg" in io:
            io["after_xg"](P, xg_keys)
        for t in range(ntile):
            b = t
            t0 = t * TT
            hk = ("hT", t)
            for n in range(2):
                bank, pk = psr.next()
                P.op("pe", mm_fm(bank, 512 + n * 128, b), reads=[hk] + wbk, writes=[pk])
                P.op("act", lambda e, bank=bank, n=n: e.copy(out=u_s[:, n, :], in_=psum[:, bank, :]),
                     reads=[pk], writes=[("u_s", n)])
            for blk in range(4):
                bank, pk = psr.next()
                s = blk % 2
                P.op("pe", mm_tm(bank, 768, 256, b, blk), reads=[hk] + wbk, writes=[pk])
                P.op("dve", lambda e, bank=bank, s=s: e.bn_stats(out=stats[:, s, :], in_=psum[:, bank, 0:256]),
                     reads=[pk], writes=[("stats", s)])
                P.op("dve", lambda e, s=s: e.bn_aggr(out=mv[:, s, :], in_=stats[:, s, :]),
                     reads=[("stats", s)], writes=[("mv", s)])
                P.op("act", lambda e, s=s: e.activation(out=vr[:, s, :], in_=mv[:, s, 1:2], func=AF.Ln, bias=EPS, scale=1.0),
                     reads=[("mv", s)], writes=[("vr", s)])
                P.op("act", lambda e, s=s: e.activation(out=vr[:, s, :], in_=vr[:, s, :], func=AF.Exp, scale=-0.5),
                     reads=[("vr", s)], writes=[("vr", s)])
                P.op("dve", lambda e, bank=bank, s=s: e.tensor_scalar(
                    out=vtmp[:, s, :], in0=psum[:, bank, 0:256], scalar1=mv[:, s, 0:1], scalar2=vr[:, s, :],
                    op0=ALU.subtract, op1=ALU.mult), reads=[pk, ("mv", s), ("vr", s)], writes=[("vtmp", s)])
                P.op("pool", lambda e, s=s: e.tensor_tensor(out=vtmp[:, s, :], in0=vtmp[:, s, :], in1=sgg_s[:], op=ALU.mult),
                     reads=[("vtmp", s), "sgg"], writes=[("vtmp", s)])
                P.op("pool", lambda e, s=s, blk=blk: e.tensor_tensor(out=vn[:, blk, :], in0=vtmp[:, s, :], in1=sgb_s[:], op=ALU.add),
                     reads=[("vtmp", s), "sgb"], writes=[("vn", blk)])
            for n in range(2):
                bank, pk = psr.next()

                def mix(e, bank=bank, n=n):
                    for blk in range(4):
                        for gg in range(2):
                            g = 2 * n + gg
                            i = e.matmul(psum[gg * 64:(gg + 1) * 64, bank, blk * 128:(blk + 1) * 128],
                                         lhsT=vn[:, blk, g * 64:(g + 1) * 64], rhs=wsT_b[:, g, :],
                                         start=True, stop=True)
                    return i
                P.op("pe", mix, reads=[("vn", 0), ("vn", 1), ("vn", 2), ("vn", 3), "wsT_b"], writes=[pk])
                bsv = bsb_s[:, n, :].unsqueeze(1).broadcast_to([128, 4, 128])
                P.op("dve", lambda e, bank=bank, n=n, bsv=bsv: e.tensor_tensor(
                    out=mtmp[:, n, :].rearrange("p (b t) -> p b t", b=4),
                    in0=psum[:, bank, :].rearrange("p (b t) -> p b t", b=4), in1=bsv, op=ALU.add),
                    reads=[pk, "bsb"], writes=[("mtmp", n)])
                P.op("pool", lambda e, n=n: e.tensor_tensor(out=yb_s[:, n, :], in0=mtmp[:, n, :], in1=u_s[:, n, :], op=ALU.mult),
                     reads=[("mtmp", n), ("u_s", n)], writes=[("yb_s", n)])
                P.op("sp", lambda e, n=n, t0=t0: e.dma_start(out=yb_o[n, :, t0:t0 + TT], in_=yb_s[:, n, :]),
                     reads=[("yb_s", n)], dma="st_yb%d" % n)
            qrot = 0
            for which in range(2):
                for h in range(4):
                    col0 = 1024 + which * 512 + h * 128
                    bank, pk = psr.next()
                    bank2, pk2 = psr.next()
                    s = qrot % 2
                    s3 = qrot % 3
                    qrot += 1
                    P.op("pe", mm_fm(bank, col0, b), reads=[hk] + wbk, writes=[pk])
                    P.op("act", lambda e, bank=bank, s=s: e.activation(out=qsq[:, s, :], in_=psum[:, bank, :], func=AF.Square),
                         reads=[pk], writes=[("qsq", s)])
                    P.op("pe", lambda e, bank2=bank2, s=s: e.matmul(psum[:, bank2, :], lhsT=bones[:], rhs=qsq[:, s, :],
                                                                    start=True, stop=True),
                         reads=[("qsq", s), "bones"], writes=[pk2])
                    rstd_from_ps(P, psum[:, bank2, :], pk2, qr[:, s, :], ("qr", s), TT, 64 * EPS)
                    P.op("dve", lambda e, bank=bank, s=s, s3=s3, which=which: e.scalar_tensor_tensor(
                        out=qn[:, s3, :], in0=psum[:, bank, :], scalar=gqk_s[:, which:which + 1], in1=qr[:, s, :],
                        op0=ALU.mult, op1=ALU.mult), reads=[pk, ("qr", s), "gqk"], writes=[("qn", s3)])
                    P.op("sp", lambda e, h=h, which=which, s3=s3, t0=t0: e.dma_start(
                        out=qk_o(h, which)[:, t0:t0 + TT], in_=qn[:, s3, :]),
                        reads=[("qn", s3)], dma="st_qn%d" % s3)
            vb = t % 2
            for blk in range(4):
                bank, pk = psr.next()
                P.op("pe", mm_tm(bank, 2048, 512, b, blk), reads=[hk] + wbk, writes=[pk])
                eng = "act" if blk % 2 == 0 else "dve"
                if eng == "act":
                    fn = lambda e, bank=bank, blk=blk, vb=vb: e.copy(out=vv_s[:, vb, blk, :], in_=psum[:, bank, :])
                else:
                    fn = lambda e, bank=bank, blk=blk, vb=vb: e.tensor_copy(out=vv_s[:, vb, blk, :], in_=psum[:, bank, :])
                P.op(eng, fn, reads=[pk], writes=[("vv_s", vb, blk)])
            for h in range(4):
                P.op("sp", lambda e, vb=vb, t0=t0, h=h: e.dma_start(
                    out=v_o[h, t0:t0 + TT, :].rearrange("(b p) n -> p b n", p=128), in_=vv_s[:, vb, :, h * 128:(h + 1) * 128]),
                    reads=[("vv_s", vb, k) for k in range(4)], dma="st_vv%d" % vb)


def build_L2(S):
    nc = bass.Bass("TRN2", target_bir_lowering=False)
    with contextlib.ExitStack() as es:
        C = Ctx(nc, es)
        qT = C.din("qT", [128, S], BF16)
        kT = C.din("kT", [128, S], BF16)
        v = C.din("v", [S, 128], BF16)
        xaT = C.din("xaT", [64, S])
        gaT = C.din("gaT", [64, S])
        io = {
            "nsrc": 1,
            "q_src": lambda e, j: qT, "k_src": lambda e, j: kT, "v_src": lambda e, j: v,
            "xa_src": lambda e, j: xaT, "ga_src": lambda e, j: gaT,
            "cw": C.din("cw", [64, 4]), "lvec": C.din("lvec", [64, 4]), "wa": C.din("wa", [64, 64]), "wi": C.din("wi", [64, 64]),
            "lq": C.din("lq", [128, 4, 64]), "subg": C.din("subg", [128, 128]), "rb": C.din("rb", [128, 32]),
            "idx": C.din("idx", [128, 12, 128]), "maskT": C.din("maskT", [128, 128]), "ident": C.din("ident", [128, 128]),
            "lcon": C.din("lcon", [128, 2]),
        }
        yc_t = C.dout("ycT", [128, S], BF16)
        ya_t = C.dout("yaT", [64, S], BF16)
        io["ycT"] = lambda c0, n: yc_t[:, c0:c0 + n]
        io["yaT"] = lambda c0, n: ya_t[:, c0:c0 + n]
        emit_L2(C, io, S)
        C.P.emit()
    return nc


def emit_L2(C, io, S):
    NG = S // TT
    NKT = S // 128
    TL = 512
    NCH = S // TL
    nsrc = io["nsrc"]
    NS = S // nsrc
    if True:
        P = C.P
        cw, lvec, wa, wi, lq, subg, rb, idx, maskT, ident, lcon = (io[k] for k in (
            "cw", "lvec", "wa", "wi", "lq", "subg", "rb", "idx", "maskT", "ident", "lcon"))
        yc_o, ya_o = io["ycT"], io["yaT"]

        q0T = C.sb("q0T", [128, S], BF16)
        q1T = C.sb("q1T", [128, S], BF16)
        kTs = C.sb("kTs", [128, S], BF16)
        v1 = C.sb("v1", [128, NKT, 129], BF16)
        biasT = C.sb("biasT", [128, 12, 128], BF16)
        bias_f = C.sb("bias_f", [128, 12, 128], F32)
        idx_s = C.sb("idx_s", [128, 12, 128], F32)
        btmp = C.sb("btmp", [128, 2, 12, 128], F32)
        mask_s = C.sb("mask_s", [128, 128], F32)
        id_f = C.sb("id_f", [128, 128], F32)
        id_b = C.sb("id_b", [128, 128], BF16)
        rb_s = C.sb("rb_s", [128, 32], F32)
        lq_s = C.sb("lq_s", [128, 4, 64], F32)
        lq_t = C.sb("lq_t", [128, 2, 64], F32)
        lsc = C.sb("lsc", [128, 8], F32)
        lcon_s = C.sb("lcon_s", [128, 2], F32)
        subg_s = C.sb("subg_s", [128, 128], F32)
        pt = C.sb("pt", [128, 3, 2, TT], BF16)
        accs = C.sb("accs", [128, 3, TT], F32)
        rl = C.sb("rl", [128, 8], F32)
        o_s = C.sb("o_s", [128, 4, 128], F32)
        osq = C.sb("osq", [128, 128], F32)
        ss = C.sb("ss", [128, 4], F32)
        y_s = C.sb("y_s", [128, 4, 128], BF16)
        yc_s = C.sb("yc_s", [128, 2, TT], BF16)
        cw_s = C.sb("cw_s", [64, 4], F32)
        lv_s = C.sb("lv_s", [64, 4], F32)
        csc = C.sb("csc", [64, 4], F32)
        w_f = C.sb("w_f", [64, 2, 64], F32)
        w_b = C.sb("w_b", [64, 2, 64], BF16)
        xa_s = C.sb("xa_s", [64, 2, TL + 3], F32)
        ga_s = C.sb("ga_s", [64, 2, TL], F32)
        xc = C.sb("xc", [64, TL], F32)
        xcb = C.sb("xcb", [64, TL], BF16)
        r_s = C.sb("r_s", [64, TL], F32)
        i_s = C.sb("i_s", [64, TL], F32)
        a_s = C.sb("a_s", [64, TL], F32)
        m_s = C.sb("m_s", [64, TL], F32)
        h_s = C.sb("h_s", [64, 2, TL], F32)
        g_s = C.sb("g_s", [64, TL], F32)
        ya_s = C.sb("ya_s", [64, 2, TL], BF16)
        psum = C.ps("psum", [128, 8, TT], F32)
        trp = psum[:, 7, :].bitcast(BF16)

        for dst, src, key in ((cw_s[:], cw, "cw"), (lv_s[:], lvec, "lv"), (w_f[:, 0, :], wa, "wa"), (w_f[:, 1, :], wi, "wi"),
                              (lq_s[:], lq, "lq"), (subg_s[:], subg, "subg"), (rb_s[:], rb, "rb"), (idx_s[:], idx, "idx"),
                              (mask_s[:], maskT, "mask"), (id_f[:], ident, "id_f"), (lcon_s[:], lcon, "lcon")):
            P.op("sp", lambda e, dst=dst, src=src: e.dma_start(out=dst, in_=src), writes=[key], dma="c_" + key)
        P.op("dve", lambda e: e.tensor_copy(out=id_b[:], in_=id_f[:]), reads=["id_f"], writes=["id_b"])
        P.op("dve", lambda e: e.tensor_copy(out=w_b[:], in_=w_f[:]), reads=["wa", "wi"], writes=["w_b"])
        for j in range(2):
            P.op("dve", lambda e, j=j: e.scalar_tensor_tensor(out=lq_t[:, j, :], in0=lq_s[:, 2 * j, :], scalar=1.0,
                                                              in1=lq_s[:, 2 * j + 1, :], op0=ALU.mult, op1=ALU.mult,
                                                              accum_out=lsc[:, j:j + 1]),
                 reads=["lq"], writes=[("lsc", j)])
        P.op("act", lambda e: e.activation(out=lsc[:, 2:4], in_=lsc[:, 0:2], func=AF.Exp),
             reads=[("lsc", 0), ("lsc", 1)], writes=["lsce"])
        P.op("dve", lambda e: e.tensor_tensor(out=lsc[:, 4:5], in0=lsc[:, 2:3], in1=lsc[:, 3:4], op=ALU.subtract),
             reads=["lsce"], writes=["lam"])
        P.op("dve", lambda e: e.tensor_tensor(out=lsc[:, 4:5], in0=lsc[:, 4:5], in1=lcon_s[:, 0:1], op=ALU.add),
             reads=["lam", "lcon"], writes=["lam"])
        P.op("dve", lambda e: e.tensor_scalar(out=lsc[:, 5:6], in0=lsc[:, 4:5], scalar1=-1.0, scalar2=None, op0=ALU.mult),
             reads=["lam"], writes=["nlam"])
        P.op("dve", lambda e: e.tensor_scalar(out=subg_s[:], in0=subg_s[:], scalar1=lcon_s[:, 1:2], scalar2=None, op0=ALU.mult),
             reads=["subg", "lcon"], writes=["subg"])
        P.op("act", lambda e: e.activation(out=csc[:, 0:1], in_=lv_s[:, 3:4], func=AF.Exp, scale=-1.0),
             reads=["lv"], writes=["csc0"])
        P.op("act", lambda e: e.activation(out=csc[:, 0:1], in_=csc[:, 0:1], func=AF.Ln, bias=1.0, scale=1.0),
             reads=["csc0"], writes=["csc0"])
        P.op("dve", lambda e: e.tensor_scalar(out=csc[:, 1:2], in0=csc[:, 0:1], scalar1=-LRU_C, scalar2=None, op0=ALU.mult),
             reads=["csc0"], writes=["csc"])
        P.op("dve", lambda e: e.tensor_scalar(out=csc[:, 2:3], in0=csc[:, 0:1], scalar1=-2.0 * LRU_C, scalar2=None, op0=ALU.mult),
             reads=["csc0"], writes=["csc"])

        P.op("dve", lambda e: e.memset(bias_f[:], 0.0), writes=["bias_f"])
        idx_np = l2_consts()["idx"]
        for d_ in range(12):
            for bkt in sorted(set(int(v_) for v_ in np.unique(idx_np[:, d_, :]))):
                P.op("dve", lambda e, bkt=bkt, d_=d_: e.tensor_single_scalar(
                    out=btmp[:, 0, d_, :], in_=idx_s[:, d_, :], scalar=float(bkt), op=ALU.is_equal),
                    reads=["idx"], writes=[("btmp", 0)])
                P.op("dve", lambda e, bkt=bkt, d_=d_: e.scalar_tensor_tensor(
                    out=bias_f[:, d_, :], in0=btmp[:, 0, d_, :], scalar=rb_s[:, bkt:bkt + 1], in1=bias_f[:, d_, :],
                    op0=ALU.mult, op1=ALU.add), reads=[("btmp", 0), "rb", "bias_f"], writes=["bias_f"])
        P.op("dve", lambda e: e.tensor_tensor(out=bias_f[:, 0, :], in0=bias_f[:, 0, :], in1=mask_s[:], op=ALU.add),
             reads=["bias_f", "mask"], writes=["bias_f"])
        P.op("dve", lambda e: e.tensor_copy(out=biasT[:], in_=bias_f[:]), reads=["bias_f"], writes=["biasT"])

        for ch in range(NCH):
            b = ch % 2
            t0 = ch * TL
            js = t0 // NS
            l0 = t0 - js * NS
            if ch == 0:
                P.op("dve", lambda e: e.memset(xa_s[:, 0, 0:3], 0.0), writes=[("xa", 0)])
                P.op("sp", lambda e: e.dma_start(out=xa_s[:, 0, 3:3 + TL], in_=io["xa_src"](e, 0)[:, 0:TL]),
                     reads=io.get("xa_dep", []), writes=[("xa", 0)], dma="ld_xa0")
            else:
                P.op("dve", lambda e, b=b: e.tensor_copy(out=xa_s[:, b, 0:3], in_=xa_s[:, 1 - b, TL:TL + 3]),
                     reads=[("xa", 1 - b)], writes=[("xa", b)])
                P.op("sp", lambda e, b=b, js=js, l0=l0: e.dma_start(out=xa_s[:, b, 3:3 + TL], in_=io["xa_src"](e, js)[:, l0:l0 + TL]),
                     reads=io.get("xa_dep", []), writes=[("xa", b)], dma="ld_xa%d" % b)
            P.op("sp", lambda e, b=b, js=js, l0=l0: e.dma_start(out=ga_s[:, b, :], in_=io["ga_src"](e, js)[:, l0:l0 + TL]),
                 reads=io.get("ga_dep", []), writes=[("ga", b)], dma="ld_ga%d" % b)
            P.op("dve", lambda e, b=b: e.tensor_scalar(out=xc[:], in0=xa_s[:, b, 3:3 + TL], scalar1=cw_s[:, 3:4],
                                                       scalar2=lv_s[:, 0:1], op0=ALU.mult, op1=ALU.add),
                 reads=[("xa", b), "cw", "lv"], writes=["xc"])
            for tap in range(3):
                P.op("dve", lambda e, b=b, tap=tap: e.scalar_tensor_tensor(
                    out=xc[:], in0=xa_s[:, b, tap:tap + TL], scalar=cw_s[:, tap:tap + 1], in1=xc[:],
                    op0=ALU.mult, op1=ALU.add), reads=[("xa", b), "cw", "xc"], writes=["xc"])
            P.op("act", lambda e: e.copy(out=xcb[:], in_=xc[:]), reads=["xc"], writes=["xcb"])
            nh = TL // TT
            for gate in range(2):
                for hh in range(nh):
                    bank = gate * nh + hh
                    P.op("pe", lambda e, gate=gate, hh=hh, bank=bank: e.matmul(
                        psum[0:64, bank, :], lhsT=w_b[:, gate, :], rhs=xcb[:, hh * TT:(hh + 1) * TT], start=True, stop=True),
                        reads=["xcb", "w_b"], writes=[("st", bank // 2)])
            for hh in range(nh):
                P.op("act", lambda e, hh=hh: e.activation(out=r_s[:, hh * TT:(hh + 1) * TT], in_=psum[0:64, hh, :],
                                                          func=AF.Sigmoid, bias=lv_s[:, 1:2], scale=1.0),
                     reads=[("st", hh // 2), "lv"], writes=["r_s"])
            for hh in range(nh):
                P.op("act", lambda e, hh=hh: e.activation(out=i_s[:, hh * TT:(hh + 1) * TT], in_=psum[0:64, nh + hh, :],
                                                          func=AF.Sigmoid, bias=lv_s[:, 2:3], scale=1.0),
                     reads=[("st", (nh + hh) // 2), "lv"], writes=["i_s"])
            P.op("act", lambda e, b=b: e.activation(out=g_s[:], in_=ga_s[:, b, :], func=AF.Square),
                 reads=[("ga", b)], writes=["g_s"])
            P.op("act", lambda e: e.activation(out=g_s[:], in_=g_s[:], func=AF.Identity, bias=1.0, scale=0.044715),
                 reads=["g_s"], writes=["g_s"])
            P.op("dve", lambda e, b=b: e.tensor_tensor(out=g_s[:], in0=g_s[:], in1=ga_s[:, b, :], op=ALU.mult),
                 reads=["g_s", ("ga", b)], writes=["g_s"])
            P.op("act", lambda e: e.activation(out=g_s[:], in_=g_s[:], func=AF.Sigmoid, scale=1.5957691216057308),
                 reads=["g_s"], writes=["g_s"])
            P.op("dve", lambda e, b=b: e.tensor_tensor(out=g_s[:], in0=g_s[:], in1=ga_s[:, b, :], op=ALU.mult),
                 reads=["g_s", ("ga", b)], writes=["g_s"])
            P.op("act", lambda e: e.activation(out=a_s[:], in_=r_s[:], func=AF.Exp, scale=csc[:, 1:2]),
                 reads=["r_s", "csc"], writes=["a_s"])
            P.op("act", lambda e: e.activation(out=m_s[:], in_=r_s[:], func=AF.Exp, scale=csc[:, 2:3]),
                 reads=["r_s", "csc"], writes=["m_s"])
            P.op("act", lambda e: e.activation(out=m_s[:], in_=m_s[:], func=AF.Sqrt, bias=1.0, scale=-1.0),
                 reads=["m_s"], writes=["m_s"])
            P.op("dve", lambda e: e.tensor_tensor(out=i_s[:], in0=i_s[:], in1=xc[:], op=ALU.mult),
                 reads=["i_s", "xc"], writes=["i_s"])
            P.op("dve", lambda e: e.tensor_tensor(out=m_s[:], in0=m_s[:], in1=i_s[:], op=ALU.mult),
                 reads=["m_s", "i_s"], writes=["m_s"])
            init = 0.0 if ch == 0 else h_s[:, 1 - b, TL - 1:TL]
            P.op("dve", lambda e, b=b, init=init: e.tensor_tensor_scan(out=h_s[:, b, :], data0=a_s[:], data1=m_s[:],
                                                                       initial=init, op0=ALU.mult, op1=ALU.add),
                 reads=["a_s", "m_s", ("h_s", 1 - b)], writes=[("h_s", b)])
            P.op("dve", lambda e, b=b: e.tensor_tensor(out=ya_s[:, b, :], in0=h_s[:, b, :], in1=g_s[:], op=ALU.mult),
                 reads=[("h_s", b), "g_s"], writes=[("ya_s", b)])
            P.op("sp", lambda e, b=b, t0=t0: e.dma_start(out=ya_o(t0, TL), in_=ya_s[:, b, :]),
                 reads=[("ya_s", b)], writes=[("ya_dram", ch)], dma="st_ya%d" % b)
        if "after_ya" in io:
            io["after_ya"](P, NCH)

        if "pre_attn" in io:
            io["pre_attn"](P)
        P.op("pool", lambda e: e.memset(q0T[64:128, :], 0.0), writes=["q0T"])
        P.op("pool", lambda e: e.memset(q1T[0:64, :], 0.0), writes=["q1T"])
        P.op("pool", lambda e: e.memset(v1[:, :, 128:129], 1.0), writes=["v1ones"])
        if "load_qk" in io:
            io["load_qk"](P, q0T, q1T, kTs)
        else:
            P.op("sp", lambda e: e.dma_start(out=q0T[0:64, :], in_=io["q_src"](e, 0)[0:64, :]), writes=["q0T"], dma="ld_q0")
            P.op("sp", lambda e: e.dma_start(out=q1T[64:128, :], in_=io["q_src"](e, 0)[64:128, :]), writes=["q1T"], dma="ld_q1")
            P.op("sp", lambda e: e.dma_start(out=kTs[:], in_=io["k_src"](e, 0)), writes=["kTs"], dma="ld_k")
        nvd = max(nsrc, NKT // 16)
        for i in range(nvd):
            a, b_ = i * NKT // nvd, (i + 1) * NKT // nvd
            j = (a * 128) // NS
            ra = a * 128 - j * NS
            rb_ = b_ * 128 - j * NS
            P.op("sp", lambda e, a=a, b_=b_, j=j, ra=ra, rb_=rb_: e.dma_start(
                out=v1[:, a:b_, 0:128], in_=io["v_src"](e, j)[ra:rb_, :].rearrange("(kt p) e -> p kt e", p=128)),
                reads=io.get("v_dep", []), writes=[("v1", i)], dma="ld_v%d" % i)

        def acc_ap(a, lo=0, hi=129):
            return psum[:, 4 + a // 3, (a % 3) * 160 + lo:(a % 3) * 160 + hi]

        def accs_ap(a, lo=0, hi=129):
            return accs[:, a // 3, (a % 3) * 160 + lo:(a % 3) * 160 + hi]

        pairs = [(G, kt) for G in range(NG) for kt in range(4 * G + 4)]

        def geom(G, kt):
            jj = max(kt - 4 * G, 0)
            nblk = 4 - jj
            d0 = 4 * G + jj - kt
            return jj, nblk, d0, (d0 <= 8)

        def emit_qk(n):
            G, kt = pairs[n]
            jj, nblk, d0, near = geom(G, kt)
            sb_ = n % 2
            c0, c1 = jj * 128, TT
            q0 = G * TT + c0

            def fn(e):
                for c, qsrc in ((0, q0T), (1, q1T)):
                    i = e.matmul(psum[:, 2 * sb_ + c, c0:c1], lhsT=kTs[:, kt * 128:(kt + 1) * 128],
                                 rhs=qsrc[:, q0:q0 + nblk * 128], start=True, stop=not near)
                    if near:
                        i = e.matmul(psum[:, 2 * sb_ + c, c0:c1], lhsT=id_b[:],
                                     rhs=biasT[:, d0:d0 + nblk, :], start=False, stop=True)
                return i
            P.op("pe", fn, reads=["q0T", "q1T", "kTs", "id_b", "biasT"], writes=[("st", sb_)])

        def emit_exp(n):
            G, kt = pairs[n]
            jj, nblk, d0, near = geom(G, kt)
            sb_, pb = n % 2, n % 3
            c0 = jj * 128
            src = psum[:, 2 * sb_:2 * sb_ + 2, c0:TT]
            dst = pt[:, pb, :, c0:TT]
            if near:
                fn = lambda e: e.activation(out=dst, in_=src, func=AF.Exp)
            else:
                fn = lambda e: e.activation(out=dst, in_=src, func=AF.Exp, bias=rb_s[:, 15:16], scale=1.0)
            P.op("act", fn, reads=[("st", sb_), "rb"], writes=[("pt", pb)])

        def emit_pv(n):
            G, kt = pairs[n]
            jj, nblk, d0, near = geom(G, kt)
            pb = n % 3

            def fn(e):
                for i_ in range(jj, 4):
                    for c in range(2):
                        i = e.matmul(acc_ap(c * 4 + i_), lhsT=pt[:, pb, c, i_ * 128:(i_ + 1) * 128], rhs=v1[:, kt, :],
                                     start=False, stop=False, skip_group_check=True)
                return i
            P.op("pe", fn, reads=[("pt", pb), ("v1", kt * nvd // NKT), "v1ones"], writes=["acc"])

        def emit_evac(G):
            yb_ = G % 2
            P.op("dve", lambda e: e.tensor_copy(out=accs[:], in_=psum[:, 4:7, :]), reads=["acc"], writes=["accs"])
            P.op("dve", lambda e: e.reciprocal(out=rl[:, 0:6].rearrange("p (a b) -> p a b", a=2),
                                               in_=accs[:, 0:2, 128:449:160]), reads=["accs"], writes=["rl"])
            P.op("dve", lambda e: e.reciprocal(out=rl[:, 6:8], in_=accs[:, 2, 128:289:160]), reads=["accs"], writes=["rl"])
            P.op("dve", lambda e: e.tensor_scalar(out=rl[:, 4:8], in0=rl[:, 4:8], scalar1=lsc[:, 5:6], scalar2=None, op0=ALU.mult),
                 reads=["rl", "nlam"], writes=["rl"])
            for i_ in range(4):
                P.op("dve", lambda e, i_=i_: e.tensor_scalar(out=o_s[:, i_, :], in0=accs_ap(i_, 0, 128), scalar1=rl[:, i_:i_ + 1],
                                                             scalar2=None, op0=ALU.mult),
                     reads=["accs", "rl"], writes=[("o_s", i_)])
                P.op("dve", lambda e, i_=i_: e.scalar_tensor_tensor(out=o_s[:, i_, :], in0=accs_ap(4 + i_, 0, 128),
                                                                    scalar=rl[:, 4 + i_:5 + i_], in1=o_s[:, i_, :],
                                                                    op0=ALU.mult, op1=ALU.add),
                     reads=["accs", "rl", ("o_s", i_)], writes=[("o_s", i_)])
                P.op("dve", lambda e, i_=i_: e.scalar_tensor_tensor(out=osq[:], in0=o_s[:, i_, :], scalar=1.0, in1=o_s[:, i_, :],
                                                                    op0=ALU.mult, op1=ALU.mult, accum_out=ss[:, i_:i_ + 1]),
                     reads=[("o_s", i_)], writes=["osq", ("ss", i_)])
            P.op("act", lambda e: e.activation(out=ss[:], in_=ss[:], func=AF.Ln, bias=128 * EPS, scale=1.0),
                 reads=[("ss", k) for k in range(4)], writes=["ssr"])
            P.op("act", lambda e: e.activation(out=ss[:], in_=ss[:], func=AF.Exp, scale=-0.5), reads=["ssr"], writes=["ssr"])
            for i_ in range(4):
                P.op("dve", lambda e, i_=i_: e.scalar_tensor_tensor(out=y_s[:, i_, :], in0=o_s[:, i_, :], scalar=ss[:, i_:i_ + 1],
                                                                    in1=subg_s[:], op0=ALU.mult, op1=ALU.mult),
                     reads=[("o_s", i_), "ssr", "subg"], writes=[("y_s", i_)])

            def tr(e):
                for i_ in range(4):
                    i = e.transpose(trp[:, i_ * 128:(i_ + 1) * 128], y_s[:, i_, :], id_b[:])
                return i
            P.op("pe", tr, reads=[("y_s", k) for k in range(4)] + ["id_b"], writes=["trp"])
            P.op("dve", lambda e, yb_=yb_: e.tensor_copy(out=yc_s[:, yb_, :], in_=trp[:, 0:TT]), reads=["trp"], writes=[("yc_s", yb_)])
            P.op("sp", lambda e, yb_=yb_, G=G: e.dma_start(out=yc_o(G * TT, TT), in_=yc_s[:, yb_, :]),
                 reads=[("yc_s", yb_)], writes=[("yc_dram", G)], dma="st_yc%d" % yb_)
            if "after_yc" in io:
                io["after_yc"](P, G, NG)

        emit_qk(0)
        for n, (G, kt) in enumerate(pairs):
            if n + 1 < len(pairs):
                emit_qk(n + 1)
            if kt == 0:
                P.op("dve", lambda e: e.memset(psum[:, 4:7, :], 0.0), writes=["acc"])
            emit_exp(n)
            emit_pv(n)
            if kt == 4 * G + 3:
                emit_evac(G)


T3 = 256


def build_L3(NT):
    nc = bass.Bass("TRN2", target_bir_lowering=False)
    H = FFN_HIDDEN
    with contextlib.ExitStack() as es:
        C = Ctx(nc, es)
        yaT = C.din("yaT", [256, NT], BF16)
        ybT = C.din("ybT", [256, NT], BF16)
        ycT = C.din("ycT", [512, NT], BF16)

        def y_src(e, kind, i, c0, n):
            if kind == "ya":
                return yaT[64 * i:64 * i + 64, c0:c0 + n]
            if kind == "yb":
                return ybT[128 * i:128 * i + 128, c0:c0 + n]
            return ycT[128 * i:128 * i + 128, c0:c0 + n]
        xo = C.dout("xo", [1024, NT])
        io = {
            "x_in": C.din("xT", [1024, NT]), "x_mid": xo, "x_out": xo,
            "g1": C.din("g1", [128, 8]), "g2": C.din("g2", [128, 8]), "w_in": C.din("w_in", [1024, IN_COLS]),
            "bg": C.din("bg", [128, 24]), "y_src": y_src,
            "w_pa": C.din("w_pa", [256, 1024]), "w_pb": C.din("w_pb", [256, 1024]), "w_pc": C.din("w_pc", [512, 1024]),
            "w_o": C.din("w_o", [1024, 1024]), "w_g": C.din("w_g", [1024, H]), "w_u": C.din("w_u", [1024, H]),
            "w_d": C.din("w_d", [H, 1024]),
        }
        emit_L3(C, io, NT)
        C.P.emit()
    return nc


def emit_L3(C, io, NT):
    ntile = NT // T3
    H = FFN_HIDDEN
    HC = H // 128
    if True:
        P = C.P
        xT, xmid, xo = io["x_in"], io["x_mid"], io["x_out"]
        g1, g2, w_in, bg, w_pa, w_pb, w_pc, w_o, w_g, w_u, w_d = (io[k] for k in (
            "g1", "g2", "w_in", "bg", "w_pa", "w_pb", "w_pc", "w_o", "w_g", "w_u", "w_d"))

        wbuf = C.sb("wbuf", [128, 3 * 8 * H], BF16)
        stage = C.sb("stage", [128, 2, 1408], F32)
        xt = C.sb("xt", [128, 2, 8, T3], F32)
        sq = C.sb("sq", [128, 8, T3], BF16)
        hT = C.sb("hT", [128, 8, T3], BF16)
        rstd = C.sb("rstd", [128, T3], F32)
        ones = C.sb("ones", [128, 128], BF16)
        g1s = C.sb("g1s", [128, 8], F32)
        g2s = C.sb("g2s", [128, 8], F32)
        bg_s = C.sb("bg_s", [128, 24], F32)
        y_s = C.sb("y_s", [128, 2, 8, T3], BF16)
        gs = C.sb("gs", [128, 2, 3, T3], F32)
        mt = C.sb("mt", [128, 2, 3, T3], F32)
        mT = C.sb("mT", [128, 8, T3], BF16)
        sg = C.sb("sg", [128, 2, T3], F32)
        actT = C.sb("actT", [128, HC, T3], BF16)
        psum = C.ps("psum", [128, 8, 2, T3], F32)
        psr = Rot("ps", 8)

        def wview(off, kc, n):
            return wbuf[:, off:off + kc * n].rearrange("p (c n) -> p c n", c=kc)
        wgt_ = wview(0, 8, 3072)
        wpa_ = wview(24576, 2, 1024)
        wpb_ = wview(24576 + 2048, 2, 1024)
        wpc_ = wview(24576 + 4096, 4, 1024)
        wo_ = wview(24576 + 8192, 8, 1024)
        fg_ = wview(0, 8, H)
        fu_ = wview(8 * H, 8, H)
        fd_ = wview(16 * H, HC, 1024)

        P.op("dve", lambda e: e.memset(ones[:], 1.0), writes=["ones"])
        for dst, src, key in ((g1s, g1, "g1s"), (g2s, g2, "g2s"), (bg_s, bg, "bg")):
            P.op("sp", lambda e, dst=dst, src=src: e.dma_start(out=dst[:], in_=src), writes=[key], dma="c_" + key)
        for t_, k_ in ((g1s, "g1s"), (g2s, "g2s")):
            P.op("dve", lambda e, t_=t_: e.tensor_scalar(out=t_[:], in0=t_[:], scalar1=32.0, scalar2=None, op0=ALU.mult),
                 reads=[k_], writes=[k_])

        def load_x(t, src_ap, srckeys):
            b = t % 2
            src = src_ap[:, t * T3:(t + 1) * T3].rearrange("(c p) n -> p c n", p=128)
            P.op("sp", lambda e, b=b, src=src: e.dma_start(out=xt[:, b, :, :], in_=src),
                 reads=srckeys, writes=[("xt", b)], dma="xld%d" % b)

        def load_y(t):
            b = t % 2
            c0 = t * T3
            for h in range(4):
                P.op("sp", lambda e, b=b, h=h, c0=c0: e.dma_start(
                    out=y_s[(h % 2) * 64:(h % 2) * 64 + 64, b, h // 2, :], in_=io["y_src"](e, "ya", h, c0, T3)),
                    reads=io.get("y_dep", []), writes=[("y_s", b, 0)], dma="yld%d_0" % b)
            for i in range(2):
                P.op("sp", lambda e, b=b, i=i, c0=c0: e.dma_start(out=y_s[:, b, 2 + i, :], in_=io["y_src"](e, "yb", i, c0, T3)),
                     writes=[("y_s", b, 1)], dma="yld%d_1" % b)
            for h in range(4):
                P.op("sp", lambda e, b=b, h=h, c0=c0: e.dma_start(out=y_s[:, b, 4 + h, :], in_=io["y_src"](e, "yc", h, c0, T3)),
                     reads=io.get("y_dep", []), writes=[("y_s", b, 2)], dma="yld%d_2" % b)

        srot = Rot("stage", 2)
        load_x(0, xT, [])
        kg = load_weight_bf16(C, wgt_, w_in[:, 2560:5632], 1024, 3072, "wgt", stage, srot, scale_ap=g1s,
                              colblk=1024, scale_key="g1s")
        kpa = load_weight_bf16(C, wpa_, w_pa, 256, 1024, "wpa", stage, srot, colblk=1024)
        kpb = load_weight_bf16(C, wpb_, w_pb, 256, 1024, "wpb", stage, srot, colblk=1024)
        kpc = load_weight_bf16(C, wpc_, w_pc, 512, 1024, "wpc", stage, srot, colblk=1024)
        ko = load_weight_bf16(C, wo_, w_o, 1024, 1024, "wo", stage, srot, colblk=1024)
        c1keys = kg + kpa + kpb + kpc + ko
        if "pre_y" in io:
            io["pre_y"](P)
        load_y(0)

        def mm_fm(e, bank, half, wv, kc, col0, rhs_fn):
            for k in range(kc):
                i = e.matmul(psum[:, bank, half, :], lhsT=wv[:, k, col0:col0 + 128], rhs=rhs_fn(k),
                             start=(k == 0), stop=(k == kc - 1))
            return i

        def norm(t):
            b = t % 2
            bank, pk = psr.next()
            rms_tile(C, xt[:, b], ("xt", b), hT, "hT", sq, "sq", ones, psum[:, bank, 0, :], pk, rstd, "rstd", T3, 1024 * EPS)

        def resid(t, wv, kc, rhs_fn, rkeys, outkey, xdst):
            b = t % 2
            for m in range(4):
                bank, pk = psr.next()

                def fn(e, bank=bank, m=m):
                    for hf in range(2):
                        i = mm_fm(e, bank, hf, wv, kc, (2 * m + hf) * 128, rhs_fn)
                    return i
                P.op("pe", fn, reads=rkeys, writes=[pk])
                P.op("dve", lambda e, bank=bank, m=m, b=b: e.tensor_tensor(
                    out=xt[:, b, 2 * m:2 * m + 2, :], in0=xt[:, b, 2 * m:2 * m + 2, :], in1=psum[:, bank, :, :], op=ALU.add),
                    reads=[pk, ("xt", b)], writes=[("xt", b)])
            dst = xdst[:, t * T3:(t + 1) * T3].rearrange("(c p) n -> p c n", p=128)
            P.op("sp", lambda e, b=b, dst=dst: e.dma_start(out=dst, in_=xt[:, b, :, :]),
                 reads=[("xt", b)], writes=[(outkey, t)], dma="st_x%d" % b)

        projs = ((wpa_, 2, 0, kpa), (wpb_, 2, 2, kpb), (wpc_, 4, 4, kpc))
        for t in range(ntile):
            b = t % 2
            if t + 1 < ntile:
                load_x(t + 1, xT, [])
                load_y(t + 1)
            norm(t)
            for n in range(8):
                gb = n % 2
                slots = []
                for br in range(3):
                    bank, pk = psr.next()
                    slots.append((bank, pk))
                    wv, kc, off, kk = projs[br]

                    def fn(e, bank=bank, br=br, n=n, wv=wv, kc=kc, off=off, b=b):
                        mm_fm(e, bank, 0, wgt_, 8, br * 1024 + n * 128, lambda k: hT[:, k, :])
                        return mm_fm(e, bank, 1, wv, kc, n * 128, lambda k: y_s[:, b, off + k, :])
                    P.op("pe", fn, reads=["hT", ("y_s", b, br)] + kg + kk, writes=[pk])
                for br in range(3):
                    bank, pk = slots[br]
                    ch = br * 8 + n
                    P.op("act", lambda e, bank=bank, br=br, ch=ch, gb=gb: e.activation(
                        out=gs[:, gb, br, :], in_=psum[:, bank, 0, :], func=AF.Sigmoid, bias=bg_s[:, ch:ch + 1], scale=1.0),
                        reads=[pk, "bg"], writes=[("gs", gb, br)])
                    P.op("dve", lambda e, bank=bank, br=br, gb=gb: e.tensor_tensor(
                        out=mt[:, gb, br, :], in0=psum[:, bank, 1, :], in1=gs[:, gb, br, :], op=ALU.mult),
                        reads=[pk, ("gs", gb, br)], writes=[("mt", gb, br)])
                P.op("pool", lambda e, gb=gb: e.tensor_tensor(out=mt[:, gb, 0, :], in0=mt[:, gb, 0, :], in1=mt[:, gb, 1, :], op=ALU.add),
                     reads=[("mt", gb, 0), ("mt", gb, 1)], writes=[("mt", gb, 0)])
                P.op("pool", lambda e, gb=gb, n=n: e.tensor_tensor(out=mT[:, n, :], in0=mt[:, gb, 0, :], in1=mt[:, gb, 2, :], op=ALU.add),
                     reads=[("mt", gb, 0), ("mt", gb, 2)], writes=[("mT", n)])
            resid(t, wo_, 8, lambda k: mT[:, k, :], [("mT", k) for k in range(8)] + ko, "xo", xmid)

        kfg = load_weight_bf16(C, fg_, w_g, 1024, H, "fg", stage, srot, scale_ap=g2s, colblk=1408, scale_key="g2s",
                               also_writes=c1keys)
        kfu = load_weight_bf16(C, fu_, w_u, 1024, H, "fu", stage, srot, scale_ap=g2s, colblk=1408, scale_key="g2s",
                               also_writes=c1keys)
        kfd = load_weight_bf16(C, fd_, w_d, H, 1024, "fd", stage, srot, colblk=1024, also_writes=c1keys)
        load_x(0, xmid, [("xo", 0)])
        for t in range(ntile):
            b = t % 2
            if t + 1 < ntile:
                load_x(t + 1, xmid, [("xo", t + 1)])
            norm(t)
            for j in range(HC):
                sb_ = j % 2
                bank, pk = psr.next()

                def fn(e, bank=bank, j=j):
                    mm_fm(e, bank, 0, fg_, 8, j * 128, lambda k: hT[:, k, :])
                    return mm_fm(e, bank, 1, fu_, 8, j * 128, lambda k: hT[:, k, :])
                P.op("pe", fn, reads=["hT"] + kfg + kfu, writes=[pk])
                P.op("act", lambda e, bank=bank, sb_=sb_: e.activation(out=sg[:, sb_, :], in_=psum[:, bank, 0, :], func=AF.Silu),
                     reads=[pk], writes=[("sg", sb_)])
                P.op("dve", lambda e, bank=bank, sb_=sb_, j=j: e.tensor_tensor(out=actT[:, j, :], in0=psum[:, bank, 1, :], in1=sg[:, sb_, :], op=ALU.mult),
                     reads=[pk, ("sg", sb_)], writes=[("actT", j)])
            resid(t, fd_, HC, lambda k: actT[:, k, :], [("actT", k) for k in range(HC)] + kfd, "xo2", xo)


def _c(a):
    return np.ascontiguousarray(a, dtype=np.float32)


def l1_inputs(xT, inp, l):
    sgw = inp["sg_w"][l]
    sgb = inp["sg_b"][l]
    p = np.arange(128)
    bsb = np.stack([sgb[2 * n + p // 64, :] for n in range(2)], axis=1)
    gqk = np.stack([inp["q_norm_g"][l][p % 64], inp["k_norm_g"][l][p % 64]], axis=1)
    return {
        "xT": _c(xT),
        "g1": _c(inp["ln1_g"][l].reshape(8, 128).T),
        "w_in": _c(inp["w_in"][l]),
        "sgg": _c(np.broadcast_to(inp["sg_ln_g"][l], (128, 256))),
        "sgb": _c(np.broadcast_to(inp["sg_ln_b"][l], (128, 256))),
        "wsT": _c(sgw.transpose(2, 0, 1)),
        "tril": _c(np.triu(np.ones((128, 128)))),
        "bsb": _c(bsb),
        "gqk": _c(gqk),
    }


def t5_bucket_np(rel):
    import jax
    import jax.numpy as jnp
    with jax.default_device(jax.devices("cpu")[0]):
        rel = jnp.asarray(rel, jnp.int32)
        half, max_exact = 16, 8
        ret = jnp.where(rel > 0, half, 0)
        n = jnp.abs(rel)
        nf = jnp.maximum(n, 1).astype(jnp.float32)
        large = max_exact + (jnp.log(nf / max_exact) / math.log(2048 / max_exact) * (half - max_exact)).astype(jnp.int32)
        large = jnp.minimum(large, half - 1)
        return np.asarray(ret + jnp.where(n < max_exact, n, large))


_L2_CONST = {}


def l2_consts():
    if not _L2_CONST:
        k = np.arange(128)[:, None, None]
        d = np.arange(12)[None, :, None]
        q = np.arange(128)[None, None, :]
        rel = k - q - 128 * d
        _L2_CONST["idx"] = _c(t5_bucket_np(rel))
        kk = np.arange(128)[:, None]
        qq = np.arange(128)[None, :]
        _L2_CONST["maskT"] = _c(np.where((kk // 64) > (qq // 64), NEG, 0.0))
        _L2_CONST["ident"] = _c(np.eye(128))
    return _L2_CONST


def l2_inputs(qT, kT, v, xaT, gaT, inp, l, h):
    lam_init = 0.8 - 0.6 * math.exp(-0.3 * l)
    cs = l2_consts()
    ch = slice(64 * h, 64 * h + 64)
    lvec = np.stack([inp["conv_b"][l][ch], inp["lru_ba"][l][ch], inp["lru_bi"][l][ch], inp["lru_lambda"][l][ch]], axis=1)
    lq = np.stack([inp["lambda_q1"][l], inp["lambda_k1"][l], inp["lambda_q2"][l], inp["lambda_k2"][l]], axis=0)
    return {
        "qT": qT, "kT": kT, "v": v, "xaT": _c(xaT), "gaT": _c(gaT),
        "cw": _c(inp["conv_w"][l][:, ch].T),
        "lvec": _c(lvec),
        "wa": _c(inp["lru_wa"][l][h]), "wi": _c(inp["lru_wi"][l][h]),
        "lq": _c(np.broadcast_to(lq, (128, 4, 64))),
        "subg": _c(np.broadcast_to(inp["subln_g"][l], (128, 128))),
        "rb": _c(np.broadcast_to(inp["rel_bias"][:, h], (128, 32))),
        "idx": cs["idx"], "maskT": cs["maskT"], "ident": cs["ident"],
        "lcon": _c(np.broadcast_to(np.array([lam_init, (1.0 - lam_init) * math.sqrt(128.0)]), (128, 2))),
    }


def l3_inputs(xT, yaT, ybT, ycT, inp, l):
    return {
        "xT": _c(xT),
        "g1": _c(inp["ln1_g"][l].reshape(8, 128).T),
        "g2": _c(inp["ln2_g"][l].reshape(8, 128).T),
        "w_in": _c(inp["w_in"][l]),
        "bg": _c(inp["b_gate"][l].reshape(24, 128).T),
        "yaT": yaT, "ybT": ybT, "ycT": ycT,
        "w_pa": _c(inp["w_pa"][l]), "w_pb": _c(inp["w_pb"][l]), "w_pc": _c(inp["w_pc"][l]), "w_o": _c(inp["w_o"][l]),
        "w_g": _c(inp["w_ff_gate"][l]), "w_u": _c(inp["w_ff_up"][l]), "w_d": _c(inp["w_ff_down"][l]),
    }


ARENA_BYTES = 212736
GROUPS = [[0, 1, 2, 3], [4, 5, 6, 7]]


def build_fused(S, depth=DEPTH):
    NT = S // 4
    H = FFN_HIDDEN
    L = depth
    nc = bass.Bass("TRN2", target_bir_lowering=False)
    with contextlib.ExitStack() as es:
        C = Ctx(nc, es)
        C.use_arena(ARENA_BYTES)
        P = C.P
        P.use_rank = True
        xT = C.din("xT", [1024, NT])
        xo = C.dout("xo", [1024, NT])
        pin = {}
        for name, shape in (("g1", [L, 128, 8]), ("g2", [L, 128, 8]), ("w_in", [L, 1024, IN_COLS]),
                            ("sgg", [L, 128, 256]), ("sgb", [L, 128, 256]), ("wsT", [L, 128, 4, 128]),
                            ("tril", [128, 128]), ("bsb", [L, 128, 2, 128]), ("gqk", [L, 128, 2]),
                            ("cw", [L, 64, 4]), ("lvec", [L, 64, 4]), ("wa", [L, 64, 64]), ("wi", [L, 64, 64]),
                            ("lq", [L, 128, 4, 64]), ("subg", [L, 128, 128]), ("rb", [128, 32]),
                            ("idx", [128, 12, 128]), ("maskT", [128, 128]), ("ident", [128, 128]), ("lcon", [L, 128, 2]),
                            ("bg", [L, 128, 24]), ("w_pa", [L, 256, 1024]), ("w_pb", [L, 256, 1024]),
                            ("w_pc", [L, 512, 1024]), ("w_o", [L, 1024, 1024]), ("w_g", [L, 1024, H]),
                            ("w_u", [L, 1024, H]), ("w_d", [L, H, 1024])):
            pin[name] = C.din(name, shape)

        def dint(name, shape, dt):
            return nc.dram_tensor(name, list(shape), dt).ap()
        q_in = dint("q_in", [4, 128, NT], BF16)
        q_out = dint("q_out", [4, 512, NT], BF16)
        k_in = dint("k_in", [4, 128, NT], BF16)
        k_out = dint("k_out", [4, 512, NT], BF16)
        v_in = dint("v_in", [4, NT, 128], BF16)
        v_out = dint("v_out", [4, 4 * NT, 128], BF16)
        xa_in = dint("xa_in", [4, 64, NT], F32)
        xa_out = dint("xa_out", [4, 256, NT], F32)
        ga_in = dint("ga_in", [4, 64, NT], F32)
        ga_out = dint("ga_out", [4, 256, NT], F32)
        yc_in = dint("yc_in", [4, 128, NT], BF16)
        yc_out = dint("yc_out", [4, 512, NT], BF16)
        ya_in = dint("ya_in", [4, 64, NT], BF16)
        ya_out = dint("ya_out", [4, 256, NT], BF16)
        v_loc = dint("v_loc", [S, 128], BF16)
        xg_loc = dint("xg_loc", [2, 4, 64, NT], F32)
        yc_loc = dint("yc_loc", [4, 128, NT], BF16)
        ya_loc = dint("ya_loc", [4, 64, NT], BF16)
        yb_x = dint("yb_x", [2, 128, NT], BF16)
        xs1 = dint("xs1", [1024, NT], F32)
        xs2 = dint("xs2", [1024, NT], F32)

        P.dyn_spec = {}

        def rk(name="r"):
            return P.dyn[name]

        def allgather(name, src, dst, reads=()):
            P.op("pool", lambda e: e.collective_compute("AllGather", ALU.bypass, replica_groups=GROUPS, ins=[src], outs=[dst]),
                 reads=list(reads), writes=["cc_" + name], dma="cc_" + name, inc=1)

        for l in range(L):
            par = 0
            x_in = xT if l == 0 else xs2
            x_out = xo if l == L - 1 else xs2
            C.arena_reset()
            io1 = {"xT": x_in, "g1": pin["g1"][l], "w_in": pin["w_in"][l], "sgg": pin["sgg"][l], "sgb": pin["sgb"][l],
                   "wsT": pin["wsT"][l], "tril": pin["tril"], "bsb": pin["bsb"][l], "gqk": pin["gqk"][l],
                   "qk": lambda h, which: (q_in, k_in)[which][h],
                   "v": v_in,
                   "xg": lambda ch: (xa_in, ga_in)[ch // 2][2 * (ch % 2):2 * (ch % 2) + 2].rearrange("h p n -> (h p) n"),
                   "ybT": yb_x}

            def after_xg(P_, keys):
                for h in range(4):
                    allgather("xa", xa_in[h], xa_out[h], reads=keys)
                    allgather("ga", ga_in[h], ga_out[h], reads=keys)
            io1["after_xg"] = after_xg
            emit_L1(C, io1, NT)
            P.barrier()
            for h in range(4):
                allgather("q", q_in[h], q_out[h])
                allgather("k", k_in[h], k_out[h])
                allgather("v", v_in[h], v_out[h])
            C.arena_reset()
            for a_, srcg in ((0, xa_out), (1, ga_out)):
                P.op("sp", lambda e, a_=a_, srcg=srcg: e.dma_start(
                    out=xg_loc[a_].rearrange("(o j) p n -> o j p n", o=1),
                    in_=srcg.rearrange("h (j p) n -> h j p n", j=4)[bass.ds(rk(), 1), :, :, :]),
                    reads=["cc_xa", "cc_ga"], writes=[("xg_loc", a_)], dma="loc_xg%d" % a_)

            def load_qk(P_, q0T, q1T, kTs):
                def v3(t, rows):
                    return t[rows, :].rearrange("p (j n) -> p j n", j=4)

                def src(g, rows):
                    return g.rearrange("h (j p) n -> h j p n", j=4)[bass.ds(rk(), 1), :, rows, :].rearrange("o j p n -> p (o j) n")
                P_.op("sp", lambda e: e.dma_start(out=v3(q0T, slice(0, 64)), in_=src(q_out, slice(0, 64))),
                      reads=["cc_q"], writes=["q0T"], dma="ld_q0")
                P_.op("sp", lambda e: e.dma_start(out=v3(q1T, slice(64, 128)), in_=src(q_out, slice(64, 128))),
                      reads=["cc_q"], writes=["q1T"], dma="ld_q1")
                P_.op("sp", lambda e: e.dma_start(out=v3(kTs, slice(0, 128)), in_=src(k_out, slice(0, 128))),
                      reads=["cc_k"], writes=["kTs"], dma="ld_k")

            def pre_attn(P_):
                P_.op("sp", lambda e: e.dma_start(out=v_loc.rearrange("(o t) e -> o t e", o=1), in_=v_out[bass.ds(rk(), 1), :, :]),
                      reads=["cc_v"], writes=["v_loc"], dma="loc_v")

            def after_ya(P_, nch):
                per = nch // 4
                for j in range(4):
                    allgather("ya", ya_in[j], ya_out[j], reads=[("ya_dram", c_) for c_ in range(j * per, (j + 1) * per)])

            def after_yc(P_, G, ng):
                per = ng // 4
                if (G + 1) % per == 0:
                    j = G // per
                    allgather("yc", yc_in[j], yc_out[j], reads=[("yc_dram", g_) for g_ in range(j * per, (j + 1) * per)])
            io2 = {"nsrc": 4, "load_qk": load_qk, "after_ya": after_ya, "after_yc": after_yc, "pre_attn": pre_attn,
                   "v_dep": ["v_loc"], "xa_dep": [("xg_loc", 0)], "ga_dep": [("xg_loc", 1)],
                   "v_src": lambda e, j: v_loc[j * NT:(j + 1) * NT, :],
                   "xa_src": lambda e, j: xg_loc[0, j], "ga_src": lambda e, j: xg_loc[1, j],
                   "cw": pin["cw"][l], "lvec": pin["lvec"][l], "wa": pin["wa"][l], "wi": pin["wi"][l], "lq": pin["lq"][l],
                   "subg": pin["subg"][l], "rb": pin["rb"], "idx": pin["idx"], "maskT": pin["maskT"], "ident": pin["ident"],
                   "lcon": pin["lcon"][l],
                   "ycT": lambda c0, n: yc_in[c0 // NT, :, c0 % NT:c0 % NT + n],
                   "yaT": lambda c0, n: ya_in[c0 // NT, :, c0 % NT:c0 % NT + n]}
            emit_L2(C, io2, S)
            P.barrier(exclude=("cc_yc", "cc_ya"), keep=("cc_yc", "cc_ya"))
            C.arena_reset()
            def pre_y(P_):
                P_.op("sp", lambda e: e.dma_start(out=yc_loc.rearrange("(o h) p n -> o h p n", o=1),
                                                  in_=yc_out.rearrange("j (h p) n -> j h p n", h=4)[bass.ds(rk(), 1), :, :, :]),
                      reads=["cc_yc"], writes=["yc_loc"], dma="loc_yc")
                P_.op("sp", lambda e: e.dma_start(out=ya_loc.rearrange("(o h) p n -> o h p n", o=1),
                                                  in_=ya_out.rearrange("j (h p) n -> j h p n", h=4)[bass.ds(rk(), 1), :, :, :]),
                      reads=["cc_ya"], writes=["ya_loc"], dma="loc_ya")

            def y_src(e, kind, i, c0, n):
                if kind == "yb":
                    return yb_x[i, :, c0:c0 + n]
                if kind == "ya":
                    return ya_loc[i, :, c0:c0 + n]
                return yc_loc[i, :, c0:c0 + n]
            io3 = {"x_in": x_in, "x_mid": xs1, "x_out": x_out, "g1": pin["g1"][l], "g2": pin["g2"][l],
                   "w_in": pin["w_in"][l], "bg": pin["bg"][l], "y_src": y_src, "pre_y": pre_y, "y_dep": ["yc_loc", "ya_loc"], "w_pa": pin["w_pa"][l],
                   "w_pb": pin["w_pb"][l], "w_pc": pin["w_pc"][l], "w_o": pin["w_o"][l], "w_g": pin["w_g"][l],
                   "w_u": pin["w_u"][l], "w_d": pin["w_d"][l]}
            emit_L3(C, io3, NT)
            P.barrier()
        P.emit()
    return nc


def fused_inputs(inp, c, S, depth=DEPTH):
    NT = S // 4
    b, r = c // 4, c % 4
    x = inp["x"]
    xT = np.ascontiguousarray(x[b, r * NT:(r + 1) * NT].T)
    dummy = np.zeros((2, 2), np.float32)
    l1 = [l1_inputs(dummy, inp, l) for l in range(depth)]
    l2 = [l2_inputs(None, None, None, dummy, dummy, inp, l, r) for l in range(depth)]
    l3 = [l3_inputs(dummy, None, None, None, inp, l) for l in range(depth)]

    def st(lst, k):
        return np.ascontiguousarray(np.stack([d[k] for d in lst], axis=0))
    m = {"xT": xT}
    for k in ("g1", "w_in", "sgg", "sgb", "wsT", "bsb", "gqk"):
        m[k] = st(l1, k)
    m["tril"] = l1[0]["tril"]
    for k in ("cw", "lvec", "wa", "wi", "lq", "subg", "lcon"):
        m[k] = st(l2, k)
    for k in ("rb", "idx", "maskT", "ident"):
        m[k] = l2[0][k]
    for k in ("g2", "bg", "w_pa", "w_pb", "w_pc", "w_o", "w_g", "w_u", "w_d"):
        m[k] = st(l3, k)
    return m


_PROGS = {}


def kernel(**inputs):
    inp = {k: np.asarray(v) for k, v in inputs.items()}
    x = inp["x"]
    B, S, D = x.shape
    NT = S // 4
    key = ("fused", S)
    if key not in _PROGS:
        _PROGS[key] = build_fused(S)
    nc = _PROGS[key]
    in_maps = [fused_inputs(inp, c, S) for c in range(N_CORES)]
    res = run_bass_kernel_spmd(nc, in_maps, core_ids=list(range(N_CORES))).results
    out = np.empty((B, S, D), dtype=np.float32)
    for c in range(N_CORES):
        out[c // 4, (c % 4) * NT:(c % 4 + 1) * NT] = np.asarray(res[c]["xo"]).T
    return out
```

```python
import contextlib
import math
import numpy as np
import concourse.bass as bass
import concourse.mybir as mybir
from concourse.bass_utils import run_bass_kernel_spmd

F32 = mybir.dt.float32
BF16 = mybir.dt.bfloat16
AF = mybir.ActivationFunctionType
ALU = mybir.AluOpType
AX = mybir.AxisListType

D_MODEL = 1024
DEPTH = 4
N_CORES = 8
EPS = 1e-6
LRU_C = 8.0
FFN_HIDDEN = 2816
IN_COLS = 5632
TT = 512
NEG = -30000.0

STREAMS = ("pe", "act", "dve", "pool", "sp")


class Prog:
    def __init__(self, nc):
        self.nc = nc
        self.streams = {e: [] for e in STREAMS}
        self.count = {}
        self.last_write = {}
        self.readers = {}
        self.waited = {e: {} for e in STREAMS}
        self.pending = {e: [] for e in STREAMS}
        self.nops = 0
        self.use_rank = False
        self.dyn = {}

    def barrier(self, exclude=(), keep=()):
        for e in STREAMS:
            for k, v in self.count.items():
                if k in exclude:
                    continue
                if self.waited[e].get(k, 0) < v:
                    self.waited[e][k] = v
                    self.pending[e].append((k, v))
        kept = {k: self.last_write[k] for k in keep if k in self.last_write}
        self.last_write.clear()
        self.readers.clear()
        self.last_write.update(kept)

    def op(self, eng, fn, reads=(), writes=(), dma=None, inc=None):
        semkey = dma if dma is not None else eng
        if inc is None:
            inc = 16 if dma is not None else 1
        deps = {}

        def add(k, v, same_ok):
            if k == semkey and eng == "pe" and dma is None and same_ok:
                return
            if deps.get(k, 0) < v:
                deps[k] = v

        for b in reads:
            lw = self.last_write.get(b)
            if lw is not None:
                add(lw[0], lw[1], eng == "pe")
        for b in writes:
            lw = self.last_write.get(b)
            if lw is not None:
                add(lw[0], lw[1], True)
            for k, v in self.readers.get(b, {}).items():
                add(k, v, True)
        waits = self.pending[eng]
        self.pending[eng] = []
        wd = self.waited[eng]
        for k, v in deps.items():
            if wd.get(k, 0) < v:
                wd[k] = v
                waits.append((k, v))
        val = self.count.get(semkey, 0) + inc
        self.count[semkey] = val
        for b in reads:
            self.readers.setdefault(b, {})[semkey] = val
        for b in writes:
            self.last_write[b] = (semkey, val)
            self.readers[b] = {}
        self.streams[eng].append((waits, fn, semkey, inc))
        self.nops += 1

    def emit(self):
        nc = self.nc
        with contextlib.ExitStack() as es:
            sems = {k: es.enter_context(nc.semaphore("s_" + k)) for k in self.count}
            block = es.enter_context(nc.Block())
            final = list(self.count.items())

            def run(name, e):
                for waits, fn, semkey, inc in self.streams[name]:
                    for k, v in waits:
                        e.wait_ge(sems[k], v)
                    fn(e).then_inc(sems[semkey], inc)

            @block.tensor
            def _(e):
                run("pe", e)

            @block.scalar
            def _(e):
                run("act", e)

            @block.vector
            def _(e):
                run("dve", e)

            @block.gpsimd
            def _(e):
                run("pool", e)

            @block.sync
            def _(e):
                if self.use_rank:
                    r = e.snap(e.partition_id() % 4, min_val=0, max_val=3)
                    self.dyn["r"] = r
                    for name, mul in self.dyn_spec.items():
                        self.dyn[name] = e.snap(r * mul, min_val=0, max_val=3 * mul)
                run("sp", e)
                for k, v in final:
                    e.wait_ge(sems[k], v)


class Ctx:
    def __init__(self, nc, es):
        self.nc = nc
        self.es = es
        self.P = Prog(nc)
        self._n = 0

    def din(self, name, shape, dt=F32):
        return self.nc.dram_tensor(name, list(shape), dt, kind="ExternalInput").ap()

    def dout(self, name, shape, dt=F32):
        return self.nc.dram_tensor(name, list(shape), dt, kind="ExternalOutput").ap()

    def use_arena(self, nbytes):
        self.arena = self.es.enter_context(self.nc.sbuf_tensor("arena", [128, nbytes // 2], BF16))
        self.arena_n = nbytes // 2
        self.off = 0
        self.psum_t = self.es.enter_context(self.nc.psum_tensor("psum", [128, 8, 512], F32))

    def arena_reset(self):
        self.off = 0

    def sb(self, name, shape, dt=F32):
        if getattr(self, "arena", None) is None:
            return self.es.enter_context(self.nc.sbuf_tensor(name, list(shape), dt))[:]
        shape = list(shape)
        free = 1
        for d in shape[1:]:
            free *= d
        n16 = free * (2 if dt == F32 else 1)
        n16 = (n16 + 31) // 32 * 32
        assert self.off + n16 <= self.arena_n, "arena overflow at %s: %d + %d > %d" % (name, self.off, n16, self.arena_n)
        ap = self.arena[0:shape[0], self.off:self.off + free * (2 if dt == F32 else 1)]
        self.off += n16
        if dt == F32:
            ap = ap.bitcast(F32)
        if len(shape) > 2:
            names = " ".join("d%d" % i for i in range(len(shape) - 1))
            kw = {"d%d" % i: shape[1 + i] for i in range(len(shape) - 1)}
            ap = ap.rearrange("p (%s) -> p %s" % (names, names), **kw)
        return ap

    def ps(self, name, shape, dt=F32):
        if getattr(self, "arena", None) is None:
            return self.es.enter_context(self.nc.psum_tensor(name, list(shape), dt))[:]
        shape = list(shape)
        ap = self.psum_t[:]
        if shape == [128, 8, 512]:
            return ap
        assert shape == [128, 8, 2, 256], shape
        return ap.rearrange("p b (h n) -> p b h n", h=2)


class Rot:
    def __init__(self, name, n):
        self.name, self.n, self.i = name, n, 0

    def next(self):
        i = self.i % self.n
        self.i += 1
        return i, (self.name, i)


def load_weight_bf16(C, dst, src, K, N, key, stage, stage_rot, scale_ap=None, engines=("act", "dve"),
                     colblk=1536, dma_eng="sp", scale_key=None, also_writes=()):
    P = C.P
    extra = [scale_key] if scale_key is not None else []
    kc_n = K // 128
    ei = 0
    for kc in range(kc_n):
        for c0 in range(0, N, colblk):
            n = min(colblk, N - c0)
            si, skey = stage_rot.next()
            st = stage[:, si, 0:n]
            srcap = src[kc * 128:(kc + 1) * 128, c0:c0 + n]
            P.op(dma_eng, lambda e, st=st, srcap=srcap: e.dma_start(out=st, in_=srcap),
                 writes=[skey], dma="wld%d" % si)
            eng = engines[ei % len(engines)]
            ei += 1
            d = dst[:, kc, c0:c0 + n]
            if eng == "act":
                if scale_ap is not None:
                    sc = scale_ap[:, kc:kc + 1]
                    fn = lambda e, d=d, st=st, sc=sc: e.activation(out=d, in_=st, func=AF.Identity, scale=sc)
                else:
                    fn = lambda e, d=d, st=st: e.copy(out=d, in_=st)
            else:
                if scale_ap is not None:
                    sc = scale_ap[:, kc:kc + 1]
                    fn = lambda e, d=d, st=st, sc=sc: e.tensor_scalar(out=d, in0=st, scalar1=sc, scalar2=None,
                                                                      op0=ALU.mult)
                else:
                    fn = lambda e, d=d, st=st: e.tensor_copy(out=d, in_=st)
            P.op(eng, fn, reads=[skey] + extra, writes=[(key, eng)] + list(also_writes))
    return [(key, e) for e in engines]


def rstd_from_ps(P, psb, pskey, rstd, rkey, n, eps_n):
    P.op("act", lambda e: e.activation(out=rstd[:, 0:n], in_=psb[:, 0:n], func=AF.Ln, bias=eps_n, scale=1.0),
         reads=[pskey], writes=[rkey])
    P.op("act", lambda e: e.activation(out=rstd[:, 0:n], in_=rstd[:, 0:n], func=AF.Exp, scale=-0.5),
         reads=[rkey], writes=[rkey])


def rms_tile(C, xt, xkey, hT, hkey, sq, sqkey, ones, psb, pskey, rstd, rkey, n, eps_n):
    P = C.P
    P.op("act", lambda e: e.activation(out=sq[:, :, 0:n], in_=xt[:, :, 0:n], func=AF.Square),
         reads=[xkey], writes=[sqkey])

    def mm(e):
        for c in range(8):
            i = e.matmul(psb[:, 0:n], lhsT=ones[:], rhs=sq[:, c, 0:n], start=(c == 0), stop=(c == 7))
        return i
    P.op("pe", mm, reads=[sqkey, "ones"], writes=[pskey])
    rstd_from_ps(P, psb, pskey, rstd, rkey, n, eps_n)
    rb = rstd[:, 0:n].unsqueeze(1).broadcast_to([128, 8, n])
    P.op("dve", lambda e: e.tensor_tensor(out=hT[:, :, 0:n], in0=xt[:, :, 0:n], in1=rb, op=ALU.mult),
         reads=[xkey, rkey], writes=[hkey])


def build_L1(NT):
    nc = bass.Bass("TRN2", target_bir_lowering=False)
    with contextlib.ExitStack() as es:
        C = Ctx(nc, es)
        io = {
            "xT": C.din("xT", [1024, NT]), "g1": C.din("g1", [128, 8]), "w_in": C.din("w_in", [1024, IN_COLS]),
            "sgg": C.din("sgg", [128, 256]), "sgb": C.din("sgb", [128, 256]), "wsT": C.din("wsT", [128, 4, 128]),
            "tril": C.din("tril", [128, 128]), "bsb": C.din("bsb", [128, 2, 128]), "gqk": C.din("gqk", [128, 2]),
            "v": C.dout("v", [4, NT, 128], BF16), "ybT": C.dout("ybT", [2, 128, NT], BF16),
        }
        qk_t = C.dout("qk", [4, 2, 128, NT], BF16)
        xg_t = C.dout("xg", [4, 128, NT], F32)
        io["qk"] = lambda h, which: qk_t[h, which]
        io["xg"] = lambda ch: xg_t[ch]
        emit_L1(C, io, NT)
        C.P.emit()
    return nc


def emit_L1(C, io, NT):
    ntile = NT // TT
    if True:
        P = C.P
        xT, g1, w_in, sgg, sgb, wsT, tril, bsb, gqk = (io[k] for k in ("xT", "g1", "w_in", "sgg", "sgb", "wsT", "tril", "bsb", "gqk"))
        qk_o, v_o, xg_o, yb_o = io["qk"], io["v"], io["xg"], io["ybT"]

        NW = 2560
        wb = C.sb("wb", [128, 8, NW], BF16)
        stage = C.sb("stage", [128, 3, 1536], F32)
        xt = C.sb("xt", [128, 2, 8, TT], F32)
        sq = C.sb("sq", [128, 8, TT], BF16)
        hT = C.sb("hT", [128, ntile, 8, TT], BF16)
        rstd = C.sb("rstd", [128, TT], F32)
        ones = C.sb("ones", [128, 128], BF16)
        bones = C.sb("bones", [128, 128], BF16)
        g1s = C.sb("g1s", [128, 8], F32)
        sgg_s = C.sb("sgg_s", [128, 256], F32)
        sgb_s = C.sb("sgb_s", [128, 256], F32)
        wsT_f = C.sb("wsT_f", [128, 4, 128], F32)
        tril_s = C.sb("tril_s", [128, 128], F32)
        wsT_b = C.sb("wsT_b", [128, 4, 128], BF16)
        bsb_s = C.sb("bsb_s", [128, 2, 128], F32)
        gqk_s = C.sb("gqk_s", [128, 2], F32)
        xg_s = C.sb("xg_s", [128, 2, TT], F32)
        u_s = C.sb("u_s", [128, 2, TT], F32)
        qsq = C.sb("qsq", [128, 2, TT], BF16)
        qr = C.sb("qr", [128, 2, TT], F32)
        qn = C.sb("qn", [128, 3, TT], BF16)
        stats = C.sb("stats", [128, 2, 6], F32)
        mv = C.sb("mv", [128, 2, 2], F32)
        vr = C.sb("vr", [128, 2, 1], F32)
        vtmp = C.sb("vtmp", [128, 2, 256], F32)
        vn = C.sb("vn", [128, 4, 256], BF16)
        vv_s = C.sb("vv_s", [128, 2, 4, 512], BF16)
        mtmp = C.sb("mtmp", [128, 2, TT], F32)
        yb_s = C.sb("yb_s", [128, 2, TT], BF16)
        psum = C.ps("psum", [128, 8, TT], F32)
        psr = Rot("ps", 8)

        P.op("dve", lambda e: e.memset(ones[:], 1.0), writes=["ones"])
        P.op("pool", lambda e: e.memset(bones[:], 0.0), writes=["bones"])
        P.op("pool", lambda e: e.memset(bones[0:64, 0:64], 1.0), writes=["bones"])
        P.op("pool", lambda e: e.memset(bones[64:128, 64:128], 1.0), writes=["bones"])
        for dst, src, key in ((g1s, g1, "g1s"), (sgg_s, sgg, "sgg"), (sgb_s, sgb, "sgb"), (wsT_f, wsT, "wsT_f"),
                              (tril_s, tril, "tril"), (bsb_s, bsb, "bsb"), (gqk_s, gqk, "gqk")):
            P.op("sp", lambda e, dst=dst, src=src: e.dma_start(out=dst[:], in_=src), writes=[key], dma="c_" + key)
        P.op("dve", lambda e: e.tensor_scalar(out=g1s[:], in0=g1s[:], scalar1=32.0, scalar2=None, op0=ALU.mult),
             reads=["g1s"], writes=["g1s"])
        P.op("dve", lambda e: e.tensor_scalar(out=gqk_s[:, 1:2], in0=gqk_s[:, 1:2], scalar1=8.0, scalar2=None, op0=ALU.mult),
             reads=["gqk"], writes=["gqk"])
        trb = tril_s[:].unsqueeze(1).broadcast_to([128, 4, 128])
        P.op("dve", lambda e: e.tensor_tensor(out=wsT_b[:], in0=wsT_f[:], in1=trb, op=ALU.mult),
             reads=["wsT_f", "tril"], writes=["wsT_b"])

        def load_x(t):
            b = t % 2
            src = xT[:, t * TT:(t + 1) * TT].rearrange("(c p) n -> p c n", p=128)
            for hh in range(2):
                P.op("sp", lambda e, b=b, src=src, hh=hh: e.dma_start(out=xt[:, b, 4 * hh:4 * hh + 4, :],
                                                                    in_=src[:, 4 * hh:4 * hh + 4, :]),
                     writes=[("xt", b)], dma="xld%d" % b)

        load_x(0)
        wbk = load_weight_bf16(C, wb, w_in[:, 0:NW], 1024, NW, "wb", stage, Rot("stage", 3), scale_ap=g1s,
                               engines=("act", "dve"), colblk=1280, scale_key="g1s")

        def mm_fm(bank, col0, b):
            def fn(e):
                for k in range(8):
                    i = e.matmul(psum[:, bank, :], lhsT=wb[:, k, col0:col0 + 128], rhs=hT[:, b, k, :],
                                 start=(k == 0), stop=(k == 7))
                return i
            return fn

        def mm_tm(bank, col0, ncol, b, blk):
            def fn(e):
                for k in range(8):
                    i = e.matmul(psum[:, bank, 0:ncol], lhsT=hT[:, b, k, blk * 128:(blk + 1) * 128],
                                 rhs=wb[:, k, col0:col0 + ncol], start=(k == 0), stop=(k == 7))
                return i
            return fn

        def norm(t):
            b = t % 2
            bank, pk = psr.next()
            rms_tile(C, xt[:, b], ("xt", b), hT[:, t], ("hT", t), sq, "sq", ones, psum[:, bank, :], pk,
                     rstd, "rstd", TT, 1024 * EPS)

        xg_keys = []
        for t in range(ntile):
            b = t
            t0 = t * TT
            hk = ("hT", t)
            if t + 1 < ntile:
                load_x(t + 1)
            norm(t)
            for ch in range(4):
                bank, pk = psr.next()
                P.op("pe", mm_fm(bank, ch * 128, b), reads=[hk] + wbk, writes=[pk])
                s = ch % 2
                P.op("act", lambda e, bank=bank, s=s: e.copy(out=xg_s[:, s, :], in_=psum[:, bank, :]),
                     reads=[pk], writes=[("xg_s", s)])
                P.op("sp", lambda e, ch=ch, s=s, t0=t0: e.dma_start(out=xg_o(ch)[:, t0:t0 + TT], in_=xg_s[:, s, :]),
                     reads=[("xg_s", s)], writes=[("xg_dram", t, ch)], dma="st_xg%d" % s)
                xg_keys.append(("xg_dram", t, ch))
        if "after_xg" in io:
            io["after_xg"](P, xg_keys)
        for t in range(ntile):
            b = t
            t0 = t * TT
            hk = ("hT", t)
            for n in range(2):
                bank, pk = psr.next()
                P.op("pe", mm_fm(bank, 512 + n * 128, b), reads=[hk] + wbk, writes=[pk])
                P.op("act", lambda e, bank=bank, n=n: e.copy(out=u_s[:, n, :], in_=psum[:, bank, :]),
                     reads=[pk], writes=[("u_s", n)])
            for blk in range(4):
                bank, pk = psr.next()
                s = blk % 2
                P.op("pe", mm_tm(bank, 768, 256, b, blk), reads=[hk] + wbk, writes=[pk])
                P.op("dve", lambda e, bank=bank, s=s: e.bn_stats(out=stats[:, s, :], in_=psum[:, bank, 0:256]),
                     reads=[pk], writes=[("stats", s)])
                P.op("dve", lambda e, s=s: e.bn_aggr(out=mv[:, s, :], in_=stats[:, s, :]),
                     reads=[("stats", s)], writes=[("mv", s)])
                P.op("act", lambda e, s=s: e.activation(out=vr[:, s, :], in_=mv[:, s, 1:2], func=AF.Ln, bias=EPS, scale=1.0),
                     reads=[("mv", s)], writes=[("vr", s)])
                P.op("act", lambda e, s=s: e.activation(out=vr[:, s, :], in_=vr[:, s, :], func=AF.Exp, scale=-0.5),
                     reads=[("vr", s)], writes=[("vr", s)])
                P.op("dve", lambda e, bank=bank, s=s: e.tensor_scalar(
                    out=vtmp[:, s, :], in0=psum[:, bank, 0:256], scalar1=mv[:, s, 0:1], scalar2=vr[:, s, :],
                    op0=ALU.subtract, op1=ALU.mult), reads=[pk, ("mv", s), ("vr", s)], writes=[("vtmp", s)])
                P.op("pool", lambda e, s=s: e.tensor_tensor(out=vtmp[:, s, :], in0=vtmp[:, s, :], in1=sgg_s[:], op=ALU.mult),
                     reads=[("vtmp", s), "sgg"], writes=[("vtmp", s)])
                P.op("pool", lambda e, s=s, blk=blk: e.tensor_tensor(out=vn[:, blk, :], in0=vtmp[:, s, :], in1=sgb_s[:], op=ALU.add),
                     reads=[("vtmp", s), "sgb"], writes=[("vn", blk)])
            for n in range(2):
                bank, pk = psr.next()

                def mix(e, bank=bank, n=n):
                    for blk in range(4):
                        for gg in range(2):
                            g = 2 * n + gg
                            i = e.matmul(psum[gg * 64:(gg + 1) * 64, bank, blk * 128:(blk + 1) * 128],
                                         lhsT=vn[:, blk, g * 64:(g + 1) * 64], rhs=wsT_b[:, g, :],
                                         start=True, stop=True)
                    return i
                P.op("pe", mix, reads=[("vn", 0), ("vn", 1), ("vn", 2), ("vn", 3), "wsT_b"], writes=[pk])
                bsv = bsb_s[:, n, :].unsqueeze(1).broadcast_to([128, 4, 128])
                P.op("dve", lambda e, bank=bank, n=n, bsv=bsv: e.tensor_tensor(
                    out=mtmp[:, n, :].rearrange("p (b t) -> p b t", b=4),
                    in0=psum[:, bank, :].rearrange("p (b t) -> p b t", b=4), in1=bsv, op=ALU.add),
                    reads=[pk, "bsb"], writes=[("mtmp", n)])
                P.op("pool", lambda e, n=n: e.tensor_tensor(out=yb_s[:, n, :], in0=mtmp[:, n, :], in1=u_s[:, n, :], op=ALU.mult),
                     reads=[("mtmp", n), ("u_s", n)], writes=[("yb_s", n)])
                P.op("sp", lambda e, n=n, t0=t0: e.dma_start(out=yb_o[n, :, t0:t0 + TT], in_=yb_s[:, n, :]),
                     reads=[("yb_s", n)], dma="st_yb%d" % n)
            qrot = 0
            for which in range(2):
                for h in range(4):
                    col0 = 1024 + which * 512 + h * 128
                    bank, pk = psr.next()
                    bank2, pk2 = psr.next()
                    s = qrot % 2
                    s3 = qrot % 3
                    qrot += 1
                    P.op("pe", mm_fm(bank, col0, b), reads=[hk] + wbk, writes=[pk])
                    P.op("act", lambda e, bank=bank, s=s: e.activation(out=qsq[:, s, :], in_=psum[:, bank, :], func=AF.Square),
                         reads=[pk], writes=[("qsq", s)])
                    P.op("pe", lambda e, bank2=bank2, s=s: e.matmul(psum[:, bank2, :], lhsT=bones[:], rhs=qsq[:, s, :],
                                                                    start=True, stop=True),
                         reads=[("qsq", s), "bones"], writes=[pk2])
                    rstd_from_ps(P, psum[:, bank2, :], pk2, qr[:, s, :], ("qr", s), TT, 64 * EPS)
                    P.op("dve", lambda e, bank=bank, s=s, s3=s3, which=which: e.scalar_tensor_tensor(
                        out=qn[:, s3, :], in0=psum[:, bank, :], scalar=gqk_s[:, which:which + 1], in1=qr[:, s, :],
                        op0=ALU.mult, op1=ALU.mult), reads=[pk, ("qr", s), "gqk"], writes=[("qn", s3)])
                    P.op("sp", lambda e, h=h, which=which, s3=s3, t0=t0: e.dma_start(
                        out=qk_o(h, which)[:, t0:t0 + TT], in_=qn[:, s3, :]),
                        reads=[("qn", s3)], dma="st_qn%d" % s3)
            vb = t % 2
            for blk in range(4):
                bank, pk = psr.next()
                P.op("pe", mm_tm(bank, 2048, 512, b, blk), reads=[hk] + wbk, writes=[pk])
                eng = "act" if blk % 2 == 0 else "dve"
                if eng == "act":
                    fn = lambda e, bank=bank, blk=blk, vb=vb: e.copy(out=vv_s[:, vb, blk, :], in_=psum[:, bank, :])
                else:
                    fn = lambda e, bank=bank, blk=blk, vb=vb: e.tensor_copy(out=vv_s[:, vb, blk, :], in_=psum[:, bank, :])
                P.op(eng, fn, reads=[pk], writes=[("vv_s", vb, blk)])
            for h in range(4):
                P.op("sp", lambda e, vb=vb, t0=t0, h=h: e.dma_start(
                    out=v_o[h, t0:t0 + TT, :].rearrange("(b p) n -> p b n", p=128), in_=vv_s[:, vb, :, h * 128:(h + 1) * 128]),
                    reads=[("vv_s", vb, k) for k in range(4)], dma="st_vv%d" % vb)


def build_L2(S):
    nc = bass.Bass("TRN2", target_bir_lowering=False)
    with contextlib.ExitStack() as es:
        C = Ctx(nc, es)
        qT = C.din("qT", [128, S], BF16)
        kT = C.din("kT", [128, S], BF16)
        v = C.din("v", [S, 128], BF16)
        xaT = C.din("xaT", [64, S])
        gaT = C.din("gaT", [64, S])
        io = {
            "nsrc": 1,
            "q_src": lambda e, j: qT, "k_src": lambda e, j: kT, "v_src": lambda e, j: v,
            "xa_src": lambda e, j: xaT, "ga_src": lambda e, j: gaT,
            "cw": C.din("cw", [64, 4]), "lvec": C.din("lvec", [64, 4]), "wa": C.din("wa", [64, 64]), "wi": C.din("wi", [64, 64]),
            "lq": C.din("lq", [128, 4, 64]), "subg": C.din("subg", [128, 128]), "rb": C.din("rb", [128, 32]),
            "idx": C.din("idx", [128, 12, 128]), "maskT": C.din("maskT", [128, 128]), "ident": C.din("ident", [128, 128]),
            "lcon": C.din("lcon", [128, 2]),
        }
        yc_t = C.dout("ycT", [128, S], BF16)
        ya_t = C.dout("yaT", [64, S], BF16)
        io["ycT"] = lambda c0, n: yc_t[:, c0:c0 + n]
        io["yaT"] = lambda c0, n: ya_t[:, c0:c0 + n]
        emit_L2(C, io, S)
        C.P.emit()
    return nc


def emit_L2(C, io, S):
    NG = S // TT
    NKT = S // 128
    TL = 512
    NCH = S // TL
    nsrc = io["nsrc"]
    NS = S // nsrc
    if True:
        P = C.P
        cw, lvec, wa, wi, lq, subg, rb, idx, maskT, ident, lcon = (io[k] for k in (
            "cw", "lvec", "wa", "wi", "lq", "subg", "rb", "idx", "maskT", "ident", "lcon"))
        yc_o, ya_o = io["ycT"], io["yaT"]

        q0T = C.sb("q0T", [128, S], BF16)
        q1T = C.sb("q1T", [128, S], BF16)
        kTs = C.sb("kTs", [128, S], BF16)
        v1 = C.sb("v1", [128, NKT, 129], BF16)
        biasT = C.sb("biasT", [128, 12, 128], BF16)
        bias_f = C.sb("bias_f", [128, 12, 128], F32)
        idx_s = C.sb("idx_s", [128, 12, 128], F32)
        btmp = C.sb("btmp", [128, 2, 12, 128], F32)
        mask_s = C.sb("mask_s", [128, 128], F32)
        id_f = C.sb("id_f", [128, 128], F32)
        id_b = C.sb("id_b", [128, 128], BF16)
        rb_s = C.sb("rb_s", [128, 32], F32)
        lq_s = C.sb("lq_s", [128, 4, 64], F32)
        lq_t = C.sb("lq_t", [128, 2, 64], F32)
        lsc = C.sb("lsc", [128, 8], F32)
        lcon_s = C.sb("lcon_s", [128, 2], F32)
        subg_s = C.sb("subg_s", [128, 128], F32)
        pt = C.sb("pt", [128, 3, 2, TT], BF16)
        accs = C.sb("accs", [128, 3, TT], F32)
        rl = C.sb("rl", [128, 8], F32)
        o_s = C.sb("o_s", [128, 4, 128], F32)
        osq = C.sb("osq", [128, 128], F32)
        ss = C.sb("ss", [128, 4], F32)
        y_s = C.sb("y_s", [128, 4, 128], BF16)
        yc_s = C.sb("yc_s", [128, 2, TT], BF16)
        cw_s = C.sb("cw_s", [64, 4], F32)
        lv_s = C.sb("lv_s", [64, 4], F32)
        csc = C.sb("csc", [64, 4], F32)
        w_f = C.sb("w_f", [64, 2, 64], F32)
        w_b = C.sb("w_b", [64, 2, 64], BF16)
        xa_s = C.sb("xa_s", [64, 2, TL + 3], F32)
        ga_s = C.sb("ga_s", [64, 2, TL], F32)
        xc = C.sb("xc", [64, TL], F32)
        xcb = C.sb("xcb", [64, TL], BF16)
        r_s = C.sb("r_s", [64, TL], F32)
        i_s = C.sb("i_s", [64, TL], F32)
        a_s = C.sb("a_s", [64, TL], F32)
        m_s = C.sb("m_s", [64, TL], F32)
        h_s = C.sb("h_s", [64, 2, TL], F32)
        g_s = C.sb("g_s", [64, TL], F32)
        ya_s = C.sb("ya_s", [64, 2, TL], BF16)
        psum = C.ps("psum", [128, 8, TT], F32)
        trp = psum[:, 7, :].bitcast(BF16)

        for dst, src, key in ((cw_s[:], cw, "cw"), (lv_s[:], lvec, "lv"), (w_f[:, 0, :], wa, "wa"), (w_f[:, 1, :], wi, "wi"),
                              (lq_s[:], lq, "lq"), (subg_s[:], subg, "subg"), (rb_s[:], rb, "rb"), (idx_s[:], idx, "idx"),
                              (mask_s[:], maskT, "mask"), (id_f[:], ident, "id_f"), (lcon_s[:], lcon, "lcon")):
            P.op("sp", lambda e, dst=dst, src=src: e.dma_start(out=dst, in_=src), writes=[key], dma="c_" + key)
        P.op("dve", lambda e: e.tensor_copy(out=id_b[:], in_=id_f[:]), reads=["id_f"], writes=["id_b"])
        P.op("dve", lambda e: e.tensor_copy(out=w_b[:], in_=w_f[:]), reads=["wa", "wi"], writes=["w_b"])
        for j in range(2):
            P.op("dve", lambda e, j=j: e.scalar_tensor_tensor(out=lq_t[:, j, :], in0=lq_s[:, 2 * j, :], scalar=1.0,
                                                              in1=lq_s[:, 2 * j + 1, :], op0=ALU.mult, op1=ALU.mult,
                                                              accum_out=lsc[:, j:j + 1]),
                 reads=["lq"], writes=[("lsc", j)])
        P.op("act", lambda e: e.activation(out=lsc[:, 2:4], in_=lsc[:, 0:2], func=AF.Exp),
             reads=[("lsc", 0), ("lsc", 1)], writes=["lsce"])
        P.op("dve", lambda e: e.tensor_tensor(out=lsc[:, 4:5], in0=lsc[:, 2:3], in1=lsc[:, 3:4], op=ALU.subtract),
             reads=["lsce"], writes=["lam"])
        P.op("dve", lambda e: e.tensor_tensor(out=lsc[:, 4:5], in0=lsc[:, 4:5], in1=lcon_s[:, 0:1], op=ALU.add),
             reads=["lam", "lcon"], writes=["lam"])
        P.op("dve", lambda e: e.tensor_scalar(out=lsc[:, 5:6], in0=lsc[:, 4:5], scalar1=-1.0, scalar2=None, op0=ALU.mult),
             reads=["lam"], writes=["nlam"])
        P.op("dve", lambda e: e.tensor_scalar(out=subg_s[:], in0=subg_s[:], scalar1=lcon_s[:, 1:2], scalar2=None, op0=ALU.mult),
             reads=["subg", "lcon"], writes=["subg"])
        P.op("act", lambda e: e.activation(out=csc[:, 0:1], in_=lv_s[:, 3:4], func=AF.Exp, scale=-1.0),
             reads=["lv"], writes=["csc0"])
        P.op("act", lambda e: e.activation(out=csc[:, 0:1], in_=csc[:, 0:1], func=AF.Ln, bias=1.0, scale=1.0),
             reads=["csc0"], writes=["csc0"])
        P.op("dve", lambda e: e.tensor_scalar(out=csc[:, 1:2], in0=csc[:, 0:1], scalar1=-LRU_C, scalar2=None, op0=ALU.mult),
             reads=["csc0"], writes=["csc"])
        P.op("dve", lambda e: e.tensor_scalar(out=csc[:, 2:3], in0=csc[:, 0:1], scalar1=-2.0 * LRU_C, scalar2=None, op0=ALU.mult),
             reads=["csc0"], writes=["csc"])

        P.op("dve", lambda e: e.memset(bias_f[:], 0.0), writes=["bias_f"])
        idx_np = l2_consts()["idx"]
        for d_ in range(12):
            for bkt in sorted(set(int(v_) for v_ in np.unique(idx_np[:, d_, :]))):
                P.op("dve", lambda e, bkt=bkt, d_=d_: e.tensor_single_scalar(
                    out=btmp[:, 0, d_, :], in_=idx_s[:, d_, :], scalar=float(bkt), op=ALU.is_equal),
                    reads=["idx"], writes=[("btmp", 0)])
                P.op("dve", lambda e, bkt=bkt, d_=d_: e.scalar_tensor_tensor(
                    out=bias_f[:, d_, :], in0=btmp[:, 0, d_, :], scalar=rb_s[:, bkt:bkt + 1], in1=bias_f[:, d_, :],
                    op0=ALU.mult, op1=ALU.add), reads=[("btmp", 0), "rb", "bias_f"], writes=["bias_f"])
        P.op("dve", lambda e: e.tensor_tensor(out=bias_f[:, 0, :], in0=bias_f[:, 0, :], in1=mask_s[:], op=ALU.add),
             reads=["bias_f", "mask"], writes=["bias_f"])
        P.op("dve", lambda e: e.tensor_scalar(out=bias_f[:], in0=bias_f[:], scalar1=rb_s[:, 15:16], scalar2=None, op0=ALU.subtract),
             reads=["bias_f", "rb"], writes=["bias_f"])
        P.op("dve", lambda e: e.tensor_copy(out=biasT[:], in_=bias_f[:]), reads=["bias_f"], writes=["biasT"])

        for ch in range(NCH):
            b = ch % 2
            t0 = ch * TL
            js = t0 // NS
            l0 = t0 - js * NS
            if ch == 0:
                P.op("dve", lambda e: e.memset(xa_s[:, 0, 0:3], 0.0), writes=[("xa", 0)])
                P.op("sp", lambda e: e.dma_start(out=xa_s[:, 0, 3:3 + TL], in_=io["xa_src"](e, 0)[:, 0:TL]),
                     reads=io.get("xa_dep", []), writes=[("xa", 0)], dma="ld_xa0")
            else:
                P.op("dve", lambda e, b=b: e.tensor_copy(out=xa_s[:, b, 0:3], in_=xa_s[:, 1 - b, TL:TL + 3]),
                     reads=[("xa", 1 - b)], writes=[("xa", b)])
                P.op("sp", lambda e, b=b, js=js, l0=l0: e.dma_start(out=xa_s[:, b, 3:3 + TL], in_=io["xa_src"](e, js)[:, l0:l0 + TL]),
                     reads=io.get("xa_dep", []), writes=[("xa", b)], dma="ld_xa%d" % b)
            P.op("sp", lambda e, b=b, js=js, l0=l0: e.dma_start(out=ga_s[:, b, :], in_=io["ga_src"](e, js)[:, l0:l0 + TL]),
                 reads=io.get("ga_dep", []), writes=[("ga", b)], dma="ld_ga%d" % b)
            P.op("dve", lambda e, b=b: e.tensor_scalar(out=xc[:], in0=xa_s[:, b, 3:3 + TL], scalar1=cw_s[:, 3:4],
                                                       scalar2=lv_s[:, 0:1], op0=ALU.mult, op1=ALU.add),
                 reads=[("xa", b), "cw", "lv"], writes=["xc"])
            for tap in range(3):
                P.op("dve", lambda e, b=b, tap=tap: e.scalar_tensor_tensor(
                    out=xc[:], in0=xa_s[:, b, tap:tap + TL], scalar=cw_s[:, tap:tap + 1], in1=xc[:],
                    op0=ALU.mult, op1=ALU.add), reads=[("xa", b), "cw", "xc"], writes=["xc"])
            P.op("act", lambda e: e.copy(out=xcb[:], in_=xc[:]), reads=["xc"], writes=["xcb"])
            nh = TL // TT
            for gate in range(2):
                for hh in range(nh):
                    bank = gate * nh + hh
                    P.op("pe", lambda e, gate=gate, hh=hh, bank=bank: e.matmul(
                        psum[0:64, bank, :], lhsT=w_b[:, gate, :], rhs=xcb[:, hh * TT:(hh + 1) * TT], start=True, stop=True),
                        reads=["xcb", "w_b"], writes=[("st", bank // 2)])
            for hh in range(nh):
                P.op("act", lambda e, hh=hh: e.activation(out=r_s[:, hh * TT:(hh + 1) * TT], in_=psum[0:64, hh, :],
                                                          func=AF.Sigmoid, bias=lv_s[:, 1:2], scale=1.0),
                     reads=[("st", hh // 2), "lv"], writes=["r_s"])
            for hh in range(nh):
                P.op("act", lambda e, hh=hh: e.activation(out=i_s[:, hh * TT:(hh + 1) * TT], in_=psum[0:64, nh + hh, :],
                                                          func=AF.Sigmoid, bias=lv_s[:, 2:3], scale=1.0),
                     reads=[("st", (nh + hh) // 2), "lv"], writes=["i_s"])
            P.op("act", lambda e, b=b: e.activation(out=g_s[:], in_=ga_s[:, b, :], func=AF.Square),
                 reads=[("ga", b)], writes=["g_s"])
            P.op("act", lambda e: e.activation(out=g_s[:], in_=g_s[:], func=AF.Identity, bias=1.0, scale=0.044715),
                 reads=["g_s"], writes=["g_s"])
            P.op("dve", lambda e, b=b: e.tensor_tensor(out=g_s[:], in0=g_s[:], in1=ga_s[:, b, :], op=ALU.mult),
                 reads=["g_s", ("ga", b)], writes=["g_s"])
            P.op("act", lambda e: e.activation(out=g_s[:], in_=g_s[:], func=AF.Sigmoid, scale=1.5957691216057308),
                 reads=["g_s"], writes=["g_s"])
            P.op("dve", lambda e, b=b: e.tensor_tensor(out=g_s[:], in0=g_s[:], in1=ga_s[:, b, :], op=ALU.mult),
                 reads=["g_s", ("ga", b)], writes=["g_s"])
            P.op("act", lambda e: e.activation(out=a_s[:], in_=r_s[:], func=AF.Exp, scale=csc[:, 1:2]),
                 reads=["r_s", "csc"], writes=["a_s"])
            P.op("act", lambda e: e.activation(out=m_s[:], in_=r_s[:], func=AF.Exp, scale=csc[:, 2:3]),
                 reads=["r_s", "csc"], writes=["m_s"])
            P.op("act", lambda e: e.activation(out=m_s[:], in_=m_s[:], func=AF.Sqrt, bias=1.0, scale=-1.0),
                 reads=["m_s"], writes=["m_s"])
            P.op("dve", lambda e: e.tensor_tensor(out=i_s[:], in0=i_s[:], in1=xc[:], op=ALU.mult),
                 reads=["i_s", "xc"], writes=["i_s"])
            P.op("dve", lambda e: e.tensor_tensor(out=m_s[:], in0=m_s[:], in1=i_s[:], op=ALU.mult),
                 reads=["m_s", "i_s"], writes=["m_s"])
            init = 0.0 if ch == 0 else h_s[:, 1 - b, TL - 1:TL]
            P.op("dve", lambda e, b=b, init=init: e.tensor_tensor_scan(out=h_s[:, b, :], data0=a_s[:], data1=m_s[:],
                                                                       initial=init, op0=ALU.mult, op1=ALU.add),
                 reads=["a_s", "m_s", ("h_s", 1 - b)], writes=[("h_s", b)])
            P.op("dve", lambda e, b=b: e.tensor_tensor(out=ya_s[:, b, :], in0=h_s[:, b, :], in1=g_s[:], op=ALU.mult),
                 reads=[("h_s", b), "g_s"], writes=[("ya_s", b)])
            P.op("sp", lambda e, b=b, t0=t0: e.dma_start(out=ya_o(t0, TL), in_=ya_s[:, b, :]),
                 reads=[("ya_s", b)], writes=[("ya_dram", ch)], dma="st_ya%d" % b)
        if "after_ya" in io:
            io["after_ya"](P, NCH)

        if "pre_attn" in io:
            io["pre_attn"](P)
        P.op("pool", lambda e: e.memset(q0T[64:128, :], 0.0), writes=["q0T"])
        P.op("pool", lambda e: e.memset(q1T[0:64, :], 0.0), writes=["q1T"])
        P.op("pool", lambda e: e.memset(v1[:, :, 128:129], 1.0), writes=["v1ones"])
        if "load_qk" in io:
            io["load_qk"](P, q0T, q1T, kTs)
        else:
            P.op("sp", lambda e: e.dma_start(out=q0T[0:64, :], in_=io["q_src"](e, 0)[0:64, :]), writes=["q0T"], dma="ld_q0")
            P.op("sp", lambda e: e.dma_start(out=q1T[64:128, :], in_=io["q_src"](e, 0)[64:128, :]), writes=["q1T"], dma="ld_q1")
            P.op("sp", lambda e: e.dma_start(out=kTs[:], in_=io["k_src"](e, 0)), writes=["kTs"], dma="ld_k")
        nvd = max(nsrc, NKT // 16)
        for i in range(nvd):
            a, b_ = i * NKT // nvd, (i + 1) * NKT // nvd
            j = (a * 128) // NS
            ra = a * 128 - j * NS
            rb_ = b_ * 128 - j * NS
            P.op("sp", lambda e, a=a, b_=b_, j=j, ra=ra, rb_=rb_: e.dma_start(
                out=v1[:, a:b_, 0:128], in_=io["v_src"](e, j)[ra:rb_, :].rearrange("(kt p) e -> p kt e", p=128)),
                reads=io.get("v_dep", []), writes=[("v1", i)], dma="ld_v%d" % i)

        def acc_ap(a, lo=0, hi=129):
            return psum[:, 4 + a // 3, (a % 3) * 160 + lo:(a % 3) * 160 + hi]

        def accs_ap(a, lo=0, hi=129):
            return accs[:, a // 3, (a % 3) * 160 + lo:(a % 3) * 160 + hi]

        pairs = [(G, kt) for G in range(NG) for kt in range(4 * G + 4)]

        def geom(G, kt):
            jj = max(kt - 4 * G, 0)
            nblk = 4 - jj
            d0 = 4 * G + jj - kt
            return jj, nblk, d0, (d0 <= 8)

        def emit_qk(n):
            G, kt = pairs[n]
            jj, nblk, d0, near = geom(G, kt)
            sb_ = n % 2
            c0, c1 = jj * 128, TT
            q0 = G * TT + c0

            def fn(e):
                for c, qsrc in ((0, q0T), (1, q1T)):
                    i = e.matmul(psum[:, 2 * sb_ + c, c0:c1], lhsT=kTs[:, kt * 128:(kt + 1) * 128],
                                 rhs=qsrc[:, q0:q0 + nblk * 128], start=True, stop=not near)
                    if near:
                        i = e.matmul(psum[:, 2 * sb_ + c, c0:c1], lhsT=id_b[:],
                                     rhs=biasT[:, d0:d0 + nblk, :], start=False, stop=True)
                return i
            P.op("pe", fn, reads=["q0T", "q1T", "kTs", "id_b", "biasT"], writes=[("st", sb_)])

        def emit_exp(n):
            G, kt = pairs[n]
            jj, nblk, d0, near = geom(G, kt)
            sb_, pb = n % 2, n % 3
            c0 = jj * 128
            src = psum[:, 2 * sb_:2 * sb_ + 2, c0:TT]
            dst = pt[:, pb, :, c0:TT]
            fn = lambda e: e.activation(out=dst, in_=src, func=AF.Exp)
            P.op("act", fn, reads=[("st", sb_)], writes=[("pt", pb)])

        def emit_pv(n):
            G, kt = pairs[n]
            jj, nblk, d0, near = geom(G, kt)
            pb = n % 3

            def fn(e):
                for i_ in range(jj, 4):
                    for c in range(2):
                        i = e.matmul(acc_ap(c * 4 + i_), lhsT=pt[:, pb, c, i_ * 128:(i_ + 1) * 128], rhs=v1[:, kt, :],
                                     start=False, stop=False, skip_group_check=True)
                return i
            P.op("pe", fn, reads=[("pt", pb), ("v1", kt * nvd // NKT), "v1ones"], writes=["acc"])

        def emit_evac(G):
            yb_ = G % 2
            P.op("dve", lambda e: e.tensor_copy(out=accs[:], in_=psum[:, 4:7, :]), reads=["acc"], writes=["accs"])
            P.op("dve", lambda e: e.reciprocal(out=rl[:, 0:6].rearrange("p (a b) -> p a b", a=2),
                                               in_=accs[:, 0:2, 128:449:160]), reads=["accs"], writes=["rl"])
            P.op("dve", lambda e: e.reciprocal(out=rl[:, 6:8], in_=accs[:, 2, 128:289:160]), reads=["accs"], writes=["rl"])
            P.op("dve", lambda e: e.tensor_scalar(out=rl[:, 4:8], in0=rl[:, 4:8], scalar1=lsc[:, 5:6], scalar2=None, op0=ALU.mult),
                 reads=["rl", "nlam"], writes=["rl"])
            for i_ in range(4):
                P.op("dve", lambda e, i_=i_: e.tensor_scalar(out=o_s[:, i_, :], in0=accs_ap(i_, 0, 128), scalar1=rl[:, i_:i_ + 1],
                                                             scalar2=None, op0=ALU.mult),
                     reads=["accs", "rl"], writes=[("o_s", i_)])
                P.op("dve", lambda e, i_=i_: e.scalar_tensor_tensor(out=o_s[:, i_, :], in0=accs_ap(4 + i_, 0, 128),
                                                                    scalar=rl[:, 4 + i_:5 + i_], in1=o_s[:, i_, :],
                                                                    op0=ALU.mult, op1=ALU.add),
                     reads=["accs", "rl", ("o_s", i_)], writes=[("o_s", i_)])
                P.op("dve", lambda e, i_=i_: e.scalar_tensor_tensor(out=osq[:], in0=o_s[:, i_, :], scalar=1.0, in1=o_s[:, i_, :],
                                                                    op0=ALU.mult, op1=ALU.mult, accum_out=ss[:, i_:i_ + 1]),
                     reads=[("o_s", i_)], writes=["osq", ("ss", i_)])
            P.op("act", lambda e: e.activation(out=ss[:], in_=ss[:], func=AF.Ln, bias=128 * EPS, scale=1.0),
                 reads=[("ss", k) for k in range(4)], writes=["ssr"])
            P.op("act", lambda e: e.activation(out=ss[:], in_=ss[:], func=AF.Exp, scale=-0.5), reads=["ssr"], writes=["ssr"])
            for i_ in range(4):
                P.op("dve", lambda e, i_=i_: e.scalar_tensor_tensor(out=y_s[:, i_, :], in0=o_s[:, i_, :], scalar=ss[:, i_:i_ + 1],
                                                                    in1=subg_s[:], op0=ALU.mult, op1=ALU.mult),
                     reads=[("o_s", i_), "ssr", "subg"], writes=[("y_s", i_)])

            def tr(e):
                for i_ in range(4):
                    i = e.transpose(trp[:, i_ * 128:(i_ + 1) * 128], y_s[:, i_, :], id_b[:])
                return i
            P.op("pe", tr, reads=[("y_s", k) for k in range(4)] + ["id_b"], writes=["trp"])
            P.op("dve", lambda e, yb_=yb_: e.tensor_copy(out=yc_s[:, yb_, :], in_=trp[:, 0:TT]), reads=["trp"], writes=[("yc_s", yb_)])
            P.op("sp", lambda e, yb_=yb_, G=G: e.dma_start(out=yc_o(G * TT, TT), in_=yc_s[:, yb_, :]),
                 reads=[("yc_s", yb_)], writes=[("yc_dram", G)], dma="st_yc%d" % yb_)
            if "after_yc" in io:
                io["after_yc"](P, G, NG)

        emit_qk(0)
        for n, (G, kt) in enumerate(pairs):
            if n + 1 < len(pairs):
                emit_qk(n + 1)
            if kt == 0:
                P.op("dve", lambda e: e.memset(psum[:, 4:7, :], 0.0), writes=["acc"])
            emit_exp(n)
            emit_pv(n)
            if kt == 4 * G + 3:
                emit_evac(G)


T3 = 256


def build_L3(NT):
    nc = bass.Bass("TRN2", target_bir_lowering=False)
    H = FFN_HIDDEN
    with contextlib.ExitStack() as es:
        C = Ctx(nc, es)
        yaT = C.din("yaT", [256, NT], BF16)
        ybT = C.din("ybT", [256, NT], BF16)
        ycT = C.din("ycT", [512, NT], BF16)

        def y_src(e, kind, i, c0, n):
            if kind == "ya":
                return yaT[64 * i:64 * i + 64, c0:c0 + n]
            if kind == "yb":
                return ybT[128 * i:128 * i + 128, c0:c0 + n]
            return ycT[128 * i:128 * i + 128, c0:c0 + n]
        xo = C.dout("xo", [1024, NT])
        io = {
            "x_in": C.din("xT", [1024, NT]), "x_mid": xo, "x_out": xo,
            "g1": C.din("g1", [128, 8]), "g2": C.din("g2", [128, 8]), "w_in": C.din("w_in", [1024, IN_COLS]),
            "bg": C.din("bg", [128, 24]), "y_src": y_src,
            "w_pa": C.din("w_pa", [256, 1024]), "w_pb": C.din("w_pb", [256, 1024]), "w_pc": C.din("w_pc", [512, 1024]),
            "w_o": C.din("w_o", [1024, 1024]), "w_g": C.din("w_g", [1024, H]), "w_u": C.din("w_u", [1024, H]),
            "w_d": C.din("w_d", [H, 1024]),
        }
        emit_L3(C, io, NT)
        C.P.emit()
    return nc


def emit_L3(C, io, NT):
    ntile = NT // T3
    H = FFN_HIDDEN
    HC = H // 128
    if True:
        P = C.P
        xT, xmid, xo = io["x_in"], io["x_mid"], io["x_out"]
        g1, g2, w_in, bg, w_pa, w_pb, w_pc, w_o, w_g, w_u, w_d = (io[k] for k in (
            "g1", "g2", "w_in", "bg", "w_pa", "w_pb", "w_pc", "w_o", "w_g", "w_u", "w_d"))

        wbuf = C.sb("wbuf", [128, 3 * 8 * H], BF16)
        stage = C.sb("stage", [128, 2, 1024], F32)
        xt = C.sb("xt", [128, 2, 8, T3], F32)
        sq = C.sb("sq", [128, 8, T3], BF16)
        hT = C.sb("hT", [128, 2, 8, T3], BF16)
        rstd = C.sb("rstd", [128, T3], F32)
        ones = C.sb("ones", [128, 128], BF16)
        g1s = C.sb("g1s", [128, 8], F32)
        g2s = C.sb("g2s", [128, 8], F32)
        bg_s = C.sb("bg_s", [128, 24], F32)
        y_s = C.sb("y_s", [128, 2, 8, T3], BF16)
        gs = C.sb("gs", [128, 2, 3, T3], F32)
        mt = C.sb("mt", [128, 2, 3, T3], F32)
        mT = C.sb("mT", [128, 8, T3], BF16)
        sg = C.sb("sg", [128, 2, T3], F32)
        actT = C.sb("actT", [128, HC, T3], BF16)
        psum = C.ps("psum", [128, 8, 2, T3], F32)
        psr = Rot("ps", 8)

        def wview(off, kc, n):
            return wbuf[:, off:off + kc * n].rearrange("p (c n) -> p c n", c=kc)
        wgt_ = wview(0, 8, 3072)
        wpa_ = wview(24576, 2, 1024)
        wpb_ = wview(24576 + 2048, 2, 1024)
        wpc_ = wview(24576 + 4096, 4, 1024)
        wo_ = wview(24576 + 8192, 8, 1024)
        fg_ = wview(0, 8, H)
        fu_ = wview(8 * H, 8, H)
        fd_ = wview(16 * H, HC, 1024)

        P.op("dve", lambda e: e.memset(ones[:], 1.0), writes=["ones"])
        for dst, src, key in ((g1s, g1, "g1s"), (g2s, g2, "g2s"), (bg_s, bg, "bg")):
            P.op("sp", lambda e, dst=dst, src=src: e.dma_start(out=dst[:], in_=src), writes=[key], dma="c_" + key)
        for t_, k_ in ((g1s, "g1s"), (g2s, "g2s")):
            P.op("dve", lambda e, t_=t_: e.tensor_scalar(out=t_[:], in0=t_[:], scalar1=32.0, scalar2=None, op0=ALU.mult),
                 reads=[k_], writes=[k_])

        def load_x(t, src_ap, srckeys):
            b = t % 2
            src = src_ap[:, t * T3:(t + 1) * T3].rearrange("(c p) n -> p c n", p=128)
            P.op("sp", lambda e, b=b, src=src: e.dma_start(out=xt[:, b, :, :], in_=src),
                 reads=srckeys, writes=[("xt", b)], dma="xld%d" % b)

        def load_y(t):
            b = t % 2
            c0 = t * T3
            for h in range(4):
                P.op("sp", lambda e, b=b, h=h, c0=c0: e.dma_start(
                    out=y_s[(h % 2) * 64:(h % 2) * 64 + 64, b, h // 2, :], in_=io["y_src"](e, "ya", h, c0, T3)),
                    reads=io.get("y_dep", []), writes=[("y_s", b, 0)], dma="yld%d_0" % b)
            for i in range(2):
                P.op("sp", lambda e, b=b, i=i, c0=c0: e.dma_start(out=y_s[:, b, 2 + i, :], in_=io["y_src"](e, "yb", i, c0, T3)),
                     writes=[("y_s", b, 1)], dma="yld%d_1" % b)
            for h in range(4):
                P.op("sp", lambda e, b=b, h=h, c0=c0: e.dma_start(out=y_s[:, b, 4 + h, :], in_=io["y_src"](e, "yc", h, c0, T3)),
                     reads=io.get("y_dep", []), writes=[("y_s", b, 2)], dma="yld%d_2" % b)

        srot = Rot("stage", 2)
        load_x(0, xT, [])
        kg = load_weight_bf16(C, wgt_, w_in[:, 2560:5632], 1024, 3072, "wgt", stage, srot, scale_ap=g1s,
                              colblk=1024, scale_key="g1s")
        kpa = load_weight_bf16(C, wpa_, w_pa, 256, 1024, "wpa", stage, srot, colblk=1024)
        kpb = load_weight_bf16(C, wpb_, w_pb, 256, 1024, "wpb", stage, srot, colblk=1024)
        kpc = load_weight_bf16(C, wpc_, w_pc, 512, 1024, "wpc", stage, srot, colblk=1024)
        ko = load_weight_bf16(C, wo_, w_o, 1024, 1024, "wo", stage, srot, colblk=1024)
        c1keys = kg + kpa + kpb + kpc + ko
        if "pre_y" in io:
            io["pre_y"](P)
        load_y(0)

        def mm_fm(e, bank, half, wv, kc, col0, rhs_fn):
            for k in range(kc):
                i = e.matmul(psum[:, bank, half, :], lhsT=wv[:, k, col0:col0 + 128], rhs=rhs_fn(k),
                             start=(k == 0), stop=(k == kc - 1))
            return i

        def norm(t):
            b = t % 2
            bank, pk = psr.next()
            rms_tile(C, xt[:, b], ("xt", b), hT[:, b], ("hT", b), sq, "sq", ones, psum[:, bank, 0, :], pk, rstd, "rstd", T3, 1024 * EPS)

        def resid(t, wv, kc, rhs_fn, rkeys, outkey, xdst):
            b = t % 2
            for m in range(4):
                bank, pk = psr.next()

                def fn(e, bank=bank, m=m):
                    for hf in range(2):
                        i = mm_fm(e, bank, hf, wv, kc, (2 * m + hf) * 128, rhs_fn)
                    return i
                P.op("pe", fn, reads=rkeys, writes=[pk])
                P.op("dve", lambda e, bank=bank, m=m, b=b: e.tensor_tensor(
                    out=xt[:, b, 2 * m:2 * m + 2, :], in0=xt[:, b, 2 * m:2 * m + 2, :], in1=psum[:, bank, :, :], op=ALU.add),
                    reads=[pk, ("xt", b)], writes=[("xt", b)])
            dst = xdst[:, t * T3:(t + 1) * T3].rearrange("(c p) n -> p c n", p=128)
            P.op("sp", lambda e, b=b, dst=dst: e.dma_start(out=dst, in_=xt[:, b, :, :]),
                 reads=[("xt", b)], writes=[(outkey, t)], dma="st_x%d" % b)

        projs = ((wpa_, 2, 0, kpa), (wpb_, 2, 2, kpb), (wpc_, 4, 4, kpc))
        for t in range(ntile):
            b = t % 2
            if t + 1 < ntile:
                load_x(t + 1, xT, [])
                load_y(t + 1)
            if t == 0:
                norm(0)
            for n in range(8):
                gb = n % 2
                slots = []
                for br in range(3):
                    bank, pk = psr.next()
                    slots.append((bank, pk))
                    wv, kc, off, kk = projs[br]

                    def fn(e, bank=bank, br=br, n=n, wv=wv, kc=kc, off=off, b=b):
                        mm_fm(e, bank, 0, wgt_, 8, br * 1024 + n * 128, lambda k: hT[:, b, k, :])
                        return mm_fm(e, bank, 1, wv, kc, n * 128, lambda k: y_s[:, b, off + k, :])
                    P.op("pe", fn, reads=[("hT", b), ("y_s", b, br)] + kg + kk, writes=[pk])
                for br in range(3):
                    bank, pk = slots[br]
                    ch = br * 8 + n
                    P.op("act", lambda e, bank=bank, br=br, ch=ch, gb=gb: e.activation(
                        out=gs[:, gb, br, :], in_=psum[:, bank, 0, :], func=AF.Sigmoid, bias=bg_s[:, ch:ch + 1], scale=1.0),
                        reads=[pk, "bg"], writes=[("gs", gb, br)])
                    P.op("dve", lambda e, bank=bank, br=br, gb=gb: e.tensor_tensor(
                        out=mt[:, gb, br, :], in0=psum[:, bank, 1, :], in1=gs[:, gb, br, :], op=ALU.mult),
                        reads=[pk, ("gs", gb, br)], writes=[("mt", gb, br)])
                P.op("pool", lambda e, gb=gb: e.tensor_tensor(out=mt[:, gb, 0, :], in0=mt[:, gb, 0, :], in1=mt[:, gb, 1, :], op=ALU.add),
                     reads=[("mt", gb, 0), ("mt", gb, 1)], writes=[("mt", gb, 0)])
                P.op("pool", lambda e, gb=gb, n=n: e.tensor_tensor(out=mT[:, n, :], in0=mt[:, gb, 0, :], in1=mt[:, gb, 2, :], op=ALU.add),
                     reads=[("mt", gb, 0), ("mt", gb, 2)], writes=[("mT", n)])
            if t + 1 < ntile:
                norm(t + 1)
            resid(t, wo_, 8, lambda k: mT[:, k, :], [("mT", k) for k in range(8)] + ko, "xo", xmid)

        kfg = load_weight_bf16(C, fg_, w_g, 1024, H, "fg", stage, srot, scale_ap=g2s, colblk=1024, scale_key="g2s",
                               also_writes=c1keys)
        kfu = load_weight_bf16(C, fu_, w_u, 1024, H, "fu", stage, srot, scale_ap=g2s, colblk=1024, scale_key="g2s",
                               also_writes=c1keys)
        kfd = load_weight_bf16(C, fd_, w_d, H, 1024, "fd", stage, srot, colblk=1024, also_writes=c1keys)
        load_x(0, xmid, [("xo", 0)])
        for t in range(ntile):
            b = t % 2
            if t + 1 < ntile:
                load_x(t + 1, xmid, [("xo", t + 1)])
            if t == 0:
                norm(0)
            for j in range(HC):
                sb_ = j % 2
                bank, pk = psr.next()

                def fn(e, bank=bank, j=j, b=b):
                    mm_fm(e, bank, 0, fg_, 8, j * 128, lambda k: hT[:, b, k, :])
                    return mm_fm(e, bank, 1, fu_, 8, j * 128, lambda k: hT[:, b, k, :])
                P.op("pe", fn, reads=[("hT", b)] + kfg + kfu, writes=[pk])
                P.op("act", lambda e, bank=bank, sb_=sb_: e.activation(out=sg[:, sb_, :], in_=psum[:, bank, 0, :], func=AF.Silu),
                     reads=[pk], writes=[("sg", sb_)])
                P.op("dve", lambda e, bank=bank, sb_=sb_, j=j: e.tensor_tensor(out=actT[:, j, :], in0=psum[:, bank, 1, :], in1=sg[:, sb_, :], op=ALU.mult),
                     reads=[pk, ("sg", sb_)], writes=[("actT", j)])
            if t + 1 < ntile:
                norm(t + 1)
            resid(t, fd_, HC, lambda k: actT[:, k, :], [("actT", k) for k in range(HC)] + kfd, "xo2", xo)


def _c(a):
    return np.ascontiguousarray(a, dtype=np.float32)


def l1_inputs(xT, inp, l):
    sgw = inp["sg_w"][l]
    sgb = inp["sg_b"][l]
    p = np.arange(128)
    bsb = np.stack([sgb[2 * n + p // 64, :] for n in range(2)], axis=1)
    gqk = np.stack([inp["q_norm_g"][l][p % 64], inp["k_norm_g"][l][p % 64]], axis=1)
    return {
        "xT": _c(xT),
        "g1": _c(inp["ln1_g"][l].reshape(8, 128).T),
        "w_in": _c(inp["w_in"][l]),
        "sgg": _c(np.broadcast_to(inp["sg_ln_g"][l], (128, 256))),
        "sgb": _c(np.broadcast_to(inp["sg_ln_b"][l], (128, 256))),
        "wsT": _c(sgw.transpose(2, 0, 1)),
        "tril": _c(np.triu(np.ones((128, 128)))),
        "bsb": _c(bsb),
        "gqk": _c(gqk),
    }


def t5_bucket_np(rel):
    import jax
    import jax.numpy as jnp
    with jax.default_device(jax.devices("cpu")[0]):
        rel = jnp.asarray(rel, jnp.int32)
        half, max_exact = 16, 8
        ret = jnp.where(rel > 0, half, 0)
        n = jnp.abs(rel)
        nf = jnp.maximum(n, 1).astype(jnp.float32)
        large = max_exact + (jnp.log(nf / max_exact) / math.log(2048 / max_exact) * (half - max_exact)).astype(jnp.int32)
        large = jnp.minimum(large, half - 1)
        return np.asarray(ret + jnp.where(n < max_exact, n, large))


_L2_CONST = {}


def l2_consts():
    if not _L2_CONST:
        k = np.arange(128)[:, None, None]
        d = np.arange(12)[None, :, None]
        q = np.arange(128)[None, None, :]
        rel = k - q - 128 * d
        _L2_CONST["idx"] = _c(t5_bucket_np(rel))
        kk = np.arange(128)[:, None]
        qq = np.arange(128)[None, :]
        _L2_CONST["maskT"] = _c(np.where((kk // 64) > (qq // 64), NEG, 0.0))
        _L2_CONST["ident"] = _c(np.eye(128))
    return _L2_CONST


def l2_inputs(qT, kT, v, xaT, gaT, inp, l, h):
    lam_init = 0.8 - 0.6 * math.exp(-0.3 * l)
    cs = l2_consts()
    ch = slice(64 * h, 64 * h + 64)
    lvec = np.stack([inp["conv_b"][l][ch], inp["lru_ba"][l][ch], inp["lru_bi"][l][ch], inp["lru_lambda"][l][ch]], axis=1)
    lq = np.stack([inp["lambda_q1"][l], inp["lambda_k1"][l], inp["lambda_q2"][l], inp["lambda_k2"][l]], axis=0)
    return {
        "qT": qT, "kT": kT, "v": v, "xaT": _c(xaT), "gaT": _c(gaT),
        "cw": _c(inp["conv_w"][l][:, ch].T),
        "lvec": _c(lvec),
        "wa": _c(inp["lru_wa"][l][h]), "wi": _c(inp["lru_wi"][l][h]),
        "lq": _c(np.broadcast_to(lq, (128, 4, 64))),
        "subg": _c(np.broadcast_to(inp["subln_g"][l], (128, 128))),
        "rb": _c(np.broadcast_to(inp["rel_bias"][:, h], (128, 32))),
        "idx": cs["idx"], "maskT": cs["maskT"], "ident": cs["ident"],
        "lcon": _c(np.broadcast_to(np.array([lam_init, (1.0 - lam_init) * math.sqrt(128.0)]), (128, 2))),
    }


def l3_inputs(xT, yaT, ybT, ycT, inp, l):
    return {
        "xT": _c(xT),
        "g1": _c(inp["ln1_g"][l].reshape(8, 128).T),
        "g2": _c(inp["ln2_g"][l].reshape(8, 128).T),
        "w_in": _c(inp["w_in"][l]),
        "bg": _c(inp["b_gate"][l].reshape(24, 128).T),
        "yaT": yaT, "ybT": ybT, "ycT": ycT,
        "w_pa": _c(inp["w_pa"][l]), "w_pb": _c(inp["w_pb"][l]), "w_pc": _c(inp["w_pc"][l]), "w_o": _c(inp["w_o"][l]),
        "w_g": _c(inp["w_ff_gate"][l]), "w_u": _c(inp["w_ff_up"][l]), "w_d": _c(inp["w_ff_down"][l]),
    }


ARENA_BYTES = 212736
GROUPS = [[0, 1, 2, 3], [4, 5, 6, 7]]


def build_fused(S, depth=DEPTH):
    NT = S // 4
    H = FFN_HIDDEN
    L = depth
    nc = bass.Bass("TRN2", target_bir_lowering=False)
    with contextlib.ExitStack() as es:
        C = Ctx(nc, es)
        C.use_arena(ARENA_BYTES)
        P = C.P
        P.use_rank = True
        xT = C.din("xT", [1024, NT])
        xo = C.dout("xo", [1024, NT])
        pin = {}
        for name, shape in (("g1", [L, 128, 8]), ("g2", [L, 128, 8]), ("w_in", [L, 1024, IN_COLS]),
                            ("sgg", [L, 128, 256]), ("sgb", [L, 128, 256]), ("wsT", [L, 128, 4, 128]),
                            ("tril", [128, 128]), ("bsb", [L, 128, 2, 128]), ("gqk", [L, 128, 2]),
                            ("cw", [L, 64, 4]), ("lvec", [L, 64, 4]), ("wa", [L, 64, 64]), ("wi", [L, 64, 64]),
                            ("lq", [L, 128, 4, 64]), ("subg", [L, 128, 128]), ("rb", [128, 32]),
                            ("idx", [128, 12, 128]), ("maskT", [128, 128]), ("ident", [128, 128]), ("lcon", [L, 128, 2]),
                            ("bg", [L, 128, 24]), ("w_pa", [L, 256, 1024]), ("w_pb", [L, 256, 1024]),
                            ("w_pc", [L, 512, 1024]), ("w_o", [L, 1024, 1024]), ("w_g", [L, 1024, H]),
                            ("w_u", [L, 1024, H]), ("w_d", [L, H, 1024])):
            pin[name] = C.din(name, shape)

        def dint(name, shape, dt):
            return nc.dram_tensor(name, list(shape), dt).ap()
        q_in = dint("q_in", [4, 128, NT], BF16)
        q_out = dint("q_out", [4, 512, NT], BF16)
        k_in = dint("k_in", [4, 128, NT], BF16)
        k_out = dint("k_out", [4, 512, NT], BF16)
        v_in = dint("v_in", [4, NT, 128], BF16)
        v_out = dint("v_out", [4, 4 * NT, 128], BF16)
        xa_in = dint("xa_in", [4, 64, NT], F32)
        xa_out = dint("xa_out", [4, 256, NT], F32)
        ga_in = dint("ga_in", [4, 64, NT], F32)
        ga_out = dint("ga_out", [4, 256, NT], F32)
        yc_in = dint("yc_in", [4, 128, NT], BF16)
        yc_out = dint("yc_out", [4, 512, NT], BF16)
        ya_in = dint("ya_in", [4, 64, NT], BF16)
        ya_out = dint("ya_out", [4, 256, NT], BF16)
        v_loc = dint("v_loc", [S, 128], BF16)
        xg_loc = dint("xg_loc", [2, 4, 64, NT], F32)
        yc_loc = dint("yc_loc", [4, 128, NT], BF16)
        ya_loc = dint("ya_loc", [4, 64, NT], BF16)
        yb_x = dint("yb_x", [2, 128, NT], BF16)
        xs1 = dint("xs1", [1024, NT], F32)
        xs2 = dint("xs2", [1024, NT], F32)

        P.dyn_spec = {}

        def rk(name="r"):
            return P.dyn[name]

        def allgather(name, src, dst, reads=()):
            P.op("pool", lambda e: e.collective_compute("AllGather", ALU.bypass, replica_groups=GROUPS, ins=[src], outs=[dst]),
                 reads=list(reads), writes=["cc_" + name], dma="cc_" + name, inc=1)

        for l in range(L):
            par = 0
            x_in = xT if l == 0 else xs2
            x_out = xo if l == L - 1 else xs2
            C.arena_reset()
            io1 = {"xT": x_in, "g1": pin["g1"][l], "w_in": pin["w_in"][l], "sgg": pin["sgg"][l], "sgb": pin["sgb"][l],
                   "wsT": pin["wsT"][l], "tril": pin["tril"], "bsb": pin["bsb"][l], "gqk": pin["gqk"][l],
                   "qk": lambda h, which: (q_in, k_in)[which][h],
                   "v": v_in,
                   "xg": lambda ch: (xa_in, ga_in)[ch // 2][2 * (ch % 2):2 * (ch % 2) + 2].rearrange("h p n -> (h p) n"),
                   "ybT": yb_x}

            def after_xg(P_, keys):
                for h in range(4):
                    allgather("xa", xa_in[h], xa_out[h], reads=keys)
                    allgather("ga", ga_in[h], ga_out[h], reads=keys)
            io1["after_xg"] = after_xg
            emit_L1(C, io1, NT)
            P.barrier()
            for h in range(4):
                allgather("q", q_in[h], q_out[h])
                allgather("k", k_in[h], k_out[h])
                allgather("v", v_in[h], v_out[h])
            C.arena_reset()
            for a_, srcg in ((0, xa_out), (1, ga_out)):
                P.op("sp", lambda e, a_=a_, srcg=srcg: e.dma_start(
                    out=xg_loc[a_].rearrange("(o j) p n -> o j p n", o=1),
                    in_=srcg.rearrange("h (j p) n -> h j p n", j=4)[bass.ds(rk(), 1), :, :, :]),
                    reads=["cc_xa", "cc_ga"], writes=[("xg_loc", a_)], dma="loc_xg%d" % a_)

            def load_qk(P_, q0T, q1T, kTs):
                def v3(t, rows):
                    return t[rows, :].rearrange("p (j n) -> p j n", j=4)

                def src(g, rows):
                    return g.rearrange("h (j p) n -> h j p n", j=4)[bass.ds(rk(), 1), :, rows, :].rearrange("o j p n -> p (o j) n")
                P_.op("sp", lambda e: e.dma_start(out=v3(q0T, slice(0, 64)), in_=src(q_out, slice(0, 64))),
                      reads=["cc_q"], writes=["q0T"], dma="ld_q0")
                P_.op("sp", lambda e: e.dma_start(out=v3(q1T, slice(64, 128)), in_=src(q_out, slice(64, 128))),
                      reads=["cc_q"], writes=["q1T"], dma="ld_q1")
                P_.op("sp", lambda e: e.dma_start(out=v3(kTs, slice(0, 128)), in_=src(k_out, slice(0, 128))),
                      reads=["cc_k"], writes=["kTs"], dma="ld_k")

            def pre_attn(P_):
                P_.op("sp", lambda e: e.dma_start(out=v_loc.rearrange("(o t) e -> o t e", o=1), in_=v_out[bass.ds(rk(), 1), :, :]),
                      reads=["cc_v"], writes=["v_loc"], dma="loc_v")

            def after_ya(P_, nch):
                per = nch // 4
                for j in range(4):
                    allgather("ya", ya_in[j], ya_out[j], reads=[("ya_dram", c_) for c_ in range(j * per, (j + 1) * per)])

            def after_yc(P_, G, ng):
                per = ng // 4
                if (G + 1) % per == 0:
                    j = G // per
                    allgather("yc", yc_in[j], yc_out[j], reads=[("yc_dram", g_) for g_ in range(j * per, (j + 1) * per)])
            io2 = {"nsrc": 4, "load_qk": load_qk, "after_ya": after_ya, "after_yc": after_yc, "pre_attn": pre_attn,
                   "v_dep": ["v_loc"], "xa_dep": [("xg_loc", 0)], "ga_dep": [("xg_loc", 1)],
                   "v_src": lambda e, j: v_loc[j * NT:(j + 1) * NT, :],
                   "xa_src": lambda e, j: xg_loc[0, j], "ga_src": lambda e, j: xg_loc[1, j],
                   "cw": pin["cw"][l], "lvec": pin["lvec"][l], "wa": pin["wa"][l], "wi": pin["wi"][l], "lq": pin["lq"][l],
                   "subg": pin["subg"][l], "rb": pin["rb"], "idx": pin["idx"], "maskT": pin["maskT"], "ident": pin["ident"],
                   "lcon": pin["lcon"][l],
                   "ycT": lambda c0, n: yc_in[c0 // NT, :, c0 % NT:c0 % NT + n],
                   "yaT": lambda c0, n: ya_in[c0 // NT, :, c0 % NT:c0 % NT + n]}
            emit_L2(C, io2, S)
            P.barrier(exclude=("cc_yc", "cc_ya"), keep=("cc_yc", "cc_ya"))
            C.arena_reset()
            def pre_y(P_):
                P_.op("sp", lambda e: e.dma_start(out=yc_loc.rearrange("(o h) p n -> o h p n", o=1),
                                                  in_=yc_out.rearrange("j (h p) n -> j h p n", h=4)[bass.ds(rk(), 1), :, :, :]),
                      reads=["cc_yc"], writes=["yc_loc"], dma="loc_yc")
                P_.op("sp", lambda e: e.dma_start(out=ya_loc.rearrange("(o h) p n -> o h p n", o=1),
                                                  in_=ya_out.rearrange("j (h p) n -> j h p n", h=4)[bass.ds(rk(), 1), :, :, :]),
                      reads=["cc_ya"], writes=["ya_loc"], dma="loc_ya")

            def y_src(e, kind, i, c0, n):
                if kind == "yb":
                    return yb_x[i, :, c0:c0 + n]
                if kind == "ya":
                    return ya_loc[i, :, c0:c0 + n]
                return yc_loc[i, :, c0:c0 + n]
            io3 = {"x_in": x_in, "x_mid": xs1, "x_out": x_out, "g1": pin["g1"][l], "g2": pin["g2"][l],
                   "w_in": pin["w_in"][l], "bg": pin["bg"][l], "y_src": y_src, "pre_y": pre_y, "y_dep": ["yc_loc", "ya_loc"], "w_pa": pin["w_pa"][l],
                   "w_pb": pin["w_pb"][l], "w_pc": pin["w_pc"][l], "w_o": pin["w_o"][l], "w_g": pin["w_g"][l],
                   "w_u": pin["w_u"][l], "w_d": pin["w_d"][l]}
            emit_L3(C, io3, NT)
            P.barrier()
        P.emit()
    return nc


def fused_inputs(inp, c, S, depth=DEPTH):
    NT = S // 4
    b, r = c // 4, c % 4
    x = inp["x"]
    xT = np.ascontiguousarray(x[b, r * NT:(r + 1) * NT].T)
    dummy = np.zeros((2, 2), np.float32)
    l1 = [l1_inputs(dummy, inp, l) for l in range(depth)]
    l2 = [l2_inputs(None, None, None, dummy, dummy, inp, l, r) for l in range(depth)]
    l3 = [l3_inputs(dummy, None, None, None, inp, l) for l in range(depth)]

    def st(lst, k):
        return np.ascontiguousarray(np.stack([d[k] for d in lst], axis=0))
    m = {"xT": xT}
    for k in ("g1", "w_in", "sgg", "sgb", "wsT", "bsb", "gqk"):
        m[k] = st(l1, k)
    m["tril"] = l1[0]["tril"]
    for k in ("cw", "lvec", "wa", "wi", "lq", "subg", "lcon"):
        m[k] = st(l2, k)
    for k in ("rb", "idx", "maskT", "ident"):
        m[k] = l2[0][k]
    for k in ("g2", "bg", "w_pa", "w_pb", "w_pc", "w_o", "w_g", "w_u", "w_d"):
        m[k] = st(l3, k)
    return m


_PROGS = {}


def kernel(**inputs):
    inp = {k: np.asarray(v) for k, v in inputs.items()}
    x = inp["x"]
    B, S, D = x.shape
    NT = S // 4
    key = ("fused", S)
    if key not in _PROGS:
        _PROGS[key] = build_fused(S)
    nc = _PROGS[key]
    in_maps = [fused_inputs(inp, c, S) for c in range(N_CORES)]
    res = run_bass_kernel_spmd(nc, in_maps, core_ids=list(range(N_CORES))).results
    out = np.empty((B, S, D), dtype=np.float32)
    for c in range(N_CORES):
        out[c // 4, (c % 4) * NT:(c % 4 + 1) * NT] = np.asarray(res[c]["xo"]).T
    return out
```

```python
import contextlib
import math
import numpy as np
import concourse.bass as bass
import concourse.mybir as mybir
from concourse.bass_utils import run_bass_kernel_spmd

F32 = mybir.dt.float32
BF16 = mybir.dt.bfloat16
AF = mybir.ActivationFunctionType
ALU = mybir.AluOpType
AX = mybir.AxisListType

D_MODEL = 1024
DEPTH = 4
N_CORES = 8
EPS = 1e-6
LRU_C = 8.0
FFN_HIDDEN = 2816
IN_COLS = 5632
TT = 512
NEG = -30000.0

STREAMS = ("pe", "act", "dve", "pool", "sp")


class Prog:
    def __init__(self, nc):
        self.nc = nc
        self.streams = {e: [] for e in STREAMS}
        self.count = {}
        self.last_write = {}
        self.readers = {}
        self.waited = {e: {} for e in STREAMS}
        self.pending = {e: [] for e in STREAMS}
        self.nops = 0
        self.use_rank = False
        self.dyn = {}

    def barrier(self, exclude=(), keep=()):
        for e in STREAMS:
            for k, v in self.count.items():
                if k in exclude:
                    continue
                if self.waited[e].get(k, 0) < v:
                    self.waited[e][k] = v
                    self.pending[e].append((k, v))
        kept = {k: self.last_write[k] for k in keep if k in self.last_write}
        self.last_write.clear()
        self.readers.clear()
        self.last_write.update(kept)

    def op(self, eng, fn, reads=(), writes=(), dma=None, inc=None):
        semkey = dma if dma is not None else eng
        if inc is None:
            inc = 16 if dma is not None else 1
        deps = {}

        def add(k, v, same_ok):
            if k == semkey and eng == "pe" and dma is None and same_ok:
                return
            if deps.get(k, 0) < v:
                deps[k] = v

        for b in reads:
            lw = self.last_write.get(b)
            if lw is not None:
                add(lw[0], lw[1], eng == "pe")
        for b in writes:
            lw = self.last_write.get(b)
            if lw is not None:
                add(lw[0], lw[1], True)
            for k, v in self.readers.get(b, {}).items():
                add(k, v, True)
        waits = self.pending[eng]
        self.pending[eng] = []
        wd = self.waited[eng]
        for k, v in deps.items():
            if wd.get(k, 0) < v:
                wd[k] = v
                waits.append((k, v))
        val = self.count.get(semkey, 0) + inc
        self.count[semkey] = val
        for b in reads:
            self.readers.setdefault(b, {})[semkey] = val
        for b in writes:
            self.last_write[b] = (semkey, val)
            self.readers[b] = {}
        self.streams[eng].append((waits, fn, semkey, inc))
        self.nops += 1

    def emit(self):
        nc = self.nc
        with contextlib.ExitStack() as es:
            sems = {k: es.enter_context(nc.semaphore("s_" + k)) for k in self.count}
            block = es.enter_context(nc.Block())
            final = list(self.count.items())

            def run(name, e):
                for waits, fn, semkey, inc in self.streams[name]:
                    for k, v in waits:
                        e.wait_ge(sems[k], v)
                    fn(e).then_inc(sems[semkey], inc)

            @block.tensor
            def _(e):
                run("pe", e)

            @block.scalar
            def _(e):
                run("act", e)

            @block.vector
            def _(e):
                run("dve", e)

            @block.gpsimd
            def _(e):
                run("pool", e)

            @block.sync
            def _(e):
                if self.use_rank:
                    r = e.snap(e.partition_id() % 4, min_val=0, max_val=3)
                    self.dyn["r"] = r
                    for name, mul in self.dyn_spec.items():
                        self.dyn[name] = e.snap(r * mul, min_val=0, max_val=3 * mul)
                run("sp", e)
                for k, v in final:
                    e.wait_ge(sems[k], v)


class Ctx:
    def __init__(self, nc, es):
        self.nc = nc
        self.es = es
        self.P = Prog(nc)
        self._n = 0

    def din(self, name, shape, dt=F32):
        return self.nc.dram_tensor(name, list(shape), dt, kind="ExternalInput").ap()

    def dout(self, name, shape, dt=F32):
        return self.nc.dram_tensor(name, list(shape), dt, kind="ExternalOutput").ap()

    def use_arena(self, nbytes):
        self.arena = self.es.enter_context(self.nc.sbuf_tensor("arena", [128, nbytes // 2], BF16))
        self.arena_n = nbytes // 2
        self.off = 0
        self.psum_t = self.es.enter_context(self.nc.psum_tensor("psum", [128, 8, 512], F32))

    def arena_reset(self):
        self.off = 0

    def sb(self, name, shape, dt=F32):
        if getattr(self, "arena", None) is None:
            return self.es.enter_context(self.nc.sbuf_tensor(name, list(shape), dt))[:]
        shape = list(shape)
        free = 1
        for d in shape[1:]:
            free *= d
        n16 = free * (2 if dt == F32 else 1)
        n16 = (n16 + 31) // 32 * 32
        assert self.off + n16 <= self.arena_n, "arena overflow at %s: %d + %d > %d" % (name, self.off, n16, self.arena_n)
        ap = self.arena[0:shape[0], self.off:self.off + free * (2 if dt == F32 else 1)]
        self.off += n16
        if dt == F32:
            ap = ap.bitcast(F32)
        if len(shape) > 2:
            names = " ".join("d%d" % i for i in range(len(shape) - 1))
            kw = {"d%d" % i: shape[1 + i] for i in range(len(shape) - 1)}
            ap = ap.rearrange("p (%s) -> p %s" % (names, names), **kw)
        return ap

    def ps(self, name, shape, dt=F32):
        if getattr(self, "arena", None) is None:
            return self.es.enter_context(self.nc.psum_tensor(name, list(shape), dt))[:]
        shape = list(shape)
        ap = self.psum_t[:]
        if shape == [128, 8, 512]:
            return ap
        assert shape == [128, 8, 2, 256], shape
        return ap.rearrange("p b (h n) -> p b h n", h=2)


class Rot:
    def __init__(self, name, n):
        self.name, self.n, self.i = name, n, 0

    def next(self):
        i = self.i % self.n
        self.i += 1
        return i, (self.name, i)


def load_weight_bf16(C, dst, src, K, N, key, stage, stage_rot, scale_ap=None, engines=("act", "dve"),
                     colblk=1536, dma_eng="sp", scale_key=None, also_writes=()):
    P = C.P
    extra = [scale_key] if scale_key is not None else []
    kc_n = K // 128
    ei = 0
    for kc in range(kc_n):
        for c0 in range(0, N, colblk):
            n = min(colblk, N - c0)
            si, skey = stage_rot.next()
            st = stage[:, si, 0:n]
            srcap = src[kc * 128:(kc + 1) * 128, c0:c0 + n]
            P.op(dma_eng, lambda e, st=st, srcap=srcap: e.dma_start(out=st, in_=srcap),
                 writes=[skey], dma="wld%d" % si)
            eng = engines[ei % len(engines)]
            ei += 1
            d = dst[:, kc, c0:c0 + n]
            if eng == "act":
                if scale_ap is not None:
                    sc = scale_ap[:, kc:kc + 1]
                    fn = lambda e, d=d, st=st, sc=sc: e.activation(out=d, in_=st, func=AF.Identity, scale=sc)
                else:
                    fn = lambda e, d=d, st=st: e.copy(out=d, in_=st)
            else:
                if scale_ap is not None:
                    sc = scale_ap[:, kc:kc + 1]
                    fn = lambda e, d=d, st=st, sc=sc: e.tensor_scalar(out=d, in0=st, scalar1=sc, scalar2=None,
                                                                      op0=ALU.mult)
                else:
                    fn = lambda e, d=d, st=st: e.tensor_copy(out=d, in_=st)
            P.op(eng, fn, reads=[skey] + extra, writes=[(key, eng)] + list(also_writes))
    return [(key, e) for e in engines]


def rstd_from_ps(P, psb, pskey, rstd, rkey, n, eps_n):
    P.op("act", lambda e: e.activation(out=rstd[:, 0:n], in_=psb[:, 0:n], func=AF.Ln, bias=eps_n, scale=1.0),
         reads=[pskey], writes=[rkey])
    P.op("act", lambda e: e.activation(out=rstd[:, 0:n], in_=rstd[:, 0:n], func=AF.Exp, scale=-0.5),
         reads=[rkey], writes=[rkey])


def rms_tile(C, xt, xkey, hT, hkey, sq, sqkey, ones, psb, pskey, rstd, rkey, n, eps_n):
    P = C.P
    P.op("act", lambda e: e.activation(out=sq[:, :, 0:n], in_=xt[:, :, 0:n], func=AF.Square),
         reads=[xkey], writes=[sqkey])

    def mm(e):
        for c in range(8):
            i = e.matmul(psb[:, 0:n], lhsT=ones[:], rhs=sq[:, c, 0:n], start=(c == 0), stop=(c == 7))
        return i
    P.op("pe", mm, reads=[sqkey, "ones"], writes=[pskey])
    rstd_from_ps(P, psb, pskey, rstd, rkey, n, eps_n)
    rb = rstd[:, 0:n].unsqueeze(1).broadcast_to([128, 8, n])
    P.op("dve", lambda e: e.tensor_tensor(out=hT[:, :, 0:n], in0=xt[:, :, 0:n], in1=rb, op=ALU.mult),
         reads=[xkey, rkey], writes=[hkey])


def build_L1(NT):
    nc = bass.Bass("TRN2", target_bir_lowering=False)
    with contextlib.ExitStack() as es:
        C = Ctx(nc, es)
        io = {
            "xT": C.din("xT", [1024, NT]), "g1": C.din("g1", [128, 8]), "w_in": C.din("w_in", [1024, IN_COLS]),
            "sgg": C.din("sgg", [128, 256]), "sgb": C.din("sgb", [128, 256]), "wsT": C.din("wsT", [128, 4, 128]),
            "tril": C.din("tril", [128, 128]), "bsb": C.din("bsb", [128, 2, 128]), "gqk": C.din("gqk", [128, 2]),
            "v": C.dout("v", [4, NT, 128], BF16), "ybT": C.dout("ybT", [2, 128, NT], BF16),
        }
        qk_t = C.dout("qk", [4, 2, 128, NT], BF16)
        xg_t = C.dout("xg", [4, 128, NT], F32)
        io["qk"] = lambda h, which: qk_t[h, which]
        io["xg"] = lambda ch: xg_t[ch]
        emit_L1(C, io, NT)
        C.P.emit()
    return nc


def emit_L1(C, io, NT):
    ntile = NT // TT
    if True:
        P = C.P
        xT, g1, w_in, sgg, sgb, wsT, tril, bsb, gqk = (io[k] for k in ("xT", "g1", "w_in", "sgg", "sgb", "wsT", "tril", "bsb", "gqk"))
        qk_o, v_o, xg_o, yb_o = io["qk"], io["v"], io["xg"], io["ybT"]

        NW = 2560
        wb = C.sb("wb", [128, 8, NW], BF16)
        stage = C.sb("stage", [128, 3, 1536], F32)
        xt = C.sb("xt", [128, 2, 8, TT], F32)
        sq = C.sb("sq", [128, 8, TT], BF16)
        hT = C.sb("hT", [128, ntile, 8, TT], BF16)
        rstd = C.sb("rstd", [128, TT], F32)
        ones = C.sb("ones", [128, 128], BF16)
        bones = C.sb("bones", [128, 128], BF16)
        g1s = C.sb("g1s", [128, 8], F32)
        sgg_s = C.sb("sgg_s", [128, 256], F32)
        sgb_s = C.sb("sgb_s", [128, 256], F32)
        wsT_f = C.sb("wsT_f", [128, 4, 128], F32)
        tril_s = C.sb("tril_s", [128, 128], F32)
        wsT_b = C.sb("wsT_b", [128, 4, 128], BF16)
        bsb_s = C.sb("bsb_s", [128, 2, 128], F32)
        gqk_s = C.sb("gqk_s", [128, 2], F32)
        xg_s = C.sb("xg_s", [128, 2, TT], F32)
        u_s = C.sb("u_s", [128, 2, TT], F32)
        qsq = C.sb("qsq", [128, 2, TT], BF16)
        qr = C.sb("qr", [128, 2, TT], F32)
        qn = C.sb("qn", [128, 3, TT], BF16)
        stats = C.sb("stats", [128, 2, 6], F32)
        mv = C.sb("mv", [128, 2, 2], F32)
        vr = C.sb("vr", [128, 2, 1], F32)
        vtmp = C.sb("vtmp", [128, 2, 256], F32)
        vn = C.sb("vn", [128, 4, 256], BF16)
        vv_s = C.sb("vv_s", [128, 2, 4, 512], BF16)
        mtmp = C.sb("mtmp", [128, 2, TT], F32)
        yb_s = C.sb("yb_s", [128, 2, TT], BF16)
        psum = C.ps("psum", [128, 8, TT], F32)
        psr = Rot("ps", 8)

        P.op("dve", lambda e: e.memset(ones[:], 1.0), writes=["ones"])
        P.op("pool", lambda e: e.memset(bones[:], 0.0), writes=["bones"])
        P.op("pool", lambda e: e.memset(bones[0:64, 0:64], 1.0), writes=["bones"])
        P.op("pool", lambda e: e.memset(bones[64:128, 64:128], 1.0), writes=["bones"])
        for dst, src, key in ((g1s, g1, "g1s"), (sgg_s, sgg, "sgg"), (sgb_s, sgb, "sgb"), (wsT_f, wsT, "wsT_f"),
                              (tril_s, tril, "tril"), (bsb_s, bsb, "bsb"), (gqk_s, gqk, "gqk")):
            P.op("sp", lambda e, dst=dst, src=src: e.dma_start(out=dst[:], in_=src), writes=[key], dma="c_" + key)
        P.op("dve", lambda e: e.tensor_scalar(out=g1s[:], in0=g1s[:], scalar1=32.0, scalar2=None, op0=ALU.mult),
             reads=["g1s"], writes=["g1s"])
        P.op("dve", lambda e: e.tensor_scalar(out=gqk_s[:, 1:2], in0=gqk_s[:, 1:2], scalar1=8.0, scalar2=None, op0=ALU.mult),
             reads=["gqk"], writes=["gqk"])
        trb = tril_s[:].unsqueeze(1).broadcast_to([128, 4, 128])
        P.op("dve", lambda e: e.tensor_tensor(out=wsT_b[:], in0=wsT_f[:], in1=trb, op=ALU.mult),
             reads=["wsT_f", "tril"], writes=["wsT_b"])

        def load_x(t):
            b = t % 2
            src = xT[:, t * TT:(t + 1) * TT].rearrange("(c p) n -> p c n", p=128)
            for hh in range(2):
                P.op("sp", lambda e, b=b, src=src, hh=hh: e.dma_start(out=xt[:, b, 4 * hh:4 * hh + 4, :],
                                                                    in_=src[:, 4 * hh:4 * hh + 4, :]),
                     writes=[("xt", b)], dma="xld%d" % b)

        load_x(0)
        wbk = load_weight_bf16(C, wb, w_in[:, 0:NW], 1024, NW, "wb", stage, Rot("stage", 3), scale_ap=g1s,
                               engines=("act", "dve"), colblk=1280, scale_key="g1s")

        def mm_fm(bank, col0, b):
            def fn(e):
                for k in range(8):
                    i = e.matmul(psum[:, bank, :], lhsT=wb[:, k, col0:col0 + 128], rhs=hT[:, b, k, :],
                                 start=(k == 0), stop=(k == 7))
                return i
            return fn

        def mm_tm(bank, col0, ncol, b, blk):
            def fn(e):
                for k in range(8):
                    i = e.matmul(psum[:, bank, 0:ncol], lhsT=hT[:, b, k, blk * 128:(blk + 1) * 128],
                                 rhs=wb[:, k, col0:col0 + ncol], start=(k == 0), stop=(k == 7))
                return i
            return fn

        def norm(t):
            b = t % 2
            bank, pk = psr.next()
            rms_tile(C, xt[:, b], ("xt", b), hT[:, t], ("hT", t), sq, "sq", ones, psum[:, bank, :], pk,
                     rstd, "rstd", TT, 1024 * EPS)

        xg_keys = []
        for t in range(ntile):
            b = t
            t0 = t * TT
            hk = ("hT", t)
            if t + 1 < ntile:
                load_x(t + 1)
            norm(t)
            for ch in range(4):
                bank, pk = psr.next()
                P.op("pe", mm_fm(bank, ch * 128, b), reads=[hk] + wbk, writes=[pk])
                s = ch % 2
                P.op("act", lambda e, bank=bank, s=s: e.copy(out=xg_s[:, s, :], in_=psum[:, bank, :]),
                     reads=[pk], writes=[("xg_s", s)])
                P.op("sp", lambda e, ch=ch, s=s, t0=t0: e.dma_start(out=xg_o(ch)[:, t0:t0 + TT], in_=xg_s[:, s, :]),
                     reads=[("xg_s", s)], writes=[("xg_dram", t, ch)], dma="st_xg%d" % s)
                xg_keys.append(("xg_dram", t, ch))
        if "after_xg" in io:
            io["after_xg"](P, xg_keys)
        for t in range(ntile):
            b = t
            t0 = t * TT
            hk = ("hT", t)
            for n in range(2):
                bank, pk = psr.next()
                P.op("pe", mm_fm(bank, 512 + n * 128, b), reads=[hk] + wbk, writes=[pk])
                P.op("act", lambda e, bank=bank, n=n: e.copy(out=u_s[:, n, :], in_=psum[:, bank, :]),
                     reads=[pk], writes=[("u_s", n)])
            for blk in range(4):
                bank, pk = psr.next()
                s = blk % 2
                P.op("pe", mm_tm(bank, 768, 256, b, blk), reads=[hk] + wbk, writes=[pk])
                P.op("dve", lambda e, bank=bank, s=s: e.bn_stats(out=stats[:, s, :], in_=psum[:, bank, 0:256]),
                     reads=[pk], writes=[("stats", s)])
                P.op("dve", lambda e, s=s: e.bn_aggr(out=mv[:, s, :], in_=stats[:, s, :]),
                     reads=[("stats", s)], writes=[("mv", s)])
                P.op("act", lambda e, s=s: e.activation(out=vr[:, s, :], in_=mv[:, s, 1:2], func=AF.Ln, bias=EPS, scale=1.0),
                     reads=[("mv", s)], writes=[("vr", s)])
                P.op("act", lambda e, s=s: e.activation(out=vr[:, s, :], in_=vr[:, s, :], func=AF.Exp, scale=-0.5),
                     reads=[("vr", s)], writes=[("vr", s)])
                P.op("dve", lambda e, bank=bank, s=s: e.tensor_scalar(
                    out=vtmp[:, s, :], in0=psum[:, bank, 0:256], scalar1=mv[:, s, 0:1], scalar2=vr[:, s, :],
                    op0=ALU.subtract, op1=ALU.mult), reads=[pk, ("mv", s), ("vr", s)], writes=[("vtmp", s)])
                P.op("dve", lambda e, s=s: e.tensor_tensor(out=vtmp[:, s, :], in0=vtmp[:, s, :], in1=sgg_s[:], op=ALU.mult),
                     reads=[("vtmp", s), "sgg"], writes=[("vtmp", s)])
                P.op("dve", lambda e, s=s, blk=blk: e.tensor_tensor(out=vn[:, blk, :], in0=vtmp[:, s, :], in1=sgb_s[:], op=ALU.add),
                     reads=[("vtmp", s), "sgb"], writes=[("vn", blk)])
            for n in range(2):
                bank, pk = psr.next()

                def mix(e, bank=bank, n=n):
                    for blk in range(4):
                        for gg in range(2):
                            g = 2 * n + gg
                            i = e.matmul(psum[gg * 64:(gg + 1) * 64, bank, blk * 128:(blk + 1) * 128],
                                         lhsT=vn[:, blk, g * 64:(g + 1) * 64], rhs=wsT_b[:, g, :],
                                         start=True, stop=True)
                    return i
                P.op("pe", mix, reads=[("vn", 0), ("vn", 1), ("vn", 2), ("vn", 3), "wsT_b"], writes=[pk])
                bsv = bsb_s[:, n, :].unsqueeze(1).broadcast_to([128, 4, 128])
                P.op("dve", lambda e, bank=bank, n=n, bsv=bsv: e.tensor_tensor(
                    out=mtmp[:, n, :].rearrange("p (b t) -> p b t", b=4),
                    in0=psum[:, bank, :].rearrange("p (b t) -> p b t", b=4), in1=bsv, op=ALU.add),
                    reads=[pk, "bsb"], writes=[("mtmp", n)])
                P.op("dve", lambda e, n=n: e.tensor_tensor(out=yb_s[:, n, :], in0=mtmp[:, n, :], in1=u_s[:, n, :], op=ALU.mult),
                     reads=[("mtmp", n), ("u_s", n)], writes=[("yb_s", n)])
                P.op("sp", lambda e, n=n, t0=t0: e.dma_start(out=yb_o[n, :, t0:t0 + TT], in_=yb_s[:, n, :]),
                     reads=[("yb_s", n)], dma="st_yb%d" % n)
            qrot = 0
            for which in range(2):
                for h in range(4):
                    col0 = 1024 + which * 512 + h * 128
                    bank, pk = psr.next()
                    bank2, pk2 = psr.next()
                    s = qrot % 2
                    s3 = qrot % 3
                    qrot += 1
                    P.op("pe", mm_fm(bank, col0, b), reads=[hk] + wbk, writes=[pk])
                    P.op("act", lambda e, bank=bank, s=s: e.activation(out=qsq[:, s, :], in_=psum[:, bank, :], func=AF.Square),
                         reads=[pk], writes=[("qsq", s)])
                    P.op("pe", lambda e, bank2=bank2, s=s: e.matmul(psum[:, bank2, :], lhsT=bones[:], rhs=qsq[:, s, :],
                                                                    start=True, stop=True),
                         reads=[("qsq", s), "bones"], writes=[pk2])
                    rstd_from_ps(P, psum[:, bank2, :], pk2, qr[:, s, :], ("qr", s), TT, 64 * EPS)
                    P.op("dve", lambda e, bank=bank, s=s, s3=s3, which=which: e.scalar_tensor_tensor(
                        out=qn[:, s3, :], in0=psum[:, bank, :], scalar=gqk_s[:, which:which + 1], in1=qr[:, s, :],
                        op0=ALU.mult, op1=ALU.mult), reads=[pk, ("qr", s), "gqk"], writes=[("qn", s3)])
                    P.op("sp", lambda e, h=h, which=which, s3=s3, t0=t0: e.dma_start(
                        out=qk_o(h, which)[:, t0:t0 + TT], in_=qn[:, s3, :]),
                        reads=[("qn", s3)], dma="st_qn%d" % s3)
            vb = t % 2
            for blk in range(4):
                bank, pk = psr.next()
                P.op("pe", mm_tm(bank, 2048, 512, b, blk), reads=[hk] + wbk, writes=[pk])
                eng = "act" if blk % 2 == 0 else "dve"
                if eng == "act":
                    fn = lambda e, bank=bank, blk=blk, vb=vb: e.copy(out=vv_s[:, vb, blk, :], in_=psum[:, bank, :])
                else:
                    fn = lambda e, bank=bank, blk=blk, vb=vb: e.tensor_copy(out=vv_s[:, vb, blk, :], in_=psum[:, bank, :])
                P.op(eng, fn, reads=[pk], writes=[("vv_s", vb, blk)])
            for h in range(4):
                P.op("sp", lambda e, vb=vb, t0=t0, h=h: e.dma_start(
                    out=v_o[h, t0:t0 + TT, :].rearrange("(b p) n -> p b n", p=128), in_=vv_s[:, vb, :, h * 128:(h + 1) * 128]),
                    reads=[("vv_s", vb, k) for k in range(4)], dma="st_vv%d" % vb)


def build_L2(S):
    nc = bass.Bass("TRN2", target_bir_lowering=False)
    with contextlib.ExitStack() as es:
        C = Ctx(nc, es)
        qT = C.din("qT", [128, S], BF16)
        kT = C.din("kT", [128, S], BF16)
        v = C.din("v", [S, 128], BF16)
        xaT = C.din("xaT", [64, S])
        gaT = C.din("gaT", [64, S])
        io = {
            "nsrc": 1,
            "q_src": lambda e, j: qT, "k_src": lambda e, j: kT, "v_src": lambda e, j: v,
            "xa_src": lambda e, j: xaT, "ga_src": lambda e, j: gaT,
            "cw": C.din("cw", [64, 4]), "lvec": C.din("lvec", [64, 4]), "wa": C.din("wa", [64, 64]), "wi": C.din("wi", [64, 64]),
            "lq": C.din("lq", [128, 4, 64]), "subg": C.din("subg", [128, 128]), "rb": C.din("rb", [128, 32]),
            "idx": C.din("idx", [128, 12, 128]), "maskT": C.din("maskT", [128, 128]), "ident": C.din("ident", [128, 128]),
            "lcon": C.din("lcon", [128, 2]),
        }
        yc_t = C.dout("ycT", [128, S], BF16)
        ya_t = C.dout("yaT", [64, S], BF16)
        io["ycT"] = lambda c0, n: yc_t[:, c0:c0 + n]
        io["yaT"] = lambda c0, n: ya_t[:, c0:c0 + n]
        emit_L2(C, io, S)
        C.P.emit()
    return nc


def emit_L2(C, io, S):
    NG = S // TT
    NKT = S // 128
    TL = 512
    NCH = S // TL
    nsrc = io["nsrc"]
    NS = S // nsrc
    if True:
        P = C.P
        cw, lvec, wa, wi, lq, subg, rb, idx, maskT, ident, lcon = (io[k] for k in (
            "cw", "lvec", "wa", "wi", "lq", "subg", "rb", "idx", "maskT", "ident", "lcon"))
        yc_o, ya_o = io["ycT"], io["yaT"]

        q0T = C.sb("q0T", [128, S], BF16)
        q1T = C.sb("q1T", [128, S], BF16)
        kTs = C.sb("kTs", [128, S], BF16)
        v1 = C.sb("v1", [128, NKT, 129], BF16)
        biasT = C.sb("biasT", [128, 12, 128], BF16)
        bias_f = C.sb("bias_f", [128, 12, 128], F32)
        idx_s = C.sb("idx_s", [128, 12, 128], F32)
        btmp = C.sb("btmp", [128, 2, 12, 128], F32)
        mask_s = C.sb("mask_s", [128, 128], F32)
        id_f = C.sb("id_f", [128, 128], F32)
        id_b = C.sb("id_b", [128, 128], BF16)
        rb_s = C.sb("rb_s", [128, 32], F32)
        lq_s = C.sb("lq_s", [128, 4, 64], F32)
        lq_t = C.sb("lq_t", [128, 2, 64], F32)
        lsc = C.sb("lsc", [128, 8], F32)
        lcon_s = C.sb("lcon_s", [128, 2], F32)
        subg_s = C.sb("subg_s", [128, 128], F32)
        pt = C.sb("pt", [128, 3, 2, TT], BF16)
        accs = C.sb("accs", [128, 3, TT], F32)
        rl = C.sb("rl", [128, 8], F32)
        o_s = C.sb("o_s", [128, 4, 128], F32)
        osq = C.sb("osq", [128, 128], F32)
        ss = C.sb("ss", [128, 4], F32)
        y_s = C.sb("y_s", [128, 4, 128], BF16)
        yc_s = C.sb("yc_s", [128, 2, TT], BF16)
        cw_s = C.sb("cw_s", [64, 4], F32)
        lv_s = C.sb("lv_s", [64, 4], F32)
        csc = C.sb("csc", [64, 4], F32)
        w_f = C.sb("w_f", [64, 2, 64], F32)
        w_b = C.sb("w_b", [64, 2, 64], BF16)
        xa_s = C.sb("xa_s", [64, 2, TL + 3], F32)
        ga_s = C.sb("ga_s", [64, 2, TL], F32)
        xc = C.sb("xc", [64, TL], F32)
        xcb = C.sb("xcb", [64, TL], BF16)
        r_s = C.sb("r_s", [64, TL], F32)
        i_s = C.sb("i_s", [64, TL], F32)
        a_s = C.sb("a_s", [64, TL], F32)
        m_s = C.sb("m_s", [64, TL], F32)
        h_s = C.sb("h_s", [64, 2, TL], F32)
        g_s = C.sb("g_s", [64, TL], F32)
        ya_s = C.sb("ya_s", [64, 2, TL], BF16)
        psum = C.ps("psum", [128, 8, TT], F32)
        trp = psum[:, 7, :].bitcast(BF16)

        for dst, src, key in ((cw_s[:], cw, "cw"), (lv_s[:], lvec, "lv"), (w_f[:, 0, :], wa, "wa"), (w_f[:, 1, :], wi, "wi"),
                              (lq_s[:], lq, "lq"), (subg_s[:], subg, "subg"), (rb_s[:], rb, "rb"), (idx_s[:], idx, "idx"),
                              (mask_s[:], maskT, "mask"), (id_f[:], ident, "id_f"), (lcon_s[:], lcon, "lcon")):
            P.op("sp", lambda e, dst=dst, src=src: e.dma_start(out=dst, in_=src), writes=[key], dma="c_" + key)
        P.op("dve", lambda e: e.tensor_copy(out=id_b[:], in_=id_f[:]), reads=["id_f"], writes=["id_b"])
        P.op("dve", lambda e: e.tensor_copy(out=w_b[:], in_=w_f[:]), reads=["wa", "wi"], writes=["w_b"])
        for j in range(2):
            P.op("dve", lambda e, j=j: e.scalar_tensor_tensor(out=lq_t[:, j, :], in0=lq_s[:, 2 * j, :], scalar=1.0,
                                                              in1=lq_s[:, 2 * j + 1, :], op0=ALU.mult, op1=ALU.mult,
                                                              accum_out=lsc[:, j:j + 1]),
                 reads=["lq"], writes=[("lsc", j)])
        P.op("act", lambda e: e.activation(out=lsc[:, 2:4], in_=lsc[:, 0:2], func=AF.Exp),
             reads=[("lsc", 0), ("lsc", 1)], writes=["lsce"])
        P.op("dve", lambda e: e.tensor_tensor(out=lsc[:, 4:5], in0=lsc[:, 2:3], in1=lsc[:, 3:4], op=ALU.subtract),
             reads=["lsce"], writes=["lam"])
        P.op("dve", lambda e: e.tensor_tensor(out=lsc[:, 4:5], in0=lsc[:, 4:5], in1=lcon_s[:, 0:1], op=ALU.add),
             reads=["lam", "lcon"], writes=["lam"])
        P.op("dve", lambda e: e.tensor_scalar(out=lsc[:, 5:6], in0=lsc[:, 4:5], scalar1=-1.0, scalar2=None, op0=ALU.mult),
             reads=["lam"], writes=["nlam"])
        P.op("dve", lambda e: e.tensor_scalar(out=subg_s[:], in0=subg_s[:], scalar1=lcon_s[:, 1:2], scalar2=None, op0=ALU.mult),
             reads=["subg", "lcon"], writes=["subg"])
        P.op("act", lambda e: e.activation(out=csc[:, 0:1], in_=lv_s[:, 3:4], func=AF.Exp, scale=-1.0),
             reads=["lv"], writes=["csc0"])
        P.op("act", lambda e: e.activation(out=csc[:, 0:1], in_=csc[:, 0:1], func=AF.Ln, bias=1.0, scale=1.0),
             reads=["csc0"], writes=["csc0"])
        P.op("dve", lambda e: e.tensor_scalar(out=csc[:, 1:2], in0=csc[:, 0:1], scalar1=-LRU_C, scalar2=None, op0=ALU.mult),
             reads=["csc0"], writes=["csc"])
        P.op("dve", lambda e: e.tensor_scalar(out=csc[:, 2:3], in0=csc[:, 0:1], scalar1=-2.0 * LRU_C, scalar2=None, op0=ALU.mult),
             reads=["csc0"], writes=["csc"])

        P.op("dve", lambda e: e.memset(bias_f[:], 0.0), writes=["bias_f"])
        idx_np = l2_consts()["idx"]
        for d_ in range(12):
            for bkt in sorted(set(int(v_) for v_ in np.unique(idx_np[:, d_, :]))):
                P.op("dve", lambda e, bkt=bkt, d_=d_: e.tensor_single_scalar(
                    out=btmp[:, 0, d_, :], in_=idx_s[:, d_, :], scalar=float(bkt), op=ALU.is_equal),
                    reads=["idx"], writes=[("btmp", 0)])
                P.op("dve", lambda e, bkt=bkt, d_=d_: e.scalar_tensor_tensor(
                    out=bias_f[:, d_, :], in0=btmp[:, 0, d_, :], scalar=rb_s[:, bkt:bkt + 1], in1=bias_f[:, d_, :],
                    op0=ALU.mult, op1=ALU.add), reads=[("btmp", 0), "rb", "bias_f"], writes=["bias_f"])
        P.op("dve", lambda e: e.tensor_tensor(out=bias_f[:, 0, :], in0=bias_f[:, 0, :], in1=mask_s[:], op=ALU.add),
             reads=["bias_f", "mask"], writes=["bias_f"])
        P.op("dve", lambda e: e.tensor_scalar(out=bias_f[:], in0=bias_f[:], scalar1=rb_s[:, 15:16], scalar2=None, op0=ALU.subtract),
             reads=["bias_f", "rb"], writes=["bias_f"])
        P.op("dve", lambda e: e.tensor_copy(out=biasT[:], in_=bias_f[:]), reads=["bias_f"], writes=["biasT"])

        for ch in range(NCH):
            b = ch % 2
            t0 = ch * TL
            js = t0 // NS
            l0 = t0 - js * NS
            if ch == 0:
                P.op("dve", lambda e: e.memset(xa_s[:, 0, 0:3], 0.0), writes=[("xa", 0)])
                P.op("sp", lambda e: e.dma_start(out=xa_s[:, 0, 3:3 + TL], in_=io["xa_src"](e, 0)[:, 0:TL]),
                     reads=io.get("xa_dep", []), writes=[("xa", 0)], dma="ld_xa0")
            else:
                P.op("dve", lambda e, b=b: e.tensor_copy(out=xa_s[:, b, 0:3], in_=xa_s[:, 1 - b, TL:TL + 3]),
                     reads=[("xa", 1 - b)], writes=[("xa", b)])
                P.op("sp", lambda e, b=b, js=js, l0=l0: e.dma_start(out=xa_s[:, b, 3:3 + TL], in_=io["xa_src"](e, js)[:, l0:l0 + TL]),
                     reads=io.get("xa_dep", []), writes=[("xa", b)], dma="ld_xa%d" % b)
            P.op("sp", lambda e, b=b, js=js, l0=l0: e.dma_start(out=ga_s[:, b, :], in_=io["ga_src"](e, js)[:, l0:l0 + TL]),
                 reads=io.get("ga_dep", []), writes=[("ga", b)], dma="ld_ga%d" % b)
            P.op("dve", lambda e, b=b: e.tensor_scalar(out=xc[:], in0=xa_s[:, b, 3:3 + TL], scalar1=cw_s[:, 3:4],
                                                       scalar2=lv_s[:, 0:1], op0=ALU.mult, op1=ALU.add),
                 reads=[("xa", b), "cw", "lv"], writes=["xc"])
            for tap in range(3):
                P.op("dve", lambda e, b=b, tap=tap: e.scalar_tensor_tensor(
                    out=xc[:], in0=xa_s[:, b, tap:tap + TL], scalar=cw_s[:, tap:tap + 1], in1=xc[:],
                    op0=ALU.mult, op1=ALU.add), reads=[("xa", b), "cw", "xc"], writes=["xc"])
            P.op("act", lambda e: e.copy(out=xcb[:], in_=xc[:]), reads=["xc"], writes=["xcb"])
            nh = TL // TT
            for gate in range(2):
                for hh in range(nh):
                    bank = gate * nh + hh
                    P.op("pe", lambda e, gate=gate, hh=hh, bank=bank: e.matmul(
                        psum[0:64, bank, :], lhsT=w_b[:, gate, :], rhs=xcb[:, hh * TT:(hh + 1) * TT], start=True, stop=True),
                        reads=["xcb", "w_b"], writes=[("st", bank // 2)])
            for hh in range(nh):
                P.op("act", lambda e, hh=hh: e.activation(out=r_s[:, hh * TT:(hh + 1) * TT], in_=psum[0:64, hh, :],
                                                          func=AF.Sigmoid, bias=lv_s[:, 1:2], scale=1.0),
                     reads=[("st", hh // 2), "lv"], writes=["r_s"])
            for hh in range(nh):
                P.op("act", lambda e, hh=hh: e.activation(out=i_s[:, hh * TT:(hh + 1) * TT], in_=psum[0:64, nh + hh, :],
                                                          func=AF.Sigmoid, bias=lv_s[:, 2:3], scale=1.0),
                     reads=[("st", (nh + hh) // 2), "lv"], writes=["i_s"])
            P.op("act", lambda e, b=b: e.activation(out=g_s[:], in_=ga_s[:, b, :], func=AF.Square),
                 reads=[("ga", b)], writes=["g_s"])
            P.op("act", lambda e: e.activation(out=g_s[:], in_=g_s[:], func=AF.Identity, bias=1.0, scale=0.044715),
                 reads=["g_s"], writes=["g_s"])
            P.op("dve", lambda e, b=b: e.tensor_tensor(out=g_s[:], in0=g_s[:], in1=ga_s[:, b, :], op=ALU.mult),
                 reads=["g_s", ("ga", b)], writes=["g_s"])
            P.op("act", lambda e: e.activation(out=g_s[:], in_=g_s[:], func=AF.Sigmoid, scale=1.5957691216057308),
                 reads=["g_s"], writes=["g_s"])
            P.op("dve", lambda e, b=b: e.tensor_tensor(out=g_s[:], in0=g_s[:], in1=ga_s[:, b, :], op=ALU.mult),
                 reads=["g_s", ("ga", b)], writes=["g_s"])
            P.op("act", lambda e: e.activation(out=a_s[:], in_=r_s[:], func=AF.Exp, scale=csc[:, 1:2]),
                 reads=["r_s", "csc"], writes=["a_s"])
            P.op("act", lambda e: e.activation(out=m_s[:], in_=r_s[:], func=AF.Exp, scale=csc[:, 2:3]),
                 reads=["r_s", "csc"], writes=["m_s"])
            P.op("act", lambda e: e.activation(out=m_s[:], in_=m_s[:], func=AF.Sqrt, bias=1.0, scale=-1.0),
                 reads=["m_s"], writes=["m_s"])
            P.op("dve", lambda e: e.tensor_tensor(out=i_s[:], in0=i_s[:], in1=xc[:], op=ALU.mult),
                 reads=["i_s", "xc"], writes=["i_s"])
            P.op("dve", lambda e: e.tensor_tensor(out=m_s[:], in0=m_s[:], in1=i_s[:], op=ALU.mult),
                 reads=["m_s", "i_s"], writes=["m_s"])
            init = 0.0 if ch == 0 else h_s[:, 1 - b, TL - 1:TL]
            P.op("dve", lambda e, b=b, init=init: e.tensor_tensor_scan(out=h_s[:, b, :], data0=a_s[:], data1=m_s[:],
                                                                       initial=init, op0=ALU.mult, op1=ALU.add),
                 reads=["a_s", "m_s", ("h_s", 1 - b)], writes=[("h_s", b)])
            P.op("dve", lambda e, b=b: e.tensor_tensor(out=ya_s[:, b, :], in0=h_s[:, b, :], in1=g_s[:], op=ALU.mult),
                 reads=[("h_s", b), "g_s"], writes=[("ya_s", b)])
            P.op("sp", lambda e, b=b, t0=t0: e.dma_start(out=ya_o(t0, TL), in_=ya_s[:, b, :]),
                 reads=[("ya_s", b)], writes=[("ya_dram", ch)], dma="st_ya%d" % b)
        if "after_ya" in io:
            io["after_ya"](P, NCH)

        if "pre_attn" in io:
            io["pre_attn"](P)
        P.op("pool", lambda e: e.memset(q0T[64:128, :], 0.0), writes=["q0T"])
        P.op("pool", lambda e: e.memset(q1T[0:64, :], 0.0), writes=["q1T"])
        P.op("pool", lambda e: e.memset(v1[:, :, 128:129], 1.0), writes=["v1ones"])
        if "load_qk" in io:
            io["load_qk"](P, q0T, q1T, kTs)
        else:
            P.op("sp", lambda e: e.dma_start(out=q0T[0:64, :], in_=io["q_src"](e, 0)[0:64, :]), writes=["q0T"], dma="ld_q0")
            P.op("sp", lambda e: e.dma_start(out=q1T[64:128, :], in_=io["q_src"](e, 0)[64:128, :]), writes=["q1T"], dma="ld_q1")
            P.op("sp", lambda e: e.dma_start(out=kTs[:], in_=io["k_src"](e, 0)), writes=["kTs"], dma="ld_k")
        nvd = max(nsrc, NKT // 16)
        for i in range(nvd):
            a, b_ = i * NKT // nvd, (i + 1) * NKT // nvd
            j = (a * 128) // NS
            ra = a * 128 - j * NS
            rb_ = b_ * 128 - j * NS
            P.op("sp", lambda e, a=a, b_=b_, j=j, ra=ra, rb_=rb_: e.dma_start(
                out=v1[:, a:b_, 0:128], in_=io["v_src"](e, j)[ra:rb_, :].rearrange("(kt p) e -> p kt e", p=128)),
                reads=io.get("v_dep", []), writes=[("v1", i)], dma="ld_v%d" % i)

        def acc_ap(a, lo=0, hi=129):
            return psum[:, 4 + a // 3, (a % 3) * 160 + lo:(a % 3) * 160 + hi]

        def accs_ap(a, lo=0, hi=129):
            return accs[:, a // 3, (a % 3) * 160 + lo:(a % 3) * 160 + hi]

        pairs = [(G, kt) for G in range(NG) for kt in range(4 * G + 4)]

        def geom(G, kt):
            jj = max(kt - 4 * G, 0)
            nblk = 4 - jj
            d0 = 4 * G + jj - kt
            return jj, nblk, d0, (d0 <= 8)

        def emit_qk(n):
            G, kt = pairs[n]
            jj, nblk, d0, near = geom(G, kt)
            sb_ = n % 2
            c0, c1 = jj * 128, TT
            q0 = G * TT + c0

            def fn(e):
                for c, qsrc in ((0, q0T), (1, q1T)):
                    i = e.matmul(psum[:, 2 * sb_ + c, c0:c1], lhsT=kTs[:, kt * 128:(kt + 1) * 128],
                                 rhs=qsrc[:, q0:q0 + nblk * 128], start=True, stop=not near)
                    if near:
                        i = e.matmul(psum[:, 2 * sb_ + c, c0:c1], lhsT=id_b[:],
                                     rhs=biasT[:, d0:d0 + nblk, :], start=False, stop=True)
                return i
            P.op("pe", fn, reads=["q0T", "q1T", "kTs", "id_b", "biasT"], writes=[("st", sb_)])

        def emit_exp(n):
            G, kt = pairs[n]
            jj, nblk, d0, near = geom(G, kt)
            sb_, pb = n % 2, n % 3
            c0 = jj * 128
            src = psum[:, 2 * sb_:2 * sb_ + 2, c0:TT]
            dst = pt[:, pb, :, c0:TT]
            fn = lambda e: e.activation(out=dst, in_=src, func=AF.Exp)
            P.op("act", fn, reads=[("st", sb_)], writes=[("pt", pb)])

        def emit_pv(n):
            G, kt = pairs[n]
            jj, nblk, d0, near = geom(G, kt)
            pb = n % 3

            def fn(e):
                for i_ in range(jj, 4):
                    for c in range(2):
                        i = e.matmul(acc_ap(c * 4 + i_), lhsT=pt[:, pb, c, i_ * 128:(i_ + 1) * 128], rhs=v1[:, kt, :],
                                     start=False, stop=False, skip_group_check=True)
                return i
            P.op("pe", fn, reads=[("pt", pb), ("v1", kt * nvd // NKT), "v1ones"], writes=["acc"])

        def emit_evac(G):
            yb_ = G % 2
            P.op("dve", lambda e: e.tensor_copy(out=accs[:], in_=psum[:, 4:7, :]), reads=["acc"], writes=["accs"])
            P.op("dve", lambda e: e.reciprocal(out=rl[:, 0:6].rearrange("p (a b) -> p a b", a=2),
                                               in_=accs[:, 0:2, 128:449:160]), reads=["accs"], writes=["rl"])
            P.op("dve", lambda e: e.reciprocal(out=rl[:, 6:8], in_=accs[:, 2, 128:289:160]), reads=["accs"], writes=["rl"])
            P.op("dve", lambda e: e.tensor_scalar(out=rl[:, 4:8], in0=rl[:, 4:8], scalar1=lsc[:, 5:6], scalar2=None, op0=ALU.mult),
                 reads=["rl", "nlam"], writes=["rl"])
            for i_ in range(4):
                P.op("dve", lambda e, i_=i_: e.tensor_scalar(out=o_s[:, i_, :], in0=accs_ap(i_, 0, 128), scalar1=rl[:, i_:i_ + 1],
                                                             scalar2=None, op0=ALU.mult),
                     reads=["accs", "rl"], writes=[("o_s", i_)])
                P.op("dve", lambda e, i_=i_: e.scalar_tensor_tensor(out=o_s[:, i_, :], in0=accs_ap(4 + i_, 0, 128),
                                                                    scalar=rl[:, 4 + i_:5 + i_], in1=o_s[:, i_, :],
                                                                    op0=ALU.mult, op1=ALU.add),
                     reads=["accs", "rl", ("o_s", i_)], writes=[("o_s", i_)])
                P.op("dve", lambda e, i_=i_: e.scalar_tensor_tensor(out=osq[:], in0=o_s[:, i_, :], scalar=1.0, in1=o_s[:, i_, :],
                                                                    op0=ALU.mult, op1=ALU.mult, accum_out=ss[:, i_:i_ + 1]),
                     reads=[("o_s", i_)], writes=["osq", ("ss", i_)])
            P.op("act", lambda e: e.activation(out=ss[:], in_=ss[:], func=AF.Ln, bias=128 * EPS, scale=1.0),
                 reads=[("ss", k) for k in range(4)], writes=["ssr"])
            P.op("act", lambda e: e.activation(out=ss[:], in_=ss[:], func=AF.Exp, scale=-0.5), reads=["ssr"], writes=["ssr"])
            for i_ in range(4):
                P.op("dve", lambda e, i_=i_: e.scalar_tensor_tensor(out=y_s[:, i_, :], in0=o_s[:, i_, :], scalar=ss[:, i_:i_ + 1],
                                                                    in1=subg_s[:], op0=ALU.mult, op1=ALU.mult),
                     reads=[("o_s", i_), "ssr", "subg"], writes=[("y_s", i_)])

            def tr(e):
                for i_ in range(4):
                    i = e.transpose(trp[:, i_ * 128:(i_ + 1) * 128], y_s[:, i_, :], id_b[:])
                return i
            P.op("pe", tr, reads=[("y_s", k) for k in range(4)] + ["id_b"], writes=["trp"])
            P.op("dve", lambda e, yb_=yb_: e.tensor_copy(out=yc_s[:, yb_, :], in_=trp[:, 0:TT]), reads=["trp"], writes=[("yc_s", yb_)])
            P.op("sp", lambda e, yb_=yb_, G=G: e.dma_start(out=yc_o(G * TT, TT), in_=yc_s[:, yb_, :]),
                 reads=[("yc_s", yb_)], writes=[("yc_dram", G)], dma="st_yc%d" % yb_)
            if "after_yc" in io:
                io["after_yc"](P, G, NG)

        emit_qk(0)
        for n, (G, kt) in enumerate(pairs):
            if n + 1 < len(pairs):
                emit_qk(n + 1)
            if kt == 0:
                P.op("dve", lambda e: e.memset(psum[:, 4:7, :], 0.0), writes=["acc"])
            emit_exp(n)
            emit_pv(n)
            if kt == 4 * G + 3:
                emit_evac(G)


T3 = 256


def build_L3(NT):
    nc = bass.Bass("TRN2", target_bir_lowering=False)
    H = FFN_HIDDEN
    with contextlib.ExitStack() as es:
        C = Ctx(nc, es)
        yaT = C.din("yaT", [256, NT], BF16)
        ybT = C.din("ybT", [256, NT], BF16)
        ycT = C.din("ycT", [512, NT], BF16)

        def y_src(e, kind, i, c0, n):
            if kind == "ya":
                return yaT[64 * i:64 * i + 64, c0:c0 + n]
            if kind == "yb":
                return ybT[128 * i:128 * i + 128, c0:c0 + n]
            return ycT[128 * i:128 * i + 128, c0:c0 + n]
        xo = C.dout("xo", [1024, NT])
        io = {
            "x_in": C.din("xT", [1024, NT]), "x_mid": xo, "x_out": xo,
            "g1": C.din("g1", [128, 8]), "g2": C.din("g2", [128, 8]), "w_in": C.din("w_in", [1024, IN_COLS]),
            "bg": C.din("bg", [128, 24]), "y_src": y_src,
            "w_pa": C.din("w_pa", [256, 1024]), "w_pb": C.din("w_pb", [256, 1024]), "w_pc": C.din("w_pc", [512, 1024]),
            "w_o": C.din("w_o", [1024, 1024]), "w_g": C.din("w_g", [1024, H]), "w_u": C.din("w_u", [1024, H]),
            "w_d": C.din("w_d", [H, 1024]),
        }
        emit_L3(C, io, NT)
        C.P.emit()
    return nc


def emit_L3(C, io, NT):
    ntile = NT // T3
    H = FFN_HIDDEN
    HC = H // 128
    if True:
        P = C.P
        xT, xmid, xo = io["x_in"], io["x_mid"], io["x_out"]
        g1, g2, w_in, bg, w_pa, w_pb, w_pc, w_o, w_g, w_u, w_d = (io[k] for k in (
            "g1", "g2", "w_in", "bg", "w_pa", "w_pb", "w_pc", "w_o", "w_g", "w_u", "w_d"))

        wbuf = C.sb("wbuf", [128, 3 * 8 * H], BF16)
        stage = C.sb("stage", [128, 2, 1024], F32)
        xt = C.sb("xt", [128, 2, 8, T3], F32)
        sq = C.sb("sq", [128, 8, T3], BF16)
        hT = C.sb("hT", [128, 2, 8, T3], BF16)
        rstd = C.sb("rstd", [128, T3], F32)
        ones = C.sb("ones", [128, 128], BF16)
        g1s = C.sb("g1s", [128, 8], F32)
        g2s = C.sb("g2s", [128, 8], F32)
        bg_s = C.sb("bg_s", [128, 24], F32)
        y_s = C.sb("y_s", [128, 2, 8, T3], BF16)
        gs = C.sb("gs", [128, 2, 3, T3], F32)
        mt = C.sb("mt", [128, 2, 3, T3], F32)
        mT = C.sb("mT", [128, 8, T3], BF16)
        sg = C.sb("sg", [128, 2, T3], F32)
        actT = C.sb("actT", [128, HC, T3], BF16)
        psum = C.ps("psum", [128, 8, 2, T3], F32)
        psr = Rot("ps", 8)

        def wview(off, kc, n):
            return wbuf[:, off:off + kc * n].rearrange("p (c n) -> p c n", c=kc)
        wgt_ = wview(0, 8, 3072)
        wpa_ = wview(24576, 2, 1024)
        wpb_ = wview(24576 + 2048, 2, 1024)
        wpc_ = wview(24576 + 4096, 4, 1024)
        wo_ = wview(24576 + 8192, 8, 1024)
        fg_ = wview(0, 8, H)
        fu_ = wview(8 * H, 8, H)
        fd_ = wview(16 * H, HC, 1024)

        P.op("dve", lambda e: e.memset(ones[:], 1.0), writes=["ones"])
        for dst, src, key in ((g1s, g1, "g1s"), (g2s, g2, "g2s"), (bg_s, bg, "bg")):
            P.op("sp", lambda e, dst=dst, src=src: e.dma_start(out=dst[:], in_=src), writes=[key], dma="c_" + key)
        for t_, k_ in ((g1s, "g1s"), (g2s, "g2s")):
            P.op("dve", lambda e, t_=t_: e.tensor_scalar(out=t_[:], in0=t_[:], scalar1=32.0, scalar2=None, op0=ALU.mult),
                 reads=[k_], writes=[k_])

        def load_x(t, src_ap, srckeys):
            b = t % 2
            src = src_ap[:, t * T3:(t + 1) * T3].rearrange("(c p) n -> p c n", p=128)
            P.op("sp", lambda e, b=b, src=src: e.dma_start(out=xt[:, b, :, :], in_=src),
                 reads=srckeys, writes=[("xt", b)], dma="xld%d" % b)

        def load_y(t):
            b = t % 2
            c0 = t * T3
            for h in range(4):
                P.op("sp", lambda e, b=b, h=h, c0=c0: e.dma_start(
                    out=y_s[(h % 2) * 64:(h % 2) * 64 + 64, b, h // 2, :], in_=io["y_src"](e, "ya", h, c0, T3)),
                    reads=io.get("y_dep", []), writes=[("y_s", b, 0)], dma="yld%d_0" % b)
            for i in range(2):
                P.op("sp", lambda e, b=b, i=i, c0=c0: e.dma_start(out=y_s[:, b, 2 + i, :], in_=io["y_src"](e, "yb", i, c0, T3)),
                     writes=[("y_s", b, 1)], dma="yld%d_1" % b)
            for h in range(4):
                P.op("sp", lambda e, b=b, h=h, c0=c0: e.dma_start(out=y_s[:, b, 4 + h, :], in_=io["y_src"](e, "yc", h, c0, T3)),
                     reads=io.get("y_dep", []), writes=[("y_s", b, 2)], dma="yld%d_2" % b)

        srot = Rot("stage", 2)
        load_x(0, xT, [])
        kg = load_weight_bf16(C, wgt_, w_in[:, 2560:5632], 1024, 3072, "wgt", stage, srot, scale_ap=g1s,
                              colblk=1024, scale_key="g1s")
        kpa = load_weight_bf16(C, wpa_, w_pa, 256, 1024, "wpa", stage, srot, colblk=1024)
        kpb = load_weight_bf16(C, wpb_, w_pb, 256, 1024, "wpb", stage, srot, colblk=1024)
        kpc = load_weight_bf16(C, wpc_, w_pc, 512, 1024, "wpc", stage, srot, colblk=1024)
        ko = load_weight_bf16(C, wo_, w_o, 1024, 1024, "wo", stage, srot, colblk=1024)
        c1keys = kg + kpa + kpb + kpc + ko
        if "pre_y" in io:
            io["pre_y"](P)
        load_y(0)

        def mm_fm(e, bank, half, wv, kc, col0, rhs_fn):
            for k in range(kc):
                i = e.matmul(psum[:, bank, half, :], lhsT=wv[:, k, col0:col0 + 128], rhs=rhs_fn(k),
                             start=(k == 0), stop=(k == kc - 1))
            return i

        def norm(t):
            b = t % 2
            bank, pk = psr.next()
            rms_tile(C, xt[:, b], ("xt", b), hT[:, b], ("hT", b), sq, "sq", ones, psum[:, bank, 0, :], pk, rstd, "rstd", T3, 1024 * EPS)

        def resid(t, wv, kc, rhs_fn, rkeys, outkey, xdst):
            b = t % 2
            for m in range(4):
                bank, pk = psr.next()

                def fn(e, bank=bank, m=m):
                    for hf in range(2):
                        i = mm_fm(e, bank, hf, wv, kc, (2 * m + hf) * 128, rhs_fn)
                    return i
                P.op("pe", fn, reads=rkeys, writes=[pk])
                P.op("dve", lambda e, bank=bank, m=m, b=b: e.tensor_tensor(
                    out=xt[:, b, 2 * m:2 * m + 2, :], in0=xt[:, b, 2 * m:2 * m + 2, :], in1=psum[:, bank, :, :], op=ALU.add),
                    reads=[pk, ("xt", b)], writes=[("xt", b)])
            dst = xdst[:, t * T3:(t + 1) * T3].rearrange("(c p) n -> p c n", p=128)
            P.op("sp", lambda e, b=b, dst=dst: e.dma_start(out=dst, in_=xt[:, b, :, :]),
                 reads=[("xt", b)], writes=[(outkey, t)], dma="st_x%d" % b)

        projs = ((wpa_, 2, 0, kpa), (wpb_, 2, 2, kpb), (wpc_, 4, 4, kpc))
        for t in range(ntile):
            b = t % 2
            if t + 1 < ntile:
                load_x(t + 1, xT, [])
                load_y(t + 1)
            if t == 0:
                norm(0)
            for n in range(8):
                gb = n % 2
                slots = []
                for br in range(3):
                    bank, pk = psr.next()
                    slots.append((bank, pk))
                    wv, kc, off, kk = projs[br]

                    def fn(e, bank=bank, br=br, n=n, wv=wv, kc=kc, off=off, b=b):
                        mm_fm(e, bank, 0, wgt_, 8, br * 1024 + n * 128, lambda k: hT[:, b, k, :])
                        return mm_fm(e, bank, 1, wv, kc, n * 128, lambda k: y_s[:, b, off + k, :])
                    P.op("pe", fn, reads=[("hT", b), ("y_s", b, br)] + kg + kk, writes=[pk])
                for br in range(3):
                    bank, pk = slots[br]
                    ch = br * 8 + n
                    P.op("act", lambda e, bank=bank, br=br, ch=ch, gb=gb: e.activation(
                        out=gs[:, gb, br, :], in_=psum[:, bank, 0, :], func=AF.Sigmoid, bias=bg_s[:, ch:ch + 1], scale=1.0),
                        reads=[pk, "bg"], writes=[("gs", gb, br)])
                    P.op("dve", lambda e, bank=bank, br=br, gb=gb: e.tensor_tensor(
                        out=mt[:, gb, br, :], in0=psum[:, bank, 1, :], in1=gs[:, gb, br, :], op=ALU.mult),
                        reads=[pk, ("gs", gb, br)], writes=[("mt", gb, br)])
                P.op("dve", lambda e, gb=gb: e.tensor_tensor(out=mt[:, gb, 0, :], in0=mt[:, gb, 0, :], in1=mt[:, gb, 1, :], op=ALU.add),
                     reads=[("mt", gb, 0), ("mt", gb, 1)], writes=[("mt", gb, 0)])
                P.op("dve", lambda e, gb=gb, n=n: e.tensor_tensor(out=mT[:, n, :], in0=mt[:, gb, 0, :], in1=mt[:, gb, 2, :], op=ALU.add),
                     reads=[("mt", gb, 0), ("mt", gb, 2)], writes=[("mT", n)])
            if t + 1 < ntile:
                norm(t + 1)
            resid(t, wo_, 8, lambda k: mT[:, k, :], [("mT", k) for k in range(8)] + ko, "xo", xmid)

        kfg = load_weight_bf16(C, fg_, w_g, 1024, H, "fg", stage, srot, scale_ap=g2s, colblk=1024, scale_key="g2s",
                               also_writes=c1keys)
        kfu = load_weight_bf16(C, fu_, w_u, 1024, H, "fu", stage, srot, scale_ap=g2s, colblk=1024, scale_key="g2s",
                               also_writes=c1keys)
        kfd = load_weight_bf16(C, fd_, w_d, H, 1024, "fd", stage, srot, colblk=1024, also_writes=c1keys)
        load_x(0, xmid, [("xo", 0)])
        for t in range(ntile):
            b = t % 2
            if t + 1 < ntile:
                load_x(t + 1, xmid, [("xo", t + 1)])
            if t == 0:
                norm(0)
            for j in range(HC):
                sb_ = j % 2
                bank, pk = psr.next()

                def fn(e, bank=bank, j=j, b=b):
                    mm_fm(e, bank, 0, fg_, 8, j * 128, lambda k: hT[:, b, k, :])
                    return mm_fm(e, bank, 1, fu_, 8, j * 128, lambda k: hT[:, b, k, :])
                P.op("pe", fn, reads=[("hT", b)] + kfg + kfu, writes=[pk])
                P.op("act", lambda e, bank=bank, sb_=sb_: e.activation(out=sg[:, sb_, :], in_=psum[:, bank, 0, :], func=AF.Silu),
                     reads=[pk], writes=[("sg", sb_)])
                P.op("dve", lambda e, bank=bank, sb_=sb_, j=j: e.tensor_tensor(out=actT[:, j, :], in0=psum[:, bank, 1, :], in1=sg[:, sb_, :], op=ALU.mult),
                     reads=[pk, ("sg", sb_)], writes=[("actT", j)])
            if t + 1 < ntile:
                norm(t + 1)
            resid(t, fd_, HC, lambda k: actT[:, k, :], [("actT", k) for k in range(HC)] + kfd, "xo2", xo)


def _c(a):
    return np.ascontiguousarray(a, dtype=np.float32)


def l1_inputs(xT, inp, l):
    sgw = inp["sg_w"][l]
    sgb = inp["sg_b"][l]
    p = np.arange(128)
    bsb = np.stack([sgb[2 * n + p // 64, :] for n in range(2)], axis=1)
    gqk = np.stack([inp["q_norm_g"][l][p % 64], inp["k_norm_g"][l][p % 64]], axis=1)
    return {
        "xT": _c(xT),
        "g1": _c(inp["ln1_g"][l].reshape(8, 128).T),
        "w_in": _c(inp["w_in"][l]),
        "sgg": _c(np.broadcast_to(inp["sg_ln_g"][l], (128, 256))),
        "sgb": _c(np.broadcast_to(inp["sg_ln_b"][l], (128, 256))),
        "wsT": _c(sgw.transpose(2, 0, 1)),
        "tril": _c(np.triu(np.ones((128, 128)))),
        "bsb": _c(bsb),
        "gqk": _c(gqk),
    }


def t5_bucket_np(rel):
    import jax
    import jax.numpy as jnp
    with jax.default_device(jax.devices("cpu")[0]):
        rel = jnp.asarray(rel, jnp.int32)
        half, max_exact = 16, 8
        ret = jnp.where(rel > 0, half, 0)
        n = jnp.abs(rel)
        nf = jnp.maximum(n, 1).astype(jnp.float32)
        large = max_exact + (jnp.log(nf / max_exact) / math.log(2048 / max_exact) * (half - max_exact)).astype(jnp.int32)
        large = jnp.minimum(large, half - 1)
        return np.asarray(ret + jnp.where(n < max_exact, n, large))


_L2_CONST = {}


def l2_consts():
    if not _L2_CONST:
        k = np.arange(128)[:, None, None]
        d = np.arange(12)[None, :, None]
        q = np.arange(128)[None, None, :]
        rel = k - q - 128 * d
        _L2_CONST["idx"] = _c(t5_bucket_np(rel))
        kk = np.arange(128)[:, None]
        qq = np.arange(128)[None, :]
        _L2_CONST["maskT"] = _c(np.where((kk // 64) > (qq // 64), NEG, 0.0))
        _L2_CONST["ident"] = _c(np.eye(128))
    return _L2_CONST


def l2_inputs(qT, kT, v, xaT, gaT, inp, l, h):
    lam_init = 0.8 - 0.6 * math.exp(-0.3 * l)
    cs = l2_consts()
    ch = slice(64 * h, 64 * h + 64)
    lvec = np.stack([inp["conv_b"][l][ch], inp["lru_ba"][l][ch], inp["lru_bi"][l][ch], inp["lru_lambda"][l][ch]], axis=1)
    lq = np.stack([inp["lambda_q1"][l], inp["lambda_k1"][l], inp["lambda_q2"][l], inp["lambda_k2"][l]], axis=0)
    return {
        "qT": qT, "kT": kT, "v": v, "xaT": _c(xaT), "gaT": _c(gaT),
        "cw": _c(inp["conv_w"][l][:, ch].T),
        "lvec": _c(lvec),
        "wa": _c(inp["lru_wa"][l][h]), "wi": _c(inp["lru_wi"][l][h]),
        "lq": _c(np.broadcast_to(lq, (128, 4, 64))),
        "subg": _c(np.broadcast_to(inp["subln_g"][l], (128, 128))),
        "rb": _c(np.broadcast_to(inp["rel_bias"][:, h], (128, 32))),
        "idx": cs["idx"], "maskT": cs["maskT"], "ident": cs["ident"],
        "lcon": _c(np.broadcast_to(np.array([lam_init, (1.0 - lam_init) * math.sqrt(128.0)]), (128, 2))),
    }


def l3_inputs(xT, yaT, ybT, ycT, inp, l):
    return {
        "xT": _c(xT),
        "g1": _c(inp["ln1_g"][l].reshape(8, 128).T),
        "g2": _c(inp["ln2_g"][l].reshape(8, 128).T),
        "w_in": _c(inp["w_in"][l]),
        "bg": _c(inp["b_gate"][l].reshape(24, 128).T),
        "yaT": yaT, "ybT": ybT, "ycT": ycT,
        "w_pa": _c(inp["w_pa"][l]), "w_pb": _c(inp["w_pb"][l]), "w_pc": _c(inp["w_pc"][l]), "w_o": _c(inp["w_o"][l]),
        "w_g": _c(inp["w_ff_gate"][l]), "w_u": _c(inp["w_ff_up"][l]), "w_d": _c(inp["w_ff_down"][l]),
    }


ARENA_BYTES = 212736
GROUPS = [[0, 1, 2, 3], [4, 5, 6, 7]]


def build_fused(S, depth=DEPTH):
    NT = S // 4
    H = FFN_HIDDEN
    L = depth
    nc = bass.Bass("TRN2", target_bir_lowering=False)
    with contextlib.ExitStack() as es:
        C = Ctx(nc, es)
        C.use_arena(ARENA_BYTES)
        P = C.P
        P.use_rank = True
        xT = C.din("xT", [1024, NT])
        xo = C.dout("xo", [1024, NT])
        pin = {}
        for name, shape in (("g1", [L, 128, 8]), ("g2", [L, 128, 8]), ("w_in", [L, 1024, IN_COLS]),
                            ("sgg", [L, 128, 256]), ("sgb", [L, 128, 256]), ("wsT", [L, 128, 4, 128]),
                            ("tril", [128, 128]), ("bsb", [L, 128, 2, 128]), ("gqk", [L, 128, 2]),
                            ("cw", [L, 64, 4]), ("lvec", [L, 64, 4]), ("wa", [L, 64, 64]), ("wi", [L, 64, 64]),
                            ("lq", [L, 128, 4, 64]), ("subg", [L, 128, 128]), ("rb", [128, 32]),
                            ("idx", [128, 12, 128]), ("maskT", [128, 128]), ("ident", [128, 128]), ("lcon", [L, 128, 2]),
                            ("bg", [L, 128, 24]), ("w_pa", [L, 256, 1024]), ("w_pb", [L, 256, 1024]),
                            ("w_pc", [L, 512, 1024]), ("w_o", [L, 1024, 1024]), ("w_g", [L, 1024, H]),
                            ("w_u", [L, 1024, H]), ("w_d", [L, H, 1024])):
            pin[name] = C.din(name, shape)

        def dint(name, shape, dt):
            return nc.dram_tensor(name, list(shape), dt).ap()
        q_in = dint("q_in", [4, 128, NT], BF16)
        q_out = dint("q_out", [4, 512, NT], BF16)
        k_in = dint("k_in", [4, 128, NT], BF16)
        k_out = dint("k_out", [4, 512, NT], BF16)
        v_in = dint("v_in", [4, NT, 128], BF16)
        v_out = dint("v_out", [4, 4 * NT, 128], BF16)
        xa_in = dint("xa_in", [4, 64, NT], F32)
        xa_out = dint("xa_out", [4, 256, NT], F32)
        ga_in = dint("ga_in", [4, 64, NT], F32)
        ga_out = dint("ga_out", [4, 256, NT], F32)
        yc_in = dint("yc_in", [4, 128, NT], BF16)
        yc_out = dint("yc_out", [4, 512, NT], BF16)
        ya_in = dint("ya_in", [4, 64, NT], BF16)
        ya_out = dint("ya_out", [4, 256, NT], BF16)
        v_loc = dint("v_loc", [S, 128], BF16)
        xg_loc = dint("xg_loc", [2, 4, 64, NT], F32)
        yc_loc = dint("yc_loc", [4, 128, NT], BF16)
        ya_loc = dint("ya_loc", [4, 64, NT], BF16)
        yb_x = dint("yb_x", [2, 128, NT], BF16)
        xs1 = dint("xs1", [1024, NT], F32)
        xs2 = dint("xs2", [1024, NT], F32)

        P.dyn_spec = {}

        def rk(name="r"):
            return P.dyn[name]

        def allgather(name, src, dst, reads=()):
            P.op("pool", lambda e: e.collective_compute("AllGather", ALU.bypass, replica_groups=GROUPS, ins=[src], outs=[dst]),
                 reads=list(reads), writes=["cc_" + name], dma="cc_" + name, inc=1)

        for l in range(L):
            par = 0
            x_in = xT if l == 0 else xs2
            x_out = xo if l == L - 1 else xs2
            C.arena_reset()
            io1 = {"xT": x_in, "g1": pin["g1"][l], "w_in": pin["w_in"][l], "sgg": pin["sgg"][l], "sgb": pin["sgb"][l],
                   "wsT": pin["wsT"][l], "tril": pin["tril"], "bsb": pin["bsb"][l], "gqk": pin["gqk"][l],
                   "qk": lambda h, which: (q_in, k_in)[which][h],
                   "v": v_in,
                   "xg": lambda ch: (xa_in, ga_in)[ch // 2][2 * (ch % 2):2 * (ch % 2) + 2].rearrange("h p n -> (h p) n"),
                   "ybT": yb_x}

            def after_xg(P_, keys):
                for h in range(4):
                    allgather("xa", xa_in[h], xa_out[h], reads=keys)
                    allgather("ga", ga_in[h], ga_out[h], reads=keys)
            io1["after_xg"] = after_xg
            emit_L1(C, io1, NT)
            P.barrier()
            for h in range(4):
                allgather("q", q_in[h], q_out[h])
                allgather("k", k_in[h], k_out[h])
                allgather("v", v_in[h], v_out[h])
            C.arena_reset()
            for a_, srcg in ((0, xa_out), (1, ga_out)):
                P.op("sp", lambda e, a_=a_, srcg=srcg: e.dma_start(
                    out=xg_loc[a_].rearrange("(o j) p n -> o j p n", o=1),
                    in_=srcg.rearrange("h (j p) n -> h j p n", j=4)[bass.ds(rk(), 1), :, :, :]),
                    reads=["cc_xa", "cc_ga"], writes=[("xg_loc", a_)], dma="loc_xg%d" % a_)

            def load_qk(P_, q0T, q1T, kTs):
                def v3(t, rows):
                    return t[rows, :].rearrange("p (j n) -> p j n", j=4)

                def src(g, rows):
                    return g.rearrange("h (j p) n -> h j p n", j=4)[bass.ds(rk(), 1), :, rows, :].rearrange("o j p n -> p (o j) n")
                P_.op("sp", lambda e: e.dma_start(out=v3(q0T, slice(0, 64)), in_=src(q_out, slice(0, 64))),
                      reads=["cc_q"], writes=["q0T"], dma="ld_q0")
                P_.op("sp", lambda e: e.dma_start(out=v3(q1T, slice(64, 128)), in_=src(q_out, slice(64, 128))),
                      reads=["cc_q"], writes=["q1T"], dma="ld_q1")
                P_.op("sp", lambda e: e.dma_start(out=v3(kTs, slice(0, 128)), in_=src(k_out, slice(0, 128))),
                      reads=["cc_k"], writes=["kTs"], dma="ld_k")

            def pre_attn(P_):
                P_.op("sp", lambda e: e.dma_start(out=v_loc.rearrange("(o t) e -> o t e", o=1), in_=v_out[bass.ds(rk(), 1), :, :]),
                      reads=["cc_v"], writes=["v_loc"], dma="loc_v")

            def after_ya(P_, nch):
                per = nch // 4
                for j in range(4):
                    allgather("ya", ya_in[j], ya_out[j], reads=[("ya_dram", c_) for c_ in range(j * per, (j + 1) * per)])

            def after_yc(P_, G, ng):
                per = ng // 4
                if (G + 1) % per == 0:
                    j = G // per
                    allgather("yc", yc_in[j], yc_out[j], reads=[("yc_dram", g_) for g_ in range(j * per, (j + 1) * per)])
            io2 = {"nsrc": 4, "load_qk": load_qk, "after_ya": after_ya, "after_yc": after_yc, "pre_attn": pre_attn,
                   "v_dep": ["v_loc"], "xa_dep": [("xg_loc", 0)], "ga_dep": [("xg_loc", 1)],
                   "v_src": lambda e, j: v_loc[j * NT:(j + 1) * NT, :],
                   "xa_src": lambda e, j: xg_loc[0, j], "ga_src": lambda e, j: xg_loc[1, j],
                   "cw": pin["cw"][l], "lvec": pin["lvec"][l], "wa": pin["wa"][l], "wi": pin["wi"][l], "lq": pin["lq"][l],
                   "subg": pin["subg"][l], "rb": pin["rb"], "idx": pin["idx"], "maskT": pin["maskT"], "ident": pin["ident"],
                   "lcon": pin["lcon"][l],
                   "ycT": lambda c0, n: yc_in[c0 // NT, :, c0 % NT:c0 % NT + n],
                   "yaT": lambda c0, n: ya_in[c0 // NT, :, c0 % NT:c0 % NT + n]}
            emit_L2(C, io2, S)
            P.barrier(exclude=("cc_yc", "cc_ya"), keep=("cc_yc", "cc_ya"))
            C.arena_reset()
            def pre_y(P_):
                P_.op("sp", lambda e: e.dma_start(out=yc_loc.rearrange("(o h) p n -> o h p n", o=1),
                                                  in_=yc_out.rearrange("j (h p) n -> j h p n", h=4)[bass.ds(rk(), 1), :, :, :]),
                      reads=["cc_yc"], writes=["yc_loc"], dma="loc_yc")
                P_.op("sp", lambda e: e.dma_start(out=ya_loc.rearrange("(o h) p n -> o h p n", o=1),
                                                  in_=ya_out.rearrange("j (h p) n -> j h p n", h=4)[bass.ds(rk(), 1), :, :, :]),
                      reads=["cc_ya"], writes=["ya_loc"], dma="loc_ya")

            def y_src(e, kind, i, c0, n):
                if kind == "yb":
                    return yb_x[i, :, c0:c0 + n]
                if kind == "ya":
                    return ya_loc[i, :, c0:c0 + n]
                return yc_loc[i, :, c0:c0 + n]
            io3 = {"x_in": x_in, "x_mid": xs1, "x_out": x_out, "g1": pin["g1"][l], "g2": pin["g2"][l],
                   "w_in": pin["w_in"][l], "bg": pin["bg"][l], "y_src": y_src, "pre_y": pre_y, "y_dep": ["yc_loc", "ya_loc"], "w_pa": pin["w_pa"][l],
                   "w_pb": pin["w_pb"][l], "w_pc": pin["w_pc"][l], "w_o": pin["w_o"][l], "w_g": pin["w_g"][l],
                   "w_u": pin["w_u"][l], "w_d": pin["w_d"][l]}
            emit_L3(C, io3, NT)
            P.barrier()
        P.emit()
    return nc


def fused_inputs(inp, c, S, depth=DEPTH):
    NT = S // 4
    b, r = c // 4, c % 4
    x = inp["x"]
    xT = np.ascontiguousarray(x[b, r * NT:(r + 1) * NT].T)
    dummy = np.zeros((2, 2), np.float32)
    l1 = [l1_inputs(dummy, inp, l) for l in range(depth)]
    l2 = [l2_inputs(None, None, None, dummy, dummy, inp, l, r) for l in range(depth)]
    l3 = [l3_inputs(dummy, None, None, None, inp, l) for l in range(depth)]

    def st(lst, k):
        return np.ascontiguousarray(np.stack([d[k] for d in lst], axis=0))
    m = {"xT": xT}
    for k in ("g1", "w_in", "sgg", "sgb", "wsT", "bsb", "gqk"):
        m[k] = st(l1, k)
    m["tril"] = l1[0]["tril"]
    for k in ("cw", "lvec", "wa", "wi", "lq", "subg", "lcon"):
        m[k] = st(l2, k)
    for k in ("rb", "idx", "maskT", "ident"):
        m[k] = l2[0][k]
    for k in ("g2", "bg", "w_pa", "w_pb", "w_pc", "w_o", "w_g", "w_u", "w_d"):
        m[k] = st(l3, k)
    return m


_PROGS = {}


def kernel(**inputs):
    inp = {k: np.asarray(v) for k, v in inputs.items()}
    x = inp["x"]
    B, S, D = x.shape
    NT = S // 4
    key = ("fused", S)
    if key not in _PROGS:
        _PROGS[key] = build_fused(S)
    nc = _PROGS[key]
    in_maps = [fused_inputs(inp, c, S) for c in range(N_CORES)]
    res = run_bass_kernel_spmd(nc, in_maps, core_ids=list(range(N_CORES))).results
    out = np.empty((B, S, D), dtype=np.float32)
    for c in range(N_CORES):
        out[c // 4, (c % 4) * NT:(c % 4 + 1) * NT] = np.asarray(res[c]["xo"]).T
    return out
```

```python
import contextlib
import math
import numpy as np
import concourse.bass as bass
import concourse.mybir as mybir
from concourse.bass_utils import run_bass_kernel_spmd

F32 = mybir.dt.float32
BF16 = mybir.dt.bfloat16
AF = mybir.ActivationFunctionType
ALU = mybir.AluOpType
AX = mybir.AxisListType

D_MODEL = 1024
DEPTH = 4
N_CORES = 8
EPS = 1e-6
LRU_C = 8.0
FFN_HIDDEN = 2816
IN_COLS = 5632
TT = 512
NEG = -30000.0

STREAMS = ("pe", "act", "dve", "pool", "sp")


class Prog:
    def __init__(self, nc):
        self.nc = nc
        self.streams = {e: [] for e in STREAMS}
        self.count = {}
        self.last_write = {}
        self.readers = {}
        self.waited = {e: {} for e in STREAMS}
        self.pending = {e: [] for e in STREAMS}
        self.nops = 0
        self.use_rank = False
        self.dyn = {}

    def barrier(self, exclude=(), keep=()):
        for e in STREAMS:
            for k, v in self.count.items():
                if k in exclude:
                    continue
                if self.waited[e].get(k, 0) < v:
                    self.waited[e][k] = v
                    self.pending[e].append((k, v))
        kept = {k: self.last_write[k] for k in keep if k in self.last_write}
        self.last_write.clear()
        self.readers.clear()
        self.last_write.update(kept)

    def op(self, eng, fn, reads=(), writes=(), dma=None, inc=None):
        semkey = dma if dma is not None else eng
        if inc is None:
            inc = 16 if dma is not None else 1
        deps = {}

        def add(k, v, same_ok):
            if k == semkey and eng == "pe" and dma is None and same_ok:
                return
            if deps.get(k, 0) < v:
                deps[k] = v

        for b in reads:
            lw = self.last_write.get(b)
            if lw is not None:
                add(lw[0], lw[1], eng == "pe")
        for b in writes:
            lw = self.last_write.get(b)
            if lw is not None:
                add(lw[0], lw[1], True)
            for k, v in self.readers.get(b, {}).items():
                add(k, v, True)
        waits = self.pending[eng]
        self.pending[eng] = []
        wd = self.waited[eng]
        for k, v in deps.items():
            if wd.get(k, 0) < v:
                wd[k] = v
                waits.append((k, v))
        val = self.count.get(semkey, 0) + inc
        self.count[semkey] = val
        for b in reads:
            self.readers.setdefault(b, {})[semkey] = val
        for b in writes:
            self.last_write[b] = (semkey, val)
            self.readers[b] = {}
        self.streams[eng].append((waits, fn, semkey, inc))
        self.nops += 1

    def emit(self):
        nc = self.nc
        with contextlib.ExitStack() as es:
            sems = {k: es.enter_context(nc.semaphore("s_" + k)) for k in self.count}
            block = es.enter_context(nc.Block())
            final = list(self.count.items())

            def run(name, e):
                for waits, fn, semkey, inc in self.streams[name]:
                    for k, v in waits:
                        e.wait_ge(sems[k], v)
                    fn(e).then_inc(sems[semkey], inc)

            @block.tensor
            def _(e):
                run("pe", e)

            @block.scalar
            def _(e):
                run("act", e)

            @block.vector
            def _(e):
                run("dve", e)

            @block.gpsimd
            def _(e):
                run("pool", e)

            @block.sync
            def _(e):
                if self.use_rank:
                    r = e.snap(e.partition_id() % 4, min_val=0, max_val=3)
                    self.dyn["r"] = r
                    for name, mul in self.dyn_spec.items():
                        self.dyn[name] = e.snap(r * mul, min_val=0, max_val=3 * mul)
                run("sp", e)
                for k, v in final:
                    e.wait_ge(sems[k], v)


class Ctx:
    def __init__(self, nc, es):
        self.nc = nc
        self.es = es
        self.P = Prog(nc)
        self._n = 0

    def din(self, name, shape, dt=F32):
        return self.nc.dram_tensor(name, list(shape), dt, kind="ExternalInput").ap()

    def dout(self, name, shape, dt=F32):
        return self.nc.dram_tensor(name, list(shape), dt, kind="ExternalOutput").ap()

    def use_arena(self, nbytes):
        self.arena = self.es.enter_context(self.nc.sbuf_tensor("arena", [128, nbytes // 2], BF16))
        self.arena_n = nbytes // 2
        self.off = 0
        self.psum_t = self.es.enter_context(self.nc.psum_tensor("psum", [128, 8, 512], F32))

    def arena_reset(self):
        self.off = 0

    def sb(self, name, shape, dt=F32):
        if getattr(self, "arena", None) is None:
            return self.es.enter_context(self.nc.sbuf_tensor(name, list(shape), dt))[:]
        shape = list(shape)
        free = 1
        for d in shape[1:]:
            free *= d
        n16 = free * (2 if dt == F32 else 1)
        n16 = (n16 + 31) // 32 * 32
        assert self.off + n16 <= self.arena_n, "arena overflow at %s: %d + %d > %d" % (name, self.off, n16, self.arena_n)
        ap = self.arena[0:shape[0], self.off:self.off + free * (2 if dt == F32 else 1)]
        self.off += n16
        if dt == F32:
            ap = ap.bitcast(F32)
        if len(shape) > 2:
            names = " ".join("d%d" % i for i in range(len(shape) - 1))
            kw = {"d%d" % i: shape[1 + i] for i in range(len(shape) - 1)}
            ap = ap.rearrange("p (%s) -> p %s" % (names, names), **kw)
        return ap

    def ps(self, name, shape, dt=F32):
        if getattr(self, "arena", None) is None:
            return self.es.enter_context(self.nc.psum_tensor(name, list(shape), dt))[:]
        shape = list(shape)
        ap = self.psum_t[:]
        if shape == [128, 8, 512]:
            return ap
        assert shape == [128, 8, 2, 256], shape
        return ap.rearrange("p b (h n) -> p b h n", h=2)


class Rot:
    def __init__(self, name, n):
        self.name, self.n, self.i = name, n, 0

    def next(self):
        i = self.i % self.n
        self.i += 1
        return i, (self.name, i)


def load_weight_bf16(C, dst, src, K, N, key, stage, stage_rot, scale_ap=None, engines=("act", "dve"),
                     colblk=1536, dma_eng="sp", scale_key=None, also_writes=()):
    P = C.P
    extra = [scale_key] if scale_key is not None else []
    kc_n = K // 128
    ei = 0
    for kc in range(kc_n):
        for c0 in range(0, N, colblk):
            n = min(colblk, N - c0)
            si, skey = stage_rot.next()
            st = stage[:, si, 0:n]
            srcap = src[kc * 128:(kc + 1) * 128, c0:c0 + n]
            P.op(dma_eng, lambda e, st=st, srcap=srcap: e.dma_start(out=st, in_=srcap),
                 writes=[skey], dma="wld%d" % si)
            eng = engines[ei % len(engines)]
            ei += 1
            d = dst[:, kc, c0:c0 + n]
            if eng == "act":
                if scale_ap is not None:
                    sc = scale_ap[:, kc:kc + 1]
                    fn = lambda e, d=d, st=st, sc=sc: e.activation(out=d, in_=st, func=AF.Identity, scale=sc)
                else:
                    fn = lambda e, d=d, st=st: e.copy(out=d, in_=st)
            else:
                if scale_ap is not None:
                    sc = scale_ap[:, kc:kc + 1]
                    fn = lambda e, d=d, st=st, sc=sc: e.tensor_scalar(out=d, in0=st, scalar1=sc, scalar2=None,
                                                                      op0=ALU.mult)
                else:
                    fn = lambda e, d=d, st=st: e.tensor_copy(out=d, in_=st)
            P.op(eng, fn, reads=[skey] + extra, writes=[(key, eng)] + list(also_writes))
    return [(key, e) for e in engines]


def rstd_from_ps(P, psb, pskey, rstd, rkey, n, eps_n):
    P.op("act", lambda e: e.activation(out=rstd[:, 0:n], in_=psb[:, 0:n], func=AF.Ln, bias=eps_n, scale=1.0),
         reads=[pskey], writes=[rkey])
    P.op("act", lambda e: e.activation(out=rstd[:, 0:n], in_=rstd[:, 0:n], func=AF.Exp, scale=-0.5),
         reads=[rkey], writes=[rkey])


def rms_tile(C, xt, xkey, hT, hkey, sq, sqkey, ones, psb, pskey, rstd, rkey, n, eps_n):
    P = C.P
    P.op("act", lambda e: e.activation(out=sq[:, :, 0:n], in_=xt[:, :, 0:n], func=AF.Square),
         reads=[xkey], writes=[sqkey])

    def mm(e):
        for c in range(8):
            i = e.matmul(psb[:, 0:n], lhsT=ones[:], rhs=sq[:, c, 0:n], start=(c == 0), stop=(c == 7))
        return i
    P.op("pe", mm, reads=[sqkey, "ones"], writes=[pskey])
    rstd_from_ps(P, psb, pskey, rstd, rkey, n, eps_n)
    rb = rstd[:, 0:n].unsqueeze(1).broadcast_to([128, 8, n])
    P.op("dve", lambda e: e.tensor_tensor(out=hT[:, :, 0:n], in0=xt[:, :, 0:n], in1=rb, op=ALU.mult),
         reads=[xkey, rkey], writes=[hkey])


def build_L1(NT):
    nc = bass.Bass("TRN2", target_bir_lowering=False)
    with contextlib.ExitStack() as es:
        C = Ctx(nc, es)
        io = {
            "xT": C.din("xT", [1024, NT]), "g1": C.din("g1", [128, 8]), "w_in": C.din("w_in", [1024, IN_COLS]),
            "sgg": C.din("sgg", [128, 256]), "sgb": C.din("sgb", [128, 256]), "wsT": C.din("wsT", [128, 4, 128]),
            "tril": C.din("tril", [128, 128]), "bsb": C.din("bsb", [128, 2, 128]), "gqk": C.din("gqk", [128, 2]),
            "v": C.dout("v", [4, NT, 128], BF16), "ybT": C.dout("ybT", [2, 128, NT], BF16),
        }
        qk_t = C.dout("qk", [4, 2, 128, NT], BF16)
        xg_t = C.dout("xg", [4, 128, NT], F32)
        io["qk"] = lambda h, which: qk_t[h, which]
        io["xg"] = lambda ch: xg_t[ch]
        emit_L1(C, io, NT)
        C.P.emit()
    return nc


def emit_L1(C, io, NT):
    ntile = NT // TT
    if True:
        P = C.P
        xT, g1, w_in, sgg, sgb, wsT, tril, bsb, gqk = (io[k] for k in ("xT", "g1", "w_in", "sgg", "sgb", "wsT", "tril", "bsb", "gqk"))
        qk_o, v_o, xg_o, yb_o = io["qk"], io["v"], io["xg"], io["ybT"]

        NW = 2560
        wb = C.sb("wb", [128, 8, NW], BF16)
        stage = C.sb("stage", [128, 3, 1536], F32)
        xt = C.sb("xt", [128, 2, 8, TT], F32)
        sq = C.sb("sq", [128, 8, TT], BF16)
        hT = C.sb("hT", [128, ntile, 8, TT], BF16)
        rstd = C.sb("rstd", [128, TT], F32)
        ones = C.sb("ones", [128, 128], BF16)
        bones = C.sb("bones", [128, 128], BF16)
        g1s = C.sb("g1s", [128, 8], F32)
        sgg_s = C.sb("sgg_s", [128, 256], F32)
        sgb_s = C.sb("sgb_s", [128, 256], F32)
        wsT_f = C.sb("wsT_f", [128, 4, 128], F32)
        tril_s = C.sb("tril_s", [128, 128], F32)
        wsT_b = C.sb("wsT_b", [128, 4, 128], BF16)
        bsb_s = C.sb("bsb_s", [128, 2, 128], F32)
        gqk_s = C.sb("gqk_s", [128, 2], F32)
        xg_s = C.sb("xg_s", [128, 2, TT], F32)
        u_s = C.sb("u_s", [128, 2, TT], F32)
        qsq = C.sb("qsq", [128, 2, TT], BF16)
        qr = C.sb("qr", [128, 2, TT], F32)
        qn = C.sb("qn", [128, 3, TT], BF16)
        stats = C.sb("stats", [128, 2, 6], F32)
        mv = C.sb("mv", [128, 2, 2], F32)
        vr = C.sb("vr", [128, 2, 1], F32)
        vtmp = C.sb("vtmp", [128, 2, 256], F32)
        vn = C.sb("vn", [128, 4, 256], BF16)
        vv_s = C.sb("vv_s", [128, 2, 4, 512], BF16)
        mtmp = C.sb("mtmp", [128, 2, TT], F32)
        yb_s = C.sb("yb_s", [128, 2, TT], BF16)
        psum = C.ps("psum", [128, 8, TT], F32)
        psr = Rot("ps", 8)

        P.op("dve", lambda e: e.memset(ones[:], 1.0), writes=["ones"])
        P.op("pool", lambda e: e.memset(bones[:], 0.0), writes=["bones"])
        P.op("pool", lambda e: e.memset(bones[0:64, 0:64], 1.0), writes=["bones"])
        P.op("pool", lambda e: e.memset(bones[64:128, 64:128], 1.0), writes=["bones"])
        for dst, src, key in ((g1s, g1, "g1s"), (sgg_s, sgg, "sgg"), (sgb_s, sgb, "sgb"), (wsT_f, wsT, "wsT_f"),
                              (tril_s, tril, "tril"), (bsb_s, bsb, "bsb"), (gqk_s, gqk, "gqk")):
            P.op("sp", lambda e, dst=dst, src=src: e.dma_start(out=dst[:], in_=src), writes=[key], dma="c_" + key)
        P.op("dve", lambda e: e.tensor_scalar(out=g1s[:], in0=g1s[:], scalar1=32.0, scalar2=None, op0=ALU.mult),
             reads=["g1s"], writes=["g1s"])
        P.op("dve", lambda e: e.tensor_scalar(out=gqk_s[:, 1:2], in0=gqk_s[:, 1:2], scalar1=8.0, scalar2=None, op0=ALU.mult),
             reads=["gqk"], writes=["gqk"])
        trb = tril_s[:].unsqueeze(1).broadcast_to([128, 4, 128])
        P.op("dve", lambda e: e.tensor_tensor(out=wsT_b[:], in0=wsT_f[:], in1=trb, op=ALU.mult),
             reads=["wsT_f", "tril"], writes=["wsT_b"])

        def load_x(t):
            b = t % 2
            src = xT[:, t * TT:(t + 1) * TT].rearrange("(c p) n -> p c n", p=128)
            for hh in range(2):
                P.op("sp", lambda e, b=b, src=src, hh=hh: e.dma_start(out=xt[:, b, 4 * hh:4 * hh + 4, :],
                                                                    in_=src[:, 4 * hh:4 * hh + 4, :]),
                     writes=[("xt", b)], dma="xld%d" % b)

        load_x(0)
        wbk = load_weight_bf16(C, wb, w_in[:, 0:NW], 1024, NW, "wb", stage, Rot("stage", 3), scale_ap=g1s,
                               engines=("act", "dve"), colblk=1280, scale_key="g1s")

        def mm_fm(bank, col0, b):
            def fn(e):
                for k in range(8):
                    i = e.matmul(psum[:, bank, :], lhsT=wb[:, k, col0:col0 + 128], rhs=hT[:, b, k, :],
                                 start=(k == 0), stop=(k == 7))
                return i
            return fn

        def mm_tm(bank, col0, ncol, b, blk):
            def fn(e):
                for k in range(8):
                    i = e.matmul(psum[:, bank, 0:ncol], lhsT=hT[:, b, k, blk * 128:(blk + 1) * 128],
                                 rhs=wb[:, k, col0:col0 + ncol], start=(k == 0), stop=(k == 7))
                return i
            return fn

        def norm(t):
            b = t % 2
            bank, pk = psr.next()
            rms_tile(C, xt[:, b], ("xt", b), hT[:, t], ("hT", t), sq, "sq", ones, psum[:, bank, :], pk,
                     rstd, "rstd", TT, 1024 * EPS)

        xg_keys = []
        for t in range(ntile):
            b = t
            t0 = t * TT
            hk = ("hT", t)
            if t + 1 < ntile:
                load_x(t + 1)
            norm(t)
            for ch in range(4):
                bank, pk = psr.next()
                P.op("pe", mm_fm(bank, ch * 128, b), reads=[hk] + wbk, writes=[pk])
                s = ch % 2
                P.op("act", lambda e, bank=bank, s=s: e.copy(out=xg_s[:, s, :], in_=psum[:, bank, :]),
                     reads=[pk], writes=[("xg_s", s)])
                P.op("sp", lambda e, ch=ch, s=s, t0=t0: e.dma_start(out=xg_o(ch)[:, t0:t0 + TT], in_=xg_s[:, s, :]),
                     reads=[("xg_s", s)], writes=[("xg_dram", t, ch)], dma="st_xg%d" % s)
                xg_keys.append(("xg_dram", t, ch))
        if "after_xg" in io:
            io["after_xg"](P, xg_keys)
        for t in range(ntile):
            b = t
            t0 = t * TT
            hk = ("hT", t)
            for n in range(2):
                bank, pk = psr.next()
                P.op("pe", mm_fm(bank, 512 + n * 128, b), reads=[hk] + wbk, writes=[pk])
                P.op("act", lambda e, bank=bank, n=n: e.copy(out=u_s[:, n, :], in_=psum[:, bank, :]),
                     reads=[pk], writes=[("u_s", n)])
            for blk in range(4):
                bank, pk = psr.next()
                s = blk % 2
                P.op("pe", mm_tm(bank, 768, 256, b, blk), reads=[hk] + wbk, writes=[pk])
                P.op("dve", lambda e, bank=bank, s=s: e.bn_stats(out=stats[:, s, :], in_=psum[:, bank, 0:256]),
                     reads=[pk], writes=[("stats", s)])
                P.op("dve", lambda e, s=s: e.bn_aggr(out=mv[:, s, :], in_=stats[:, s, :]),
                     reads=[("stats", s)], writes=[("mv", s)])
                P.op("act", lambda e, s=s: e.activation(out=vr[:, s, :], in_=mv[:, s, 1:2], func=AF.Ln, bias=EPS, scale=1.0),
                     reads=[("mv", s)], writes=[("vr", s)])
                P.op("act", lambda e, s=s: e.activation(out=vr[:, s, :], in_=vr[:, s, :], func=AF.Exp, scale=-0.5),
                     reads=[("vr", s)], writes=[("vr", s)])
                P.op("dve", lambda e, bank=bank, s=s: e.tensor_scalar(
                    out=vtmp[:, s, :], in0=psum[:, bank, 0:256], scalar1=mv[:, s, 0:1], scalar2=vr[:, s, :],
                    op0=ALU.subtract, op1=ALU.mult), reads=[pk, ("mv", s), ("vr", s)], writes=[("vtmp", s)])
                P.op("dve", lambda e, s=s: e.tensor_tensor(out=vtmp[:, s, :], in0=vtmp[:, s, :], in1=sgg_s[:], op=ALU.mult),
                     reads=[("vtmp", s), "sgg"], writes=[("vtmp", s)])
                P.op("dve", lambda e, s=s, blk=blk: e.tensor_tensor(out=vn[:, blk, :], in0=vtmp[:, s, :], in1=sgb_s[:], op=ALU.add),
                     reads=[("vtmp", s), "sgb"], writes=[("vn", blk)])
            for n in range(2):
                bank, pk = psr.next()

                def mix(e, bank=bank, n=n):
                    for blk in range(4):
                        for gg in range(2):
                            g = 2 * n + gg
                            i = e.matmul(psum[gg * 64:(gg + 1) * 64, bank, blk * 128:(blk + 1) * 128],
                                         lhsT=vn[:, blk, g * 64:(g + 1) * 64], rhs=wsT_b[:, g, :],
                                         start=True, stop=True)
                    return i
                P.op("pe", mix, reads=[("vn", 0), ("vn", 1), ("vn", 2), ("vn", 3), "wsT_b"], writes=[pk])
                bsv = bsb_s[:, n, :].unsqueeze(1).broadcast_to([128, 4, 128])
                P.op("dve", lambda e, bank=bank, n=n, bsv=bsv: e.tensor_tensor(
                    out=mtmp[:, n, :].rearrange("p (b t) -> p b t", b=4),
                    in0=psum[:, bank, :].rearrange("p (b t) -> p b t", b=4), in1=bsv, op=ALU.add),
                    reads=[pk, "bsb"], writes=[("mtmp", n)])
                P.op("dve", lambda e, n=n: e.tensor_tensor(out=yb_s[:, n, :], in0=mtmp[:, n, :], in1=u_s[:, n, :], op=ALU.mult),
                     reads=[("mtmp", n), ("u_s", n)], writes=[("yb_s", n)])
                P.op("sp", lambda e, n=n, t0=t0: e.dma_start(out=yb_o[n, :, t0:t0 + TT], in_=yb_s[:, n, :]),
                     reads=[("yb_s", n)], dma="st_yb%d" % n)
            qrot = 0
            for which in range(2):
                for h in range(4):
                    col0 = 1024 + which * 512 + h * 128
                    bank, pk = psr.next()
                    bank2, pk2 = psr.next()
                    s = qrot % 2
                    s3 = qrot % 3
                    qrot += 1
                    P.op("pe", mm_fm(bank, col0, b), reads=[hk] + wbk, writes=[pk])
                    P.op("act", lambda e, bank=bank, s=s: e.activation(out=qsq[:, s, :], in_=psum[:, bank, :], func=AF.Square),
                         reads=[pk], writes=[("qsq", s)])
                    P.op("pe", lambda e, bank2=bank2, s=s: e.matmul(psum[:, bank2, :], lhsT=bones[:], rhs=qsq[:, s, :],
                                                                    start=True, stop=True),
                         reads=[("qsq", s), "bones"], writes=[pk2])
                    rstd_from_ps(P, psum[:, bank2, :], pk2, qr[:, s, :], ("qr", s), TT, 64 * EPS)
                    P.op("dve", lambda e, bank=bank, s=s, s3=s3, which=which: e.scalar_tensor_tensor(
                        out=qn[:, s3, :], in0=psum[:, bank, :], scalar=gqk_s[:, which:which + 1], in1=qr[:, s, :],
                        op0=ALU.mult, op1=ALU.mult), reads=[pk, ("qr", s), "gqk"], writes=[("qn", s3)])
                    P.op("sp", lambda e, h=h, which=which, s3=s3, t0=t0: e.dma_start(
                        out=qk_o(h, which)[:, t0:t0 + TT], in_=qn[:, s3, :]),
                        reads=[("qn", s3)], dma="st_qn%d" % s3)
            vb = t % 2
            for blk in range(4):
                bank, pk = psr.next()
                P.op("pe", mm_tm(bank, 2048, 512, b, blk), reads=[hk] + wbk, writes=[pk])
                eng = "act" if blk % 2 == 0 else "dve"
                if eng == "act":
                    fn = lambda e, bank=bank, blk=blk, vb=vb: e.copy(out=vv_s[:, vb, blk, :], in_=psum[:, bank, :])
                else:
                    fn = lambda e, bank=bank, blk=blk, vb=vb: e.tensor_copy(out=vv_s[:, vb, blk, :], in_=psum[:, bank, :])
                P.op(eng, fn, reads=[pk], writes=[("vv_s", vb, blk)])
            for h in range(4):
                P.op("sp", lambda e, vb=vb, t0=t0, h=h: e.dma_start(
                    out=v_o[h, t0:t0 + TT, :].rearrange("(b p) n -> p b n", p=128), in_=vv_s[:, vb, :, h * 128:(h + 1) * 128]),
                    reads=[("vv_s", vb, k) for k in range(4)], dma="st_vv%d" % vb)


def build_L2(S):
    nc = bass.Bass("TRN2", target_bir_lowering=False)
    with contextlib.ExitStack() as es:
        C = Ctx(nc, es)
        qT = C.din("qT", [128, S], BF16)
        kT = C.din("kT", [128, S], BF16)
        v = C.din("v", [S, 128], BF16)
        xaT = C.din("xaT", [64, S])
        gaT = C.din("gaT", [64, S])
        io = {
            "nsrc": 1,
            "q_src": lambda e, j: qT, "k_src": lambda e, j: kT, "v_src": lambda e, j: v,
            "xa_src": lambda e, j: xaT, "ga_src": lambda e, j: gaT,
            "cw": C.din("cw", [64, 4]), "lvec": C.din("lvec", [64, 4]), "wa": C.din("wa", [64, 64]), "wi": C.din("wi", [64, 64]),
            "lq": C.din("lq", [128, 4, 64]), "subg": C.din("subg", [128, 128]), "rb": C.din("rb", [128, 32]),
            "idx": C.din("idx", [128, 12, 128]), "maskT": C.din("maskT", [128, 128]), "ident": C.din("ident", [128, 128]),
            "lcon": C.din("lcon", [128, 2]),
        }
        yc_t = C.dout("ycT", [128, S], BF16)
        ya_t = C.dout("yaT", [64, S], BF16)
        io["ycT"] = lambda c0, n: yc_t[:, c0:c0 + n]
        io["yaT"] = lambda c0, n: ya_t[:, c0:c0 + n]
        emit_L2(C, io, S)
        C.P.emit()
    return nc


def emit_L2(C, io, S):
    NG = S // TT
    NKT = S // 128
    TL = 512
    NCH = S // TL
    nsrc = io["nsrc"]
    NS = S // nsrc
    if True:
        P = C.P
        cw, lvec, wa, wi, lq, subg, rb, idx, maskT, ident, lcon = (io[k] for k in (
            "cw", "lvec", "wa", "wi", "lq", "subg", "rb", "idx", "maskT", "ident", "lcon"))
        yc_o, ya_o = io["ycT"], io["yaT"]

        q0T = C.sb("q0T", [128, S], BF16)
        q1T = C.sb("q1T", [128, S], BF16)
        kTs = C.sb("kTs", [128, S], BF16)
        v1 = C.sb("v1", [128, NKT, 129], BF16)
        biasT = C.sb("biasT", [128, 12, 128], BF16)
        bias_f = C.sb("bias_f", [128, 12, 128], F32)
        idx_s = C.sb("idx_s", [128, 12, 128], F32)
        btmp = C.sb("btmp", [128, 2, 12, 128], F32)
        mask_s = C.sb("mask_s", [128, 128], F32)
        id_f = C.sb("id_f", [128, 128], F32)
        id_b = C.sb("id_b", [128, 128], BF16)
        rb_s = C.sb("rb_s", [128, 32], F32)
        lq_s = C.sb("lq_s", [128, 4, 64], F32)
        lq_t = C.sb("lq_t", [128, 2, 64], F32)
        lsc = C.sb("lsc", [128, 8], F32)
        lcon_s = C.sb("lcon_s", [128, 2], F32)
        subg_s = C.sb("subg_s", [128, 128], F32)
        pt = C.sb("pt", [128, 3, 2, TT], BF16)
        accs = C.sb("accs", [128, 3, TT], F32)
        rl = C.sb("rl", [128, 8], F32)
        o_s = C.sb("o_s", [128, 4, 128], F32)
        osq = C.sb("osq", [128, 128], F32)
        ss = C.sb("ss", [128, 4], F32)
        y_s = C.sb("y_s", [128, 4, 128], BF16)
        yc_s = C.sb("yc_s", [128, 2, TT], BF16)
        cw_s = C.sb("cw_s", [64, 4], F32)
        lv_s = C.sb("lv_s", [64, 4], F32)
        csc = C.sb("csc", [64, 4], F32)
        w_f = C.sb("w_f", [64, 2, 64], F32)
        w_b = C.sb("w_b", [64, 2, 64], BF16)
        xa_s = C.sb("xa_s", [64, 2, TL + 3], F32)
        ga_s = C.sb("ga_s", [64, 2, TL], F32)
        xc = C.sb("xc", [64, TL], F32)
        xcb = C.sb("xcb", [64, TL], BF16)
        r_s = C.sb("r_s", [64, TL], F32)
        i_s = C.sb("i_s", [64, TL], F32)
        a_s = C.sb("a_s", [64, TL], F32)
        m_s = C.sb("m_s", [64, TL], F32)
        h_s = C.sb("h_s", [64, 2, TL], F32)
        g_s = C.sb("g_s", [64, TL], F32)
        ya_s = C.sb("ya_s", [64, 2, TL], BF16)
        psum = C.ps("psum", [128, 8, TT], F32)
        trp = psum[:, 7, :].bitcast(BF16)

        for dst, src, key in ((cw_s[:], cw, "cw"), (lv_s[:], lvec, "lv"), (w_f[:, 0, :], wa, "wa"), (w_f[:, 1, :], wi, "wi"),
                              (lq_s[:], lq, "lq"), (subg_s[:], subg, "subg"), (rb_s[:], rb, "rb"), (idx_s[:], idx, "idx"),
                              (mask_s[:], maskT, "mask"), (id_f[:], ident, "id_f"), (lcon_s[:], lcon, "lcon")):
            P.op("sp", lambda e, dst=dst, src=src: e.dma_start(out=dst, in_=src), writes=[key], dma="c_" + key)
        P.op("dve", lambda e: e.tensor_copy(out=id_b[:], in_=id_f[:]), reads=["id_f"], writes=["id_b"])
        P.op("dve", lambda e: e.tensor_copy(out=w_b[:], in_=w_f[:]), reads=["wa", "wi"], writes=["w_b"])
        for j in range(2):
            P.op("dve", lambda e, j=j: e.scalar_tensor_tensor(out=lq_t[:, j, :], in0=lq_s[:, 2 * j, :], scalar=1.0,
                                                              in1=lq_s[:, 2 * j + 1, :], op0=ALU.mult, op1=ALU.mult,
                                                              accum_out=lsc[:, j:j + 1]),
                 reads=["lq"], writes=[("lsc", j)])
        P.op("act", lambda e: e.activation(out=lsc[:, 2:4], in_=lsc[:, 0:2], func=AF.Exp),
             reads=[("lsc", 0), ("lsc", 1)], writes=["lsce"])
        P.op("dve", lambda e: e.tensor_tensor(out=lsc[:, 4:5], in0=lsc[:, 2:3], in1=lsc[:, 3:4], op=ALU.subtract),
             reads=["lsce"], writes=["lam"])
        P.op("dve", lambda e: e.tensor_tensor(out=lsc[:, 4:5], in0=lsc[:, 4:5], in1=lcon_s[:, 0:1], op=ALU.add),
             reads=["lam", "lcon"], writes=["lam"])
        P.op("dve", lambda e: e.tensor_scalar(out=lsc[:, 5:6], in0=lsc[:, 4:5], scalar1=-1.0, scalar2=None, op0=ALU.mult),
             reads=["lam"], writes=["nlam"])
        P.op("dve", lambda e: e.tensor_scalar(out=subg_s[:], in0=subg_s[:], scalar1=lcon_s[:, 1:2], scalar2=None, op0=ALU.mult),
             reads=["subg", "lcon"], writes=["subg"])
        P.op("act", lambda e: e.activation(out=csc[:, 0:1], in_=lv_s[:, 3:4], func=AF.Exp, scale=-1.0),
             reads=["lv"], writes=["csc0"])
        P.op("act", lambda e: e.activation(out=csc[:, 0:1], in_=csc[:, 0:1], func=AF.Ln, bias=1.0, scale=1.0),
             reads=["csc0"], writes=["csc0"])
        P.op("dve", lambda e: e.tensor_scalar(out=csc[:, 1:2], in0=csc[:, 0:1], scalar1=-LRU_C, scalar2=None, op0=ALU.mult),
             reads=["csc0"], writes=["csc"])
        P.op("dve", lambda e: e.tensor_scalar(out=csc[:, 2:3], in0=csc[:, 0:1], scalar1=-2.0 * LRU_C, scalar2=None, op0=ALU.mult),
             reads=["csc0"], writes=["csc"])

        P.op("dve", lambda e: e.memset(bias_f[:], 0.0), writes=["bias_f"])
        idx_np = l2_consts()["idx"]
        for d_ in range(12):
            for bkt in sorted(set(int(v_) for v_ in np.unique(idx_np[:, d_, :]))):
                P.op("dve", lambda e, bkt=bkt, d_=d_: e.tensor_single_scalar(
                    out=btmp[:, 0, d_, :], in_=idx_s[:, d_, :], scalar=float(bkt), op=ALU.is_equal),
                    reads=["idx"], writes=[("btmp", 0)])
                P.op("dve", lambda e, bkt=bkt, d_=d_: e.scalar_tensor_tensor(
                    out=bias_f[:, d_, :], in0=btmp[:, 0, d_, :], scalar=rb_s[:, bkt:bkt + 1], in1=bias_f[:, d_, :],
                    op0=ALU.mult, op1=ALU.add), reads=[("btmp", 0), "rb", "bias_f"], writes=["bias_f"])
        P.op("dve", lambda e: e.tensor_tensor(out=bias_f[:, 0, :], in0=bias_f[:, 0, :], in1=mask_s[:], op=ALU.add),
             reads=["bias_f", "mask"], writes=["bias_f"])
        P.op("dve", lambda e: e.tensor_scalar(out=bias_f[:], in0=bias_f[:], scalar1=rb_s[:, 15:16], scalar2=None, op0=ALU.subtract),
             reads=["bias_f", "rb"], writes=["bias_f"])
        P.op("dve", lambda e: e.tensor_copy(out=biasT[:], in_=bias_f[:]), reads=["bias_f"], writes=["biasT"])

        for ch in range(NCH):
            b = ch % 2
            t0 = ch * TL
            js = t0 // NS
            l0 = t0 - js * NS
            if ch == 0:
                P.op("dve", lambda e: e.memset(xa_s[:, 0, 0:3], 0.0), writes=[("xa", 0)])
                P.op("sp", lambda e: e.dma_start(out=xa_s[:, 0, 3:3 + TL], in_=io["xa_src"](e, 0)[:, 0:TL]),
                     reads=io.get("xa_dep", []), writes=[("xa", 0)], dma="ld_xa0")
            else:
                P.op("dve", lambda e, b=b: e.tensor_copy(out=xa_s[:, b, 0:3], in_=xa_s[:, 1 - b, TL:TL + 3]),
                     reads=[("xa", 1 - b)], writes=[("xa", b)])
                P.op("sp", lambda e, b=b, js=js, l0=l0: e.dma_start(out=xa_s[:, b, 3:3 + TL], in_=io["xa_src"](e, js)[:, l0:l0 + TL]),
                     reads=io.get("xa_dep", []), writes=[("xa", b)], dma="ld_xa%d" % b)
            P.op("sp", lambda e, b=b, js=js, l0=l0: e.dma_start(out=ga_s[:, b, :], in_=io["ga_src"](e, js)[:, l0:l0 + TL]),
                 reads=io.get("ga_dep", []), writes=[("ga", b)], dma="ld_ga%d" % b)
            P.op("dve", lambda e, b=b: e.tensor_scalar(out=xc[:], in0=xa_s[:, b, 3:3 + TL], scalar1=cw_s[:, 3:4],
                                                       scalar2=lv_s[:, 0:1], op0=ALU.mult, op1=ALU.add),
                 reads=[("xa", b), "cw", "lv"], writes=["xc"])
            for tap in range(3):
                P.op("dve", lambda e, b=b, tap=tap: e.scalar_tensor_tensor(
                    out=xc[:], in0=xa_s[:, b, tap:tap + TL], scalar=cw_s[:, tap:tap + 1], in1=xc[:],
                    op0=ALU.mult, op1=ALU.add), reads=[("xa", b), "cw", "xc"], writes=["xc"])
            P.op("act", lambda e: e.copy(out=xcb[:], in_=xc[:]), reads=["xc"], writes=["xcb"])
            nh = TL // TT
            for gate in range(2):
                for hh in range(nh):
                    bank = gate * nh + hh
                    P.op("pe", lambda e, gate=gate, hh=hh, bank=bank: e.matmul(
                        psum[0:64, bank, :], lhsT=w_b[:, gate, :], rhs=xcb[:, hh * TT:(hh + 1) * TT], start=True, stop=True),
                        reads=["xcb", "w_b"], writes=[("st", bank // 2)])
            for hh in range(nh):
                P.op("act", lambda e, hh=hh: e.activation(out=r_s[:, hh * TT:(hh + 1) * TT], in_=psum[0:64, hh, :],
                                                          func=AF.Sigmoid, bias=lv_s[:, 1:2], scale=1.0),
                     reads=[("st", hh // 2), "lv"], writes=["r_s"])
            for hh in range(nh):
                P.op("act", lambda e, hh=hh: e.activation(out=i_s[:, hh * TT:(hh + 1) * TT], in_=psum[0:64, nh + hh, :],
                                                          func=AF.Sigmoid, bias=lv_s[:, 2:3], scale=1.0),
                     reads=[("st", (nh + hh) // 2), "lv"], writes=["i_s"])
            P.op("act", lambda e, b=b: e.activation(out=g_s[:], in_=ga_s[:, b, :], func=AF.Square),
                 reads=[("ga", b)], writes=["g_s"])
            P.op("act", lambda e: e.activation(out=g_s[:], in_=g_s[:], func=AF.Identity, bias=1.0, scale=0.044715),
                 reads=["g_s"], writes=["g_s"])
            P.op("dve", lambda e, b=b: e.tensor_tensor(out=g_s[:], in0=g_s[:], in1=ga_s[:, b, :], op=ALU.mult),
                 reads=["g_s", ("ga", b)], writes=["g_s"])
            P.op("act", lambda e: e.activation(out=g_s[:], in_=g_s[:], func=AF.Sigmoid, scale=1.5957691216057308),
                 reads=["g_s"], writes=["g_s"])
            P.op("dve", lambda e, b=b: e.tensor_tensor(out=g_s[:], in0=g_s[:], in1=ga_s[:, b, :], op=ALU.mult),
                 reads=["g_s", ("ga", b)], writes=["g_s"])
            P.op("act", lambda e: e.activation(out=a_s[:], in_=r_s[:], func=AF.Exp, scale=csc[:, 1:2]),
                 reads=["r_s", "csc"], writes=["a_s"])
            P.op("act", lambda e: e.activation(out=m_s[:], in_=r_s[:], func=AF.Exp, scale=csc[:, 2:3]),
                 reads=["r_s", "csc"], writes=["m_s"])
            P.op("act", lambda e: e.activation(out=m_s[:], in_=m_s[:], func=AF.Sqrt, bias=1.0, scale=-1.0),
                 reads=["m_s"], writes=["m_s"])
            P.op("dve", lambda e: e.tensor_tensor(out=i_s[:], in0=i_s[:], in1=xc[:], op=ALU.mult),
                 reads=["i_s", "xc"], writes=["i_s"])
            P.op("dve", lambda e: e.tensor_tensor(out=m_s[:], in0=m_s[:], in1=i_s[:], op=ALU.mult),
                 reads=["m_s", "i_s"], writes=["m_s"])
            init = 0.0 if ch == 0 else h_s[:, 1 - b, TL - 1:TL]
            P.op("dve", lambda e, b=b, init=init: e.tensor_tensor_scan(out=h_s[:, b, :], data0=a_s[:], data1=m_s[:],
                                                                       initial=init, op0=ALU.mult, op1=ALU.add),
                 reads=["a_s", "m_s", ("h_s", 1 - b)], writes=[("h_s", b)])
            P.op("dve", lambda e, b=b: e.tensor_tensor(out=ya_s[:, b, :], in0=h_s[:, b, :], in1=g_s[:], op=ALU.mult),
                 reads=[("h_s", b), "g_s"], writes=[("ya_s", b)])
            P.op("sp", lambda e, b=b, t0=t0: e.dma_start(out=ya_o(t0, TL), in_=ya_s[:, b, :]),
                 reads=[("ya_s", b)], writes=[("ya_dram", ch)], dma="st_ya%d" % b)
        if "after_ya" in io:
            io["after_ya"](P, NCH)

        if "pre_attn" in io:
            io["pre_attn"](P)
        P.op("dve", lambda e: e.memset(q0T[64:128, :], 0.0), writes=["q0T"])
        P.op("dve", lambda e: e.memset(q1T[0:64, :], 0.0), writes=["q1T"])
        P.op("dve", lambda e: e.memset(v1[:, :, 128:129], 1.0), writes=["v1ones"])
        if "load_qk" in io:
            io["load_qk"](P, q0T, q1T, kTs)
        else:
            P.op("sp", lambda e: e.dma_start(out=q0T[0:64, :], in_=io["q_src"](e, 0)[0:64, :]), writes=["q0T"], dma="ld_q0")
            P.op("sp", lambda e: e.dma_start(out=q1T[64:128, :], in_=io["q_src"](e, 0)[64:128, :]), writes=["q1T"], dma="ld_q1")
            P.op("sp", lambda e: e.dma_start(out=kTs[:], in_=io["k_src"](e, 0)), writes=["kTs"], dma="ld_k")
        nvd = max(nsrc, NKT // 16)
        for i in range(nvd):
            a, b_ = i * NKT // nvd, (i + 1) * NKT // nvd
            j = (a * 128) // NS
            ra = a * 128 - j * NS
            rb_ = b_ * 128 - j * NS
            P.op("sp", lambda e, a=a, b_=b_, j=j, ra=ra, rb_=rb_: e.dma_start(
                out=v1[:, a:b_, 0:128], in_=io["v_src"](e, j)[ra:rb_, :].rearrange("(kt p) e -> p kt e", p=128)),
                reads=io.get("v_dep", []), writes=[("v1", i)], dma="ld_v%d" % i)

        def acc_ap(a, lo=0, hi=129):
            return psum[:, 4 + a // 3, (a % 3) * 160 + lo:(a % 3) * 160 + hi]

        def accs_ap(a, lo=0, hi=129):
            return accs[:, a // 3, (a % 3) * 160 + lo:(a % 3) * 160 + hi]

        pairs = [(G, kt) for G in range(NG) for kt in range(4 * G + 4)]

        def geom(G, kt):
            jj = max(kt - 4 * G, 0)
            nblk = 4 - jj
            d0 = 4 * G + jj - kt
            return jj, nblk, d0, (d0 <= 8)

        def emit_qk(n):
            G, kt = pairs[n]
            jj, nblk, d0, near = geom(G, kt)
            sb_ = n % 2
            c0, c1 = jj * 128, TT
            q0 = G * TT + c0

            def fn(e):
                for c, qsrc in ((0, q0T), (1, q1T)):
                    i = e.matmul(psum[:, 2 * sb_ + c, c0:c1], lhsT=kTs[:, kt * 128:(kt + 1) * 128],
                                 rhs=qsrc[:, q0:q0 + nblk * 128], start=True, stop=not near)
                    if near:
                        i = e.matmul(psum[:, 2 * sb_ + c, c0:c1], lhsT=id_b[:],
                                     rhs=biasT[:, d0:d0 + nblk, :], start=False, stop=True)
                return i
            P.op("pe", fn, reads=["q0T", "q1T", "kTs", "id_b", "biasT"], writes=[("st", sb_)])

        def emit_exp(n):
            G, kt = pairs[n]
            jj, nblk, d0, near = geom(G, kt)
            sb_, pb = n % 2, n % 3
            c0 = jj * 128
            src = psum[:, 2 * sb_:2 * sb_ + 2, c0:TT]
            dst = pt[:, pb, :, c0:TT]
            fn = lambda e: e.activation(out=dst, in_=src, func=AF.Exp)
            P.op("act", fn, reads=[("st", sb_)], writes=[("pt", pb)])

        def emit_pv(n):
            G, kt = pairs[n]
            jj, nblk, d0, near = geom(G, kt)
            pb = n % 3

            def fn(e):
                for i_ in range(jj, 4):
                    for c in range(2):
                        i = e.matmul(acc_ap(c * 4 + i_), lhsT=pt[:, pb, c, i_ * 128:(i_ + 1) * 128], rhs=v1[:, kt, :],
                                     start=False, stop=False, skip_group_check=True)
                return i
            P.op("pe", fn, reads=[("pt", pb), ("v1", kt * nvd // NKT), "v1ones"], writes=["acc"])

        def emit_evac(G):
            yb_ = G % 2
            P.op("dve", lambda e: e.tensor_copy(out=accs[:], in_=psum[:, 4:7, :]), reads=["acc"], writes=["accs"])
            P.op("dve", lambda e: e.reciprocal(out=rl[:, 0:6].rearrange("p (a b) -> p a b", a=2),
                                               in_=accs[:, 0:2, 128:449:160]), reads=["accs"], writes=["rl"])
            P.op("dve", lambda e: e.reciprocal(out=rl[:, 6:8], in_=accs[:, 2, 128:289:160]), reads=["accs"], writes=["rl"])
            P.op("dve", lambda e: e.tensor_scalar(out=rl[:, 4:8], in0=rl[:, 4:8], scalar1=lsc[:, 5:6], scalar2=None, op0=ALU.mult),
                 reads=["rl", "nlam"], writes=["rl"])
            for i_ in range(4):
                P.op("dve", lambda e, i_=i_: e.tensor_scalar(out=o_s[:, i_, :], in0=accs_ap(i_, 0, 128), scalar1=rl[:, i_:i_ + 1],
                                                             scalar2=None, op0=ALU.mult),
                     reads=["accs", "rl"], writes=[("o_s", i_)])
                P.op("dve", lambda e, i_=i_: e.scalar_tensor_tensor(out=o_s[:, i_, :], in0=accs_ap(4 + i_, 0, 128),
                                                                    scalar=rl[:, 4 + i_:5 + i_], in1=o_s[:, i_, :],
                                                                    op0=ALU.mult, op1=ALU.add),
                     reads=["accs", "rl", ("o_s", i_)], writes=[("o_s", i_)])
                P.op("dve", lambda e, i_=i_: e.scalar_tensor_tensor(out=osq[:], in0=o_s[:, i_, :], scalar=1.0, in1=o_s[:, i_, :],
                                                                    op0=ALU.mult, op1=ALU.mult, accum_out=ss[:, i_:i_ + 1]),
                     reads=[("o_s", i_)], writes=["osq", ("ss", i_)])
            P.op("act", lambda e: e.activation(out=ss[:], in_=ss[:], func=AF.Ln, bias=128 * EPS, scale=1.0),
                 reads=[("ss", k) for k in range(4)], writes=["ssr"])
            P.op("act", lambda e: e.activation(out=ss[:], in_=ss[:], func=AF.Exp, scale=-0.5), reads=["ssr"], writes=["ssr"])
            for i_ in range(4):
                P.op("dve", lambda e, i_=i_: e.scalar_tensor_tensor(out=y_s[:, i_, :], in0=o_s[:, i_, :], scalar=ss[:, i_:i_ + 1],
                                                                    in1=subg_s[:], op0=ALU.mult, op1=ALU.mult),
                     reads=[("o_s", i_), "ssr", "subg"], writes=[("y_s", i_)])

            def tr(e):
                for i_ in range(4):
                    i = e.transpose(trp[:, i_ * 128:(i_ + 1) * 128], y_s[:, i_, :], id_b[:])
                return i
            P.op("pe", tr, reads=[("y_s", k) for k in range(4)] + ["id_b"], writes=["trp"])
            P.op("dve", lambda e, yb_=yb_: e.tensor_copy(out=yc_s[:, yb_, :], in_=trp[:, 0:TT]), reads=["trp"], writes=[("yc_s", yb_)])
            P.op("sp", lambda e, yb_=yb_, G=G: e.dma_start(out=yc_o(G * TT, TT), in_=yc_s[:, yb_, :]),
                 reads=[("yc_s", yb_)], writes=[("yc_dram", G)], dma="st_yc%d" % yb_)
            if "after_yc" in io:
                io["after_yc"](P, G, NG)

        emit_qk(0)
        for n, (G, kt) in enumerate(pairs):
            if n + 1 < len(pairs):
                emit_qk(n + 1)
            if kt == 0:
                P.op("dve", lambda e: e.memset(psum[:, 4:7, :], 0.0), writes=["acc"])
            emit_exp(n)
            emit_pv(n)
            if kt == 4 * G + 3:
                emit_evac(G)


T3 = 256


def build_L3(NT):
    nc = bass.Bass("TRN2", target_bir_lowering=False)
    H = FFN_HIDDEN
    with contextlib.ExitStack() as es:
        C = Ctx(nc, es)
        yaT = C.din("yaT", [256, NT], BF16)
        ybT = C.din("ybT", [256, NT], BF16)
        ycT = C.din("ycT", [512, NT], BF16)

        def y_src(e, kind, i, c0, n):
            if kind == "ya":
                return yaT[64 * i:64 * i + 64, c0:c0 + n]
            if kind == "yb":
                return ybT[128 * i:128 * i + 128, c0:c0 + n]
            return ycT[128 * i:128 * i + 128, c0:c0 + n]
        xo = C.dout("xo", [1024, NT])
        io = {
            "x_in": C.din("xT", [1024, NT]), "x_mid": xo, "x_out": xo,
            "g1": C.din("g1", [128, 8]), "g2": C.din("g2", [128, 8]), "w_in": C.din("w_in", [1024, IN_COLS]),
            "bg": C.din("bg", [128, 24]), "y_src": y_src,
            "w_pa": C.din("w_pa", [256, 1024]), "w_pb": C.din("w_pb", [256, 1024]), "w_pc": C.din("w_pc", [512, 1024]),
            "w_o": C.din("w_o", [1024, 1024]), "w_g": C.din("w_g", [1024, H]), "w_u": C.din("w_u", [1024, H]),
            "w_d": C.din("w_d", [H, 1024]),
        }
        emit_L3(C, io, NT)
        C.P.emit()
    return nc


def emit_L3(C, io, NT):
    ntile = NT // T3
    H = FFN_HIDDEN
    HC = H // 128
    if True:
        P = C.P
        xT, xmid, xo = io["x_in"], io["x_mid"], io["x_out"]
        g1, g2, w_in, bg, w_pa, w_pb, w_pc, w_o, w_g, w_u, w_d = (io[k] for k in (
            "g1", "g2", "w_in", "bg", "w_pa", "w_pb", "w_pc", "w_o", "w_g", "w_u", "w_d"))

        wbuf = C.sb("wbuf", [128, 3 * 8 * H], BF16)
        stage = C.sb("stage", [128, 2, 1024], F32)
        xt = C.sb("xt", [128, 2, 8, T3], F32)
        sq = C.sb("sq", [128, 8, T3], BF16)
        hT = C.sb("hT", [128, 2, 8, T3], BF16)
        rstd = C.sb("rstd", [128, T3], F32)
        ones = C.sb("ones", [128, 128], BF16)
        g1s = C.sb("g1s", [128, 8], F32)
        g2s = C.sb("g2s", [128, 8], F32)
        bg_s = C.sb("bg_s", [128, 24], F32)
        y_s = C.sb("y_s", [128, 2, 8, T3], BF16)
        gs = C.sb("gs", [128, 2, 3, T3], F32)
        mt = C.sb("mt", [128, 2, 3, T3], F32)
        mT = C.sb("mT", [128, 8, T3], BF16)
        sg = C.sb("sg", [128, 2, T3], F32)
        actT = C.sb("actT", [128, HC, T3], BF16)
        psum = C.ps("psum", [128, 8, 2, T3], F32)
        psr = Rot("ps", 8)

        def wview(off, kc, n):
            return wbuf[:, off:off + kc * n].rearrange("p (c n) -> p c n", c=kc)
        wgt_ = wview(0, 8, 3072)
        wpa_ = wview(24576, 2, 1024)
        wpb_ = wview(24576 + 2048, 2, 1024)
        wpc_ = wview(24576 + 4096, 4, 1024)
        wo_ = wview(24576 + 8192, 8, 1024)
        fg_ = wview(0, 8, H)
        fu_ = wview(8 * H, 8, H)
        fd_ = wview(16 * H, HC, 1024)

        P.op("dve", lambda e: e.memset(ones[:], 1.0), writes=["ones"])
        for dst, src, key in ((g1s, g1, "g1s"), (g2s, g2, "g2s"), (bg_s, bg, "bg")):
            P.op("sp", lambda e, dst=dst, src=src: e.dma_start(out=dst[:], in_=src), writes=[key], dma="c_" + key)
        for t_, k_ in ((g1s, "g1s"), (g2s, "g2s")):
            P.op("dve", lambda e, t_=t_: e.tensor_scalar(out=t_[:], in0=t_[:], scalar1=32.0, scalar2=None, op0=ALU.mult),
                 reads=[k_], writes=[k_])

        def load_x(t, src_ap, srckeys):
            b = t % 2
            src = src_ap[:, t * T3:(t + 1) * T3].rearrange("(c p) n -> p c n", p=128)
            P.op("sp", lambda e, b=b, src=src: e.dma_start(out=xt[:, b, :, :], in_=src),
                 reads=srckeys, writes=[("xt", b)], dma="xld%d" % b)

        def load_y(t):
            b = t % 2
            c0 = t * T3
            for h in range(4):
                P.op("sp", lambda e, b=b, h=h, c0=c0: e.dma_start(
                    out=y_s[(h % 2) * 64:(h % 2) * 64 + 64, b, h // 2, :], in_=io["y_src"](e, "ya", h, c0, T3)),
                    reads=io.get("y_dep", []), writes=[("y_s", b, 0)], dma="yld%d_0" % b)
            for i in range(2):
                P.op("sp", lambda e, b=b, i=i, c0=c0: e.dma_start(out=y_s[:, b, 2 + i, :], in_=io["y_src"](e, "yb", i, c0, T3)),
                     writes=[("y_s", b, 1)], dma="yld%d_1" % b)
            for h in range(4):
                P.op("sp", lambda e, b=b, h=h, c0=c0: e.dma_start(out=y_s[:, b, 4 + h, :], in_=io["y_src"](e, "yc", h, c0, T3)),
                     reads=io.get("y_dep", []), writes=[("y_s", b, 2)], dma="yld%d_2" % b)

        srot = Rot("stage", 2)
        load_x(0, xT, [])
        kg = load_weight_bf16(C, wgt_, w_in[:, 2560:5632], 1024, 3072, "wgt", stage, srot, scale_ap=g1s,
                              colblk=1024, scale_key="g1s")
        kpa = load_weight_bf16(C, wpa_, w_pa, 256, 1024, "wpa", stage, srot, colblk=1024)
        kpb = load_weight_bf16(C, wpb_, w_pb, 256, 1024, "wpb", stage, srot, colblk=1024)
        kpc = load_weight_bf16(C, wpc_, w_pc, 512, 1024, "wpc", stage, srot, colblk=1024)
        ko = load_weight_bf16(C, wo_, w_o, 1024, 1024, "wo", stage, srot, colblk=1024)
        c1keys = kg + kpa + kpb + kpc + ko
        if "pre_y" in io:
            io["pre_y"](P)
        load_y(0)

        def mm_fm(e, bank, half, wv, kc, col0, rhs_fn):
            for k in range(kc):
                i = e.matmul(psum[:, bank, half, :], lhsT=wv[:, k, col0:col0 + 128], rhs=rhs_fn(k),
                             start=(k == 0), stop=(k == kc - 1))
            return i

        def norm(t):
            b = t % 2
            bank, pk = psr.next()
            rms_tile(C, xt[:, b], ("xt", b), hT[:, b], ("hT", b), sq, "sq", ones, psum[:, bank, 0, :], pk, rstd, "rstd", T3, 1024 * EPS)

        def resid(t, wv, kc, rhs_fn, rkeys, outkey, xdst):
            b = t % 2
            for m in range(4):
                bank, pk = psr.next()

                def fn(e, bank=bank, m=m):
                    for hf in range(2):
                        i = mm_fm(e, bank, hf, wv, kc, (2 * m + hf) * 128, rhs_fn)
                    return i
                P.op("pe", fn, reads=rkeys, writes=[pk])
                P.op("dve", lambda e, bank=bank, m=m, b=b: e.tensor_tensor(
                    out=xt[:, b, 2 * m:2 * m + 2, :], in0=xt[:, b, 2 * m:2 * m + 2, :], in1=psum[:, bank, :, :], op=ALU.add),
                    reads=[pk, ("xt", b)], writes=[("xt", b)])
            dst = xdst[:, t * T3:(t + 1) * T3].rearrange("(c p) n -> p c n", p=128)
            P.op("sp", lambda e, b=b, dst=dst: e.dma_start(out=dst, in_=xt[:, b, :, :]),
                 reads=[("xt", b)], writes=[(outkey, t)], dma="st_x%d" % b)

        projs = ((wpa_, 2, 0, kpa), (wpb_, 2, 2, kpb), (wpc_, 4, 4, kpc))
        for t in range(ntile):
            b = t % 2
            if t + 1 < ntile:
                load_x(t + 1, xT, [])
                load_y(t + 1)
            if t == 0:
                norm(0)
            for n in range(8):
                gb = n % 2
                slots = []
                for br in range(3):
                    bank, pk = psr.next()
                    slots.append((bank, pk))
                    wv, kc, off, kk = projs[br]

                    def fn(e, bank=bank, br=br, n=n, wv=wv, kc=kc, off=off, b=b):
                        mm_fm(e, bank, 0, wgt_, 8, br * 1024 + n * 128, lambda k: hT[:, b, k, :])
                        return mm_fm(e, bank, 1, wv, kc, n * 128, lambda k: y_s[:, b, off + k, :])
                    P.op("pe", fn, reads=[("hT", b), ("y_s", b, br)] + kg + kk, writes=[pk])
                for br in range(3):
                    bank, pk = slots[br]
                    ch = br * 8 + n
                    P.op("act", lambda e, bank=bank, br=br, ch=ch, gb=gb: e.activation(
                        out=gs[:, gb, br, :], in_=psum[:, bank, 0, :], func=AF.Sigmoid, bias=bg_s[:, ch:ch + 1], scale=1.0),
                        reads=[pk, "bg"], writes=[("gs", gb, br)])
                    P.op("dve", lambda e, bank=bank, br=br, gb=gb: e.tensor_tensor(
                        out=mt[:, gb, br, :], in0=psum[:, bank, 1, :], in1=gs[:, gb, br, :], op=ALU.mult),
                        reads=[pk, ("gs", gb, br)], writes=[("mt", gb, br)])
                P.op("dve", lambda e, gb=gb: e.tensor_tensor(out=mt[:, gb, 0, :], in0=mt[:, gb, 0, :], in1=mt[:, gb, 1, :], op=ALU.add),
                     reads=[("mt", gb, 0), ("mt", gb, 1)], writes=[("mt", gb, 0)])
                P.op("dve", lambda e, gb=gb, n=n: e.tensor_tensor(out=mT[:, n, :], in0=mt[:, gb, 0, :], in1=mt[:, gb, 2, :], op=ALU.add),
                     reads=[("mt", gb, 0), ("mt", gb, 2)], writes=[("mT", n)])
            if t + 1 < ntile:
                norm(t + 1)
            resid(t, wo_, 8, lambda k: mT[:, k, :], [("mT", k) for k in range(8)] + ko, "xo", xmid)

        kfg = load_weight_bf16(C, fg_, w_g, 1024, H, "fg", stage, srot, scale_ap=g2s, colblk=1024, scale_key="g2s",
                               also_writes=c1keys)
        kfu = load_weight_bf16(C, fu_, w_u, 1024, H, "fu", stage, srot, scale_ap=g2s, colblk=1024, scale_key="g2s",
                               also_writes=c1keys)
        kfd = load_weight_bf16(C, fd_, w_d, H, 1024, "fd", stage, srot, colblk=1024, also_writes=c1keys)
        load_x(0, xmid, [("xo", 0)])
        for t in range(ntile):
            b = t % 2
            if t + 1 < ntile:
                load_x(t + 1, xmid, [("xo", t + 1)])
            if t == 0:
                norm(0)
            for j in range(HC):
                sb_ = j % 2
                bank, pk = psr.next()

                def fn(e, bank=bank, j=j, b=b):
                    mm_fm(e, bank, 0, fg_, 8, j * 128, lambda k: hT[:, b, k, :])
                    return mm_fm(e, bank, 1, fu_, 8, j * 128, lambda k: hT[:, b, k, :])
                P.op("pe", fn, reads=[("hT", b)] + kfg + kfu, writes=[pk])
                P.op("act", lambda e, bank=bank, sb_=sb_: e.activation(out=sg[:, sb_, :], in_=psum[:, bank, 0, :], func=AF.Silu),
                     reads=[pk], writes=[("sg", sb_)])
                P.op("dve", lambda e, bank=bank, sb_=sb_, j=j: e.tensor_tensor(out=actT[:, j, :], in0=psum[:, bank, 1, :], in1=sg[:, sb_, :], op=ALU.mult),
                     reads=[pk, ("sg", sb_)], writes=[("actT", j)])
            if t + 1 < ntile:
                norm(t + 1)
            resid(t, fd_, HC, lambda k: actT[:, k, :], [("actT", k) for k in range(HC)] + kfd, "xo2", xo)


def _c(a):
    return np.ascontiguousarray(a, dtype=np.float32)


def l1_inputs(xT, inp, l):
    sgw = inp["sg_w"][l]
    sgb = inp["sg_b"][l]
    p = np.arange(128)
    bsb = np.stack([sgb[2 * n + p // 64, :] for n in range(2)], axis=1)
    gqk = np.stack([inp["q_norm_g"][l][p % 64], inp["k_norm_g"][l][p % 64]], axis=1)
    return {
        "xT": _c(xT),
        "g1": _c(inp["ln1_g"][l].reshape(8, 128).T),
        "w_in": _c(inp["w_in"][l]),
        "sgg": _c(np.broadcast_to(inp["sg_ln_g"][l], (128, 256))),
        "sgb": _c(np.broadcast_to(inp["sg_ln_b"][l], (128, 256))),
        "wsT": _c(sgw.transpose(2, 0, 1)),
        "tril": _c(np.triu(np.ones((128, 128)))),
        "bsb": _c(bsb),
        "gqk": _c(gqk),
    }


def t5_bucket_np(rel):
    import jax
    import jax.numpy as jnp
    with jax.default_device(jax.devices("cpu")[0]):
        rel = jnp.asarray(rel, jnp.int32)
        half, max_exact = 16, 8
        ret = jnp.where(rel > 0, half, 0)
        n = jnp.abs(rel)
        nf = jnp.maximum(n, 1).astype(jnp.float32)
        large = max_exact + (jnp.log(nf / max_exact) / math.log(2048 / max_exact) * (half - max_exact)).astype(jnp.int32)
        large = jnp.minimum(large, half - 1)
        return np.asarray(ret + jnp.where(n < max_exact, n, large))


_L2_CONST = {}


def l2_consts():
    if not _L2_CONST:
        k = np.arange(128)[:, None, None]
        d = np.arange(12)[None, :, None]
        q = np.arange(128)[None, None, :]
        rel = k - q - 128 * d
        _L2_CONST["idx"] = _c(t5_bucket_np(rel))
        kk = np.arange(128)[:, None]
        qq = np.arange(128)[None, :]
        _L2_CONST["maskT"] = _c(np.where((kk // 64) > (qq // 64), NEG, 0.0))
        _L2_CONST["ident"] = _c(np.eye(128))
    return _L2_CONST


def l2_inputs(qT, kT, v, xaT, gaT, inp, l, h):
    lam_init = 0.8 - 0.6 * math.exp(-0.3 * l)
    cs = l2_consts()
    ch = slice(64 * h, 64 * h + 64)
    lvec = np.stack([inp["conv_b"][l][ch], inp["lru_ba"][l][ch], inp["lru_bi"][l][ch], inp["lru_lambda"][l][ch]], axis=1)
    lq = np.stack([inp["lambda_q1"][l], inp["lambda_k1"][l], inp["lambda_q2"][l], inp["lambda_k2"][l]], axis=0)
    return {
        "qT": qT, "kT": kT, "v": v, "xaT": _c(xaT), "gaT": _c(gaT),
        "cw": _c(inp["conv_w"][l][:, ch].T),
        "lvec": _c(lvec),
        "wa": _c(inp["lru_wa"][l][h]), "wi": _c(inp["lru_wi"][l][h]),
        "lq": _c(np.broadcast_to(lq, (128, 4, 64))),
        "subg": _c(np.broadcast_to(inp["subln_g"][l], (128, 128))),
        "rb": _c(np.broadcast_to(inp["rel_bias"][:, h], (128, 32))),
        "idx": cs["idx"], "maskT": cs["maskT"], "ident": cs["ident"],
        "lcon": _c(np.broadcast_to(np.array([lam_init, (1.0 - lam_init) * math.sqrt(128.0)]), (128, 2))),
    }


def l3_inputs(xT, yaT, ybT, ycT, inp, l):
    return {
        "xT": _c(xT),
        "g1": _c(inp["ln1_g"][l].reshape(8, 128).T),
        "g2": _c(inp["ln2_g"][l].reshape(8, 128).T),
        "w_in": _c(inp["w_in"][l]),
        "bg": _c(inp["b_gate"][l].reshape(24, 128).T),
        "yaT": yaT, "ybT": ybT, "ycT": ycT,
        "w_pa": _c(inp["w_pa"][l]), "w_pb": _c(inp["w_pb"][l]), "w_pc": _c(inp["w_pc"][l]), "w_o": _c(inp["w_o"][l]),
        "w_g": _c(inp["w_ff_gate"][l]), "w_u": _c(inp["w_ff_up"][l]), "w_d": _c(inp["w_ff_down"][l]),
    }


ARENA_BYTES = 212736
GROUPS = [[0, 1, 2, 3], [4, 5, 6, 7]]


def build_fused(S, depth=DEPTH):
    NT = S // 4
    H = FFN_HIDDEN
    L = depth
    nc = bass.Bass("TRN2", target_bir_lowering=False)
    with contextlib.ExitStack() as es:
        C = Ctx(nc, es)
        C.use_arena(ARENA_BYTES)
        P = C.P
        P.use_rank = True
        xT = C.din("xT", [1024, NT])
        xo = C.dout("xo", [1024, NT])
        pin = {}
        for name, shape in (("g1", [L, 128, 8]), ("g2", [L, 128, 8]), ("w_in", [L, 1024, IN_COLS]),
                            ("sgg", [L, 128, 256]), ("sgb", [L, 128, 256]), ("wsT", [L, 128, 4, 128]),
                            ("tril", [128, 128]), ("bsb", [L, 128, 2, 128]), ("gqk", [L, 128, 2]),
                            ("cw", [L, 64, 4]), ("lvec", [L, 64, 4]), ("wa", [L, 64, 64]), ("wi", [L, 64, 64]),
                            ("lq", [L, 128, 4, 64]), ("subg", [L, 128, 128]), ("rb", [128, 32]),
                            ("idx", [128, 12, 128]), ("maskT", [128, 128]), ("ident", [128, 128]), ("lcon", [L, 128, 2]),
                            ("bg", [L, 128, 24]), ("w_pa", [L, 256, 1024]), ("w_pb", [L, 256, 1024]),
                            ("w_pc", [L, 512, 1024]), ("w_o", [L, 1024, 1024]), ("w_g", [L, 1024, H]),
                            ("w_u", [L, 1024, H]), ("w_d", [L, H, 1024])):
            pin[name] = C.din(name, shape)

        def dint(name, shape, dt):
            return nc.dram_tensor(name, list(shape), dt).ap()
        q_in = dint("q_in", [4, 128, NT], BF16)
        q_out = dint("q_out", [4, 512, NT], BF16)
        k_in = dint("k_in", [4, 128, NT], BF16)
        k_out = dint("k_out", [4, 512, NT], BF16)
        v_in = dint("v_in", [4, NT, 128], BF16)
        v_out = dint("v_out", [4, 4 * NT, 128], BF16)
        xa_in = dint("xa_in", [4, 64, NT], F32)
        xa_out = dint("xa_out", [4, 256, NT], F32)
        ga_in = dint("ga_in", [4, 64, NT], F32)
        ga_out = dint("ga_out", [4, 256, NT], F32)
        yc_in = dint("yc_in", [4, 128, NT], BF16)
        yc_out = dint("yc_out", [4, 512, NT], BF16)
        ya_in = dint("ya_in", [4, 64, NT], BF16)
        ya_out = dint("ya_out", [4, 256, NT], BF16)
        v_loc = dint("v_loc", [S, 128], BF16)
        xg_loc = dint("xg_loc", [2, 4, 64, NT], F32)
        yc_loc = dint("yc_loc", [4, 128, NT], BF16)
        ya_loc = dint("ya_loc", [4, 64, NT], BF16)
        yb_x = dint("yb_x", [2, 128, NT], BF16)
        xs1 = dint("xs1", [1024, NT], F32)
        xs2 = dint("xs2", [1024, NT], F32)

        P.dyn_spec = {}

        def rk(name="r"):
            return P.dyn[name]

        def allgather(name, src, dst, reads=()):
            P.op("pool", lambda e: e.collective_compute("AllGather", ALU.bypass, replica_groups=GROUPS, ins=[src], outs=[dst]),
                 reads=list(reads), writes=["cc_" + name], dma="cc_" + name, inc=1)

        for l in range(L):
            par = 0
            x_in = xT if l == 0 else xs2
            x_out = xo if l == L - 1 else xs2
            C.arena_reset()
            io1 = {"xT": x_in, "g1": pin["g1"][l], "w_in": pin["w_in"][l], "sgg": pin["sgg"][l], "sgb": pin["sgb"][l],
                   "wsT": pin["wsT"][l], "tril": pin["tril"], "bsb": pin["bsb"][l], "gqk": pin["gqk"][l],
                   "qk": lambda h, which: (q_in, k_in)[which][h],
                   "v": v_in,
                   "xg": lambda ch: (xa_in, ga_in)[ch // 2][2 * (ch % 2):2 * (ch % 2) + 2].rearrange("h p n -> (h p) n"),
                   "ybT": yb_x}

            def after_xg(P_, keys):
                for h in range(4):
                    allgather("xa", xa_in[h], xa_out[h], reads=keys)
                    allgather("ga", ga_in[h], ga_out[h], reads=keys)
            io1["after_xg"] = after_xg
            emit_L1(C, io1, NT)
            P.barrier()
            for h in range(4):
                allgather("q", q_in[h], q_out[h])
                allgather("k", k_in[h], k_out[h])
                allgather("v", v_in[h], v_out[h])
            C.arena_reset()
            for a_, srcg in ((0, xa_out), (1, ga_out)):
                P.op("sp", lambda e, a_=a_, srcg=srcg: e.dma_start(
                    out=xg_loc[a_].rearrange("(o j) p n -> o j p n", o=1),
                    in_=srcg.rearrange("h (j p) n -> h j p n", j=4)[bass.ds(rk(), 1), :, :, :]),
                    reads=["cc_xa", "cc_ga"], writes=[("xg_loc", a_)], dma="loc_xg%d" % a_)

            def load_qk(P_, q0T, q1T, kTs):
                def v3(t, rows):
                    return t[rows, :].rearrange("p (j n) -> p j n", j=4)

                def src(g, rows):
                    return g.rearrange("h (j p) n -> h j p n", j=4)[bass.ds(rk(), 1), :, rows, :].rearrange("o j p n -> p (o j) n")
                P_.op("sp", lambda e: e.dma_start(out=v3(q0T, slice(0, 64)), in_=src(q_out, slice(0, 64))),
                      reads=["cc_q"], writes=["q0T"], dma="ld_q0")
                P_.op("sp", lambda e: e.dma_start(out=v3(q1T, slice(64, 128)), in_=src(q_out, slice(64, 128))),
                      reads=["cc_q"], writes=["q1T"], dma="ld_q1")
                P_.op("sp", lambda e: e.dma_start(out=v3(kTs, slice(0, 128)), in_=src(k_out, slice(0, 128))),
                      reads=["cc_k"], writes=["kTs"], dma="ld_k")

            def pre_attn(P_):
                P_.op("sp", lambda e: e.dma_start(out=v_loc.rearrange("(o t) e -> o t e", o=1), in_=v_out[bass.ds(rk(), 1), :, :]),
                      reads=["cc_v"], writes=["v_loc"], dma="loc_v")

            def after_ya(P_, nch):
                per = nch // 4
                for j in range(4):
                    allgather("ya", ya_in[j], ya_out[j], reads=[("ya_dram", c_) for c_ in range(j * per, (j + 1) * per)])

            def after_yc(P_, G, ng):
                per = ng // 4
                if (G + 1) % per == 0:
                    j = G // per
                    allgather("yc", yc_in[j], yc_out[j], reads=[("yc_dram", g_) for g_ in range(j * per, (j + 1) * per)])
            io2 = {"nsrc": 4, "load_qk": load_qk, "after_ya": after_ya, "after_yc": after_yc, "pre_attn": pre_attn,
                   "v_dep": ["v_loc"], "xa_dep": [("xg_loc", 0)], "ga_dep": [("xg_loc", 1)],
                   "v_src": lambda e, j: v_loc[j * NT:(j + 1) * NT, :],
                   "xa_src": lambda e, j: xg_loc[0, j], "ga_src": lambda e, j: xg_loc[1, j],
                   "cw": pin["cw"][l], "lvec": pin["lvec"][l], "wa": pin["wa"][l], "wi": pin["wi"][l], "lq": pin["lq"][l],
                   "subg": pin["subg"][l], "rb": pin["rb"], "idx": pin["idx"], "maskT": pin["maskT"], "ident": pin["ident"],
                   "lcon": pin["lcon"][l],
                   "ycT": lambda c0, n: yc_in[c0 // NT, :, c0 % NT:c0 % NT + n],
                   "yaT": lambda c0, n: ya_in[c0 // NT, :, c0 % NT:c0 % NT + n]}
            emit_L2(C, io2, S)
            P.barrier(exclude=("cc_yc", "cc_ya"), keep=("cc_yc", "cc_ya"))
            C.arena_reset()
            def pre_y(P_):
                P_.op("sp", lambda e: e.dma_start(out=yc_loc.rearrange("(o h) p n -> o h p n", o=1),
                                                  in_=yc_out.rearrange("j (h p) n -> j h p n", h=4)[bass.ds(rk(), 1), :, :, :]),
                      reads=["cc_yc"], writes=["yc_loc"], dma="loc_yc")
                P_.op("sp", lambda e: e.dma_start(out=ya_loc.rearrange("(o h) p n -> o h p n", o=1),
                                                  in_=ya_out.rearrange("j (h p) n -> j h p n", h=4)[bass.ds(rk(), 1), :, :, :]),
                      reads=["cc_ya"], writes=["ya_loc"], dma="loc_ya")

            def y_src(e, kind, i, c0, n):
                if kind == "yb":
                    return yb_x[i, :, c0:c0 + n]
                if kind == "ya":
                    return ya_loc[i, :, c0:c0 + n]
                return yc_loc[i, :, c0:c0 + n]
            io3 = {"x_in": x_in, "x_mid": xs1, "x_out": x_out, "g1": pin["g1"][l], "g2": pin["g2"][l],
                   "w_in": pin["w_in"][l], "bg": pin["bg"][l], "y_src": y_src, "pre_y": pre_y, "y_dep": ["yc_loc", "ya_loc"], "w_pa": pin["w_pa"][l],
                   "w_pb": pin["w_pb"][l], "w_pc": pin["w_pc"][l], "w_o": pin["w_o"][l], "w_g": pin["w_g"][l],
                   "w_u": pin["w_u"][l], "w_d": pin["w_d"][l]}
            emit_L3(C, io3, NT)
            P.barrier()
        P.emit()
    return nc


def fused_inputs(inp, c, S, depth=DEPTH):
    NT = S // 4
    b, r = c // 4, c % 4
    x = inp["x"]
    xT = np.ascontiguousarray(x[b, r * NT:(r + 1) * NT].T)
    dummy = np.zeros((2, 2), np.float32)
    l1 = [l1_inputs(dummy, inp, l) for l in range(depth)]
    l2 = [l2_inputs(None, None, None, dummy, dummy, inp, l, r) for l in range(depth)]
    l3 = [l3_inputs(dummy, None, None, None, inp, l) for l in range(depth)]

    def st(lst, k):
        return np.ascontiguousarray(np.stack([d[k] for d in lst], axis=0))
    m = {"xT": xT}
    for k in ("g1", "w_in", "sgg", "sgb", "wsT", "bsb", "gqk"):
        m[k] = st(l1, k)
    m["tril"] = l1[0]["tril"]
    for k in ("cw", "lvec", "wa", "wi", "lq", "subg", "lcon"):
        m[k] = st(l2, k)
    for k in ("rb", "idx", "maskT", "ident"):
        m[k] = l2[0][k]
    for k in ("g2", "bg", "w_pa", "w_pb", "w_pc", "w_o", "w_g", "w_u", "w_d"):
        m[k] = st(l3, k)
    return m


_PROGS = {}


def kernel(**inputs):
    inp = {k: np.asarray(v) for k, v in inputs.items()}
    x = inp["x"]
    B, S, D = x.shape
    NT = S // 4
    key = ("fused", S)
    if key not in _PROGS:
        _PROGS[key] = build_fused(S)
    nc = _PROGS[key]
    in_maps = [fused_inputs(inp, c, S) for c in range(N_CORES)]
    res = run_bass_kernel_spmd(nc, in_maps, core_ids=list(range(N_CORES))).results
    out = np.empty((B, S, D), dtype=np.float32)
    for c in range(N_CORES):
        out[c // 4, (c % 4) * NT:(c % 4 + 1) * NT] = np.asarray(res[c]["xo"]).T
    return out
```
